# Optimizing a Trainium2 kernel written in Bass

```python
import math
import jax, jax.numpy as jnp
from jax import lax
import numpy as np

D_MODEL = 1024
BATCH = 2
SEQ = 8192
DEPTH = 1
DEC_BATCH = 4
DEC_SEQ = 8192
PAST_LEN = 128

ATT_HEADS = 8
HEAD_DIM = 64
ATT_WIDTH = ATT_HEADS * HEAD_DIM
SSM_WIDTH = D_MODEL - ATT_WIDTH
SSM_GROUP_CH = 16
SSM_GROUPS = SSM_WIDTH // SSM_GROUP_CH
SSM_STATE = 64
MIX_WIDTH = ATT_WIDTH + SSM_WIDTH
IN_COLS = 3 * ATT_WIDTH + SSM_WIDTH
D_FF = 4 * D_MODEL
PLE_DIM = 256
NUM_BUCKETS = 32
REL_MAX_DISTANCE = 1024
DIL_WINDOWS = (128, 512, 2048)
DIL_RATES = (1, 4, 16)
DT_MIN = 0.001
DT_MAX = 0.1
RMS_EPS = 1e-6
NEG_INF = -1e30

kernel_name = 'hybrid_s5_dilated_attention_encoder'


def rms_norm(x, g):
    x32 = x.astype(jnp.float32)
    y = x32 * lax.rsqrt(jnp.mean(x32 * x32, axis=-1, keepdims=True) + RMS_EPS)
    return (y * g.astype(jnp.float32)).astype(x.dtype)


def t5_bucket(rel):
    half = NUM_BUCKETS // 2
    n = -rel
    ret = jnp.where(n < 0, half, 0)
    n = jnp.abs(n)
    max_exact = half // 2
    nf = jnp.maximum(n, 1).astype(jnp.float32)
    large = max_exact + (jnp.log(nf / max_exact) / math.log(REL_MAX_DISTANCE / max_exact)
                         * (half - max_exact)).astype(jnp.int32)
    large = jnp.minimum(large, half - 1)
    return ret + jnp.where(n < max_exact, n, large)


def dilated_branch(q, k, v, rel_bias, window, dilation):
    Bn, S, H, E = q.shape
    radius = window // (2 * dilation)
    blk = radius
    L = S // dilation
    nb = -(-L // blk)
    Lp = nb * blk

    def split(t):
        return t.reshape(Bn, L, dilation, H, E)

    qb = jnp.pad(split(q), ((0, 0), (0, Lp - L), (0, 0), (0, 0), (0, 0))).reshape(Bn, nb, blk, dilation, H, E)

    def windows(t):
        tp = jnp.pad(split(t), ((0, 0), (blk, Lp - L + blk), (0, 0), (0, 0), (0, 0)))
        tp = tp.reshape(Bn, nb + 2, blk, dilation, H, E)
        return jnp.concatenate([tp[:, :-2], tp[:, 1:-1], tp[:, 2:]], axis=2)

    kw = windows(k)
    vw = windows(v)
    offset = jnp.arange(3 * blk)[None, :] - blk - jnp.arange(blk)[:, None]
    key_m = (jnp.arange(nb)[:, None] - 1) * blk + jnp.arange(3 * blk)[None, :]
    mask = (jnp.abs(offset) <= radius)[None] & ((key_m >= 0) & (key_m < L))[:, None, :]
    bias = rel_bias[t5_bucket(offset * dilation)].astype(jnp.float32).transpose(2, 0, 1)
    s = jnp.einsum('bnqrhe,bnkrhe->bnrhqk', qb, kw) * (HEAD_DIM ** -0.5) + bias
    s = jnp.where(mask[None, :, None, None], s, NEG_INF)
    m = jnp.max(s, axis=-1, keepdims=True)
    pr = jnp.exp(s - m)
    den = jnp.sum(pr, axis=-1)
    o = jnp.einsum('bnrhqk,bnkrhe->bnqrhe', pr, vw) / den.transpose(0, 1, 4, 2, 3)[..., None]
    lse = (m[..., 0] + jnp.log(den)).transpose(0, 1, 4, 2, 3)
    o = o.reshape(Bn, Lp, dilation, H, E)[:, :L].reshape(Bn, S, H, E)
    lse = lse.reshape(Bn, Lp, dilation, H)[:, :L].reshape(Bn, S, H)
    return o, lse


def attention_mixer(q, k, v, rel_bias):
    Bn, S, _ = q.shape
    q = q.astype(jnp.float32).reshape(Bn, S, ATT_HEADS, HEAD_DIM)
    k = k.astype(jnp.float32).reshape(Bn, S, ATT_HEADS, HEAD_DIM)
    v = v.astype(jnp.float32).reshape(Bn, S, ATT_HEADS, HEAD_DIM)
    outs, lses = [], []
    for w, d in zip(DIL_WINDOWS, DIL_RATES):
        o, l = dilated_branch(q, k, v, rel_bias, w, d)
        outs.append(o)
        lses.append(l)
    wts = jax.nn.softmax(jnp.stack(lses, axis=0), axis=0)
    o = jnp.sum(wts[..., None] * jnp.stack(outs, axis=0), axis=0)
    return o.reshape(Bn, S, ATT_WIDTH)


def _complex_affine_combine(e1, e2):
    a1r, a1i, b1r, b1i = e1
    a2r, a2i, b2r, b2i = e2
    return (a2r * a1r - a2i * a1i,
            a2r * a1i + a2i * a1r,
            a2r * b1r - a2i * b1i + b2r,
            a2r * b1i + a2i * b1r + b2i)


def ssm_direction(u, a_re, a_im, log_dt, b_re, b_im, c_re, c_im, reverse):
    f32 = jnp.float32
    a_re = a_re.astype(f32)
    a_im = a_im.astype(f32)
    dt = jnp.exp(log_dt.astype(f32))[:, None]
    mag = jnp.exp(a_re * dt)
    ab_re = mag * jnp.cos(a_im * dt)
    ab_im = mag * jnp.sin(a_im * dt)
    inv = 1.0 / (a_re * a_re + a_im * a_im)
    f_re = ((ab_re - 1.0) * a_re + ab_im * a_im) * inv
    f_im = (ab_im * a_re - (ab_re - 1.0) * a_im) * inv
    b_re = b_re.astype(f32)
    b_im = b_im.astype(f32)
    bb_re = f_re[..., None] * b_re - f_im[..., None] * b_im
    bb_im = f_re[..., None] * b_im + f_im[..., None] * b_re
    bu_re = jnp.einsum('bsgh,gnh->bsgn', u, bb_re)
    bu_im = jnp.einsum('bsgh,gnh->bsgn', u, bb_im)
    elems = (jnp.broadcast_to(ab_re, bu_re.shape), jnp.broadcast_to(ab_im, bu_re.shape), bu_re, bu_im)
    _, _, h_re, h_im = lax.associative_scan(_complex_affine_combine, elems, reverse=reverse, axis=1)
    return (jnp.einsum('bsgn,ghn->bsgh', h_re, c_re.astype(f32))
            - jnp.einsum('bsgn,ghn->bsgh', h_im, c_im.astype(f32)))


def ssm_mixer(u, a_re, a_im, log_dt, b_re, b_im, c_re, c_im, d, w_glu, b_glu):
    Bn, S, _ = u.shape
    u32 = u.astype(jnp.float32).reshape(Bn, S, SSM_GROUPS, SSM_GROUP_CH)
    y = d.astype(jnp.float32) * u32
    for direction in range(2):
        y = y + ssm_direction(u32, a_re[direction], a_im[direction], log_dt[direction],
                              b_re[direction], b_im[direction], c_re[direction], c_im[direction],
                              reverse=(direction == 1))
    g = jax.nn.gelu(y.reshape(Bn, S, SSM_WIDTH))
    out = g * jax.nn.sigmoid(g @ w_glu.astype(jnp.float32) + b_glu.astype(jnp.float32))
    return out


def encoder_layer(h, p_i, rel_bias, g_mix, w_in, a_re, a_im, log_dt, b_re, b_im, c_re, c_im, d,
                  w_glu, b_glu, g_att_out, g_ssm_out, w_out, g_mlp, w_mlp1, w_mlp2,
                  g_ple, w_ple_gate, w_ple_proj):
    a = rms_norm(h, g_mix)
    z = a @ w_in
    q = z[..., :ATT_WIDTH]
    k = z[..., ATT_WIDTH:2 * ATT_WIDTH]
    v = z[..., 2 * ATT_WIDTH:3 * ATT_WIDTH]
    u = z[..., 3 * ATT_WIDTH:]
    att = attention_mixer(q, k, v, rel_bias).astype(h.dtype)
    ssm = ssm_mixer(u, a_re, a_im, log_dt, b_re, b_im, c_re, c_im, d, w_glu, b_glu).astype(h.dtype)
    mix = jnp.concatenate([rms_norm(att, g_att_out), rms_norm(ssm, g_ssm_out)], axis=-1)
    h = h + mix @ w_out
    f = rms_norm(h, g_mlp)
    h = h + jnp.square(jax.nn.relu(f @ w_mlp1)) @ w_mlp2
    e = rms_norm(h, g_ple)
    h = h + jax.nn.sigmoid(e @ w_ple_gate) * (p_i @ w_ple_proj)
    return h


def encoder_trunk(x, p, weights):
    (rel_bias, g_mix, w_in, ssm_a_re, ssm_a_im, ssm_log_dt, ssm_b_re, ssm_b_im, ssm_c_re, ssm_c_im,
     ssm_d, w_glu, b_glu, g_att_out, g_ssm_out, w_out, g_mlp, w_mlp1, w_mlp2, g_ple, w_ple_gate,
     w_ple_proj, g_final) = weights
    h = x
    for i in range(DEPTH):
        h = encoder_layer(h, p[i], rel_bias, g_mix[i], w_in[i], ssm_a_re[i], ssm_a_im[i], ssm_log_dt[i],
                          ssm_b_re[i], ssm_b_im[i], ssm_c_re[i], ssm_c_im[i], ssm_d[i], w_glu[i], b_glu[i],
                          g_att_out[i], g_ssm_out[i], w_out[i], g_mlp[i], w_mlp1[i], w_mlp2[i],
                          g_ple[i], w_ple_gate[i], w_ple_proj[i])
    return rms_norm(h, g_final)


def setup_inputs(seed: int = 0) -> dict:
    key = jax.random.key(seed)
    ks = jax.random.split(key, 32)
    f32 = jnp.float32

    def nrm(k, shape, s):
        return jax.random.normal(k, shape, f32) * s

    G, N, HC = SSM_GROUPS, SSM_STATE, SSM_GROUP_CH
    n_idx = jnp.arange(N, dtype=f32)
    return {
        'x_prompt': nrm(ks[0], (BATCH, SEQ, D_MODEL), 1.0),
        'x_sample': nrm(ks[1], (DEC_BATCH, DEC_SEQ, D_MODEL), 1.0),
        'p_prompt': nrm(ks[2], (DEPTH, BATCH, SEQ, PLE_DIM), 1.0),
        'p_sample': nrm(ks[3], (DEPTH, DEC_BATCH, DEC_SEQ, PLE_DIM), 1.0),
        'rel_bias': nrm(ks[4], (NUM_BUCKETS, ATT_HEADS), 0.5),
        'g_mix': 1.0 + nrm(ks[5], (DEPTH, D_MODEL), 0.02),
        'w_in': nrm(ks[6], (DEPTH, D_MODEL, IN_COLS), D_MODEL ** -0.5),
        'ssm_a_re': -0.5 + nrm(ks[7], (DEPTH, 2, G, N), 0.01),
        'ssm_a_im': math.pi * n_idx + nrm(ks[8], (DEPTH, 2, G, N), 0.01),
        'ssm_log_dt': jax.random.uniform(ks[9], (DEPTH, 2, G), f32, math.log(DT_MIN), math.log(DT_MAX)),
        'ssm_b_re': nrm(ks[10], (DEPTH, 2, G, N, HC), (2 * HC) ** -0.5),
        'ssm_b_im': nrm(ks[11], (DEPTH, 2, G, N, HC), (2 * HC) ** -0.5),
        'ssm_c_re': nrm(ks[12], (DEPTH, 2, G, HC, N), N ** -0.5),
        'ssm_c_im': nrm(ks[13], (DEPTH, 2, G, HC, N), N ** -0.5),
        'ssm_d': nrm(ks[14], (DEPTH, G, HC), 0.5),
        'w_glu': nrm(ks[15], (DEPTH, SSM_WIDTH, SSM_WIDTH), SSM_WIDTH ** -0.5),
        'b_glu': nrm(ks[16], (DEPTH, SSM_WIDTH), 0.02),
        'g_att_out': 1.0 + nrm(ks[17], (DEPTH, ATT_WIDTH), 0.02),
        'g_ssm_out': 1.0 + nrm(ks[18], (DEPTH, SSM_WIDTH), 0.02),
        'w_out': nrm(ks[19], (DEPTH, MIX_WIDTH, D_MODEL), MIX_WIDTH ** -0.5),
        'g_mlp': 1.0 + nrm(ks[20], (DEPTH, D_MODEL), 0.02),
        'w_mlp1': nrm(ks[21], (DEPTH, D_MODEL, D_FF), D_MODEL ** -0.5),
        'w_mlp2': nrm(ks[22], (DEPTH, D_FF, D_MODEL), D_FF ** -0.5),
        'g_ple': 1.0 + nrm(ks[23], (DEPTH, D_MODEL), 0.02),
        'w_ple_gate': nrm(ks[24], (DEPTH, D_MODEL, D_MODEL), D_MODEL ** -0.5),
        'w_ple_proj': nrm(ks[25], (DEPTH, PLE_DIM, D_MODEL), PLE_DIM ** -0.5),
        'g_final': 1.0 + nrm(ks[26], (D_MODEL,), 0.02),
    }


def reference(x_prompt, x_sample, p_prompt, p_sample, rel_bias, g_mix, w_in, ssm_a_re, ssm_a_im, ssm_log_dt,
              ssm_b_re, ssm_b_im, ssm_c_re, ssm_c_im, ssm_d, w_glu, b_glu, g_att_out, g_ssm_out, w_out,
              g_mlp, w_mlp1, w_mlp2, g_ple, w_ple_gate, w_ple_proj, g_final):
    weights = (rel_bias, g_mix, w_in, ssm_a_re, ssm_a_im, ssm_log_dt, ssm_b_re, ssm_b_im, ssm_c_re, ssm_c_im,
               ssm_d, w_glu, b_glu, g_att_out, g_ssm_out, w_out, g_mlp, w_mlp1, w_mlp2, g_ple, w_ple_gate,
               w_ple_proj, g_final)
    y_prompt = encoder_trunk(x_prompt, p_prompt, weights)
    y_sample = encoder_trunk(x_sample, p_sample, weights)
    return (y_prompt, y_sample)
```

```python
import math
import numpy as np
from contextlib import ExitStack
import concourse.bass as bass
import concourse.mybir as mybir
from concourse.bass_utils import run_bass_kernel_spmd

F32 = mybir.dt.float32
BF16 = mybir.dt.bfloat16
I32 = mybir.dt.int32
AF = mybir.ActivationFunctionType
ALU = mybir.AluOpType

ENGS = ("pe", "act", "dve", "pool", "sp")
D = 1024
EPS = 1e-6


class Buf:
    __slots__ = ("name", "last_w", "readers")

    def __init__(self, name=""):
        self.name = name
        self.last_w = None
        self.readers = []


class Op:
    __slots__ = ("eng", "fn", "reads", "writes", "is_dma", "deps", "idx", "signal", "count", "sem", "semkey")

    def __init__(self, eng, fn, reads, writes, is_dma):
        self.eng = eng
        self.fn = fn
        self.reads = reads
        self.writes = writes
        self.is_dma = is_dma
        self.deps = set()
        self.signal = False
        self.count = 0
        self.sem = None
        self.semkey = None


class Prog:
    def __init__(self, nc, n_dma_sems=40):
        self.nc = nc
        self.ops = []
        self.n_dma_sems = n_dma_sems
        self.ALL = Buf("ALL")

    def op(self, eng, fn, reads=(), writes=()):
        self._add(Op(eng, fn, tuple(reads) + (self.ALL,), tuple(writes), False))

    def dma(self, queue, fn, reads=(), writes=()):
        self._add(Op(queue, fn, tuple(reads) + (self.ALL,), tuple(writes), True))

    def barrier(self):
        self._add(Op("dve", lambda e: e.nop() if False else e.memset(self._bar[:], 0.0), (), (self.ALL,), False))

    def _add(self, o):
        o.idx = len(self.ops)
        for b in o.reads:
            if b.last_w is not None:
                o.deps.add(b.last_w)
        for b in o.writes:
            if b.last_w is not None:
                o.deps.add(b.last_w)
            for r in b.readers:
                o.deps.add(r)
        for b in o.reads:
            b.readers.append(o.idx)
        for b in o.writes:
            b.last_w = o.idx
            b.readers = []
        o.deps.discard(o.idx)
        self.ops.append(o)

    def emit(self, stack):
        nc = self.nc
        ops = self.ops
        needed = []
        for o in ops:
            nd = []
            for d in o.deps:
                y = ops[d]
                if y.is_dma or o.is_dma or y.eng != o.eng:
                    nd.append(d)
                elif o.eng != "pe":
                    if any((b in y.writes) for b in o.reads if b is not self.ALL) or \
                       any((b in y.writes or b in y.reads) for b in o.writes if b is not self.ALL):
                        nd.append(d)
            needed.append(nd)
            for d in nd:
                ops[d].signal = True
        for o in ops:
            if o.is_dma:
                o.signal = True
        eng_sem = {e: stack.enter_context(nc.semaphore("s_" + e)) for e in ENGS}
        nqs = {"sp": self.n_dma_sems - 16, "pool": 4, "act": 6, "dve": 6}
        dma_sems = {q: [stack.enter_context(nc.semaphore("d%s%d" % (q, i))) for i in range(nqs[q])] for q in nqs}
        cnt = {e: 0 for e in ENGS}
        dcnt = {q: [0] * nqs[q] for q in nqs}
        rr = {q: 0 for q in nqs}
        for o in ops:
            if o.is_dma:
                q = o.eng
                nq = nqs[q]
                k = rr[q] % nq
                rr[q] += 1
                dcnt[q][k] += 16
                o.sem, o.count, o.semkey = dma_sems[q][k], dcnt[q][k], ("d", q, k)
            elif o.signal:
                cnt[o.eng] += 1
                o.sem, o.count, o.semkey = eng_sem[o.eng], cnt[o.eng], ("e", o.eng)
        waited = {e: {} for e in ENGS}
        block = stack.enter_context(nc.Block())
        per_eng = {e: [] for e in ENGS}
        for o in ops:
            per_eng[o.eng].append(o)
        final_waits = {}
        for o in ops:
            if o.is_dma:
                final_waits[o.semkey] = (o.sem, o.count)

        def make(ename):
            def body(eng):
                w = waited[ename]
                for o in per_eng[ename]:
                    req = {}
                    for d in needed[o.idx]:
                        y = ops[d]
                        if req.get(y.semkey, (None, 0))[1] < y.count:
                            req[y.semkey] = (y.sem, y.count)
                    for key, (sem, c) in req.items():
                        if w.get(key, 0) < c:
                            eng.wait_ge(sem, c)
                            w[key] = c
                    if o.is_dma and o.count > 16 and w.get(o.semkey, 0) < o.count - 16:
                        eng.wait_ge(o.sem, o.count - 16)
                        w[o.semkey] = o.count - 16
                    ins = o.fn(eng)
                    if o.signal:
                        ins.then_inc(o.sem, 16 if o.is_dma else 1)
                if ename == "sp":
                    for key, (sem, c) in final_waits.items():
                        if w.get(key, 0) < c:
                            eng.wait_ge(sem, c)
                            w[key] = c
            return body

        block.tensor(make("pe"))
        block.scalar(make("act"))
        block.vector(make("dve"))
        block.gpsimd(make("pool"))
        block.sync(make("sp"))


class T:
    __slots__ = ("ap", "b")

    def __init__(self, ap, name=""):
        self.ap = ap
        self.b = Buf(name)


class Arena:
    def __init__(self, big, ncols):
        self.big = big
        self.ncols = ncols
        self.off = 0
        self.marks = []

    def push(self):
        self.marks.append(self.off)

    def pop(self):
        self.off = self.marks.pop()

    def alloc(self, free_shape, dtype, name=""):
        n = int(np.prod(free_shape))
        n32 = n if dtype in (F32, I32) else (n + 1) // 2
        assert self.off + n32 <= self.ncols, ("SBUF arena overflow", name, self.off, n32, self.ncols)
        v = self.big[:, self.off:self.off + n32]
        self.off += n32
        if dtype not in (F32,):
            v = v.bitcast(dtype)
            if n % 2 and dtype == BF16:
                v = v[:, 0:n]
        if len(free_shape) == 2:
            v = v.rearrange("p (a b) -> p a b", a=free_shape[0])
        elif len(free_shape) == 3:
            v = v.rearrange("p (a b c) -> p a b c", a=free_shape[0], b=free_shape[1])
        elif len(free_shape) == 4:
            v = v.rearrange("p (a b c d) -> p a b c d", a=free_shape[0], b=free_shape[1], c=free_shape[2])
        return T(v, name)


def _bs(ts_):
    return [t.b for t in ts_]


class K:
    def __init__(self, P):
        self.P = P

    def dma(self, q, out, in_, r=(), w=(), slow=False):
        if slow:
            self.P.dma(q, lambda e: e.dma_start(out=out, in_=in_, allow_slow_non_contiguous=True), _bs(r), _bs(w))
        else:
            self.P.dma(q, lambda e: e.dma_start(out=out, in_=in_), _bs(r), _bs(w))

    def mm(self, out, lhsT, rhs, start, stop, r=(), w=()):
        self.P.op("pe", lambda e: e.matmul(out, lhsT=lhsT, rhs=rhs, start=start, stop=stop), _bs(r), _bs(w))

    def tr(self, out, in_, ident, r=(), w=()):
        self.P.op("pe", lambda e: e.transpose(out, in_, ident), _bs(r), _bs(w))

    def act(self, out, in_, func, r=(), w=(), scale=None, bias=None, accum=None):
        kw = {}
        if scale is not None:
            kw["scale"] = scale
        if bias is not None:
            kw["bias"] = bias
        if accum is not None:
            kw["accum_out"] = accum
        self.P.op("act", lambda e: e.activation(out=out, in_=in_, func=func, **kw), _bs(r), _bs(w))

    def tt(self, eng, out, a, b, op, r=(), w=()):
        self.P.op(eng, lambda e: e.tensor_tensor(out=out, in0=a, in1=b, op=op), _bs(r), _bs(w))

    def ts(self, eng, out, a, s1, op0, s2=None, op1=None, r=(), w=()):
        if op1 is None:
            self.P.op(eng, lambda e: e.tensor_scalar(out=out, in0=a, scalar1=s1, scalar2=None, op0=op0), _bs(r), _bs(w))
        else:
            self.P.op(eng, lambda e: e.tensor_scalar(out=out, in0=a, scalar1=s1, scalar2=s2, op0=op0, op1=op1), _bs(r), _bs(w))

    def stt(self, out, a, s, b, op0, op1, r=(), w=()):
        self.P.op("dve", lambda e: e.scalar_tensor_tensor(out=out, in0=a, scalar=s, in1=b, op0=op0, op1=op1), _bs(r), _bs(w))

    def cp(self, eng, out, in_, r=(), w=()):
        if eng == "act":
            self.P.op("act", lambda e: e.activation(out=out, in_=in_, func=AF.Copy), _bs(r), _bs(w))
        else:
            self.P.op(eng, lambda e: e.tensor_copy(out=out, in_=in_), _bs(r), _bs(w))

    def memset(self, eng, ap, val, w=()):
        self.P.op(eng, lambda e: e.memset(ap, val), (), _bs(w))

    def recip(self, out, in_, r=(), w=()):
        self.P.op("dve", lambda e: e.reciprocal(out=out, in_=in_), _bs(r), _bs(w))

    def scan(self, out, d0, d1, init, r=(), w=()):
        self.P.op("dve", lambda e: e.tensor_tensor_scan(out=out, data0=d0, data1=d1, initial=init, op0=ALU.mult, op1=ALU.add), _bs(r), _bs(w))

    def asel(self, out, in_, pattern, cmp, fill, base, cm, r=(), w=()):
        self.P.op("pool", lambda e: e.affine_select(out=out, in_=in_, pattern=pattern, compare_op=cmp, fill=fill, base=base, channel_multiplier=cm), _bs(r), _bs(w))


def rev_ap(ap2d_last_col, n):
    return bass.AP(ap2d_last_col.tensor, ap2d_last_col.offset, [list(ap2d_last_col.ap[0]), [-1, n]])


def strided(ap_first, step, n):
    return bass.AP(ap_first.tensor, ap_first.offset, [list(ap_first.ap[0]), [step, n]])


def t5_bucket_np(rel):
    half = 16
    n = -rel
    ret = np.where(n < 0, half, 0)
    n = np.abs(n)
    max_exact = 8
    nf = np.maximum(n, 1).astype(np.float32)
    large = max_exact + (np.log(nf / np.float32(max_exact)) / np.float32(math.log(1024 / max_exact)) * (half - max_exact)).astype(np.int32)
    large = np.minimum(large, half - 1)
    return ret + np.where(n < max_exact, n, large)


class Ctx:
    pass


def build(S, debug=None):
    nc = bass.Bass("TRN2", target_bir_lowering=False)
    NT = S // 128
    C = Ctx()
    C.nc, C.S, C.NT = nc, S, NT
    din = {}

    def inp(name, shape):
        din[name] = nc.dram_tensor(name, list(shape), F32, kind="ExternalInput").ap()
        return din[name]

    x = inp("x", [S, D])
    pin = inp("p", [S, 256])
    rel_bias = inp("rel_bias", [32, 8])
    g_mix = inp("g_mix", [1, 1024])
    w_in = inp("w_in", [1, 1024, 2048])
    a_re = inp("ssm_a_re", [1, 2, 32, 64])
    a_im = inp("ssm_a_im", [1, 2, 32, 64])
    log_dt = inp("ssm_log_dt", [1, 2, 32])
    b_re = inp("ssm_b_re", [1, 2, 32, 64, 16])
    b_im = inp("ssm_b_im", [1, 2, 32, 64, 16])
    c_re = inp("ssm_c_re", [1, 2, 32, 16, 64])
    c_im = inp("ssm_c_im", [1, 2, 32, 16, 64])
    ssm_d = inp("ssm_d", [1, 32, 16])
    w_glu = inp("w_glu", [1, 512, 512])
    b_glu = inp("b_glu", [1, 512])
    g_att = inp("g_att_out", [1, 512])
    g_ssm = inp("g_ssm_out", [1, 512])
    w_out = inp("w_out", [1, 1024, 1024])
    g_mlp = inp("g_mlp", [1, 1024])
    w_mlp1 = inp("w_mlp1", [1, 1024, 4096])
    w_mlp2 = inp("w_mlp2", [1, 4096, 1024])
    g_ple = inp("g_ple", [1, 1024])
    w_gate = inp("w_ple_gate", [1, 1024, 1024])
    w_proj = inp("w_ple_proj", [1, 256, 1024])
    g_final = inp("g_final", [1024])
    y = nc.dram_tensor("y", [S, D], F32, kind="ExternalOutput").ap()

    def scratch(name, shape, dt):
        kind = "ExternalOutput" if (debug and name in debug) else "Internal"
        return nc.dram_tensor(name, list(shape), dt, kind=kind).ap()

    qT = scratch("qT", [4, 128, S], BF16)
    kT = scratch("kT", [4, 128, S], BF16)
    uT = scratch("uT", [4, 128, S], BF16)
    gT = scratch("gT", [4, 128, S], BF16)
    Vp = scratch("Vp", [S + 2048, 8, 128], BF16)
    hs = scratch("hs", [S, D], F32)
    hs2 = scratch("hs2", [S, D], F32)
    Fd = scratch("Fd", [3 * 8 * 512 + 512], F32)

    with ExitStack() as st:
        NCOL = 49100
        big = st.enter_context(nc.sbuf_tensor("big", [128, NCOL], F32))
        psb = [st.enter_context(nc.psum_tensor("ps%d" % i, [128, 512], F32)) for i in range(8)]
        P = Prog(nc)
        k = K(P)
        A = Arena(big, NCOL)
        bar = A.alloc([1], F32, "bar")
        P._bar = bar.ap
        PS = [T(psb[i][:], "ps%d" % i) for i in range(8)]
        C.P, C.k, C.A, C.PS = P, k, A, PS
        identf = A.alloc([128], F32, "identf")
        identb = A.alloc([128], BF16, "identb")
        ones_b = A.alloc([128], BF16, "ones_b")
        k.memset("pool", identf.ap, 0.0, w=[identf])
        k.asel(identf.ap, identf.ap, [[-1, 128]], ALU.not_equal, 1.0, 0, 1, r=[identf], w=[identf])
        k.cp("dve", identb.ap, identf.ap, r=[identf], w=[identb])
        k.memset("pool", ones_b.ap, 1.0, w=[ones_b])
        C.identf, C.identb, C.ones_b = identf, identb, ones_b

        def gcol(src_1d_ap, n, name):
            t = A.alloc([n // 128], F32, name)
            k.dma("sp", t.ap, src_1d_ap.rearrange("(k p) -> p k", p=128), w=[t], slow=True)
            return t
        C.gcol = gcol

        nhalf = A.alloc([1], F32, "nhalf")
        k.memset("pool", nhalf.ap, -0.5, w=[nhalf])

        def rms_rstd(ss_t, n, tmp_t, out_t, ss_ap=None, wide=False):
            if wide:
                k.ts("dve", tmp_t.ap, ss_t.ap if ss_ap is None else ss_ap, 1.0 / n, ALU.mult, EPS, ALU.add, r=[ss_t], w=[tmp_t])
                k.act(tmp_t.ap, tmp_t.ap, AF.Ln, r=[tmp_t], w=[tmp_t])
                k.act(out_t.ap, tmp_t.ap, AF.Exp, scale=-0.5, r=[tmp_t], w=[out_t])
            else:
                k.ts("pool", tmp_t.ap, ss_t.ap if ss_ap is None else ss_ap, 1.0 / n, ALU.mult, EPS, ALU.add, r=[ss_t], w=[tmp_t])
                k.tt("pool", out_t.ap, tmp_t.ap, nhalf.ap, ALU.pow, r=[tmp_t, nhalf], w=[out_t])
        C.rms_rstd = rms_rstd

        def load_weight_bf16(dst, src2d, nk, ncols, gc, tag):
            A.push()
            tmps = [A.alloc([min(ncols, 2048)], F32, tag + "tmp%d" % i) for i in range(2)]
            i = 0
            for kk in range(nk):
                for c0 in range(0, ncols, 2048):
                    cw = min(2048, ncols - c0)
                    tm = tmps[i % 2]
                    k.dma("sp", tm.ap[:, 0:cw], src2d[kk * 128:(kk + 1) * 128, c0:c0 + cw], w=[tm])
                    if i % 2 == 0:
                        if gc is None:
                            k.cp("dve", dst.ap[:, kk, c0:c0 + cw], tm.ap[:, 0:cw], r=[tm], w=[dst])
                        else:
                            k.ts("dve", dst.ap[:, kk, c0:c0 + cw], tm.ap[:, 0:cw], gc.ap[:, kk:kk + 1], ALU.mult, r=[tm, gc], w=[dst])
                    else:
                        if gc is None:
                            k.cp("act", dst.ap[:, kk, c0:c0 + cw], tm.ap[:, 0:cw], r=[tm], w=[dst])
                        else:
                            k.act(dst.ap[:, kk, c0:c0 + cw], tm.ap[:, 0:cw], AF.Copy, scale=gc.ap[:, kk:kk + 1], r=[tm, gc], w=[dst])
                    i += 1
            A.pop()
            P.barrier()
        C.load_weight_bf16 = load_weight_bf16

        phases = debug.get("phases", "ASBCD") if debug else "ASBCD"
        if "A" in phases:
            phase_A(C, x, g_mix, w_in, qT, kT, uT, Vp)
            P.barrier()
        if "S" in phases:
            phase_S(C, uT, gT, a_re, a_im, log_dt, b_re, b_im, c_re, c_im, ssm_d)
            P.barrier()
        if "B" in phases:
            phase_B(C, x, qT, kT, Vp, gT, hs, rel_bias, Fd, w_glu, b_glu, g_att, g_ssm, w_out)
            P.barrier()
        if "X" in phases:
            for i in range(S // 512):
                k.dma("sp", hs[i * 512:(i + 1) * 512, :], x[i * 512:(i + 1) * 512, :])
            P.barrier()
        if "C" in phases:
            phase_C(C, hs, hs2, g_mlp, w_mlp1, w_mlp2)
            P.barrier()
        if "D" in phases:
            phase_D(C, hs2, pin, y, g_ple, w_gate, w_proj, g_final)
        P.emit(st)
    return nc


def phase_A(C, x, g_mix, w_in, qT, kT, uT, Vp):
    k, A, PS, S, NT = C.k, C.A, C.PS, C.S, C.NT
    A.push()
    gm = C.gcol(g_mix[0], 1024, "gm")
    win = A.alloc([8, 2048], BF16, "win")
    C.load_weight_bf16(win, w_in[0], 8, 2048, gm, "win")
    xb = [A.alloc([1024], F32, "xa%d" % i) for i in range(2)]
    junk = A.alloc([1024], F32, "junk")
    ab = [A.alloc([1024], BF16, "ab%d" % i) for i in range(2)]
    ss = [A.alloc([1], F32, "ss%d" % i) for i in range(2)]
    tmp = [A.alloc([1], F32, "tmp%d" % i) for i in range(2)]
    rstd = [A.alloc([1], F32, "rstd%d" % i) for i in range(2)]
    aT = [A.alloc([8, 512], BF16, "aT%d" % i) for i in range(2)]
    zst = [A.alloc([512], BF16, "zst%d" % i) for i in range(4)]
    vst = [A.alloc([8, 128], BF16, "vst%d" % i) for i in range(2)]
    zero = A.alloc([8, 128], BF16, "zero")
    k.memset("dve", zero.ap, 0.0, w=[zero])
    for v in vst:
        k.memset("pool", v.ap, 1.0, w=[v])
    for i in range(8):
        k.dma("sp", Vp[i * 128:(i + 1) * 128], zero.ap, r=[zero])
        k.dma("sp", Vp[1024 + S + i * 128:1024 + S + (i + 1) * 128], zero.ap, r=[zero])
    zi = [0]

    def normstage(b):
        at = aT[b % 2]
        for t in range(4):
            ti = 4 * b + t
            xt, a_, s_, tm, rs = xb[ti % 2], ab[ti % 2], ss[ti % 2], tmp[ti % 2], rstd[ti % 2]
            k.dma("sp", xt.ap, x[ti * 128:(ti + 1) * 128, :], w=[xt])
            k.memset("pool", s_.ap, 0.0, w=[s_])
            k.act(junk.ap, xt.ap, AF.Square, accum=s_.ap, r=[xt, s_], w=[junk, s_])
            C.rms_rstd(s_, 1024, tm, rs)
            k.ts("dve", a_.ap, xt.ap, rs.ap[:, 0:1], ALU.mult, r=[xt, rs], w=[a_])
            pst = PS[ti % 2]
            psv = pst.ap.bitcast(BF16).rearrange("p (a b) -> p a b", a=8)
            for kk in range(8):
                k.tr(psv[:, kk, :], a_.ap[:, kk * 128:(kk + 1) * 128], C.identb.ap, r=[a_, C.identb], w=[pst])
            k.cp("dve", at.ap[:, :, t * 128:(t + 1) * 128], psv, r=[pst], w=[at])

    def projstage(b):
        at = aT[b % 2]
        for (dst, f0, sc) in ((qT, 0, 0.125), (kT, 4, None), (uT, 12, None)):
            for j in range(4):
                pz = PS[2 + zi[0] % 3]
                for kk in range(8):
                    k.mm(pz.ap, win.ap[:, kk, (f0 + j) * 128:(f0 + j + 1) * 128], at.ap[:, kk, :], kk == 0, kk == 7, r=[win, at], w=[pz])
                z = zst[zi[0] % 4]
                k.act(z.ap, pz.ap, AF.Copy, scale=(sc if sc is not None else 1.0), r=[pz], w=[z])
                k.dma("act", dst[j, :, b * 512:(b + 1) * 512], z.ap, r=[z])
                zi[0] += 1
        for t in range(4):
            ti = 4 * b + t
            pv = PS[5 + ti % 3]
            for kk in range(8):
                k.mm(pv.ap, at.ap[:, kk, t * 128:(t + 1) * 128], win.ap[:, kk, 1024:1536], kk == 0, kk == 7, r=[win, at], w=[pv])
            v = vst[ti % 2]
            pvv = pv.ap.rearrange("p (a b c) -> p a b c", a=4, b=2)
            vv = v.ap.rearrange("p (a b) c -> p a b c", b=2)
            k.cp("dve", vv[:, :, 0, 0:64], pvv[:, :, 0, :], r=[pv], w=[v])
            k.cp("act", vv[:, :, 1, 64:128], pvv[:, :, 1, :], r=[pv], w=[v])
            k.dma("act", Vp[1024 + ti * 128:1024 + (ti + 1) * 128], v.ap, r=[v])
    NB = S // 512
    normstage(0)
    for b in range(NB):
        if b + 1 < NB:
            normstage(b + 1)
        projstage(b)
    A.pop()


def phase_S(C, uT, gT, a_re, a_im, log_dt, b_re, b_im, c_re, c_im, ssm_d):
    k, A, PS, S, P = C.k, C.A, C.PS, C.S, C.P
    NCH = S // 8
    CB = min(512, NCH)
    NCB = NCH // CB
    LV = int(round(math.log2(NCH)))
    TWO_PI = 2.0 * math.pi
    A.push()
    Z = A.alloc([8, 240], BF16, "Z")
    T0 = A.alloc([32, 128], BF16, "T0")
    WS = A.alloc([32, 2, 128], BF16, "WS")
    WOF = A.alloc([32, 2, 128], BF16, "WOF")
    WOB = A.alloc([32, 2, 128], BF16, "WOB")
    k.memset("pool", WOF.ap, 0.0, w=[WOF])
    k.memset("pool", WOB.ap, 0.0, w=[WOB])
    Wpow = A.alloc([LV, 32, 2], F32, "Wpow")
    rho8 = A.alloc([32], F32, "rho8")
    Eb = A.alloc([32, 2, 32], F32, "Eb")
    k.memset("pool", Z.ap, 0.0, w=[Z])
    for gi in range(8):
        k.asel(Z.ap[:, gi, 112:128], Z.ap[:, gi, 112:128], [[-1, 16]], ALU.not_equal, 1.0, -16 * gi, 1, r=[Z], w=[Z])
    A.push()

    def new(shape, name):
        return A.alloc(shape, F32, name)
    are, aim, ldt = new([32], "are"), new([32], "aim"), new([32], "ldt")
    bre, bim = new([32, 16], "bre"), new([32, 16], "bim")
    Lr, Li = new([32, 8, 16], "Lr"), new([32, 8, 16], "Li")
    Rr, Ri = new([32, 8, 16], "Rr"), new([32, 8, 16], "Ri")
    dcol = new([32], "dcol")
    for dr in range(2):
        ps_ = slice(dr * 64, dr * 64 + 64)
        k.dma("sp", are.ap[ps_, :], a_re[0, dr].rearrange("g n -> n g"), w=[are], slow=True)
        k.dma("sp", aim.ap[ps_, :], a_im[0, dr].rearrange("g n -> n g"), w=[aim], slow=True)
        k.dma("sp", ldt.ap[ps_, :], log_dt[0, dr:dr + 1, :].partition_broadcast(64), w=[ldt])
        k.dma("sp", bre.ap[ps_], b_re[0, dr].rearrange("g n h -> n g h"), w=[bre], slow=True)
        k.dma("sp", bim.ap[ps_], b_im[0, dr].rearrange("g n h -> n g h"), w=[bim], slow=True)
    for s_ in range(8):
        k.dma("sp", dcol.ap[s_ * 16:(s_ + 1) * 16, :], ssm_d[0].rearrange("g h -> h g"), w=[dcol], slow=True)
    ct = new([128], "ct")
    for (src, dst) in ((c_re, Rr), (c_im, Ri)):
        for o in range(4):
            k.dma("sp", ct.ap[:, 0:64], src[0, 0, 8 * o:8 * o + 8].rearrange("g h n -> (g h) n"), w=[ct])
            k.dma("sp", ct.ap[:, 64:128], src[0, 1, 8 * o:8 * o + 8].rearrange("g h n -> (g h) n"), w=[ct])
            k.tr(PS[0].ap[:, 0:128], ct.ap, C.identf.ap, r=[ct, C.identf], w=[PS[0]])
            k.cp("dve", dst.ap[:, 8 * o:8 * o + 8, 0, :], PS[0].ap[:, 0:128].rearrange("p (a b) -> p a b", b=16), r=[PS[0]], w=[dst])
    sc = [new([32], "sc%d" % i) for i in range(12)]
    sci = A.alloc([32], I32, "sci")

    def mul(o, a, b, eng="dve"):
        k.tt(eng, o.ap, a.ap, b.ap, ALU.mult, r=[a, b], w=[o])

    def sin_of(out, th, shift):
        t1, tf, r_, m_ = sc[8], sc[9], sc[10], sc[11]
        k.ts("dve", r_.ap, th.ap, shift, ALU.add, r=[th], w=[r_])
        k.ts("dve", t1.ap, r_.ap, 1.0 / TWO_PI, ALU.mult, r=[r_], w=[t1])
        k.cp("dve", sci.ap, t1.ap, r=[t1], w=[sci])
        k.cp("dve", tf.ap, sci.ap, r=[sci], w=[tf])
        k.stt(r_.ap, tf.ap, -TWO_PI, r_.ap, ALU.mult, ALU.add, r=[tf, r_], w=[r_])
        k.ts("dve", m_.ap, r_.ap, math.pi, ALU.is_gt, -TWO_PI, ALU.mult, r=[r_], w=[m_])
        k.tt("dve", r_.ap, r_.ap, m_.ap, ALU.add, r=[r_, m_], w=[r_])
        k.ts("dve", m_.ap, r_.ap, -math.pi, ALU.is_lt, TWO_PI, ALU.mult, r=[r_], w=[m_])
        k.tt("dve", r_.ap, r_.ap, m_.ap, ALU.add, r=[r_, m_], w=[r_])
        k.ts("dve", r_.ap, r_.ap, 3.1415925, ALU.min, -3.1415925, ALU.max, r=[r_], w=[r_])
        k.act(out.ap, r_.ap, AF.Sin, r=[r_], w=[out])
    dt_, lam, th, mag, c1, s1, abr, abi = [new([32], "p%d" % i) for i in range(8)]
    k.act(dt_.ap, ldt.ap, AF.Exp, r=[ldt], w=[dt_])
    mul(lam, are, dt_)
    mul(th, aim, dt_)
    k.act(mag.ap, lam.ap, AF.Exp, r=[lam], w=[mag])
    sin_of(s1, th, 0.0)
    sin_of(c1, th, math.pi / 2)
    mul(abr, mag, c1)
    mul(abi, mag, s1)
    inv, fre, fim, am1 = [new([32], "q%d" % i) for i in range(4)]
    mul(sc[0], are, are)
    mul(sc[1], aim, aim)
    k.tt("dve", sc[0].ap, sc[0].ap, sc[1].ap, ALU.add, r=[sc[0], sc[1]], w=[sc[0]])
    k.recip(inv.ap, sc[0].ap, r=[sc[0]], w=[inv])
    k.ts("dve", am1.ap, abr.ap, -1.0, ALU.add, r=[abr], w=[am1])
    mul(sc[0], am1, are)
    mul(sc[1], abi, aim)
    k.tt("dve", sc[0].ap, sc[0].ap, sc[1].ap, ALU.add, r=[sc[0], sc[1]], w=[sc[0]])
    mul(fre, sc[0], inv)
    mul(sc[0], abi, are)
    mul(sc[1], am1, aim)
    k.tt("dve", sc[0].ap, sc[0].ap, sc[1].ap, ALU.subtract, r=[sc[0], sc[1]], w=[sc[0]])
    mul(fim, sc[0], inv)

    def cmul(ore, oim, ar_, ai_, br_, bi_, t1, t2, r, w):
        k.tt("dve", t1, ar_, br_, ALU.mult, r=r, w=w)
        k.tt("dve", t2, ai_, bi_, ALU.mult, r=r, w=w)
        k.tt("dve", ore, t1, t2, ALU.subtract, r=r, w=w)
        k.tt("dve", t1, ar_, bi_, ALU.mult, r=r, w=w)
        k.tt("dve", t2, ai_, br_, ALU.mult, r=r, w=w)
        k.tt("dve", oim, t1, t2, ALU.add, r=r, w=w)
    X = T(None, "prepX")
    allr = [X, are, aim, bre, bim, Rr, Ri, Lr, Li, abr, abi, mag, fre, fim, lam]
    big1, big2 = new([16, 128], "big1"), new([16, 128], "big2")

    def b16(t_):
        return t_.ap.unsqueeze(2).to_broadcast([128, 32, 16])

    def b128(t_):
        return t_.ap.unsqueeze(2).to_broadcast([128, 32, 128])
    t16a = big1.ap.rearrange("p a b -> p (a b)")[:, 0:512].rearrange("p (g h) -> p g h", h=16)
    t16b = big2.ap.rearrange("p a b -> p (a b)")[:, 0:512].rearrange("p (g h) -> p g h", h=16)
    cmul(Lr.ap[:, :, 0, :], Li.ap[:, :, 0, :], b16(fre), b16(fim), bre.ap, bim.ap, t16a, t16b, allr, [X])
    aivr, aivi, im2 = new([32], "aivr"), new([32], "aivi"), new([32], "im2")
    mul(sc[0], mag, mag)
    k.recip(im2.ap, sc[0].ap, r=[sc[0]], w=[im2])
    mul(aivr, abr, im2)
    k.stt(aivi.ap, abi.ap, -1.0, im2.ap, ALU.mult, ALU.mult, r=[abi, im2], w=[aivi])
    MLr, MLi, MRr, MRi = [new([32], "M%d" % i) for i in range(4)]
    lo, hi = slice(0, 64), slice(64, 128)
    for (dst, s_lo, s_hi) in ((MLr, aivr, abr), (MLi, aivi, abi), (MRr, abr, aivr), (MRi, abi, aivi)):
        k.cp("dve", dst.ap[lo], s_lo.ap[lo], r=[s_lo], w=[dst])
        k.cp("dve", dst.ap[hi], s_hi.ap[hi], r=[s_hi], w=[dst])
    allr += [MLr, MLi, MRr, MRi]
    for s_ in range(7):
        cmul(Lr.ap[:, :, s_ + 1, :], Li.ap[:, :, s_ + 1, :], Lr.ap[:, :, s_, :], Li.ap[:, :, s_, :], b16(MLr), b16(MLi), t16a, t16b, allr, [X])
        cmul(Rr.ap[:, :, s_ + 1, :], Ri.ap[:, :, s_ + 1, :], Rr.ap[:, :, s_, :], Ri.ap[:, :, s_, :], b16(MRr), b16(MRi), t16a, t16b, allr, [X])
    pwr = [abr] + [new([32], "pwr%d" % i) for i in range(7)]
    pwi = [abi] + [new([32], "pwi%d" % i) for i in range(7)]
    for i in range(7):
        cmul(pwr[i + 1].ap, pwi[i + 1].ap, pwr[i].ap, pwi[i].ap, abr.ap, abi.ap, sc[0].ap, sc[1].ap, allr, [X])
    FSr, FSi, FOr, FOi = [new([32], "F%d" % i) for i in range(4)]
    k.cp("dve", FSr.ap[lo], pwr[6].ap[lo], r=allr, w=[X])
    k.cp("dve", FSi.ap[lo], pwi[6].ap[lo], r=allr, w=[X])
    k.memset("dve", FSr.ap[hi], 1.0, w=[X])
    k.memset("dve", FSi.ap[hi], 0.0, w=[X])
    k.cp("dve", FOr.ap[lo], abr.ap[lo], r=allr, w=[X])
    k.cp("dve", FOi.ap[lo], abi.ap[lo], r=allr, w=[X])
    k.cp("dve", FOr.ap[hi], pwr[7].ap[hi], r=allr, w=[X])
    k.cp("dve", FOi.ap[hi], pwi[7].ap[hi], r=allr, w=[X])
    Lr3 = Lr.ap.rearrange("p g s h -> p g (s h)")
    Li3 = Li.ap.rearrange("p g s h -> p g (s h)")
    Rr3 = Rr.ap.rearrange("p g s h -> p g (s h)")
    Ri3 = Ri.ap.rearrange("p g s h -> p g (s h)")
    WSr, WSi = new([16, 128], "WSr"), new([16, 128], "WSi")

    def b128h(t_, h_):
        return t_.ap[:, 16 * h_:16 * h_ + 16].unsqueeze(2).to_broadcast([128, 16, 128])
    for h_ in range(2):
        gs = slice(16 * h_, 16 * h_ + 16)
        cmul(WSr.ap, WSi.ap, Lr3[:, gs], Li3[:, gs], b128h(FSr, h_), b128h(FSi, h_), big1.ap, big2.ap, allr, [X])
        for gl in range(16):
            g = 16 * h_ + gl
            for xi, src in enumerate((WSr, WSi)):
                pst = PS[(2 * g + xi) % 4]
                k.tr(pst.ap[:, 0:128], src.ap[:, gl, :], C.identf.ap, r=[X, C.identf], w=[pst])
                k.cp("act" if xi else "dve", WS.ap[:, g, xi, :], pst.ap[:, 0:128], r=[pst], w=[WS])
        cmul(WSr.ap, WSi.ap, Rr3[:, gs], Ri3[:, gs], b128h(FOr, h_), b128h(FOi, h_), big1.ap, big2.ap, allr, [X])
        for (dst, ps_) in ((WOF, lo), (WOB, hi)):
            k.cp("dve", dst.ap[ps_, gs, 0, :], WSr.ap[ps_], r=[X], w=[dst])
            k.ts("dve", dst.ap[ps_, gs, 1, :], WSi.ap[ps_], -1.0, ALU.mult, r=[X], w=[dst])
    Mlow, Mup, onesf = new([128], "Mlow"), new([128], "Mup"), new([128], "onesf")
    k.memset("pool", onesf.ap, 1.0, w=[onesf])
    k.asel(Mlow.ap.rearrange("p (t h) -> p t h", h=16), onesf.ap.rearrange("p (t h) -> p t h", h=16), [[16, 8], [0, 16]], ALU.is_ge, 0.0, 15, -1, r=[onesf], w=[Mlow])
    k.asel(Mup.ap.rearrange("p (t h) -> p t h", h=16), onesf.ap.rearrange("p (t h) -> p t h", h=16), [[-16, 8], [0, 16]], ALU.is_ge, 0.0, 0, 1, r=[onesf], w=[Mup])
    tacc = [new([128], "tacc%d" % i) for i in range(2)]
    for g in range(32):
        pf, pf2, pb_, pb2 = PS[4], PS[5], PS[6], PS[7]
        k.mm(pf.ap[:, 0:128], Lr3[lo, g, :], Rr3[lo, g, :], True, True, r=[X], w=[pf])
        k.mm(pf2.ap[:, 0:128], Li3[lo, g, :], Ri3[lo, g, :], True, True, r=[X], w=[pf2])
        k.mm(pb_.ap[:, 0:128], Lr3[hi, g, :], Rr3[hi, g, :], True, True, r=[X], w=[pb_])
        k.mm(pb2.ap[:, 0:128], Li3[hi, g, :], Ri3[hi, g, :], True, True, r=[X], w=[pb2])
        ta, tb = tacc[0], tacc[1]
        k.cp("act", ta.ap, pf2.ap[:, 0:128], r=[pf2], w=[ta])
        k.tt("dve", ta.ap, pf.ap[:, 0:128], ta.ap, ALU.subtract, r=[pf, ta], w=[ta])
        k.tt("dve", ta.ap, ta.ap, Mlow.ap, ALU.mult, r=[ta, Mlow], w=[ta])
        k.cp("act", tb.ap, pb2.ap[:, 0:128], r=[pb2], w=[tb])
        k.tt("dve", tb.ap, pb_.ap[:, 0:128], tb.ap, ALU.subtract, r=[pb_, tb], w=[tb])
        k.tt("dve", tb.ap, tb.ap, Mup.ap, ALU.mult, r=[tb, Mup], w=[tb])
        k.tt("dve", ta.ap, ta.ap, tb.ap, ALU.add, r=[ta, tb], w=[ta])
        k.stt(T0.ap[:, g, :], C.identf.ap, dcol.ap[:, g:g + 1], ta.ap, ALU.mult, ALU.add, r=[C.identf, dcol, ta], w=[T0])
    k.act(rho8.ap, lam.ap, AF.Exp, scale=8.0, r=[lam], w=[rho8])
    ir8 = new([32], "ir8")
    k.act(ir8.ap, lam.ap, AF.Exp, scale=-8.0, r=[lam], w=[ir8])
    k.tt("dve", Wpow.ap[:, 0, :, 0], pwr[7].ap, ir8.ap, ALU.mult, r=allr + [ir8], w=[Wpow])
    k.tt("dve", Wpow.ap[:, 0, :, 1], pwi[7].ap, ir8.ap, ALU.mult, r=allr + [ir8], w=[Wpow])
    for l in range(LV - 1):
        wr, wi = Wpow.ap[:, l, :, 0], Wpow.ap[:, l, :, 1]
        k.tt("dve", sc[0].ap, wr, wr, ALU.mult, r=[Wpow], w=[sc[0]])
        k.tt("dve", sc[1].ap, wi, wi, ALU.mult, r=[Wpow], w=[sc[1]])
        k.tt("dve", Wpow.ap[:, l + 1, :, 0], sc[0].ap, sc[1].ap, ALU.subtract, r=[sc[0], sc[1]], w=[Wpow])
        k.tt("dve", sc[2].ap, wr, wi, ALU.mult, r=[Wpow], w=[sc[2]])
        k.ts("dve", Wpow.ap[:, l + 1, :, 1], sc[2].ap, 2.0, ALU.mult, r=[sc[2]], w=[Wpow])
    k.memset("dve", Eb.ap[:, :, 0, 0:1], 1.0, w=[Eb])
    k.memset("dve", Eb.ap[:, :, 1, 0:1], 0.0, w=[Eb])
    eb1 = big1.ap.rearrange("p a b -> p (a b)")[:, 0:512].rearrange("p (g m) -> p g m", m=16)
    eb2 = big2.ap.rearrange("p a b -> p (a b)")[:, 0:512].rearrange("p (g m) -> p g m", m=16)
    for l in range(5):
        m = 1 << l
        wr = Wpow.ap[:, l, :, 0].unsqueeze(2).to_broadcast([128, 32, m])
        wi = Wpow.ap[:, l, :, 1].unsqueeze(2).to_broadcast([128, 32, m])
        ec0, es0 = Eb.ap[:, :, 0, 0:m], Eb.ap[:, :, 1, 0:m]
        ec1, es1 = Eb.ap[:, :, 0, m:2 * m], Eb.ap[:, :, 1, m:2 * m]
        t1, t2 = eb1[:, :, 0:m], eb2[:, :, 0:m]
        k.tt("dve", t1, es0, wi, ALU.mult, r=[Eb, Wpow, X], w=[X])
        k.tt("dve", t2, ec0, wr, ALU.mult, r=[Eb, Wpow, X], w=[X])
        k.tt("dve", ec1, t2, t1, ALU.subtract, r=[X, Eb], w=[Eb])
        k.tt("dve", t1, ec0, wi, ALU.mult, r=[Eb, Wpow, X], w=[X])
        k.tt("dve", t2, es0, wr, ALU.mult, r=[Eb, Wpow, X], w=[X])
        k.tt("dve", es1, t2, t1, ALU.add, r=[X, Eb], w=[Eb])
    A.pop()
    P.barrier()
    uTo = A.alloc([S], BF16, "uTo")
    gTo = A.alloc([S], BF16, "gTo")
    Yg = A.alloc([8, NCH], BF16, "Yg")
    Ugs = [A.alloc([NCH], BF16, "Ug%d" % i) for i in range(2)]
    Es = [A.alloc([2, NCH], F32, "E%d" % i) for i in range(2)]
    G = A.alloc([2, NCH], F32, "G")
    Hs = A.alloc([2, NCH], F32, "Hs")
    Hbs = [A.alloc([2, NCH + 2], BF16, "Hb%d" % i) for i in range(2)]
    tq = [A.alloc([CB], F32, "tq%d" % i) for i in range(4)]
    ge = [A.alloc([CB], F32, "ge%d" % i) for i in range(3)]
    for hb_ in Hbs:
        k.memset("pool", hb_.ap, 0.0, w=[hb_])
    tr_ = [A.alloc([CB], F32, "tr%d" % i) for i in range(4)]
    Ug3 = Ugs + [A.alloc([NCH], BF16, "Ug2")]

    def egen(g):
        E = Es[g % 2]
        Ec, Es_ = E.ap[:, 0, :], E.ap[:, 1, :]
        k.cp("dve", E.ap[:, :, 0:32], Eb.ap[:, g, :, :], r=[Eb], w=[E])
        for l in range(5, LV):
            m = 1 << l
            wr, wi = Wpow.ap[:, l, g, 0:1], Wpow.ap[:, l, g, 1:2]
            t1, t2 = tq[0].ap[:, 0:m], tq[1].ap[:, 0:m]
            k.ts("dve", t1, Es_[:, 0:m], wi, ALU.mult, r=[E, Wpow], w=[tq[0]])
            k.stt(Ec[:, m:2 * m], Ec[:, 0:m], wr, t1, ALU.mult, ALU.subtract, r=[E, Wpow, tq[0]], w=[E])
            k.ts("dve", t2, Ec[:, 0:m], wi, ALU.mult, r=[E, Wpow], w=[tq[1]])
            k.stt(Es_[:, m:2 * m], Es_[:, 0:m], wr, t2, ALU.mult, ALU.add, r=[E, Wpow, tq[1]], w=[E])

    def insel(g):
        gi, Ug = g % 8, Ug3[g % 3]
        for cb in range(NCB):
            pu = PS[cb % 2]
            for s_ in range(8):
                k.mm(pu.ap[:, 0:CB], Z.ap[:, gi, 112 - 16 * s_:240 - 16 * s_], strided(uTo.ap[:, cb * CB * 8 + s_:cb * CB * 8 + s_ + 1], 8, CB), s_ == 0, s_ == 7, r=[Z, uTo], w=[pu])
            k.cp("act", Ug.ap[:, cb * CB:(cb + 1) * CB], pu.ap[:, 0:CB], r=[pu], w=[Ug])

    def summ(g):
        Ug = Ug3[g % 3]
        for cb in range(NCB):
            for xi in range(2):
                pS = PS[2 + 2 * xi + cb]
                k.mm(pS.ap[0:64, 0:CB], WS.ap[:, g, xi, 0:64], Ug.ap[:, cb * CB:(cb + 1) * CB], True, True, r=[WS, Ug], w=[pS])
                k.mm(pS.ap[64:128, 0:CB], WS.ap[:, g, xi, 64:128], rev_ap(Ug.ap[:, NCH - 1 - cb * CB:NCH - cb * CB], CB), True, True, r=[WS, Ug], w=[pS])

    def demod(g):
        E = Es[g % 2]
        Ec, Es_ = E.ap[:, 0, :], E.ap[:, 1, :]
        for cb in range(NCB):
            cs = slice(cb * CB, (cb + 1) * CB)
            pr, pi_ = PS[2 + cb], PS[4 + cb]
            k.tt("dve", tq[0].ap, pr.ap[:, 0:CB], Ec[:, cs], ALU.mult, r=[pr, E], w=[tq[0]])
            k.tt("dve", tq[1].ap, pi_.ap[:, 0:CB], Es_[:, cs], ALU.mult, r=[pi_, E], w=[tq[1]])
            k.tt("dve", G.ap[:, 0, cs], tq[0].ap, tq[1].ap, ALU.add, r=[tq[0], tq[1]], w=[G])
            k.tt("dve", tq[2].ap, pi_.ap[:, 0:CB], Ec[:, cs], ALU.mult, r=[pi_, E], w=[tq[2]])
            k.tt("dve", tq[3].ap, pr.ap[:, 0:CB], Es_[:, cs], ALU.mult, r=[pr, E], w=[tq[3]])
            k.tt("dve", G.ap[:, 1, cs], tq[2].ap, tq[3].ap, ALU.subtract, r=[tq[2], tq[3]], w=[G])

    def scan_(g):
        dec = rho8.ap[:, g:g + 1].to_broadcast([128, NCH])
        k.scan(Hs.ap[:, 0, :], dec, G.ap[:, 0, :], 0.0, r=[G, rho8], w=[Hs])
        k.scan(Hs.ap[:, 1, :], dec, G.ap[:, 1, :], 0.0, r=[G, rho8], w=[Hs])

    def remod(g):
        E, Hb = Es[g % 2], Hbs[g % 2]
        Ec, Es_ = E.ap[:, 0, :], E.ap[:, 1, :]
        for cb in range(NCB):
            cs = slice(cb * CB, (cb + 1) * CB)
            co = slice(1 + cb * CB, 1 + (cb + 1) * CB)
            k.tt("dve", tr_[0].ap, Hs.ap[:, 0, cs], Ec[:, cs], ALU.mult, r=[Hs, E], w=[tr_[0]])
            k.tt("dve", tr_[1].ap, Hs.ap[:, 1, cs], Es_[:, cs], ALU.mult, r=[Hs, E], w=[tr_[1]])
            k.tt("dve", Hb.ap[:, 0, co], tr_[0].ap, tr_[1].ap, ALU.subtract, r=[tr_[0], tr_[1]], w=[Hb])
            k.tt("dve", tr_[2].ap, Hs.ap[:, 1, cs], Ec[:, cs], ALU.mult, r=[Hs, E], w=[tr_[2]])
            k.tt("dve", tr_[3].ap, Hs.ap[:, 0, cs], Es_[:, cs], ALU.mult, r=[Hs, E], w=[tr_[3]])
            k.tt("dve", Hb.ap[:, 1, co], tr_[2].ap, tr_[3].ap, ALU.add, r=[tr_[2], tr_[3]], w=[Hb])

    def outs(g):
        gi, Ug, Hb = g % 8, Ug3[g % 3], Hbs[g % 2]
        for cb in range(NCB):
            cs = slice(cb * CB, (cb + 1) * CB)
            py = PS[6 + cb % 2]
            k.mm(py.ap[:, 0:CB], T0.ap[:, g, :], Ug.ap[:, cs], True, False, r=[T0, Ug], w=[py])
            for xi in range(2):
                k.mm(py.ap[:, 0:CB], WOF.ap[:, g, xi, :], Hb.ap[:, xi, cb * CB:(cb + 1) * CB], False, False, r=[WOF, Hb], w=[py])
            for xi in range(2):
                st_ = NCH - 1 - cb * CB
                k.mm(py.ap[:, 0:CB], WOB.ap[:, g, xi, :], rev_ap(Hb.ap[:, xi, st_:st_ + 1], CB), False, xi == 1, r=[WOB, Hb], w=[py])
            k.cp("act", Yg.ap[:, gi, cs], py.ap[:, 0:CB], r=[py], w=[Yg])

    for o in range(4):
        k.dma("sp", uTo.ap, uT[o], w=[uTo])
        g0 = 8 * o
        insel(g0)
        insel(g0 + 1)
        summ(g0)
        egen(g0)
        for gi in range(8):
            g = g0 + gi
            demod(g)
            if gi + 1 < 8:
                summ(g + 1)
            scan_(g)
            if gi + 1 < 8:
                egen(g + 1)
            remod(g)
            if gi + 2 < 8:
                insel(g + 2)
            outs(g)
        for t in range(8):
            for cb in range(NCB):
                po = PS[(t * NCB + cb) % 2]
                for gi in range(8):
                    k.mm(po.ap[:, 0:CB], Z.ap[:, t, 112 - 16 * gi:240 - 16 * gi], Yg.ap[:, gi, cb * CB:(cb + 1) * CB], gi == 0, gi == 7, r=[Z, Yg], w=[po])
                yv = po.ap[:, 0:CB]
                k.act(ge[0].ap, yv, AF.Square, r=[po], w=[ge[0]])
                k.ts("dve", ge[0].ap, ge[0].ap, 0.044715, ALU.mult, 1.0, ALU.add, r=[ge[0]], w=[ge[0]])
                k.tt("dve", ge[1].ap, ge[0].ap, yv, ALU.mult, r=[ge[0], po], w=[ge[1]])
                k.act(ge[2].ap, ge[1].ap, AF.Sigmoid, scale=1.5957691216057308, r=[ge[1]], w=[ge[2]])
                k.tt("dve", strided(gTo.ap[:, cb * CB * 8 + t:cb * CB * 8 + t + 1], 8, CB), yv, ge[2].ap, ALU.mult, r=[po, ge[2]], w=[gTo])
        k.dma("sp", gT[o], gTo.ap, r=[gTo])
    A.pop()


def ap3(base, d1, d2):
    return bass.AP(base.tensor, base.offset, [list(base.ap[0]), list(d1), list(d2)])


def phase_B(C, x, qT, kT, Vp, gT, hs, rel_bias, Fd, w_glu, b_glu, g_att, g_ssm, w_out):
    k, A, PS, S, P = C.k, C.A, C.PS, C.S, C.P
    NSB = S // 2048
    import os
    DIL = (1, 4, 16)
    DSEL = [int(v) for v in os.environ.get('PB_DIL', '1,4,16').split(',')]
    A.push()
    EB = A.alloc([3, 8, 2, 128], BF16, "EB")
    A.push()
    rb0 = A.alloc([256], F32, "rb0")
    F0 = A.alloc([3, 8, 512], F32, "F0")
    Tt = A.alloc([24, 257], F32, "Tt")
    k.dma("sp", rb0.ap, rel_bias.rearrange("(o b) h -> o (b h)", o=1).partition_broadcast(128), w=[rb0])
    k.act(rb0.ap, rb0.ap, AF.Exp, r=[rb0], w=[rb0])
    k.memset("dve", F0.ap, 0.0, w=[F0])
    rbv = rb0.ap.rearrange("p (b h) -> p b h", h=8)
    for di, d in enumerate(DIL):
        js = np.arange(-64, 65)
        bk = t5_bucket_np(js * d)
        s0 = 0
        while s0 < len(js):
            e0 = s0
            while e0 + 1 < len(js) and bk[e0 + 1] == bk[s0]:
                e0 += 1
            ln = e0 - s0 + 1
            x0 = int(js[s0]) + 192
            bb = int(bk[s0])
            for h in range(8):
                k.ts("dve", F0.ap[:, di, h, x0:x0 + ln], F0.ap[:, di, h, x0:x0 + ln], rbv[:, bb, h:h + 1], ALU.add, r=[rb0, F0], w=[F0])
            s0 = e0 + 1
    fdt = T(None, "Fd")
    for di in range(3):
        k.dma("sp", Fd[di * 4096:(di + 1) * 4096].rearrange("(o n) -> o n", o=1), F0.ap[0:1, di].rearrange("p b c -> p (b c)"), r=[F0], w=[fdt])
    for di in range(3):
        for h in range(8):
            j = di * 8 + h
            k.dma("sp", Tt.ap[:, j, :], bass.AP(Fd.tensor, j * 512, [[1, 128], [1, 257]]), r=[fdt], w=[Tt])
    ttb = Tt.ap[:, 0, 0:1]
    pstep = list(Tt.ap.ap[0])
    for di in range(3):
        for ab, c0 in ((0, 128), (1, 256)):
            src = bass.AP(Tt.ap.tensor, Tt.ap[:, di * 8, c0:c0 + 1].offset, [pstep, [257, 8], [-1, 128]])
            k.cp("dve", EB.ap[:, di, :, ab, :], src, r=[Tt], w=[EB])
    A.pop()
    P.barrier()
    import os
    STOP = int(os.environ.get("PB_STOP", "9"))
    if STOP <= 1:
        A.pop()
        return
    ONEH = A.alloc([2, 128], BF16, "ONEH")
    k.memset("pool", ONEH.ap, 0.0, w=[ONEH])
    k.memset("pool", ONEH.ap[:, 0, 0:64], 1.0, w=[ONEH])
    k.memset("pool", ONEH.ap[:, 1, 64:128], 1.0, w=[ONEH])
    gc = A.alloc([8], F32, "gcB")
    k.dma("sp", gc.ap[:, 0:4], g_att[0].rearrange("(k p) -> p k", p=128), w=[gc], slow=True)
    k.dma("sp", gc.ap[:, 4:8], g_ssm[0].rearrange("(k p) -> p k", p=128), w=[gc], slow=True)
    bgl = C.gcol(b_glu[0], 512, "bgl")
    wout = A.alloc([8, 1024], BF16, "wout")
    wglu = A.alloc([4, 512], BF16, "wglu")
    C.load_weight_bf16(wout, w_out[0], 8, 1024, gc, "wout")
    C.load_weight_bf16(wglu, w_glu[0], 4, 512, None, "wglu")
    qs = A.alloc([2, 2048], BF16, "qs")
    ks = A.alloc([2, 4096], BF16, "ks")
    acc = A.alloc([4, 2048], F32, "acc")
    dsw = A.alloc([2, 2048], F32, "dsw")
    attT = A.alloc([4, 2048], BF16, "attT")
    pTs = [A.alloc([8, 128], BF16, "pT%d" % i) for i in range(3)]
    pTbs = [T(None, "pTb%d" % i) for i in range(3)]
    vas = [A.alloc([4, 128], BF16, "va%d" % i) for i in range(4)]
    vbs = [A.alloc([4, 128], BF16, "vb%d" % i) for i in range(4)]
    dflat = dsw.ap.rearrange("p a b -> p (a b)")
    accflat = acc.ap.rearrange("p a b -> p (a b)")

    def alias_bf(i, name):
        return T(dflat[:, i * 1024:(i + 1) * 1024].bitcast(BF16).rearrange("p (a b) -> p a b", a=4), name)
    gts = [alias_bf(i, "gt%d" % i) for i in range(2)]
    sqbs = [alias_bf(2 + i, "sqb%d" % i) for i in range(2)]
    sqbs2 = [T(accflat[:, i * 1024:(i + 1) * 1024].bitcast(BF16).rearrange("p (a b) -> p a b", a=4), "sqc%d" % i) for i in range(2)]
    sTs = [A.alloc([4, 512], BF16, "sT%d" % i) for i in range(2)]
    rs5s = [A.alloc([512], F32, "rs5%d" % i) for i in range(4)]
    sigs = [A.alloc([512], BF16, "sig%d" % i) for i in range(2)]
    mixTs = [A.alloc([8, 512], BF16, "mixT%d" % i) for i in range(2)]
    xb = [A.alloc([1024], F32, "xB%d" % i) for i in range(3)]
    pend = []
    bi = 0
    for SB in range(NSB):
        tok0 = SB * 2048
        for half in range(2):
            k.dma("sp", qs.ap, qT[2 * half:2 * half + 2, :, tok0:tok0 + 2048].rearrange("f p t -> p f t"), w=[qs])
            lo, hi = tok0 - 1024, tok0 + 3072
            vlo, vhi = max(lo, 0), min(hi, S)
            if vlo > lo:
                k.memset("pool", ks.ap[:, :, 0:vlo - lo], 0.0, w=[ks])
            if vhi < hi:
                k.memset("pool", ks.ap[:, :, vhi - lo:4096], 0.0, w=[ks])
            k.dma("sp", ks.ap[:, :, vlo - lo:vhi - lo], kT[2 * half:2 * half + 2, :, vlo:vhi].rearrange("f p t -> p f t"), w=[ks])
            blocks = []
            first = True
            for di, d in enumerate(DIL):
                if d not in DSEL:
                    continue
                for r in range(d):
                    for mb in range(16 // d):
                        blocks.append((di, d, r, mb, first))
                first = False

            def emit_scores(j):
                di, d, r, mb, isfirst = blocks[j]
                q0 = r + d * 128 * mb
                va, vb = vas[j % 4], vbs[j % 4]
                rowA = 1024 + tok0 + q0 - 64 * d
                rowB = 1024 + tok0 + q0 + 64 * d
                k.dma("sp", va.ap, Vp[rowA:rowA + 127 * d + 1:d, 4 * half:4 * half + 4, :], w=[va])
                k.dma("sp", vb.ap, Vp[rowB:rowB + 127 * d + 1:d, 4 * half:4 * half + 4, :], w=[vb])
                for hl in range(4):
                    p0, hpl = 64 * (hl % 2), hl // 2
                    qa = strided(qs.ap[p0:p0 + 64, hpl, q0:q0 + 1], d, 128)
                    for ab in range(2):
                        kc = 1024 + q0 + (-64 * d if ab == 0 else 64 * d)
                        ka = strided(ks.ap[p0:p0 + 64, hpl, kc:kc + 1], d, 128)
                        pss = PS[2 * (j % 3) + hl % 2]
                        cslot = (hl // 2) * 2 + ab
                        k.mm(pss.ap[:, cslot * 128:(cslot + 1) * 128], ka, qa, True, True, r=[ks, qs], w=[pss])

            def emit_soft(j):
                di, d, r, mb, isfirst = blocks[j]
                pT, pTb = pTs[j % 3], pTbs[j % 3]
                ps0, ps1 = PS[2 * (j % 3)], PS[2 * (j % 3) + 1]
                pTf = pT.ap.rearrange("p a b -> p (a b)")
                k.act(pTf[:, 0:512], ps0.ap, AF.Exp, r=[ps0], w=[pT])
                k.act(pTf[:, 512:1024], ps1.ap, AF.Exp, r=[ps1], w=[pTb])
                for hh in range(2):
                    ebv = EB.ap[:, di, 4 * half + hh:4 * half + 4:2, :, :]
                    pv4 = pT.ap[:, hh * 4:(hh + 1) * 4, :].rearrange("p (a b) c -> p a b c", b=2)
                    tok = pT if hh == 0 else pTb
                    k.tt("dve" if hh == 0 else "pool", pv4, pv4, ebv, ALU.mult, r=[tok, EB], w=[tok])

            def emit_pv(j):
                di, d, r, mb, isfirst = blocks[j]
                q0 = r + d * 128 * mb
                pT, pTb, va, vb = pTs[j % 3], pTbs[j % 3], vas[j % 4], vbs[j % 4]
                pnd = PS[6 + j % 2]
                for hl in range(4):
                    hh, hpl = hl % 2, hl // 2
                    for ab in range(2):
                        vt = va if ab == 0 else vb
                        k.mm(pnd.ap[:, hl * 128:(hl + 1) * 128], vt.ap[:, hl, :], pT.ap[:, hh * 4 + hpl * 2 + ab, :], ab == 0, ab == 1, r=[vt, pT, pTb], w=[pnd])
                av = ap3(acc.ap[:, 0, q0:q0 + 1], [2048, 4], [d, 128])
                pv = pnd.ap.rearrange("p (a b) -> p a b", a=4)
                if isfirst:
                    k.cp("act", av, pv, r=[pnd], w=[acc])
                else:
                    k.tt("dve", av, pv, av, ALU.add, r=[pnd, acc], w=[acc])
            nb_ = len(blocks)
            emit_scores(0)
            if nb_ > 1:
                emit_scores(1)
            emit_soft(0)
            for j in range(nb_):
                if j + 2 < nb_:
                    emit_scores(j + 2)
                if j + 1 < nb_:
                    emit_soft(j + 1)
                emit_pv(j)
            for hpl in range(2):
                k.dma("sp", dsw.ap[0:64, hpl, :], acc.ap[64:128, 2 * hpl, :], r=[acc], w=[dsw])
                k.dma("sp", dsw.ap[64:128, hpl, :], acc.ap[0:64, 2 * hpl + 1, :], r=[acc], w=[dsw])
            k.act(dsw.ap, dsw.ap, AF.Ln, r=[dsw], w=[dsw])
            k.act(dsw.ap, dsw.ap, AF.Exp, scale=-1.0, r=[dsw], w=[dsw])
            for hpl in range(2):
                k.tt("dve", attT.ap[0:64, 2 * half + hpl, :], acc.ap[0:64, 2 * hpl, :], dsw.ap[0:64, hpl, :], ALU.mult, r=[acc, dsw], w=[attT])
                k.tt("pool", attT.ap[64:128, 2 * half + hpl, :], acc.ap[64:128, 2 * hpl + 1, :], dsw.ap[64:128, hpl, :], ALU.mult, r=[acc, dsw], w=[attT])
        def stageN(bb):
            t0 = tok0 + bb * 512
            mx, sq_, sT_, gt_ = mixTs[bb % 2], sqbs[bb % 2], sTs[bb % 2], gts[bb % 2]
            rsa, rss = rs5s[2 * (bb % 2)], rs5s[2 * (bb % 2) + 1]
            av = attT.ap[:, :, bb * 512:(bb + 1) * 512]
            k.dma("sp", gt_.ap, gT[:, :, t0:t0 + 512].rearrange("f p t -> p f t"), w=[gt_])
            k.act(sq_.ap, av, AF.Square, r=[attT], w=[sq_])
            for kk in range(4):
                k.mm(PS[0].ap, C.ones_b.ap, sq_.ap[:, kk, :], kk == 0, kk == 3, r=[sq_, C.ones_b], w=[PS[0]])
            k.ts("dve", rsa.ap, PS[0].ap, 1.0 / 512, ALU.mult, EPS, ALU.add, r=[PS[0]], w=[rsa])
            for j in range(4):
                sg_, pz = sigs[j % 2], PS[2 + j % 2]
                for kk in range(4):
                    k.mm(pz.ap, wglu.ap[:, kk, j * 128:(j + 1) * 128], gt_.ap[:, kk, :], kk == 0, kk == 3, r=[wglu, gt_], w=[pz])
                k.act(sg_.ap, pz.ap, AF.Sigmoid, bias=bgl.ap[:, j:j + 1], r=[pz, bgl], w=[sg_])
                k.tt("dve", sT_.ap[:, j, :], gt_.ap[:, j, :], sg_.ap, ALU.mult, r=[gt_, sg_], w=[sT_])
            sq2 = sqbs2[bb % 2]
            k.act(sq2.ap, sT_.ap, AF.Square, r=[sT_], w=[sq2])
            for kk in range(4):
                k.mm(PS[1].ap, C.ones_b.ap, sq2.ap[:, kk, :], kk == 0, kk == 3, r=[sq2, C.ones_b], w=[PS[1]])
            k.ts("dve", rss.ap, PS[1].ap, 1.0 / 512, ALU.mult, EPS, ALU.add, r=[PS[1]], w=[rss])
            for r_ in (rsa, rss):
                k.act(r_.ap, r_.ap, AF.Ln, r=[r_], w=[r_])
            for r_ in (rsa, rss):
                k.act(r_.ap, r_.ap, AF.Exp, scale=-0.5, r=[r_], w=[r_])
            k.tt("dve", mx.ap[:, 0:4, :], av, rsa.ap.unsqueeze(1).to_broadcast([128, 4, 512]), ALU.mult, r=[attT, rsa], w=[mx])
            k.tt("pool", mx.ap[:, 4:8, :], sT_.ap, rss.ap.unsqueeze(1).to_broadcast([128, 4, 512]), ALU.mult, r=[sT_, rss], w=[mx])

        def stageW(bb):
            t0 = tok0 + bb * 512
            mx = mixTs[bb % 2]
            for t in range(4):
                ti = t0 // 128 + t
                xt = xb[ti % 3]
                k.dma("sp", xt.ap, x[ti * 128:(ti + 1) * 128, :], w=[xt])
                if pend:
                    pti, pxt = pend.pop()
                    k.dma("pool", hs[pti * 128:(pti + 1) * 128, :], pxt.ap, r=[pxt])
                for h2 in range(2):
                    po = PS[4 + (2 * t + h2) % 4]
                    cs = slice(h2 * 512, (h2 + 1) * 512)
                    for kk in range(8):
                        k.mm(po.ap, mx.ap[:, kk, t * 128:(t + 1) * 128], wout.ap[:, kk, cs], kk == 0, kk == 7, r=[mx, wout], w=[po])
                    k.tt("dve", xt.ap[:, cs], po.ap, xt.ap[:, cs], ALU.add, r=[po, xt], w=[xt])
                pend.append((ti, xt))
        if STOP > 3:
            P.barrier()
            stageN(0)
            for bb in range(4):
                if bb + 1 < 4:
                    stageN(bb + 1)
                stageW(bb)
            P.barrier()
    while pend:
        pti, pxt = pend.pop()
        k.dma("pool", hs[pti * 128:(pti + 1) * 128, :], pxt.ap, r=[pxt])
    A.pop()


def norm_T(C, src_ap, ss, tm, rs, junk, a_, pst, dstT, col0, ncols=1024, r_src=(), cp_eng="act"):
    k = C.k
    nk = ncols // 128
    k.memset("pool", ss.ap, 0.0, w=[ss])
    k.act(junk.ap[:, 0:ncols], src_ap, AF.Square, accum=ss.ap, r=list(r_src) + [ss], w=[junk, ss])
    C.rms_rstd(ss, ncols, tm, rs)
    k.ts("dve", a_.ap[:, 0:ncols], src_ap, rs.ap[:, 0:1], ALU.mult, r=list(r_src) + [rs], w=[a_])
    psv = pst.ap.bitcast(BF16).rearrange("p (a b) -> p a b", a=8)
    for kk in range(nk):
        k.tr(psv[:, kk, :], a_.ap[:, kk * 128:(kk + 1) * 128], C.identb.ap, r=[a_, C.identb], w=[pst])
    k.cp(cp_eng, dstT.ap[:, 0:nk, col0:col0 + 128], psv[:, 0:nk, :], r=[pst], w=[dstT])


def phase_C(C, hs, hs2, g_mlp, w_mlp1, w_mlp2):
    k, A, PS, S = C.k, C.A, C.PS, C.S
    A.push()
    gm = C.gcol(g_mlp[0], 1024, "gmlp")
    w1 = A.alloc([8, 4096], BF16, "w1")
    w2 = A.alloc([32, 1024], BF16, "w2")
    C.load_weight_bf16(w1, w_mlp1[0], 8, 4096, gm, "w1")
    C.load_weight_bf16(w2, w_mlp2[0], 32, 1024, None, "w2")
    hb = [A.alloc([2, 1024], F32, "hc%d" % i) for i in range(3)]
    pendc = []
    junk = A.alloc([1024], F32, "junkc")
    ab = [A.alloc([1024], BF16, "abc%d" % i) for i in range(2)]
    ss = [A.alloc([1], F32, "ssc%d" % i) for i in range(2)]
    tmp = [A.alloc([1], F32, "tmpc%d" % i) for i in range(2)]
    rstd = [A.alloc([1], F32, "rstdc%d" % i) for i in range(2)]
    fT = A.alloc([8, 256], BF16, "fT")
    hidT = A.alloc([32, 256], BF16, "hidT")
    rl = [A.alloc([256], F32, "rl%d" % i) for i in range(2)]
    fTs = [fT, A.alloc([8, 256], BF16, "fT1")]

    def cnorm(blk):
        h = hb[blk % 3]
        k.dma("sp", h.ap, hs[blk * 256:(blk + 1) * 256, :].rearrange("(t p) d -> p t d", p=128), w=[h])
        for t in range(2):
            i = (2 * blk + t) % 2
            norm_T(C, h.ap[:, t, :], ss[i], tmp[i], rstd[i], junk, ab[i], PS[i], fTs[blk % 2], t * 128, r_src=[h])

    def cup(blk):
        f_ = fTs[blk % 2]
        for j in range(32):
            pz = PS[2 + j % 2]
            for kk in range(8):
                k.mm(pz.ap[:, 0:256], w1.ap[:, kk, j * 128:(j + 1) * 128], f_.ap[:, kk, :], kk == 0, kk == 7, r=[w1, f_], w=[pz])
            r_ = rl[j % 2]
            k.act(r_.ap, pz.ap[:, 0:256], AF.Relu, r=[pz], w=[r_])
            k.tt("dve" if j % 2 == 0 else "pool", hidT.ap[:, j, :], r_.ap, r_.ap, ALU.mult, r=[r_], w=[hidT])

    def cdown(blk):
        h = hb[blk % 3]
        for t in range(2):
            for half in range(2):
                po = PS[4 + (2 * t + half) % 4]
                for j in range(32):
                    k.mm(po.ap, hidT.ap[:, j, t * 128:(t + 1) * 128], w2.ap[:, j, half * 512:(half + 1) * 512], j == 0, j == 31, r=[hidT, w2], w=[po])
                k.tt("dve", h.ap[:, t, half * 512:(half + 1) * 512], po.ap, h.ap[:, t, half * 512:(half + 1) * 512], ALU.add, r=[po, h], w=[h])
        if pendc:
            pb_, ph_ = pendc.pop()
            k.dma("pool", hs2[pb_ * 256:(pb_ + 1) * 256, :].rearrange("(t p) d -> p t d", p=128), ph_.ap, r=[ph_])
        pendc.append((blk, h))
    NBC = S // 256
    cnorm(0)
    for blk in range(NBC):
        cup(blk)
        if blk + 1 < NBC:
            cnorm(blk + 1)
        cdown(blk)
    while pendc:
        pb_, ph_ = pendc.pop()
        k.dma("pool", hs2[pb_ * 256:(pb_ + 1) * 256, :].rearrange("(t p) d -> p t d", p=128), ph_.ap, r=[ph_])
    A.pop()


def phase_D(C, hs2, pin, y, g_ple, w_gate, w_proj, g_final):
    k, A, PS, S, NT = C.k, C.A, C.PS, C.S, C.NT
    A.push()
    gp = C.gcol(g_ple[0], 1024, "gple")
    wg = A.alloc([8, 1024], BF16, "wg")
    wp = A.alloc([2, 1024], BF16, "wp")
    C.load_weight_bf16(wg, w_gate[0], 8, 1024, gp, "wg")
    C.load_weight_bf16(wp, w_proj[0], 2, 1024, None, "wp")
    gfin = A.alloc([1024], F32, "gfin")
    k.dma("sp", gfin.ap, g_final.rearrange("(o d) -> o d", o=1).partition_broadcast(128), w=[gfin])
    hb = [A.alloc([1024], F32, "hd%d" % i) for i in range(5)]
    pb = [A.alloc([256], F32, "pd%d" % i) for i in range(3)]
    junk = A.alloc([1024], F32, "junkd")
    ab = [A.alloc([1024], BF16, "abd%d" % i) for i in range(3)]
    pbb = [A.alloc([256], BF16, "pbb%d" % i) for i in range(3)]
    ss = [A.alloc([1], F32, "ssd%d" % i) for i in range(5)]
    tmp = [A.alloc([1], F32, "tmpd%d" % i) for i in range(5)]
    rstd = [A.alloc([1], F32, "rstdd%d" % i) for i in range(5)]
    eT = [A.alloc([8, 128], BF16, "eT%d" % i) for i in range(3)]
    pT = [A.alloc([2, 128], BF16, "pT%d" % i) for i in range(3)]
    sg = [A.alloc([512], F32, "sg%d" % i) for i in range(2)]
    def stage1(ti):
        h, p_, e_, pT_, pb_ = hb[ti % 5], pb[ti % 3], eT[ti % 3], pT[ti % 3], pbb[ti % 3]
        i = ti % 3
        k.dma("sp", h.ap, hs2[ti * 128:(ti + 1) * 128, :], w=[h])
        k.dma("sp", p_.ap, pin[ti * 128:(ti + 1) * 128, :], w=[p_])
        norm_T(C, h.ap, ss[i], tmp[i], rstd[i], junk, ab[i], PS[ti % 2], e_, 0, r_src=[h], cp_eng="dve")
        k.cp("pool", pb_.ap, p_.ap, r=[p_], w=[pb_])
        pst = PS[2 + ti % 2]
        psv = pst.ap.bitcast(BF16).rearrange("p (a b) -> p a b", a=8)
        for kk in range(2):
            k.tr(psv[:, kk, :], pb_.ap[:, kk * 128:(kk + 1) * 128], C.identb.ap, r=[pb_, C.identb], w=[pst])
        k.cp("dve", pT_.ap, psv[:, 0:2, :], r=[pst], w=[pT_])

    def stage2a(ti):
        e_, pT_ = eT[ti % 3], pT[ti % 3]
        for half in range(2):
            pg, pp = PS[4 + half], PS[6 + half]
            cs = slice(half * 512, (half + 1) * 512)
            for kk in range(8):
                k.mm(pg.ap, e_.ap[:, kk, :], wg.ap[:, kk, cs], kk == 0, kk == 7, r=[e_, wg], w=[pg])
            for kk in range(2):
                k.mm(pp.ap, pT_.ap[:, kk, :], wp.ap[:, kk, cs], kk == 0, kk == 1, r=[pT_, wp], w=[pp])

    def stage2b(ti):
        h = hb[ti % 5]
        for half in range(2):
            pg, pp, s_ = PS[4 + half], PS[6 + half], sg[half]
            cs = slice(half * 512, (half + 1) * 512)
            k.act(s_.ap, pg.ap, AF.Sigmoid, r=[pg], w=[s_])
            k.tt("dve", s_.ap, pp.ap, s_.ap, ALU.mult, r=[pp, s_], w=[s_])
            k.tt("pool", h.ap[:, cs], h.ap[:, cs], s_.ap, ALU.add, r=[h, s_], w=[h])
        j = 3 + ti % 2
        k.memset("pool", ss[j].ap, 0.0, w=[ss[j]])
        k.act(junk2.ap, h.ap, AF.Square, accum=ss[j].ap, r=[h, ss[j]], w=[junk2, ss[j]])
        C.rms_rstd(ss[j], 1024, tmp[j], rstd[j])
        k.stt(h.ap, h.ap, rstd[j].ap[:, 0:1], gfin.ap, ALU.mult, ALU.mult, r=[h, rstd[j], gfin], w=[h])

    def store(ti):
        k.dma("pool", y[ti * 128:(ti + 1) * 128, :], hb[ti % 5].ap, r=[hb[ti % 5]])

    junk2 = A.alloc([1024], F32, "junkd2")
    stage1(0)
    stage1(1)
    for ti in range(NT):
        stage2a(ti)
        if ti + 2 < NT:
            stage1(ti + 2)
        if ti >= 1:
            store(ti - 1)
        stage2b(ti)
    store(NT - 1)
    A.pop()


_NC_CACHE = {}


def kernel(**inputs):
    S = 8192
    xs = [inputs["x_prompt"][i] for i in range(2)] + [inputs["x_sample"][i] for i in range(4)]
    ps = [inputs["p_prompt"][0, i] for i in range(2)] + [inputs["p_sample"][0, i] for i in range(4)]
    xs += [np.zeros_like(xs[0]), np.zeros_like(xs[0])]
    ps += [np.zeros_like(ps[0]), np.zeros_like(ps[0])]
    if "nc" not in _NC_CACHE:
        _NC_CACHE["nc"] = build(S)
    nc = _NC_CACHE["nc"]
    wnames = ["rel_bias", "g_mix", "w_in", "ssm_a_re", "ssm_a_im", "ssm_log_dt", "ssm_b_re", "ssm_b_im", "ssm_c_re",
              "ssm_c_im", "ssm_d", "w_glu", "b_glu", "g_att_out", "g_ssm_out", "w_out", "g_mlp", "w_mlp1", "w_mlp2",
              "g_ple", "w_ple_gate", "w_ple_proj", "g_final"]
    in_maps = []
    for c in range(8):
        m = {"x": np.ascontiguousarray(xs[c], dtype=np.float32), "p": np.ascontiguousarray(ps[c], dtype=np.float32)}
        for n in wnames:
            m[n] = np.ascontiguousarray(inputs[n], dtype=np.float32)
        in_maps.append(m)
    res = run_bass_kernel_spmd(nc, in_maps, core_ids=list(range(8)))
    outs = [np.asarray(r["y"], dtype=np.float32) for r in res.results]
    y_prompt = np.stack(outs[0:2], axis=0)
    y_sample = np.stack(outs[2:6], axis=0)
    return (y_prompt, y_sample)
```

```python
import math
import numpy as np
from contextlib import ExitStack
import concourse.bass as bass
import concourse.mybir as mybir
from concourse.bass_utils import run_bass_kernel_spmd

F32 = mybir.dt.float32
BF16 = mybir.dt.bfloat16
I32 = mybir.dt.int32
AF = mybir.ActivationFunctionType
ALU = mybir.AluOpType

ENGS = ("pe", "act", "dve", "pool", "sp")
D = 1024
EPS = 1e-6


class Buf:
    __slots__ = ("name", "last_w", "readers")

    def __init__(self, name=""):
        self.name = name
        self.last_w = None
        self.readers = []


class Op:
    __slots__ = ("eng", "fn", "reads", "writes", "is_dma", "deps", "idx", "signal", "count", "sem", "semkey")

    def __init__(self, eng, fn, reads, writes, is_dma):
        self.eng = eng
        self.fn = fn
        self.reads = reads
        self.writes = writes
        self.is_dma = is_dma
        self.deps = set()
        self.signal = False
        self.count = 0
        self.sem = None
        self.semkey = None


class Prog:
    def __init__(self, nc, n_dma_sems=40):
        self.nc = nc
        self.ops = []
        self.n_dma_sems = n_dma_sems
        self.ALL = Buf("ALL")

    def op(self, eng, fn, reads=(), writes=()):
        self._add(Op(eng, fn, tuple(reads) + (self.ALL,), tuple(writes), False))

    def dma(self, queue, fn, reads=(), writes=()):
        self._add(Op(queue, fn, tuple(reads) + (self.ALL,), tuple(writes), True))

    def barrier(self):
        self._add(Op("dve", lambda e: e.nop() if False else e.memset(self._bar[:], 0.0), (), (self.ALL,), False))

    def _add(self, o):
        o.idx = len(self.ops)
        for b in o.reads:
            if b.last_w is not None:
                o.deps.add(b.last_w)
        for b in o.writes:
            if b.last_w is not None:
                o.deps.add(b.last_w)
            for r in b.readers:
                o.deps.add(r)
        for b in o.reads:
            b.readers.append(o.idx)
        for b in o.writes:
            b.last_w = o.idx
            b.readers = []
        o.deps.discard(o.idx)
        self.ops.append(o)

    def emit(self, stack):
        nc = self.nc
        ops = self.ops
        needed = []
        for o in ops:
            nd = []
            for d in o.deps:
                y = ops[d]
                if y.is_dma or o.is_dma or y.eng != o.eng:
                    nd.append(d)
                elif o.eng != "pe":
                    if any((b in y.writes) for b in o.reads if b is not self.ALL) or \
                       any((b in y.writes or b in y.reads) for b in o.writes if b is not self.ALL):
                        nd.append(d)
            needed.append(nd)
            for d in nd:
                ops[d].signal = True
        for o in ops:
            if o.is_dma:
                o.signal = True
        eng_sem = {e: stack.enter_context(nc.semaphore("s_" + e)) for e in ENGS}
        nqs = {"sp": self.n_dma_sems - 16, "pool": 4, "act": 6, "dve": 6}
        dma_sems = {q: [stack.enter_context(nc.semaphore("d%s%d" % (q, i))) for i in range(nqs[q])] for q in nqs}
        cnt = {e: 0 for e in ENGS}
        dcnt = {q: [0] * nqs[q] for q in nqs}
        rr = {q: 0 for q in nqs}
        for o in ops:
            if o.is_dma:
                q = o.eng
                nq = nqs[q]
                k = rr[q] % nq
                rr[q] += 1
                dcnt[q][k] += 16
                o.sem, o.count, o.semkey = dma_sems[q][k], dcnt[q][k], ("d", q, k)
            elif o.signal:
                cnt[o.eng] += 1
                o.sem, o.count, o.semkey = eng_sem[o.eng], cnt[o.eng], ("e", o.eng)
        waited = {e: {} for e in ENGS}
        block = stack.enter_context(nc.Block())
        per_eng = {e: [] for e in ENGS}
        for o in ops:
            per_eng[o.eng].append(o)
        final_waits = {}
        for o in ops:
            if o.is_dma:
                final_waits[o.semkey] = (o.sem, o.count)

        def make(ename):
            def body(eng):
                w = waited[ename]
                for o in per_eng[ename]:
                    req = {}
                    for d in needed[o.idx]:
                        y = ops[d]
                        if req.get(y.semkey, (None, 0))[1] < y.count:
                            req[y.semkey] = (y.sem, y.count)
                    for key, (sem, c) in req.items():
                        if w.get(key, 0) < c:
                            eng.wait_ge(sem, c)
                            w[key] = c
                    if o.is_dma and o.count > 16 and w.get(o.semkey, 0) < o.count - 16:
                        eng.wait_ge(o.sem, o.count - 16)
                        w[o.semkey] = o.count - 16
                    ins = o.fn(eng)
                    if o.signal:
                        ins.then_inc(o.sem, 16 if o.is_dma else 1)
                if ename == "sp":
                    for key, (sem, c) in final_waits.items():
                        if w.get(key, 0) < c:
                            eng.wait_ge(sem, c)
                            w[key] = c
            return body

        block.tensor(make("pe"))
        block.scalar(make("act"))
        block.vector(make("dve"))
        block.gpsimd(make("pool"))
        block.sync(make("sp"))


class T:
    __slots__ = ("ap", "b")

    def __init__(self, ap, name=""):
        self.ap = ap
        self.b = Buf(name)


class Arena:
    def __init__(self, big, ncols):
        self.big = big
        self.ncols = ncols
        self.off = 0
        self.marks = []

    def push(self):
        self.marks.append(self.off)

    def pop(self):
        self.off = self.marks.pop()

    def alloc(self, free_shape, dtype, name=""):
        n = int(np.prod(free_shape))
        n32 = n if dtype in (F32, I32) else (n + 1) // 2
        assert self.off + n32 <= self.ncols, ("SBUF arena overflow", name, self.off, n32, self.ncols)
        v = self.big[:, self.off:self.off + n32]
        self.off += n32
        if dtype not in (F32,):
            v = v.bitcast(dtype)
            if n % 2 and dtype == BF16:
                v = v[:, 0:n]
        if len(free_shape) == 2:
            v = v.rearrange("p (a b) -> p a b", a=free_shape[0])
        elif len(free_shape) == 3:
            v = v.rearrange("p (a b c) -> p a b c", a=free_shape[0], b=free_shape[1])
        elif len(free_shape) == 4:
            v = v.rearrange("p (a b c d) -> p a b c d", a=free_shape[0], b=free_shape[1], c=free_shape[2])
        return T(v, name)


def _bs(ts_):
    return [t.b for t in ts_]


class K:
    def __init__(self, P):
        self.P = P

    def dma(self, q, out, in_, r=(), w=(), slow=False):
        if slow:
            self.P.dma(q, lambda e: e.dma_start(out=out, in_=in_, allow_slow_non_contiguous=True), _bs(r), _bs(w))
        else:
            self.P.dma(q, lambda e: e.dma_start(out=out, in_=in_), _bs(r), _bs(w))

    def mm(self, out, lhsT, rhs, start, stop, r=(), w=()):
        self.P.op("pe", lambda e: e.matmul(out, lhsT=lhsT, rhs=rhs, start=start, stop=stop), _bs(r), _bs(w))

    def tr(self, out, in_, ident, r=(), w=()):
        self.P.op("pe", lambda e: e.transpose(out, in_, ident), _bs(r), _bs(w))

    def act(self, out, in_, func, r=(), w=(), scale=None, bias=None, accum=None):
        kw = {}
        if scale is not None:
            kw["scale"] = scale
        if bias is not None:
            kw["bias"] = bias
        if accum is not None:
            kw["accum_out"] = accum
        self.P.op("act", lambda e: e.activation(out=out, in_=in_, func=func, **kw), _bs(r), _bs(w))

    def tt(self, eng, out, a, b, op, r=(), w=()):
        self.P.op(eng, lambda e: e.tensor_tensor(out=out, in0=a, in1=b, op=op), _bs(r), _bs(w))

    def ts(self, eng, out, a, s1, op0, s2=None, op1=None, r=(), w=()):
        if op1 is None:
            self.P.op(eng, lambda e: e.tensor_scalar(out=out, in0=a, scalar1=s1, scalar2=None, op0=op0), _bs(r), _bs(w))
        else:
            self.P.op(eng, lambda e: e.tensor_scalar(out=out, in0=a, scalar1=s1, scalar2=s2, op0=op0, op1=op1), _bs(r), _bs(w))

    def stt(self, out, a, s, b, op0, op1, r=(), w=()):
        self.P.op("dve", lambda e: e.scalar_tensor_tensor(out=out, in0=a, scalar=s, in1=b, op0=op0, op1=op1), _bs(r), _bs(w))

    def cp(self, eng, out, in_, r=(), w=()):
        if eng == "act":
            self.P.op("act", lambda e: e.activation(out=out, in_=in_, func=AF.Copy), _bs(r), _bs(w))
        else:
            self.P.op(eng, lambda e: e.tensor_copy(out=out, in_=in_), _bs(r), _bs(w))

    def memset(self, eng, ap, val, w=()):
        self.P.op(eng, lambda e: e.memset(ap, val), (), _bs(w))

    def recip(self, out, in_, r=(), w=()):
        self.P.op("dve", lambda e: e.reciprocal(out=out, in_=in_), _bs(r), _bs(w))

    def scan(self, out, d0, d1, init, r=(), w=()):
        self.P.op("dve", lambda e: e.tensor_tensor_scan(out=out, data0=d0, data1=d1, initial=init, op0=ALU.mult, op1=ALU.add), _bs(r), _bs(w))

    def asel(self, out, in_, pattern, cmp, fill, base, cm, r=(), w=()):
        self.P.op("pool", lambda e: e.affine_select(out=out, in_=in_, pattern=pattern, compare_op=cmp, fill=fill, base=base, channel_multiplier=cm), _bs(r), _bs(w))


def rev_ap(ap2d_last_col, n):
    return bass.AP(ap2d_last_col.tensor, ap2d_last_col.offset, [list(ap2d_last_col.ap[0]), [-1, n]])


def strided(ap_first, step, n):
    return bass.AP(ap_first.tensor, ap_first.offset, [list(ap_first.ap[0]), [step, n]])


def t5_bucket_np(rel):
    half = 16
    n = -rel
    ret = np.where(n < 0, half, 0)
    n = np.abs(n)
    max_exact = 8
    nf = np.maximum(n, 1).astype(np.float32)
    large = max_exact + (np.log(nf / np.float32(max_exact)) / np.float32(math.log(1024 / max_exact)) * (half - max_exact)).astype(np.int32)
    large = np.minimum(large, half - 1)
    return ret + np.where(n < max_exact, n, large)


class Ctx:
    pass


def build(S, debug=None):
    nc = bass.Bass("TRN2", target_bir_lowering=False)
    NT = S // 128
    C = Ctx()
    C.nc, C.S, C.NT = nc, S, NT
    din = {}

    def inp(name, shape):
        din[name] = nc.dram_tensor(name, list(shape), F32, kind="ExternalInput").ap()
        return din[name]

    x = inp("x", [S, D])
    pin = inp("p", [S, 256])
    rel_bias = inp("rel_bias", [32, 8])
    g_mix = inp("g_mix", [1, 1024])
    w_in = inp("w_in", [1, 1024, 2048])
    a_re = inp("ssm_a_re", [1, 2, 32, 64])
    a_im = inp("ssm_a_im", [1, 2, 32, 64])
    log_dt = inp("ssm_log_dt", [1, 2, 32])
    b_re = inp("ssm_b_re", [1, 2, 32, 64, 16])
    b_im = inp("ssm_b_im", [1, 2, 32, 64, 16])
    c_re = inp("ssm_c_re", [1, 2, 32, 16, 64])
    c_im = inp("ssm_c_im", [1, 2, 32, 16, 64])
    ssm_d = inp("ssm_d", [1, 32, 16])
    w_glu = inp("w_glu", [1, 512, 512])
    b_glu = inp("b_glu", [1, 512])
    g_att = inp("g_att_out", [1, 512])
    g_ssm = inp("g_ssm_out", [1, 512])
    w_out = inp("w_out", [1, 1024, 1024])
    g_mlp = inp("g_mlp", [1, 1024])
    w_mlp1 = inp("w_mlp1", [1, 1024, 4096])
    w_mlp2 = inp("w_mlp2", [1, 4096, 1024])
    g_ple = inp("g_ple", [1, 1024])
    w_gate = inp("w_ple_gate", [1, 1024, 1024])
    w_proj = inp("w_ple_proj", [1, 256, 1024])
    g_final = inp("g_final", [1024])
    y = nc.dram_tensor("y", [S, D], F32, kind="ExternalOutput").ap()

    def scratch(name, shape, dt):
        kind = "ExternalOutput" if (debug and name in debug) else "Internal"
        return nc.dram_tensor(name, list(shape), dt, kind=kind).ap()

    qT = scratch("qT", [4, 128, S], BF16)
    kT = scratch("kT", [4, 128, S], BF16)
    uT = scratch("uT", [4, 128, S], BF16)
    gT = scratch("gT", [4, 128, S], BF16)
    Vp = scratch("Vp", [S + 2048, 8, 128], BF16)
    hs = scratch("hs", [S, D], F32)
    hs2 = scratch("hs2", [S, D], F32)
    Fd = scratch("Fd", [3 * 8 * 512 + 512], F32)

    with ExitStack() as st:
        NCOL = 49100
        big = st.enter_context(nc.sbuf_tensor("big", [128, NCOL], F32))
        psb = [st.enter_context(nc.psum_tensor("ps%d" % i, [128, 512], F32)) for i in range(8)]
        P = Prog(nc)
        k = K(P)
        A = Arena(big, NCOL)
        bar = A.alloc([1], F32, "bar")
        P._bar = bar.ap
        PS = [T(psb[i][:], "ps%d" % i) for i in range(8)]
        C.P, C.k, C.A, C.PS = P, k, A, PS
        identf = A.alloc([128], F32, "identf")
        identb = A.alloc([128], BF16, "identb")
        ones_b = A.alloc([128], BF16, "ones_b")
        k.memset("pool", identf.ap, 0.0, w=[identf])
        k.asel(identf.ap, identf.ap, [[-1, 128]], ALU.not_equal, 1.0, 0, 1, r=[identf], w=[identf])
        k.cp("dve", identb.ap, identf.ap, r=[identf], w=[identb])
        k.memset("pool", ones_b.ap, 1.0, w=[ones_b])
        C.identf, C.identb, C.ones_b = identf, identb, ones_b

        def gcol(src_1d_ap, n, name):
            t = A.alloc([n // 128], F32, name)
            k.dma("sp", t.ap, src_1d_ap.rearrange("(k p) -> p k", p=128), w=[t], slow=True)
            return t
        C.gcol = gcol

        nhalf = A.alloc([1], F32, "nhalf")
        k.memset("pool", nhalf.ap, -0.5, w=[nhalf])

        def rms_rstd(ss_t, n, tmp_t, out_t, ss_ap=None, wide=False):
            if wide:
                k.ts("dve", tmp_t.ap, ss_t.ap if ss_ap is None else ss_ap, 1.0 / n, ALU.mult, EPS, ALU.add, r=[ss_t], w=[tmp_t])
                k.act(tmp_t.ap, tmp_t.ap, AF.Ln, r=[tmp_t], w=[tmp_t])
                k.act(out_t.ap, tmp_t.ap, AF.Exp, scale=-0.5, r=[tmp_t], w=[out_t])
            else:
                k.ts("pool", tmp_t.ap, ss_t.ap if ss_ap is None else ss_ap, 1.0 / n, ALU.mult, EPS, ALU.add, r=[ss_t], w=[tmp_t])
                k.tt("pool", out_t.ap, tmp_t.ap, nhalf.ap, ALU.pow, r=[tmp_t, nhalf], w=[out_t])
        C.rms_rstd = rms_rstd

        def load_weight_bf16(dst, src2d, nk, ncols, gc, tag):
            A.push()
            tmps = [A.alloc([min(ncols, 2048)], F32, tag + "tmp%d" % i) for i in range(2)]
            i = 0
            for kk in range(nk):
                for c0 in range(0, ncols, 2048):
                    cw = min(2048, ncols - c0)
                    tm = tmps[i % 2]
                    k.dma("sp", tm.ap[:, 0:cw], src2d[kk * 128:(kk + 1) * 128, c0:c0 + cw], w=[tm])
                    if i % 2 == 0:
                        if gc is None:
                            k.cp("dve", dst.ap[:, kk, c0:c0 + cw], tm.ap[:, 0:cw], r=[tm], w=[dst])
                        else:
                            k.ts("dve", dst.ap[:, kk, c0:c0 + cw], tm.ap[:, 0:cw], gc.ap[:, kk:kk + 1], ALU.mult, r=[tm, gc], w=[dst])
                    else:
                        if gc is None:
                            k.cp("act", dst.ap[:, kk, c0:c0 + cw], tm.ap[:, 0:cw], r=[tm], w=[dst])
                        else:
                            k.act(dst.ap[:, kk, c0:c0 + cw], tm.ap[:, 0:cw], AF.Copy, scale=gc.ap[:, kk:kk + 1], r=[tm, gc], w=[dst])
                    i += 1
            A.pop()
            P.barrier()
        C.load_weight_bf16 = load_weight_bf16

        phases = debug.get("phases", "ASBCD") if debug else "ASBCD"
        if "A" in phases:
            phase_A(C, x, g_mix, w_in, qT, kT, uT, Vp)
            P.barrier()
        if "S" in phases:
            phase_S(C, uT, gT, a_re, a_im, log_dt, b_re, b_im, c_re, c_im, ssm_d)
            P.barrier()
        if "B" in phases:
            phase_B(C, x, qT, kT, Vp, gT, hs, rel_bias, Fd, w_glu, b_glu, g_att, g_ssm, w_out)
            P.barrier()
        if "X" in phases:
            for i in range(S // 512):
                k.dma("sp", hs[i * 512:(i + 1) * 512, :], x[i * 512:(i + 1) * 512, :])
            P.barrier()
        if "C" in phases:
            phase_C(C, hs, hs2, g_mlp, w_mlp1, w_mlp2)
            P.barrier()
        if "D" in phases:
            phase_D(C, hs2, pin, y, g_ple, w_gate, w_proj, g_final)
        P.emit(st)
    return nc


def phase_A(C, x, g_mix, w_in, qT, kT, uT, Vp):
    k, A, PS, S, NT = C.k, C.A, C.PS, C.S, C.NT
    A.push()
    gm = C.gcol(g_mix[0], 1024, "gm")
    win = A.alloc([8, 2048], BF16, "win")
    C.load_weight_bf16(win, w_in[0], 8, 2048, gm, "win")
    xb = [A.alloc([1024], F32, "xa%d" % i) for i in range(2)]
    junk = A.alloc([1024], F32, "junk")
    ab = [A.alloc([1024], BF16, "ab%d" % i) for i in range(2)]
    ss = [A.alloc([1], F32, "ss%d" % i) for i in range(2)]
    tmp = [A.alloc([1], F32, "tmp%d" % i) for i in range(2)]
    rstd = [A.alloc([1], F32, "rstd%d" % i) for i in range(2)]
    aT = [A.alloc([8, 512], BF16, "aT%d" % i) for i in range(2)]
    zst = [A.alloc([512], BF16, "zst%d" % i) for i in range(4)]
    vst = [A.alloc([8, 128], BF16, "vst%d" % i) for i in range(2)]
    zero = A.alloc([8, 128], BF16, "zero")
    k.memset("dve", zero.ap, 0.0, w=[zero])
    for v in vst:
        k.memset("pool", v.ap, 1.0, w=[v])
    for i in range(8):
        k.dma("sp", Vp[i * 128:(i + 1) * 128], zero.ap, r=[zero])
        k.dma("sp", Vp[1024 + S + i * 128:1024 + S + (i + 1) * 128], zero.ap, r=[zero])
    zi = [0]

    def normstage(b):
        at = aT[b % 2]
        for t in range(4):
            ti = 4 * b + t
            xt, a_, s_, tm, rs = xb[ti % 2], ab[ti % 2], ss[ti % 2], tmp[ti % 2], rstd[ti % 2]
            k.dma("sp", xt.ap, x[ti * 128:(ti + 1) * 128, :], w=[xt])
            k.memset("pool", s_.ap, 0.0, w=[s_])
            k.act(junk.ap, xt.ap, AF.Square, accum=s_.ap, r=[xt, s_], w=[junk, s_])
            C.rms_rstd(s_, 1024, tm, rs)
            k.ts("dve", a_.ap, xt.ap, rs.ap[:, 0:1], ALU.mult, r=[xt, rs], w=[a_])
            pst = PS[ti % 2]
            psv = pst.ap.bitcast(BF16).rearrange("p (a b) -> p a b", a=8)
            for kk in range(8):
                k.tr(psv[:, kk, :], a_.ap[:, kk * 128:(kk + 1) * 128], C.identb.ap, r=[a_, C.identb], w=[pst])
            k.cp("dve", at.ap[:, :, t * 128:(t + 1) * 128], psv, r=[pst], w=[at])

    def projstage(b):
        at = aT[b % 2]
        for (dst, f0, sc) in ((qT, 0, 0.125), (kT, 4, None), (uT, 12, None)):
            for j in range(4):
                pz = PS[2 + zi[0] % 3]
                for kk in range(8):
                    k.mm(pz.ap, win.ap[:, kk, (f0 + j) * 128:(f0 + j + 1) * 128], at.ap[:, kk, :], kk == 0, kk == 7, r=[win, at], w=[pz])
                z = zst[zi[0] % 4]
                k.act(z.ap, pz.ap, AF.Copy, scale=(sc if sc is not None else 1.0), r=[pz], w=[z])
                k.dma("act", dst[j, :, b * 512:(b + 1) * 512], z.ap, r=[z])
                zi[0] += 1
        for t in range(4):
            ti = 4 * b + t
            pv = PS[5 + ti % 3]
            for kk in range(8):
                k.mm(pv.ap, at.ap[:, kk, t * 128:(t + 1) * 128], win.ap[:, kk, 1024:1536], kk == 0, kk == 7, r=[win, at], w=[pv])
            v = vst[ti % 2]
            pvv = pv.ap.rearrange("p (a b c) -> p a b c", a=4, b=2)
            vv = v.ap.rearrange("p (a b) c -> p a b c", b=2)
            k.cp("dve", vv[:, :, 0, 0:64], pvv[:, :, 0, :], r=[pv], w=[v])
            k.cp("act", vv[:, :, 1, 64:128], pvv[:, :, 1, :], r=[pv], w=[v])
            k.dma("act", Vp[1024 + ti * 128:1024 + (ti + 1) * 128], v.ap, r=[v])
    NB = S // 512
    normstage(0)
    for b in range(NB):
        if b + 1 < NB:
            normstage(b + 1)
        projstage(b)
    A.pop()


def phase_S(C, uT, gT, a_re, a_im, log_dt, b_re, b_im, c_re, c_im, ssm_d):
    k, A, PS, S, P = C.k, C.A, C.PS, C.S, C.P
    NCH = S // 8
    CB = min(512, NCH)
    NCB = NCH // CB
    LV = int(round(math.log2(NCH)))
    TWO_PI = 2.0 * math.pi
    A.push()
    Z = A.alloc([8, 240], BF16, "Z")
    T0 = A.alloc([32, 128], BF16, "T0")
    WS = A.alloc([32, 2, 128], BF16, "WS")
    WOF = A.alloc([32, 2, 128], BF16, "WOF")
    WOB = A.alloc([32, 2, 128], BF16, "WOB")
    k.memset("pool", WOF.ap, 0.0, w=[WOF])
    k.memset("pool", WOB.ap, 0.0, w=[WOB])
    Wpow = A.alloc([LV, 32, 2], F32, "Wpow")
    rho8 = A.alloc([32], F32, "rho8")
    Eb = A.alloc([32, 2, 32], F32, "Eb")
    k.memset("pool", Z.ap, 0.0, w=[Z])
    for gi in range(8):
        k.asel(Z.ap[:, gi, 112:128], Z.ap[:, gi, 112:128], [[-1, 16]], ALU.not_equal, 1.0, -16 * gi, 1, r=[Z], w=[Z])
    A.push()

    def new(shape, name):
        return A.alloc(shape, F32, name)
    are, aim, ldt = new([32], "are"), new([32], "aim"), new([32], "ldt")
    bre, bim = new([32, 16], "bre"), new([32, 16], "bim")
    Lr, Li = new([32, 8, 16], "Lr"), new([32, 8, 16], "Li")
    Rr, Ri = new([32, 8, 16], "Rr"), new([32, 8, 16], "Ri")
    dcol = new([32], "dcol")
    for dr in range(2):
        ps_ = slice(dr * 64, dr * 64 + 64)
        k.dma("sp", are.ap[ps_, :], a_re[0, dr].rearrange("g n -> n g"), w=[are], slow=True)
        k.dma("sp", aim.ap[ps_, :], a_im[0, dr].rearrange("g n -> n g"), w=[aim], slow=True)
        k.dma("sp", ldt.ap[ps_, :], log_dt[0, dr:dr + 1, :].partition_broadcast(64), w=[ldt])
        k.dma("sp", bre.ap[ps_], b_re[0, dr].rearrange("g n h -> n g h"), w=[bre], slow=True)
        k.dma("sp", bim.ap[ps_], b_im[0, dr].rearrange("g n h -> n g h"), w=[bim], slow=True)
    for s_ in range(8):
        k.dma("sp", dcol.ap[s_ * 16:(s_ + 1) * 16, :], ssm_d[0].rearrange("g h -> h g"), w=[dcol], slow=True)
    ct = new([128], "ct")
    for (src, dst) in ((c_re, Rr), (c_im, Ri)):
        for o in range(4):
            k.dma("sp", ct.ap[:, 0:64], src[0, 0, 8 * o:8 * o + 8].rearrange("g h n -> (g h) n"), w=[ct])
            k.dma("sp", ct.ap[:, 64:128], src[0, 1, 8 * o:8 * o + 8].rearrange("g h n -> (g h) n"), w=[ct])
            k.tr(PS[0].ap[:, 0:128], ct.ap, C.identf.ap, r=[ct, C.identf], w=[PS[0]])
            k.cp("dve", dst.ap[:, 8 * o:8 * o + 8, 0, :], PS[0].ap[:, 0:128].rearrange("p (a b) -> p a b", b=16), r=[PS[0]], w=[dst])
    sc = [new([32], "sc%d" % i) for i in range(12)]
    sci = A.alloc([32], I32, "sci")

    def mul(o, a, b, eng="dve"):
        k.tt(eng, o.ap, a.ap, b.ap, ALU.mult, r=[a, b], w=[o])

    def sin_of(out, th, shift):
        t1, tf, r_, m_ = sc[8], sc[9], sc[10], sc[11]
        k.ts("dve", r_.ap, th.ap, shift, ALU.add, r=[th], w=[r_])
        k.ts("dve", t1.ap, r_.ap, 1.0 / TWO_PI, ALU.mult, r=[r_], w=[t1])
        k.cp("dve", sci.ap, t1.ap, r=[t1], w=[sci])
        k.cp("dve", tf.ap, sci.ap, r=[sci], w=[tf])
        k.stt(r_.ap, tf.ap, -TWO_PI, r_.ap, ALU.mult, ALU.add, r=[tf, r_], w=[r_])
        k.ts("dve", m_.ap, r_.ap, math.pi, ALU.is_gt, -TWO_PI, ALU.mult, r=[r_], w=[m_])
        k.tt("dve", r_.ap, r_.ap, m_.ap, ALU.add, r=[r_, m_], w=[r_])
        k.ts("dve", m_.ap, r_.ap, -math.pi, ALU.is_lt, TWO_PI, ALU.mult, r=[r_], w=[m_])
        k.tt("dve", r_.ap, r_.ap, m_.ap, ALU.add, r=[r_, m_], w=[r_])
        k.ts("dve", r_.ap, r_.ap, 3.1415925, ALU.min, -3.1415925, ALU.max, r=[r_], w=[r_])
        k.act(out.ap, r_.ap, AF.Sin, r=[r_], w=[out])
    dt_, lam, th, mag, c1, s1, abr, abi = [new([32], "p%d" % i) for i in range(8)]
    k.act(dt_.ap, ldt.ap, AF.Exp, r=[ldt], w=[dt_])
    mul(lam, are, dt_)
    mul(th, aim, dt_)
    k.act(mag.ap, lam.ap, AF.Exp, r=[lam], w=[mag])
    sin_of(s1, th, 0.0)
    sin_of(c1, th, math.pi / 2)
    mul(abr, mag, c1)
    mul(abi, mag, s1)
    inv, fre, fim, am1 = [new([32], "q%d" % i) for i in range(4)]
    mul(sc[0], are, are)
    mul(sc[1], aim, aim)
    k.tt("dve", sc[0].ap, sc[0].ap, sc[1].ap, ALU.add, r=[sc[0], sc[1]], w=[sc[0]])
    k.recip(inv.ap, sc[0].ap, r=[sc[0]], w=[inv])
    k.ts("dve", am1.ap, abr.ap, -1.0, ALU.add, r=[abr], w=[am1])
    mul(sc[0], am1, are)
    mul(sc[1], abi, aim)
    k.tt("dve", sc[0].ap, sc[0].ap, sc[1].ap, ALU.add, r=[sc[0], sc[1]], w=[sc[0]])
    mul(fre, sc[0], inv)
    mul(sc[0], abi, are)
    mul(sc[1], am1, aim)
    k.tt("dve", sc[0].ap, sc[0].ap, sc[1].ap, ALU.subtract, r=[sc[0], sc[1]], w=[sc[0]])
    mul(fim, sc[0], inv)

    def cmul(ore, oim, ar_, ai_, br_, bi_, t1, t2, r, w):
        k.tt("dve", t1, ar_, br_, ALU.mult, r=r, w=w)
        k.tt("dve", t2, ai_, bi_, ALU.mult, r=r, w=w)
        k.tt("dve", ore, t1, t2, ALU.subtract, r=r, w=w)
        k.tt("dve", t1, ar_, bi_, ALU.mult, r=r, w=w)
        k.tt("dve", t2, ai_, br_, ALU.mult, r=r, w=w)
        k.tt("dve", oim, t1, t2, ALU.add, r=r, w=w)
    X = T(None, "prepX")
    allr = [X, are, aim, bre, bim, Rr, Ri, Lr, Li, abr, abi, mag, fre, fim, lam]
    big1, big2 = new([16, 128], "big1"), new([16, 128], "big2")

    def b16(t_):
        return t_.ap.unsqueeze(2).to_broadcast([128, 32, 16])

    def b128(t_):
        return t_.ap.unsqueeze(2).to_broadcast([128, 32, 128])
    t16a = big1.ap.rearrange("p a b -> p (a b)")[:, 0:512].rearrange("p (g h) -> p g h", h=16)
    t16b = big2.ap.rearrange("p a b -> p (a b)")[:, 0:512].rearrange("p (g h) -> p g h", h=16)
    cmul(Lr.ap[:, :, 0, :], Li.ap[:, :, 0, :], b16(fre), b16(fim), bre.ap, bim.ap, t16a, t16b, allr, [X])
    aivr, aivi, im2 = new([32], "aivr"), new([32], "aivi"), new([32], "im2")
    mul(sc[0], mag, mag)
    k.recip(im2.ap, sc[0].ap, r=[sc[0]], w=[im2])
    mul(aivr, abr, im2)
    k.stt(aivi.ap, abi.ap, -1.0, im2.ap, ALU.mult, ALU.mult, r=[abi, im2], w=[aivi])
    MLr, MLi, MRr, MRi = [new([32], "M%d" % i) for i in range(4)]
    lo, hi = slice(0, 64), slice(64, 128)
    for (dst, s_lo, s_hi) in ((MLr, aivr, abr), (MLi, aivi, abi), (MRr, abr, aivr), (MRi, abi, aivi)):
        k.cp("dve", dst.ap[lo], s_lo.ap[lo], r=[s_lo], w=[dst])
        k.cp("dve", dst.ap[hi], s_hi.ap[hi], r=[s_hi], w=[dst])
    allr += [MLr, MLi, MRr, MRi]
    for s_ in range(7):
        cmul(Lr.ap[:, :, s_ + 1, :], Li.ap[:, :, s_ + 1, :], Lr.ap[:, :, s_, :], Li.ap[:, :, s_, :], b16(MLr), b16(MLi), t16a, t16b, allr, [X])
        cmul(Rr.ap[:, :, s_ + 1, :], Ri.ap[:, :, s_ + 1, :], Rr.ap[:, :, s_, :], Ri.ap[:, :, s_, :], b16(MRr), b16(MRi), t16a, t16b, allr, [X])
    pwr = [abr] + [new([32], "pwr%d" % i) for i in range(7)]
    pwi = [abi] + [new([32], "pwi%d" % i) for i in range(7)]
    for i in range(7):
        cmul(pwr[i + 1].ap, pwi[i + 1].ap, pwr[i].ap, pwi[i].ap, abr.ap, abi.ap, sc[0].ap, sc[1].ap, allr, [X])
    FSr, FSi, FOr, FOi = [new([32], "F%d" % i) for i in range(4)]
    k.cp("dve", FSr.ap[lo], pwr[6].ap[lo], r=allr, w=[X])
    k.cp("dve", FSi.ap[lo], pwi[6].ap[lo], r=allr, w=[X])
    k.memset("dve", FSr.ap[hi], 1.0, w=[X])
    k.memset("dve", FSi.ap[hi], 0.0, w=[X])
    k.cp("dve", FOr.ap[lo], abr.ap[lo], r=allr, w=[X])
    k.cp("dve", FOi.ap[lo], abi.ap[lo], r=allr, w=[X])
    k.cp("dve", FOr.ap[hi], pwr[7].ap[hi], r=allr, w=[X])
    k.cp("dve", FOi.ap[hi], pwi[7].ap[hi], r=allr, w=[X])
    Lr3 = Lr.ap.rearrange("p g s h -> p g (s h)")
    Li3 = Li.ap.rearrange("p g s h -> p g (s h)")
    Rr3 = Rr.ap.rearrange("p g s h -> p g (s h)")
    Ri3 = Ri.ap.rearrange("p g s h -> p g (s h)")
    WSr, WSi = new([16, 128], "WSr"), new([16, 128], "WSi")

    def b128h(t_, h_):
        return t_.ap[:, 16 * h_:16 * h_ + 16].unsqueeze(2).to_broadcast([128, 16, 128])
    for h_ in range(2):
        gs = slice(16 * h_, 16 * h_ + 16)
        cmul(WSr.ap, WSi.ap, Lr3[:, gs], Li3[:, gs], b128h(FSr, h_), b128h(FSi, h_), big1.ap, big2.ap, allr, [X])
        for gl in range(16):
            g = 16 * h_ + gl
            for xi, src in enumerate((WSr, WSi)):
                pst = PS[(2 * g + xi) % 4]
                k.tr(pst.ap[:, 0:128], src.ap[:, gl, :], C.identf.ap, r=[X, C.identf], w=[pst])
                k.cp("act" if xi else "dve", WS.ap[:, g, xi, :], pst.ap[:, 0:128], r=[pst], w=[WS])
        cmul(WSr.ap, WSi.ap, Rr3[:, gs], Ri3[:, gs], b128h(FOr, h_), b128h(FOi, h_), big1.ap, big2.ap, allr, [X])
        for (dst, ps_) in ((WOF, lo), (WOB, hi)):
            k.cp("dve", dst.ap[ps_, gs, 0, :], WSr.ap[ps_], r=[X], w=[dst])
            k.ts("dve", dst.ap[ps_, gs, 1, :], WSi.ap[ps_], -1.0, ALU.mult, r=[X], w=[dst])
    Mlow, Mup, onesf = new([128], "Mlow"), new([128], "Mup"), new([128], "onesf")
    k.memset("pool", onesf.ap, 1.0, w=[onesf])
    k.asel(Mlow.ap.rearrange("p (t h) -> p t h", h=16), onesf.ap.rearrange("p (t h) -> p t h", h=16), [[16, 8], [0, 16]], ALU.is_ge, 0.0, 15, -1, r=[onesf], w=[Mlow])
    k.asel(Mup.ap.rearrange("p (t h) -> p t h", h=16), onesf.ap.rearrange("p (t h) -> p t h", h=16), [[-16, 8], [0, 16]], ALU.is_ge, 0.0, 0, 1, r=[onesf], w=[Mup])
    tacc = [new([128], "tacc%d" % i) for i in range(2)]
    for g in range(32):
        pf, pf2, pb_, pb2 = PS[4], PS[5], PS[6], PS[7]
        k.mm(pf.ap[:, 0:128], Lr3[lo, g, :], Rr3[lo, g, :], True, True, r=[X], w=[pf])
        k.mm(pf2.ap[:, 0:128], Li3[lo, g, :], Ri3[lo, g, :], True, True, r=[X], w=[pf2])
        k.mm(pb_.ap[:, 0:128], Lr3[hi, g, :], Rr3[hi, g, :], True, True, r=[X], w=[pb_])
        k.mm(pb2.ap[:, 0:128], Li3[hi, g, :], Ri3[hi, g, :], True, True, r=[X], w=[pb2])
        ta, tb = tacc[0], tacc[1]
        k.cp("act", ta.ap, pf2.ap[:, 0:128], r=[pf2], w=[ta])
        k.tt("dve", ta.ap, pf.ap[:, 0:128], ta.ap, ALU.subtract, r=[pf, ta], w=[ta])
        k.tt("dve", ta.ap, ta.ap, Mlow.ap, ALU.mult, r=[ta, Mlow], w=[ta])
        k.cp("act", tb.ap, pb2.ap[:, 0:128], r=[pb2], w=[tb])
        k.tt("dve", tb.ap, pb_.ap[:, 0:128], tb.ap, ALU.subtract, r=[pb_, tb], w=[tb])
        k.tt("dve", tb.ap, tb.ap, Mup.ap, ALU.mult, r=[tb, Mup], w=[tb])
        k.tt("dve", ta.ap, ta.ap, tb.ap, ALU.add, r=[ta, tb], w=[ta])
        k.stt(T0.ap[:, g, :], C.identf.ap, dcol.ap[:, g:g + 1], ta.ap, ALU.mult, ALU.add, r=[C.identf, dcol, ta], w=[T0])
    k.act(rho8.ap, lam.ap, AF.Exp, scale=8.0, r=[lam], w=[rho8])
    ir8 = new([32], "ir8")
    k.act(ir8.ap, lam.ap, AF.Exp, scale=-8.0, r=[lam], w=[ir8])
    k.tt("dve", Wpow.ap[:, 0, :, 0], pwr[7].ap, ir8.ap, ALU.mult, r=allr + [ir8], w=[Wpow])
    k.tt("dve", Wpow.ap[:, 0, :, 1], pwi[7].ap, ir8.ap, ALU.mult, r=allr + [ir8], w=[Wpow])
    for l in range(LV - 1):
        wr, wi = Wpow.ap[:, l, :, 0], Wpow.ap[:, l, :, 1]
        k.tt("dve", sc[0].ap, wr, wr, ALU.mult, r=[Wpow], w=[sc[0]])
        k.tt("dve", sc[1].ap, wi, wi, ALU.mult, r=[Wpow], w=[sc[1]])
        k.tt("dve", Wpow.ap[:, l + 1, :, 0], sc[0].ap, sc[1].ap, ALU.subtract, r=[sc[0], sc[1]], w=[Wpow])
        k.tt("dve", sc[2].ap, wr, wi, ALU.mult, r=[Wpow], w=[sc[2]])
        k.ts("dve", Wpow.ap[:, l + 1, :, 1], sc[2].ap, 2.0, ALU.mult, r=[sc[2]], w=[Wpow])
    k.memset("dve", Eb.ap[:, :, 0, 0:1], 1.0, w=[Eb])
    k.memset("dve", Eb.ap[:, :, 1, 0:1], 0.0, w=[Eb])
    eb1 = big1.ap.rearrange("p a b -> p (a b)")[:, 0:512].rearrange("p (g m) -> p g m", m=16)
    eb2 = big2.ap.rearrange("p a b -> p (a b)")[:, 0:512].rearrange("p (g m) -> p g m", m=16)
    for l in range(5):
        m = 1 << l
        wr = Wpow.ap[:, l, :, 0].unsqueeze(2).to_broadcast([128, 32, m])
        wi = Wpow.ap[:, l, :, 1].unsqueeze(2).to_broadcast([128, 32, m])
        ec0, es0 = Eb.ap[:, :, 0, 0:m], Eb.ap[:, :, 1, 0:m]
        ec1, es1 = Eb.ap[:, :, 0, m:2 * m], Eb.ap[:, :, 1, m:2 * m]
        t1, t2 = eb1[:, :, 0:m], eb2[:, :, 0:m]
        k.tt("dve", t1, es0, wi, ALU.mult, r=[Eb, Wpow, X], w=[X])
        k.tt("dve", t2, ec0, wr, ALU.mult, r=[Eb, Wpow, X], w=[X])
        k.tt("dve", ec1, t2, t1, ALU.subtract, r=[X, Eb], w=[Eb])
        k.tt("dve", t1, ec0, wi, ALU.mult, r=[Eb, Wpow, X], w=[X])
        k.tt("dve", t2, es0, wr, ALU.mult, r=[Eb, Wpow, X], w=[X])
        k.tt("dve", es1, t2, t1, ALU.add, r=[X, Eb], w=[Eb])
    A.pop()
    P.barrier()
    uTo = A.alloc([S], BF16, "uTo")
    gTo = A.alloc([S], BF16, "gTo")
    Yg = A.alloc([8, NCH], BF16, "Yg")
    Ugs = [A.alloc([NCH], BF16, "Ug%d" % i) for i in range(2)]
    Es = [A.alloc([2, NCH], F32, "E%d" % i) for i in range(2)]
    G = A.alloc([2, NCH], F32, "G")
    Hs = A.alloc([2, NCH], F32, "Hs")
    Hbs = [A.alloc([2, NCH + 2], BF16, "Hb%d" % i) for i in range(2)]
    tq = [A.alloc([CB], F32, "tq%d" % i) for i in range(4)]
    ge = [A.alloc([CB], F32, "ge%d" % i) for i in range(3)]
    for hb_ in Hbs:
        k.memset("pool", hb_.ap, 0.0, w=[hb_])
    tr_ = [A.alloc([CB], F32, "tr%d" % i) for i in range(4)]
    Ug3 = Ugs + [A.alloc([NCH], BF16, "Ug2")]

    def egen(g):
        E = Es[g % 2]
        Ec, Es_ = E.ap[:, 0, :], E.ap[:, 1, :]
        k.cp("dve", E.ap[:, :, 0:32], Eb.ap[:, g, :, :], r=[Eb], w=[E])
        for l in range(5, LV):
            m = 1 << l
            wr, wi = Wpow.ap[:, l, g, 0:1], Wpow.ap[:, l, g, 1:2]
            t1, t2 = tq[0].ap[:, 0:m], tq[1].ap[:, 0:m]
            k.ts("dve", t1, Es_[:, 0:m], wi, ALU.mult, r=[E, Wpow], w=[tq[0]])
            k.stt(Ec[:, m:2 * m], Ec[:, 0:m], wr, t1, ALU.mult, ALU.subtract, r=[E, Wpow, tq[0]], w=[E])
            k.ts("dve", t2, Ec[:, 0:m], wi, ALU.mult, r=[E, Wpow], w=[tq[1]])
            k.stt(Es_[:, m:2 * m], Es_[:, 0:m], wr, t2, ALU.mult, ALU.add, r=[E, Wpow, tq[1]], w=[E])

    def insel(g):
        gi, Ug = g % 8, Ug3[g % 3]
        for cb in range(NCB):
            pu = PS[cb % 2]
            for s_ in range(8):
                k.mm(pu.ap[:, 0:CB], Z.ap[:, gi, 112 - 16 * s_:240 - 16 * s_], strided(uTo.ap[:, cb * CB * 8 + s_:cb * CB * 8 + s_ + 1], 8, CB), s_ == 0, s_ == 7, r=[Z, uTo], w=[pu])
            k.cp("act", Ug.ap[:, cb * CB:(cb + 1) * CB], pu.ap[:, 0:CB], r=[pu], w=[Ug])

    def summ(g):
        Ug = Ug3[g % 3]
        for cb in range(NCB):
            for xi in range(2):
                pS = PS[2 + 2 * xi + cb]
                k.mm(pS.ap[0:64, 0:CB], WS.ap[:, g, xi, 0:64], Ug.ap[:, cb * CB:(cb + 1) * CB], True, True, r=[WS, Ug], w=[pS])
                k.mm(pS.ap[64:128, 0:CB], WS.ap[:, g, xi, 64:128], rev_ap(Ug.ap[:, NCH - 1 - cb * CB:NCH - cb * CB], CB), True, True, r=[WS, Ug], w=[pS])

    def demod(g):
        E = Es[g % 2]
        Ec, Es_ = E.ap[:, 0, :], E.ap[:, 1, :]
        for cb in range(NCB):
            cs = slice(cb * CB, (cb + 1) * CB)
            pr, pi_ = PS[2 + cb], PS[4 + cb]
            k.tt("dve", tq[0].ap, pr.ap[:, 0:CB], Ec[:, cs], ALU.mult, r=[pr, E], w=[tq[0]])
            k.tt("dve", tq[1].ap, pi_.ap[:, 0:CB], Es_[:, cs], ALU.mult, r=[pi_, E], w=[tq[1]])
            k.tt("dve", G.ap[:, 0, cs], tq[0].ap, tq[1].ap, ALU.add, r=[tq[0], tq[1]], w=[G])
            k.tt("dve", tq[2].ap, pi_.ap[:, 0:CB], Ec[:, cs], ALU.mult, r=[pi_, E], w=[tq[2]])
            k.tt("dve", tq[3].ap, pr.ap[:, 0:CB], Es_[:, cs], ALU.mult, r=[pr, E], w=[tq[3]])
            k.tt("dve", G.ap[:, 1, cs], tq[2].ap, tq[3].ap, ALU.subtract, r=[tq[2], tq[3]], w=[G])

    def scan_(g):
        dec = rho8.ap[:, g:g + 1].to_broadcast([128, NCH])
        k.scan(Hs.ap[:, 0, :], dec, G.ap[:, 0, :], 0.0, r=[G, rho8], w=[Hs])
        k.scan(Hs.ap[:, 1, :], dec, G.ap[:, 1, :], 0.0, r=[G, rho8], w=[Hs])

    def remod(g):
        E, Hb = Es[g % 2], Hbs[g % 2]
        Ec, Es_ = E.ap[:, 0, :], E.ap[:, 1, :]
        for cb in range(NCB):
            cs = slice(cb * CB, (cb + 1) * CB)
            co = slice(1 + cb * CB, 1 + (cb + 1) * CB)
            k.tt("dve", tr_[0].ap, Hs.ap[:, 0, cs], Ec[:, cs], ALU.mult, r=[Hs, E], w=[tr_[0]])
            k.tt("dve", tr_[1].ap, Hs.ap[:, 1, cs], Es_[:, cs], ALU.mult, r=[Hs, E], w=[tr_[1]])
            k.tt("dve", Hb.ap[:, 0, co], tr_[0].ap, tr_[1].ap, ALU.subtract, r=[tr_[0], tr_[1]], w=[Hb])
            k.tt("dve", tr_[2].ap, Hs.ap[:, 1, cs], Ec[:, cs], ALU.mult, r=[Hs, E], w=[tr_[2]])
            k.tt("dve", tr_[3].ap, Hs.ap[:, 0, cs], Es_[:, cs], ALU.mult, r=[Hs, E], w=[tr_[3]])
            k.tt("dve", Hb.ap[:, 1, co], tr_[2].ap, tr_[3].ap, ALU.add, r=[tr_[2], tr_[3]], w=[Hb])

    def outs(g):
        gi, Ug, Hb = g % 8, Ug3[g % 3], Hbs[g % 2]
        for cb in range(NCB):
            cs = slice(cb * CB, (cb + 1) * CB)
            py = PS[6 + cb % 2]
            k.mm(py.ap[:, 0:CB], T0.ap[:, g, :], Ug.ap[:, cs], True, False, r=[T0, Ug], w=[py])
            for xi in range(2):
                k.mm(py.ap[:, 0:CB], WOF.ap[:, g, xi, :], Hb.ap[:, xi, cb * CB:(cb + 1) * CB], False, False, r=[WOF, Hb], w=[py])
            for xi in range(2):
                st_ = NCH - 1 - cb * CB
                k.mm(py.ap[:, 0:CB], WOB.ap[:, g, xi, :], rev_ap(Hb.ap[:, xi, st_:st_ + 1], CB), False, xi == 1, r=[WOB, Hb], w=[py])
            k.cp("act", Yg.ap[:, gi, cs], py.ap[:, 0:CB], r=[py], w=[Yg])

    for o in range(4):
        k.dma("sp", uTo.ap, uT[o], w=[uTo])
        g0 = 8 * o
        insel(g0)
        insel(g0 + 1)
        summ(g0)
        egen(g0)
        for gi in range(8):
            g = g0 + gi
            demod(g)
            if gi + 1 < 8:
                summ(g + 1)
            scan_(g)
            if gi + 1 < 8:
                egen(g + 1)
            remod(g)
            if gi + 2 < 8:
                insel(g + 2)
            outs(g)
        for t in range(8):
            for cb in range(NCB):
                po = PS[(t * NCB + cb) % 2]
                for gi in range(8):
                    k.mm(po.ap[:, 0:CB], Z.ap[:, t, 112 - 16 * gi:240 - 16 * gi], Yg.ap[:, gi, cb * CB:(cb + 1) * CB], gi == 0, gi == 7, r=[Z, Yg], w=[po])
                yv = po.ap[:, 0:CB]
                k.act(ge[0].ap, yv, AF.Square, r=[po], w=[ge[0]])
                k.ts("dve", ge[0].ap, ge[0].ap, 0.044715, ALU.mult, 1.0, ALU.add, r=[ge[0]], w=[ge[0]])
                k.tt("dve", ge[1].ap, ge[0].ap, yv, ALU.mult, r=[ge[0], po], w=[ge[1]])
                k.act(ge[2].ap, ge[1].ap, AF.Sigmoid, scale=1.5957691216057308, r=[ge[1]], w=[ge[2]])
                k.tt("dve", strided(gTo.ap[:, cb * CB * 8 + t:cb * CB * 8 + t + 1], 8, CB), yv, ge[2].ap, ALU.mult, r=[po, ge[2]], w=[gTo])
        k.dma("sp", gT[o], gTo.ap, r=[gTo])
    A.pop()


def ap3(base, d1, d2):
    return bass.AP(base.tensor, base.offset, [list(base.ap[0]), list(d1), list(d2)])


def phase_B(C, x, qT, kT, Vp, gT, hs, rel_bias, Fd, w_glu, b_glu, g_att, g_ssm, w_out):
    k, A, PS, S, P = C.k, C.A, C.PS, C.S, C.P
    NSB = S // 2048
    import os
    DIL = (1, 4, 16)
    DSEL = [int(v) for v in os.environ.get('PB_DIL', '1,4,16').split(',')]
    A.push()
    EB = A.alloc([3, 8, 2, 128], BF16, "EB")
    A.push()
    rb0 = A.alloc([256], F32, "rb0")
    F0 = A.alloc([3, 8, 512], F32, "F0")
    Tt = A.alloc([24, 257], F32, "Tt")
    k.dma("sp", rb0.ap, rel_bias.rearrange("(o b) h -> o (b h)", o=1).partition_broadcast(128), w=[rb0])
    k.act(rb0.ap, rb0.ap, AF.Exp, r=[rb0], w=[rb0])
    k.memset("dve", F0.ap, 0.0, w=[F0])
    rbv = rb0.ap.rearrange("p (b h) -> p b h", h=8)
    for di, d in enumerate(DIL):
        js = np.arange(-64, 65)
        bk = t5_bucket_np(js * d)
        s0 = 0
        while s0 < len(js):
            e0 = s0
            while e0 + 1 < len(js) and bk[e0 + 1] == bk[s0]:
                e0 += 1
            ln = e0 - s0 + 1
            x0 = int(js[s0]) + 192
            bb = int(bk[s0])
            k.tt("dve", F0.ap[:, di, :, x0:x0 + ln], F0.ap[:, di, :, x0:x0 + ln], rbv[:, bb, :].unsqueeze(2).to_broadcast([128, 8, ln]), ALU.add, r=[rb0, F0], w=[F0])
            s0 = e0 + 1
    fdt = T(None, "Fd")
    for di in range(3):
        k.dma("sp", Fd[di * 4096:(di + 1) * 4096].rearrange("(o n) -> o n", o=1), F0.ap[0:1, di].rearrange("p b c -> p (b c)"), r=[F0], w=[fdt])
    for di in range(3):
        for h in range(8):
            j = di * 8 + h
            k.dma("sp", Tt.ap[:, j, :], bass.AP(Fd.tensor, j * 512, [[1, 128], [1, 257]]), r=[fdt], w=[Tt])
    ttb = Tt.ap[:, 0, 0:1]
    pstep = list(Tt.ap.ap[0])
    for di in range(3):
        for ab, c0 in ((0, 128), (1, 256)):
            src = bass.AP(Tt.ap.tensor, Tt.ap[:, di * 8, c0:c0 + 1].offset, [pstep, [257, 8], [-1, 128]])
            k.cp("dve", EB.ap[:, di, :, ab, :], src, r=[Tt], w=[EB])
    A.pop()
    P.barrier()
    import os
    STOP = int(os.environ.get("PB_STOP", "9"))
    if STOP <= 1:
        A.pop()
        return
    ONEH = A.alloc([2, 128], BF16, "ONEH")
    k.memset("pool", ONEH.ap, 0.0, w=[ONEH])
    k.memset("pool", ONEH.ap[:, 0, 0:64], 1.0, w=[ONEH])
    k.memset("pool", ONEH.ap[:, 1, 64:128], 1.0, w=[ONEH])
    gc = A.alloc([8], F32, "gcB")
    k.dma("sp", gc.ap[:, 0:4], g_att[0].rearrange("(k p) -> p k", p=128), w=[gc], slow=True)
    k.dma("sp", gc.ap[:, 4:8], g_ssm[0].rearrange("(k p) -> p k", p=128), w=[gc], slow=True)
    bgl = C.gcol(b_glu[0], 512, "bgl")
    wout = A.alloc([8, 1024], BF16, "wout")
    wglu = A.alloc([4, 512], BF16, "wglu")
    C.load_weight_bf16(wout, w_out[0], 8, 1024, gc, "wout")
    C.load_weight_bf16(wglu, w_glu[0], 4, 512, None, "wglu")
    qs = A.alloc([2, 2048], BF16, "qs")
    ks = A.alloc([2, 4096], BF16, "ks")
    acc = A.alloc([4, 2048], F32, "acc")
    dsw = A.alloc([2, 2048], F32, "dsw")
    attT = A.alloc([4, 2048], BF16, "attT")
    pTs = [A.alloc([8, 128], BF16, "pT%d" % i) for i in range(3)]
    pTbs = [T(None, "pTb%d" % i) for i in range(3)]
    vas = [A.alloc([4, 128], BF16, "va%d" % i) for i in range(4)]
    vbs = [A.alloc([4, 128], BF16, "vb%d" % i) for i in range(4)]
    dflat = dsw.ap.rearrange("p a b -> p (a b)")
    accflat = acc.ap.rearrange("p a b -> p (a b)")

    def alias_bf(i, name):
        return T(dflat[:, i * 1024:(i + 1) * 1024].bitcast(BF16).rearrange("p (a b) -> p a b", a=4), name)
    gts = [alias_bf(i, "gt%d" % i) for i in range(2)]
    sqbs = [alias_bf(2 + i, "sqb%d" % i) for i in range(2)]
    sqbs2 = [T(accflat[:, i * 1024:(i + 1) * 1024].bitcast(BF16).rearrange("p (a b) -> p a b", a=4), "sqc%d" % i) for i in range(2)]
    sTs = [A.alloc([4, 512], BF16, "sT%d" % i) for i in range(2)]
    rs5s = [A.alloc([512], F32, "rs5%d" % i) for i in range(4)]
    sigs = [A.alloc([512], BF16, "sig%d" % i) for i in range(2)]
    mixTs = [A.alloc([8, 512], BF16, "mixT%d" % i) for i in range(2)]
    xb = [A.alloc([1024], F32, "xB%d" % i) for i in range(3)]
    pend = []
    bi = 0
    for SB in range(NSB):
        tok0 = SB * 2048
        for half in range(2):
            k.dma("sp", qs.ap, qT[2 * half:2 * half + 2, :, tok0:tok0 + 2048].rearrange("f p t -> p f t"), w=[qs])
            lo, hi = tok0 - 1024, tok0 + 3072
            vlo, vhi = max(lo, 0), min(hi, S)
            if vlo > lo:
                k.memset("pool", ks.ap[:, :, 0:vlo - lo], 0.0, w=[ks])
            if vhi < hi:
                k.memset("pool", ks.ap[:, :, vhi - lo:4096], 0.0, w=[ks])
            k.dma("sp", ks.ap[:, :, vlo - lo:vhi - lo], kT[2 * half:2 * half + 2, :, vlo:vhi].rearrange("f p t -> p f t"), w=[ks])
            blocks = []
            first = True
            for di, d in enumerate(DIL):
                if d not in DSEL:
                    continue
                for r in range(d):
                    for mb in range(16 // d):
                        blocks.append((di, d, r, mb, first))
                first = False

            def emit_scores(j):
                di, d, r, mb, isfirst = blocks[j]
                q0 = r + d * 128 * mb
                va, vb = vas[j % 4], vbs[j % 4]
                rowA = 1024 + tok0 + q0 - 64 * d
                rowB = 1024 + tok0 + q0 + 64 * d
                k.dma("sp", va.ap, Vp[rowA:rowA + 127 * d + 1:d, 4 * half:4 * half + 4, :], w=[va])
                k.dma("sp", vb.ap, Vp[rowB:rowB + 127 * d + 1:d, 4 * half:4 * half + 4, :], w=[vb])
                for hl in range(4):
                    p0, hpl = 64 * (hl % 2), hl // 2
                    qa = strided(qs.ap[p0:p0 + 64, hpl, q0:q0 + 1], d, 128)
                    for ab in range(2):
                        kc = 1024 + q0 + (-64 * d if ab == 0 else 64 * d)
                        ka = strided(ks.ap[p0:p0 + 64, hpl, kc:kc + 1], d, 128)
                        pss = PS[2 * (j % 3) + hl % 2]
                        cslot = (hl // 2) * 2 + ab
                        k.mm(pss.ap[:, cslot * 128:(cslot + 1) * 128], ka, qa, True, True, r=[ks, qs], w=[pss])

            def emit_soft(j):
                di, d, r, mb, isfirst = blocks[j]
                pT, pTb = pTs[j % 3], pTbs[j % 3]
                ps0, ps1 = PS[2 * (j % 3)], PS[2 * (j % 3) + 1]
                pTf = pT.ap.rearrange("p a b -> p (a b)")
                k.act(pTf[:, 0:512], ps0.ap, AF.Exp, r=[ps0], w=[pT])
                k.act(pTf[:, 512:1024], ps1.ap, AF.Exp, r=[ps1], w=[pTb])
                for hh in range(2):
                    ebv = EB.ap[:, di, 4 * half + hh:4 * half + 4:2, :, :]
                    pv4 = pT.ap[:, hh * 4:(hh + 1) * 4, :].rearrange("p (a b) c -> p a b c", b=2)
                    tok = pT if hh == 0 else pTb
                    k.tt("dve" if hh == 0 else "pool", pv4, pv4, ebv, ALU.mult, r=[tok, EB], w=[tok])

            def emit_pv(j):
                di, d, r, mb, isfirst = blocks[j]
                q0 = r + d * 128 * mb
                pT, pTb, va, vb = pTs[j % 3], pTbs[j % 3], vas[j % 4], vbs[j % 4]
                pnd = PS[6 + j % 2]
                for hl in range(4):
                    hh, hpl = hl % 2, hl // 2
                    for ab in range(2):
                        vt = va if ab == 0 else vb
                        k.mm(pnd.ap[:, hl * 128:(hl + 1) * 128], vt.ap[:, hl, :], pT.ap[:, hh * 4 + hpl * 2 + ab, :], ab == 0, ab == 1, r=[vt, pT, pTb], w=[pnd])
                av = ap3(acc.ap[:, 0, q0:q0 + 1], [2048, 4], [d, 128])
                pv = pnd.ap.rearrange("p (a b) -> p a b", a=4)
                if isfirst:
                    k.cp("act", av, pv, r=[pnd], w=[acc])
                else:
                    k.tt("dve", av, pv, av, ALU.add, r=[pnd, acc], w=[acc])
            nb_ = len(blocks)
            emit_scores(0)
            if nb_ > 1:
                emit_scores(1)
            emit_soft(0)
            for j in range(nb_):
                if j + 2 < nb_:
                    emit_scores(j + 2)
                if j + 1 < nb_:
                    emit_soft(j + 1)
                emit_pv(j)
            for hpl in range(2):
                k.dma("sp", dsw.ap[0:64, hpl, :], acc.ap[64:128, 2 * hpl, :], r=[acc], w=[dsw])
                k.dma("sp", dsw.ap[64:128, hpl, :], acc.ap[0:64, 2 * hpl + 1, :], r=[acc], w=[dsw])
            k.act(dsw.ap, dsw.ap, AF.Ln, r=[dsw], w=[dsw])
            k.act(dsw.ap, dsw.ap, AF.Exp, scale=-1.0, r=[dsw], w=[dsw])
            for hpl in range(2):
                k.tt("dve", attT.ap[0:64, 2 * half + hpl, :], acc.ap[0:64, 2 * hpl, :], dsw.ap[0:64, hpl, :], ALU.mult, r=[acc, dsw], w=[attT])
                k.tt("pool", attT.ap[64:128, 2 * half + hpl, :], acc.ap[64:128, 2 * hpl + 1, :], dsw.ap[64:128, hpl, :], ALU.mult, r=[acc, dsw], w=[attT])
        def stageN(bb):
            t0 = tok0 + bb * 512
            mx, sq_, sT_, gt_ = mixTs[bb % 2], sqbs[bb % 2], sTs[bb % 2], gts[bb % 2]
            rsa, rss = rs5s[2 * (bb % 2)], rs5s[2 * (bb % 2) + 1]
            av = attT.ap[:, :, bb * 512:(bb + 1) * 512]
            k.dma("sp", gt_.ap, gT[:, :, t0:t0 + 512].rearrange("f p t -> p f t"), w=[gt_])
            k.act(sq_.ap, av, AF.Square, r=[attT], w=[sq_])
            for kk in range(4):
                k.mm(PS[0].ap, C.ones_b.ap, sq_.ap[:, kk, :], kk == 0, kk == 3, r=[sq_, C.ones_b], w=[PS[0]])
            k.ts("dve", rsa.ap, PS[0].ap, 1.0 / 512, ALU.mult, EPS, ALU.add, r=[PS[0]], w=[rsa])
            for j in range(4):
                sg_, pz = sigs[j % 2], PS[2 + j % 2]
                for kk in range(4):
                    k.mm(pz.ap, wglu.ap[:, kk, j * 128:(j + 1) * 128], gt_.ap[:, kk, :], kk == 0, kk == 3, r=[wglu, gt_], w=[pz])
                k.act(sg_.ap, pz.ap, AF.Sigmoid, bias=bgl.ap[:, j:j + 1], r=[pz, bgl], w=[sg_])
                k.tt("dve", sT_.ap[:, j, :], gt_.ap[:, j, :], sg_.ap, ALU.mult, r=[gt_, sg_], w=[sT_])
            sq2 = sqbs2[bb % 2]
            k.act(sq2.ap, sT_.ap, AF.Square, r=[sT_], w=[sq2])
            for kk in range(4):
                k.mm(PS[1].ap, C.ones_b.ap, sq2.ap[:, kk, :], kk == 0, kk == 3, r=[sq2, C.ones_b], w=[PS[1]])
            k.ts("dve", rss.ap, PS[1].ap, 1.0 / 512, ALU.mult, EPS, ALU.add, r=[PS[1]], w=[rss])
            for r_ in (rsa, rss):
                k.act(r_.ap, r_.ap, AF.Ln, r=[r_], w=[r_])
            for r_ in (rsa, rss):
                k.act(r_.ap, r_.ap, AF.Exp, scale=-0.5, r=[r_], w=[r_])
            k.tt("dve", mx.ap[:, 0:4, :], av, rsa.ap.unsqueeze(1).to_broadcast([128, 4, 512]), ALU.mult, r=[attT, rsa], w=[mx])
            k.tt("pool", mx.ap[:, 4:8, :], sT_.ap, rss.ap.unsqueeze(1).to_broadcast([128, 4, 512]), ALU.mult, r=[sT_, rss], w=[mx])

        def stageW(bb):
            t0 = tok0 + bb * 512
            mx = mixTs[bb % 2]
            for t in range(4):
                ti = t0 // 128 + t
                xt = xb[ti % 3]
                k.dma("sp", xt.ap, x[ti * 128:(ti + 1) * 128, :], w=[xt])
                if pend:
                    pti, pxt = pend.pop()
                    k.dma("pool", hs[pti * 128:(pti + 1) * 128, :], pxt.ap, r=[pxt])
                for h2 in range(2):
                    po = PS[4 + (2 * t + h2) % 4]
                    cs = slice(h2 * 512, (h2 + 1) * 512)
                    for kk in range(8):
                        k.mm(po.ap, mx.ap[:, kk, t * 128:(t + 1) * 128], wout.ap[:, kk, cs], kk == 0, kk == 7, r=[mx, wout], w=[po])
                    k.tt("dve", xt.ap[:, cs], po.ap, xt.ap[:, cs], ALU.add, r=[po, xt], w=[xt])
                pend.append((ti, xt))
        if STOP > 3:
            P.barrier()
            stageN(0)
            for bb in range(4):
                if bb + 1 < 4:
                    stageN(bb + 1)
                stageW(bb)
            P.barrier()
    while pend:
        pti, pxt = pend.pop()
        k.dma("pool", hs[pti * 128:(pti + 1) * 128, :], pxt.ap, r=[pxt])
    A.pop()


def norm_T(C, src_ap, ss, tm, rs, junk, a_, pst, dstT, col0, ncols=1024, r_src=(), cp_eng="act"):
    k = C.k
    nk = ncols // 128
    k.memset("pool", ss.ap, 0.0, w=[ss])
    k.act(junk.ap[:, 0:ncols], src_ap, AF.Square, accum=ss.ap, r=list(r_src) + [ss], w=[junk, ss])
    C.rms_rstd(ss, ncols, tm, rs)
    k.ts("dve", a_.ap[:, 0:ncols], src_ap, rs.ap[:, 0:1], ALU.mult, r=list(r_src) + [rs], w=[a_])
    psv = pst.ap.bitcast(BF16).rearrange("p (a b) -> p a b", a=8)
    for kk in range(nk):
        k.tr(psv[:, kk, :], a_.ap[:, kk * 128:(kk + 1) * 128], C.identb.ap, r=[a_, C.identb], w=[pst])
    k.cp(cp_eng, dstT.ap[:, 0:nk, col0:col0 + 128], psv[:, 0:nk, :], r=[pst], w=[dstT])


def phase_C(C, hs, hs2, g_mlp, w_mlp1, w_mlp2):
    k, A, PS, S = C.k, C.A, C.PS, C.S
    A.push()
    gm = C.gcol(g_mlp[0], 1024, "gmlp")
    w1 = A.alloc([8, 4096], BF16, "w1")
    w2 = A.alloc([32, 1024], BF16, "w2")
    C.load_weight_bf16(w1, w_mlp1[0], 8, 4096, gm, "w1")
    C.load_weight_bf16(w2, w_mlp2[0], 32, 1024, None, "w2")
    hb = [A.alloc([2, 1024], F32, "hc%d" % i) for i in range(3)]
    pendc = []
    junk = A.alloc([1024], F32, "junkc")
    ab = [A.alloc([1024], BF16, "abc%d" % i) for i in range(2)]
    ss = [A.alloc([1], F32, "ssc%d" % i) for i in range(2)]
    tmp = [A.alloc([1], F32, "tmpc%d" % i) for i in range(2)]
    rstd = [A.alloc([1], F32, "rstdc%d" % i) for i in range(2)]
    fT = A.alloc([8, 256], BF16, "fT")
    hidT = A.alloc([32, 256], BF16, "hidT")
    rl = [A.alloc([256], F32, "rl%d" % i) for i in range(2)]
    fTs = [fT, A.alloc([8, 256], BF16, "fT1")]

    def cnorm(blk):
        h = hb[blk % 3]
        k.dma("sp", h.ap, hs[blk * 256:(blk + 1) * 256, :].rearrange("(t p) d -> p t d", p=128), w=[h])
        for t in range(2):
            i = (2 * blk + t) % 2
            norm_T(C, h.ap[:, t, :], ss[i], tmp[i], rstd[i], junk, ab[i], PS[i], fTs[blk % 2], t * 128, r_src=[h])

    def cup(blk):
        f_ = fTs[blk % 2]
        for j in range(32):
            pz = PS[2 + j % 2]
            for kk in range(8):
                k.mm(pz.ap[:, 0:256], w1.ap[:, kk, j * 128:(j + 1) * 128], f_.ap[:, kk, :], kk == 0, kk == 7, r=[w1, f_], w=[pz])
            r_ = rl[j % 2]
            k.act(r_.ap, pz.ap[:, 0:256], AF.Relu, r=[pz], w=[r_])
            k.tt("dve" if j % 2 == 0 else "pool", hidT.ap[:, j, :], r_.ap, r_.ap, ALU.mult, r=[r_], w=[hidT])

    def cdown(blk):
        h = hb[blk % 3]
        for t in range(2):
            for half in range(2):
                po = PS[4 + (2 * t + half) % 4]
                for j in range(32):
                    k.mm(po.ap, hidT.ap[:, j, t * 128:(t + 1) * 128], w2.ap[:, j, half * 512:(half + 1) * 512], j == 0, j == 31, r=[hidT, w2], w=[po])
                k.tt("dve", h.ap[:, t, half * 512:(half + 1) * 512], po.ap, h.ap[:, t, half * 512:(half + 1) * 512], ALU.add, r=[po, h], w=[h])
        if pendc:
            pb_, ph_ = pendc.pop()
            k.dma("pool", hs2[pb_ * 256:(pb_ + 1) * 256, :].rearrange("(t p) d -> p t d", p=128), ph_.ap, r=[ph_])
        pendc.append((blk, h))
    NBC = S // 256
    cnorm(0)
    for blk in range(NBC):
        cup(blk)
        if blk + 1 < NBC:
            cnorm(blk + 1)
        cdown(blk)
    while pendc:
        pb_, ph_ = pendc.pop()
        k.dma("pool", hs2[pb_ * 256:(pb_ + 1) * 256, :].rearrange("(t p) d -> p t d", p=128), ph_.ap, r=[ph_])
    A.pop()


def phase_D(C, hs2, pin, y, g_ple, w_gate, w_proj, g_final):
    k, A, PS, S, NT = C.k, C.A, C.PS, C.S, C.NT
    A.push()
    gp = C.gcol(g_ple[0], 1024, "gple")
    wg = A.alloc([8, 1024], BF16, "wg")
    wp = A.alloc([2, 1024], BF16, "wp")
    C.load_weight_bf16(wg, w_gate[0], 8, 1024, gp, "wg")
    C.load_weight_bf16(wp, w_proj[0], 2, 1024, None, "wp")
    gfin = A.alloc([1024], F32, "gfin")
    k.dma("sp", gfin.ap, g_final.rearrange("(o d) -> o d", o=1).partition_broadcast(128), w=[gfin])
    hb = [A.alloc([1024], F32, "hd%d" % i) for i in range(5)]
    pb = [A.alloc([256], F32, "pd%d" % i) for i in range(3)]
    junk = A.alloc([1024], F32, "junkd")
    ab = [A.alloc([1024], BF16, "abd%d" % i) for i in range(3)]
    pbb = [A.alloc([256], BF16, "pbb%d" % i) for i in range(3)]
    ss = [A.alloc([1], F32, "ssd%d" % i) for i in range(5)]
    tmp = [A.alloc([1], F32, "tmpd%d" % i) for i in range(5)]
    rstd = [A.alloc([1], F32, "rstdd%d" % i) for i in range(5)]
    eT = [A.alloc([8, 128], BF16, "eT%d" % i) for i in range(3)]
    pT = [A.alloc([2, 128], BF16, "pT%d" % i) for i in range(3)]
    sg = [A.alloc([512], F32, "sg%d" % i) for i in range(2)]
    def stage1(ti):
        h, p_, e_, pT_, pb_ = hb[ti % 5], pb[ti % 3], eT[ti % 3], pT[ti % 3], pbb[ti % 3]
        i = ti % 3
        k.dma("sp", h.ap, hs2[ti * 128:(ti + 1) * 128, :], w=[h])
        k.dma("sp", p_.ap, pin[ti * 128:(ti + 1) * 128, :], w=[p_])
        norm_T(C, h.ap, ss[i], tmp[i], rstd[i], junk, ab[i], PS[ti % 2], e_, 0, r_src=[h], cp_eng="dve")
        k.cp("pool", pb_.ap, p_.ap, r=[p_], w=[pb_])
        pst = PS[2 + ti % 2]
        psv = pst.ap.bitcast(BF16).rearrange("p (a b) -> p a b", a=8)
        for kk in range(2):
            k.tr(psv[:, kk, :], pb_.ap[:, kk * 128:(kk + 1) * 128], C.identb.ap, r=[pb_, C.identb], w=[pst])
        k.cp("dve", pT_.ap, psv[:, 0:2, :], r=[pst], w=[pT_])

    def stage2a(ti):
        e_, pT_ = eT[ti % 3], pT[ti % 3]
        for half in range(2):
            pg, pp = PS[4 + half], PS[6 + half]
            cs = slice(half * 512, (half + 1) * 512)
            for kk in range(8):
                k.mm(pg.ap, e_.ap[:, kk, :], wg.ap[:, kk, cs], kk == 0, kk == 7, r=[e_, wg], w=[pg])
            for kk in range(2):
                k.mm(pp.ap, pT_.ap[:, kk, :], wp.ap[:, kk, cs], kk == 0, kk == 1, r=[pT_, wp], w=[pp])

    def stage2b(ti):
        h = hb[ti % 5]
        for half in range(2):
            pg, pp, s_ = PS[4 + half], PS[6 + half], sg[half]
            cs = slice(half * 512, (half + 1) * 512)
            k.act(s_.ap, pg.ap, AF.Sigmoid, r=[pg], w=[s_])
            k.tt("dve", s_.ap, pp.ap, s_.ap, ALU.mult, r=[pp, s_], w=[s_])
            k.tt("pool", h.ap[:, cs], h.ap[:, cs], s_.ap, ALU.add, r=[h, s_], w=[h])
        j = 3 + ti % 2
        k.memset("pool", ss[j].ap, 0.0, w=[ss[j]])
        k.act(junk2.ap, h.ap, AF.Square, accum=ss[j].ap, r=[h, ss[j]], w=[junk2, ss[j]])
        C.rms_rstd(ss[j], 1024, tmp[j], rstd[j])
        k.stt(h.ap, h.ap, rstd[j].ap[:, 0:1], gfin.ap, ALU.mult, ALU.mult, r=[h, rstd[j], gfin], w=[h])

    def store(ti):
        k.dma("pool", y[ti * 128:(ti + 1) * 128, :], hb[ti % 5].ap, r=[hb[ti % 5]])

    junk2 = A.alloc([1024], F32, "junkd2")
    stage1(0)
    stage1(1)
    for ti in range(NT):
        stage2a(ti)
        if ti + 2 < NT:
            stage1(ti + 2)
        if ti >= 1:
            store(ti - 1)
        stage2b(ti)
    store(NT - 1)
    A.pop()


_NC_CACHE = {}


def kernel(**inputs):
    S = 8192
    xs = [inputs["x_prompt"][i] for i in range(2)] + [inputs["x_sample"][i] for i in range(4)]
    ps = [inputs["p_prompt"][0, i] for i in range(2)] + [inputs["p_sample"][0, i] for i in range(4)]
    xs += [np.zeros_like(xs[0]), np.zeros_like(xs[0])]
    ps += [np.zeros_like(ps[0]), np.zeros_like(ps[0])]
    if "nc" not in _NC_CACHE:
        _NC_CACHE["nc"] = build(S)
    nc = _NC_CACHE["nc"]
    wnames = ["rel_bias", "g_mix", "w_in", "ssm_a_re", "ssm_a_im", "ssm_log_dt", "ssm_b_re", "ssm_b_im", "ssm_c_re",
              "ssm_c_im", "ssm_d", "w_glu", "b_glu", "g_att_out", "g_ssm_out", "w_out", "g_mlp", "w_mlp1", "w_mlp2",
              "g_ple", "w_ple_gate", "w_ple_proj", "g_final"]
    in_maps = []
    for c in range(8):
        m = {"x": np.ascontiguousarray(xs[c], dtype=np.float32), "p": np.ascontiguousarray(ps[c], dtype=np.float32)}
        for n in wnames:
            m[n] = np.ascontiguousarray(inputs[n], dtype=np.float32)
        in_maps.append(m)
    res = run_bass_kernel_spmd(nc, in_maps, core_ids=list(range(8)))
    outs = [np.asarray(r["y"], dtype=np.float32) for r in res.results]
    y_prompt = np.stack(outs[0:2], axis=0)
    y_sample = np.stack(outs[2:6], axis=0)
    return (y_prompt, y_sample)
```

```python
import math
import numpy as np
from contextlib import ExitStack
import concourse.bass as bass
import concourse.mybir as mybir
from concourse.bass_utils import run_bass_kernel_spmd

F32 = mybir.dt.float32
BF16 = mybir.dt.bfloat16
I32 = mybir.dt.int32
AF = mybir.ActivationFunctionType
ALU = mybir.AluOpType

ENGS = ("pe", "act", "dve", "pool", "sp")
D = 1024
EPS = 1e-6


class Buf:
    __slots__ = ("name", "last_w", "readers")

    def __init__(self, name=""):
        self.name = name
        self.last_w = None
        self.readers = []


class Op:
    __slots__ = ("eng", "fn", "reads", "writes", "is_dma", "deps", "idx", "signal", "count", "sem", "semkey")

    def __init__(self, eng, fn, reads, writes, is_dma):
        self.eng = eng
        self.fn = fn
        self.reads = reads
        self.writes = writes
        self.is_dma = is_dma
        self.deps = set()
        self.signal = False
        self.count = 0
        self.sem = None
        self.semkey = None


class Prog:
    def __init__(self, nc, n_dma_sems=40):
        self.nc = nc
        self.ops = []
        self.n_dma_sems = n_dma_sems
        self.ALL = Buf("ALL")

    def op(self, eng, fn, reads=(), writes=()):
        self._add(Op(eng, fn, tuple(reads) + (self.ALL,), tuple(writes), False))

    def dma(self, queue, fn, reads=(), writes=()):
        self._add(Op(queue, fn, tuple(reads) + (self.ALL,), tuple(writes), True))

    def barrier(self):
        self._add(Op("dve", lambda e: e.nop() if False else e.memset(self._bar[:], 0.0), (), (self.ALL,), False))

    def _add(self, o):
        o.idx = len(self.ops)
        for b in o.reads:
            if b.last_w is not None:
                o.deps.add(b.last_w)
        for b in o.writes:
            if b.last_w is not None:
                o.deps.add(b.last_w)
            for r in b.readers:
                o.deps.add(r)
        for b in o.reads:
            b.readers.append(o.idx)
        for b in o.writes:
            b.last_w = o.idx
            b.readers = []
        o.deps.discard(o.idx)
        self.ops.append(o)

    def emit(self, stack):
        nc = self.nc
        ops = self.ops
        needed = []
        for o in ops:
            nd = []
            for d in o.deps:
                y = ops[d]
                if y.is_dma or o.is_dma or y.eng != o.eng:
                    nd.append(d)
                elif o.eng != "pe":
                    if any((b in y.writes) for b in o.reads if b is not self.ALL) or \
                       any((b in y.writes or b in y.reads) for b in o.writes if b is not self.ALL):
                        nd.append(d)
            needed.append(nd)
            for d in nd:
                ops[d].signal = True
        for o in ops:
            if o.is_dma:
                o.signal = True
        eng_sem = {e: stack.enter_context(nc.semaphore("s_" + e)) for e in ENGS}
        nqs = {"sp": self.n_dma_sems - 16, "pool": 4, "act": 6, "dve": 6}
        dma_sems = {q: [stack.enter_context(nc.semaphore("d%s%d" % (q, i))) for i in range(nqs[q])] for q in nqs}
        cnt = {e: 0 for e in ENGS}
        dcnt = {q: [0] * nqs[q] for q in nqs}
        rr = {q: 0 for q in nqs}
        for o in ops:
            if o.is_dma:
                q = o.eng
                nq = nqs[q]
                k = rr[q] % nq
                rr[q] += 1
                dcnt[q][k] += 16
                o.sem, o.count, o.semkey = dma_sems[q][k], dcnt[q][k], ("d", q, k)
            elif o.signal:
                cnt[o.eng] += 1
                o.sem, o.count, o.semkey = eng_sem[o.eng], cnt[o.eng], ("e", o.eng)
        waited = {e: {} for e in ENGS}
        block = stack.enter_context(nc.Block())
        per_eng = {e: [] for e in ENGS}
        for o in ops:
            per_eng[o.eng].append(o)
        final_waits = {}
        for o in ops:
            if o.is_dma:
                final_waits[o.semkey] = (o.sem, o.count)

        def make(ename):
            def body(eng):
                w = waited[ename]
                for o in per_eng[ename]:
                    req = {}
                    for d in needed[o.idx]:
                        y = ops[d]
                        if req.get(y.semkey, (None, 0))[1] < y.count:
                            req[y.semkey] = (y.sem, y.count)
                    for key, (sem, c) in req.items():
                        if w.get(key, 0) < c:
                            eng.wait_ge(sem, c)
                            w[key] = c
                    if o.is_dma and o.count > 16 and w.get(o.semkey, 0) < o.count - 16:
                        eng.wait_ge(o.sem, o.count - 16)
                        w[o.semkey] = o.count - 16
                    ins = o.fn(eng)
                    if o.signal:
                        ins.then_inc(o.sem, 16 if o.is_dma else 1)
                if ename == "sp":
                    for key, (sem, c) in final_waits.items():
                        if w.get(key, 0) < c:
                            eng.wait_ge(sem, c)
                            w[key] = c
            return body

        block.tensor(make("pe"))
        block.scalar(make("act"))
        block.vector(make("dve"))
        block.gpsimd(make("pool"))
        block.sync(make("sp"))


class T:
    __slots__ = ("ap", "b")

    def __init__(self, ap, name=""):
        self.ap = ap
        self.b = Buf(name)


class Arena:
    def __init__(self, big, ncols):
        self.big = big
        self.ncols = ncols
        self.off = 0
        self.marks = []

    def push(self):
        self.marks.append(self.off)

    def pop(self):
        self.off = self.marks.pop()

    def alloc(self, free_shape, dtype, name=""):
        n = int(np.prod(free_shape))
        n32 = n if dtype in (F32, I32) else (n + 1) // 2
        assert self.off + n32 <= self.ncols, ("SBUF arena overflow", name, self.off, n32, self.ncols)
        v = self.big[:, self.off:self.off + n32]
        self.off += n32
        if dtype not in (F32,):
            v = v.bitcast(dtype)
            if n % 2 and dtype == BF16:
                v = v[:, 0:n]
        if len(free_shape) == 2:
            v = v.rearrange("p (a b) -> p a b", a=free_shape[0])
        elif len(free_shape) == 3:
            v = v.rearrange("p (a b c) -> p a b c", a=free_shape[0], b=free_shape[1])
        elif len(free_shape) == 4:
            v = v.rearrange("p (a b c d) -> p a b c d", a=free_shape[0], b=free_shape[1], c=free_shape[2])
        return T(v, name)


def _bs(ts_):
    return [t.b for t in ts_]


class K:
    def __init__(self, P):
        self.P = P

    def dma(self, q, out, in_, r=(), w=(), slow=False):
        if slow:
            self.P.dma(q, lambda e: e.dma_start(out=out, in_=in_, allow_slow_non_contiguous=True), _bs(r), _bs(w))
        else:
            self.P.dma(q, lambda e: e.dma_start(out=out, in_=in_), _bs(r), _bs(w))

    def mm(self, out, lhsT, rhs, start, stop, r=(), w=()):
        self.P.op("pe", lambda e: e.matmul(out, lhsT=lhsT, rhs=rhs, start=start, stop=stop), _bs(r), _bs(w))

    def tr(self, out, in_, ident, r=(), w=()):
        self.P.op("pe", lambda e: e.transpose(out, in_, ident), _bs(r), _bs(w))

    def act(self, out, in_, func, r=(), w=(), scale=None, bias=None, accum=None):
        kw = {}
        if scale is not None:
            kw["scale"] = scale
        if bias is not None:
            kw["bias"] = bias
        if accum is not None:
            kw["accum_out"] = accum
        self.P.op("act", lambda e: e.activation(out=out, in_=in_, func=func, **kw), _bs(r), _bs(w))

    def tt(self, eng, out, a, b, op, r=(), w=()):
        self.P.op(eng, lambda e: e.tensor_tensor(out=out, in0=a, in1=b, op=op), _bs(r), _bs(w))

    def ts(self, eng, out, a, s1, op0, s2=None, op1=None, r=(), w=()):
        if op1 is None:
            self.P.op(eng, lambda e: e.tensor_scalar(out=out, in0=a, scalar1=s1, scalar2=None, op0=op0), _bs(r), _bs(w))
        else:
            self.P.op(eng, lambda e: e.tensor_scalar(out=out, in0=a, scalar1=s1, scalar2=s2, op0=op0, op1=op1), _bs(r), _bs(w))

    def stt(self, out, a, s, b, op0, op1, r=(), w=()):
        self.P.op("dve", lambda e: e.scalar_tensor_tensor(out=out, in0=a, scalar=s, in1=b, op0=op0, op1=op1), _bs(r), _bs(w))

    def cp(self, eng, out, in_, r=(), w=()):
        if eng == "act":
            self.P.op("act", lambda e: e.activation(out=out, in_=in_, func=AF.Copy), _bs(r), _bs(w))
        else:
            self.P.op(eng, lambda e: e.tensor_copy(out=out, in_=in_), _bs(r), _bs(w))

    def memset(self, eng, ap, val, w=()):
        self.P.op(eng, lambda e: e.memset(ap, val), (), _bs(w))

    def recip(self, out, in_, r=(), w=()):
        self.P.op("dve", lambda e: e.reciprocal(out=out, in_=in_), _bs(r), _bs(w))

    def scan(self, out, d0, d1, init, r=(), w=()):
        self.P.op("dve", lambda e: e.tensor_tensor_scan(out=out, data0=d0, data1=d1, initial=init, op0=ALU.mult, op1=ALU.add), _bs(r), _bs(w))

    def asel(self, out, in_, pattern, cmp, fill, base, cm, r=(), w=()):
        self.P.op("pool", lambda e: e.affine_select(out=out, in_=in_, pattern=pattern, compare_op=cmp, fill=fill, base=base, channel_multiplier=cm), _bs(r), _bs(w))


def rev_ap(ap2d_last_col, n):
    return bass.AP(ap2d_last_col.tensor, ap2d_last_col.offset, [list(ap2d_last_col.ap[0]), [-1, n]])


def strided(ap_first, step, n):
    return bass.AP(ap_first.tensor, ap_first.offset, [list(ap_first.ap[0]), [step, n]])


def t5_bucket_np(rel):
    half = 16
    n = -rel
    ret = np.where(n < 0, half, 0)
    n = np.abs(n)
    max_exact = 8
    nf = np.maximum(n, 1).astype(np.float32)
    large = max_exact + (np.log(nf / np.float32(max_exact)) / np.float32(math.log(1024 / max_exact)) * (half - max_exact)).astype(np.int32)
    large = np.minimum(large, half - 1)
    return ret + np.where(n < max_exact, n, large)


class Ctx:
    pass


def build(S, debug=None):
    nc = bass.Bass("TRN2", target_bir_lowering=False)
    NT = S // 128
    C = Ctx()
    C.nc, C.S, C.NT = nc, S, NT
    din = {}

    def inp(name, shape):
        din[name] = nc.dram_tensor(name, list(shape), F32, kind="ExternalInput").ap()
        return din[name]

    x = inp("x", [S, D])
    pin = inp("p", [S, 256])
    rel_bias = inp("rel_bias", [32, 8])
    g_mix = inp("g_mix", [1, 1024])
    w_in = inp("w_in", [1, 1024, 2048])
    a_re = inp("ssm_a_re", [1, 2, 32, 64])
    a_im = inp("ssm_a_im", [1, 2, 32, 64])
    log_dt = inp("ssm_log_dt", [1, 2, 32])
    b_re = inp("ssm_b_re", [1, 2, 32, 64, 16])
    b_im = inp("ssm_b_im", [1, 2, 32, 64, 16])
    c_re = inp("ssm_c_re", [1, 2, 32, 16, 64])
    c_im = inp("ssm_c_im", [1, 2, 32, 16, 64])
    ssm_d = inp("ssm_d", [1, 32, 16])
    w_glu = inp("w_glu", [1, 512, 512])
    b_glu = inp("b_glu", [1, 512])
    g_att = inp("g_att_out", [1, 512])
    g_ssm = inp("g_ssm_out", [1, 512])
    w_out = inp("w_out", [1, 1024, 1024])
    g_mlp = inp("g_mlp", [1, 1024])
    w_mlp1 = inp("w_mlp1", [1, 1024, 4096])
    w_mlp2 = inp("w_mlp2", [1, 4096, 1024])
    g_ple = inp("g_ple", [1, 1024])
    w_gate = inp("w_ple_gate", [1, 1024, 1024])
    w_proj = inp("w_ple_proj", [1, 256, 1024])
    g_final = inp("g_final", [1024])
    y = nc.dram_tensor("y", [S, D], F32, kind="ExternalOutput").ap()

    def scratch(name, shape, dt):
        kind = "ExternalOutput" if (debug and name in debug) else "Internal"
        return nc.dram_tensor(name, list(shape), dt, kind=kind).ap()

    qT = scratch("qT", [4, 128, S], BF16)
    kT = scratch("kT", [4, 128, S], BF16)
    uT = scratch("uT", [4, 128, S], BF16)
    gT = scratch("gT", [4, 128, S], BF16)
    Vp = scratch("Vp", [S + 2048, 8, 128], BF16)
    hs = scratch("hs", [S, D], F32)
    hs2 = scratch("hs2", [S, D], F32)
    Fd = scratch("Fd", [3 * 8 * 512 + 512], F32)

    with ExitStack() as st:
        NCOL = 49100
        big = st.enter_context(nc.sbuf_tensor("big", [128, NCOL], F32))
        psb = [st.enter_context(nc.psum_tensor("ps%d" % i, [128, 512], F32)) for i in range(8)]
        P = Prog(nc)
        k = K(P)
        A = Arena(big, NCOL)
        bar = A.alloc([1], F32, "bar")
        P._bar = bar.ap
        PS = [T(psb[i][:], "ps%d" % i) for i in range(8)]
        C.P, C.k, C.A, C.PS = P, k, A, PS
        identf = A.alloc([128], F32, "identf")
        identb = A.alloc([128], BF16, "identb")
        ones_b = A.alloc([128], BF16, "ones_b")
        k.memset("pool", identf.ap, 0.0, w=[identf])
        k.asel(identf.ap, identf.ap, [[-1, 128]], ALU.not_equal, 1.0, 0, 1, r=[identf], w=[identf])
        k.cp("dve", identb.ap, identf.ap, r=[identf], w=[identb])
        k.memset("pool", ones_b.ap, 1.0, w=[ones_b])
        C.identf, C.identb, C.ones_b = identf, identb, ones_b

        def gcol(src_1d_ap, n, name):
            t = A.alloc([n // 128], F32, name)
            k.dma("sp", t.ap, src_1d_ap.rearrange("(k p) -> p k", p=128), w=[t], slow=True)
            return t
        C.gcol = gcol

        nhalf = A.alloc([1], F32, "nhalf")
        k.memset("pool", nhalf.ap, -0.5, w=[nhalf])

        def rms_rstd(ss_t, n, tmp_t, out_t, ss_ap=None, wide=False):
            if wide:
                k.ts("dve", tmp_t.ap, ss_t.ap if ss_ap is None else ss_ap, 1.0 / n, ALU.mult, EPS, ALU.add, r=[ss_t], w=[tmp_t])
                k.act(tmp_t.ap, tmp_t.ap, AF.Ln, r=[tmp_t], w=[tmp_t])
                k.act(out_t.ap, tmp_t.ap, AF.Exp, scale=-0.5, r=[tmp_t], w=[out_t])
            else:
                k.ts("pool", tmp_t.ap, ss_t.ap if ss_ap is None else ss_ap, 1.0 / n, ALU.mult, EPS, ALU.add, r=[ss_t], w=[tmp_t])
                k.tt("pool", out_t.ap, tmp_t.ap, nhalf.ap, ALU.pow, r=[tmp_t, nhalf], w=[out_t])
        C.rms_rstd = rms_rstd

        def load_weight_bf16(dst, src2d, nk, ncols, gc, tag):
            A.push()
            tmps = [A.alloc([min(ncols, 2048)], F32, tag + "tmp%d" % i) for i in range(2)]
            i = 0
            for kk in range(nk):
                for c0 in range(0, ncols, 2048):
                    cw = min(2048, ncols - c0)
                    tm = tmps[i % 2]
                    k.dma("sp", tm.ap[:, 0:cw], src2d[kk * 128:(kk + 1) * 128, c0:c0 + cw], w=[tm])
                    if i % 2 == 0:
                        if gc is None:
                            k.cp("dve", dst.ap[:, kk, c0:c0 + cw], tm.ap[:, 0:cw], r=[tm], w=[dst])
                        else:
                            k.ts("dve", dst.ap[:, kk, c0:c0 + cw], tm.ap[:, 0:cw], gc.ap[:, kk:kk + 1], ALU.mult, r=[tm, gc], w=[dst])
                    else:
                        if gc is None:
                            k.cp("act", dst.ap[:, kk, c0:c0 + cw], tm.ap[:, 0:cw], r=[tm], w=[dst])
                        else:
                            k.act(dst.ap[:, kk, c0:c0 + cw], tm.ap[:, 0:cw], AF.Copy, scale=gc.ap[:, kk:kk + 1], r=[tm, gc], w=[dst])
                    i += 1
            A.pop()
            P.barrier()
        C.load_weight_bf16 = load_weight_bf16

        phases = debug.get("phases", "ASBCD") if debug else "ASBCD"
        if "A" in phases:
            phase_A(C, x, g_mix, w_in, qT, kT, uT, Vp)
            P.barrier()
        if "S" in phases:
            phase_S(C, uT, gT, a_re, a_im, log_dt, b_re, b_im, c_re, c_im, ssm_d)
            P.barrier()
        if "B" in phases:
            phase_B(C, x, qT, kT, Vp, gT, hs, rel_bias, Fd, w_glu, b_glu, g_att, g_ssm, w_out)
            P.barrier()
        if "X" in phases:
            for i in range(S // 512):
                k.dma("sp", hs[i * 512:(i + 1) * 512, :], x[i * 512:(i + 1) * 512, :])
            P.barrier()
        if "C" in phases:
            phase_C(C, hs, hs2, g_mlp, w_mlp1, w_mlp2)
            P.barrier()
        if "D" in phases:
            phase_D(C, hs2, pin, y, g_ple, w_gate, w_proj, g_final)
        P.emit(st)
    return nc


def phase_A(C, x, g_mix, w_in, qT, kT, uT, Vp):
    k, A, PS, S, NT = C.k, C.A, C.PS, C.S, C.NT
    A.push()
    gm = C.gcol(g_mix[0], 1024, "gm")
    win = A.alloc([8, 2048], BF16, "win")
    C.load_weight_bf16(win, w_in[0], 8, 2048, gm, "win")
    xb = [A.alloc([1024], F32, "xa%d" % i) for i in range(2)]
    junk = A.alloc([1024], F32, "junk")
    ab = [A.alloc([1024], BF16, "ab%d" % i) for i in range(2)]
    ss = [A.alloc([1], F32, "ss%d" % i) for i in range(2)]
    tmp = [A.alloc([1], F32, "tmp%d" % i) for i in range(2)]
    rstd = [A.alloc([1], F32, "rstd%d" % i) for i in range(2)]
    aT = [A.alloc([8, 512], BF16, "aT%d" % i) for i in range(2)]
    zst = [A.alloc([512], BF16, "zst%d" % i) for i in range(4)]
    vst = [A.alloc([8, 128], BF16, "vst%d" % i) for i in range(2)]
    zero = A.alloc([8, 128], BF16, "zero")
    k.memset("dve", zero.ap, 0.0, w=[zero])
    for v in vst:
        k.memset("pool", v.ap, 1.0, w=[v])
    for i in range(8):
        k.dma("sp", Vp[i * 128:(i + 1) * 128], zero.ap, r=[zero])
        k.dma("sp", Vp[1024 + S + i * 128:1024 + S + (i + 1) * 128], zero.ap, r=[zero])
    zi = [0]

    def normpre(b):
        for t in range(4):
            ti = 4 * b + t
            xt, a_, s_, tm, rs = xb[ti % 2], ab4[t], ss[ti % 2], tmp[ti % 2], rstd[ti % 2]
            k.dma("sp", xt.ap, x[ti * 128:(ti + 1) * 128, :], w=[xt])
            k.memset("pool", s_.ap, 0.0, w=[s_])
            k.act(junk.ap, xt.ap, AF.Square, accum=s_.ap, r=[xt, s_], w=[junk, s_])
            C.rms_rstd(s_, 1024, tm, rs)
            k.ts("dve", a_.ap, xt.ap, rs.ap[:, 0:1], ALU.mult, r=[xt, rs], w=[a_])

    def normtr(b):
        at = aT[b % 2]
        for t in range(4):
            ti = 4 * b + t
            a_ = ab4[t]
            pst = PS[ti % 2]
            psv = pst.ap.bitcast(BF16).rearrange("p (a b) -> p a b", a=8)
            for kk in range(8):
                k.tr(psv[:, kk, :], a_.ap[:, kk * 128:(kk + 1) * 128], C.identb.ap, r=[a_, C.identb], w=[pst])
            k.cp("dve", at.ap[:, :, t * 128:(t + 1) * 128], psv, r=[pst], w=[at])

    def projstage(b):
        at = aT[b % 2]
        for (dst, f0, sc) in ((qT, 0, 0.125), (kT, 4, None), (uT, 12, None)):
            for j in range(4):
                pz = PS[2 + zi[0] % 3]
                for kk in range(8):
                    k.mm(pz.ap, win.ap[:, kk, (f0 + j) * 128:(f0 + j + 1) * 128], at.ap[:, kk, :], kk == 0, kk == 7, r=[win, at], w=[pz])
                z = zst[zi[0] % 4]
                k.act(z.ap, pz.ap, AF.Copy, scale=(sc if sc is not None else 1.0), r=[pz], w=[z])
                k.dma("act", dst[j, :, b * 512:(b + 1) * 512], z.ap, r=[z])
                zi[0] += 1
        for t in range(4):
            ti = 4 * b + t
            pv = PS[5 + ti % 3]
            for kk in range(8):
                k.mm(pv.ap, at.ap[:, kk, t * 128:(t + 1) * 128], win.ap[:, kk, 1024:1536], kk == 0, kk == 7, r=[win, at], w=[pv])
            v = vst[ti % 2]
            pvv = pv.ap.rearrange("p (a b c) -> p a b c", a=4, b=2)
            vv = v.ap.rearrange("p (a b) c -> p a b c", b=2)
            k.cp("dve", vv[:, :, 0, 0:64], pvv[:, :, 0, :], r=[pv], w=[v])
            k.cp("act", vv[:, :, 1, 64:128], pvv[:, :, 1, :], r=[pv], w=[v])
            k.dma("act", Vp[1024 + ti * 128:1024 + (ti + 1) * 128], v.ap, r=[v])
    NB = S // 512
    ab4 = ab + [A.alloc([1024], BF16, "ab%d" % i) for i in range(2, 4)]
    normpre(0)
    normtr(0)
    if NB > 1:
        normpre(1)
    for b in range(NB):
        projstage(b)
        if b + 1 < NB:
            normtr(b + 1)
        if b + 2 < NB:
            normpre(b + 2)
    A.pop()


def phase_S(C, uT, gT, a_re, a_im, log_dt, b_re, b_im, c_re, c_im, ssm_d):
    k, A, PS, S, P = C.k, C.A, C.PS, C.S, C.P
    NCH = S // 8
    CB = min(512, NCH)
    NCB = NCH // CB
    LV = int(round(math.log2(NCH)))
    TWO_PI = 2.0 * math.pi
    A.push()
    Z = A.alloc([8, 240], BF16, "Z")
    T0 = A.alloc([32, 128], BF16, "T0")
    WS = A.alloc([32, 2, 128], BF16, "WS")
    WOF = A.alloc([32, 2, 128], BF16, "WOF")
    WOB = A.alloc([32, 2, 128], BF16, "WOB")
    k.memset("pool", WOF.ap, 0.0, w=[WOF])
    k.memset("pool", WOB.ap, 0.0, w=[WOB])
    Wpow = A.alloc([LV, 32, 2], F32, "Wpow")
    rho8 = A.alloc([32], F32, "rho8")
    Eb = A.alloc([32, 2, 32], F32, "Eb")
    k.memset("pool", Z.ap, 0.0, w=[Z])
    for gi in range(8):
        k.asel(Z.ap[:, gi, 112:128], Z.ap[:, gi, 112:128], [[-1, 16]], ALU.not_equal, 1.0, -16 * gi, 1, r=[Z], w=[Z])
    A.push()

    def new(shape, name):
        return A.alloc(shape, F32, name)
    are, aim, ldt = new([32], "are"), new([32], "aim"), new([32], "ldt")
    bre, bim = new([32, 16], "bre"), new([32, 16], "bim")
    Lr, Li = new([32, 8, 16], "Lr"), new([32, 8, 16], "Li")
    Rr, Ri = new([32, 8, 16], "Rr"), new([32, 8, 16], "Ri")
    dcol = new([32], "dcol")
    for dr in range(2):
        ps_ = slice(dr * 64, dr * 64 + 64)
        k.dma("sp", are.ap[ps_, :], a_re[0, dr].rearrange("g n -> n g"), w=[are], slow=True)
        k.dma("sp", aim.ap[ps_, :], a_im[0, dr].rearrange("g n -> n g"), w=[aim], slow=True)
        k.dma("sp", ldt.ap[ps_, :], log_dt[0, dr:dr + 1, :].partition_broadcast(64), w=[ldt])
        k.dma("sp", bre.ap[ps_], b_re[0, dr].rearrange("g n h -> n g h"), w=[bre], slow=True)
        k.dma("sp", bim.ap[ps_], b_im[0, dr].rearrange("g n h -> n g h"), w=[bim], slow=True)
    for s_ in range(8):
        k.dma("sp", dcol.ap[s_ * 16:(s_ + 1) * 16, :], ssm_d[0].rearrange("g h -> h g"), w=[dcol], slow=True)
    ct = new([128], "ct")
    for (src, dst) in ((c_re, Rr), (c_im, Ri)):
        for o in range(4):
            k.dma("sp", ct.ap[:, 0:64], src[0, 0, 8 * o:8 * o + 8].rearrange("g h n -> (g h) n"), w=[ct])
            k.dma("sp", ct.ap[:, 64:128], src[0, 1, 8 * o:8 * o + 8].rearrange("g h n -> (g h) n"), w=[ct])
            k.tr(PS[0].ap[:, 0:128], ct.ap, C.identf.ap, r=[ct, C.identf], w=[PS[0]])
            k.cp("dve", dst.ap[:, 8 * o:8 * o + 8, 0, :], PS[0].ap[:, 0:128].rearrange("p (a b) -> p a b", b=16), r=[PS[0]], w=[dst])
    sc = [new([32], "sc%d" % i) for i in range(12)]
    sci = A.alloc([32], I32, "sci")

    def mul(o, a, b, eng="dve"):
        k.tt(eng, o.ap, a.ap, b.ap, ALU.mult, r=[a, b], w=[o])

    def sin_of(out, th, shift):
        t1, tf, r_, m_ = sc[8], sc[9], sc[10], sc[11]
        k.ts("dve", r_.ap, th.ap, shift, ALU.add, r=[th], w=[r_])
        k.ts("dve", t1.ap, r_.ap, 1.0 / TWO_PI, ALU.mult, r=[r_], w=[t1])
        k.cp("dve", sci.ap, t1.ap, r=[t1], w=[sci])
        k.cp("dve", tf.ap, sci.ap, r=[sci], w=[tf])
        k.stt(r_.ap, tf.ap, -TWO_PI, r_.ap, ALU.mult, ALU.add, r=[tf, r_], w=[r_])
        k.ts("dve", m_.ap, r_.ap, math.pi, ALU.is_gt, -TWO_PI, ALU.mult, r=[r_], w=[m_])
        k.tt("dve", r_.ap, r_.ap, m_.ap, ALU.add, r=[r_, m_], w=[r_])
        k.ts("dve", m_.ap, r_.ap, -math.pi, ALU.is_lt, TWO_PI, ALU.mult, r=[r_], w=[m_])
        k.tt("dve", r_.ap, r_.ap, m_.ap, ALU.add, r=[r_, m_], w=[r_])
        k.ts("dve", r_.ap, r_.ap, 3.1415925, ALU.min, -3.1415925, ALU.max, r=[r_], w=[r_])
        k.act(out.ap, r_.ap, AF.Sin, r=[r_], w=[out])
    dt_, lam, th, mag, c1, s1, abr, abi = [new([32], "p%d" % i) for i in range(8)]
    k.act(dt_.ap, ldt.ap, AF.Exp, r=[ldt], w=[dt_])
    mul(lam, are, dt_)
    mul(th, aim, dt_)
    k.act(mag.ap, lam.ap, AF.Exp, r=[lam], w=[mag])
    sin_of(s1, th, 0.0)
    sin_of(c1, th, math.pi / 2)
    mul(abr, mag, c1)
    mul(abi, mag, s1)
    inv, fre, fim, am1 = [new([32], "q%d" % i) for i in range(4)]
    mul(sc[0], are, are)
    mul(sc[1], aim, aim)
    k.tt("dve", sc[0].ap, sc[0].ap, sc[1].ap, ALU.add, r=[sc[0], sc[1]], w=[sc[0]])
    k.recip(inv.ap, sc[0].ap, r=[sc[0]], w=[inv])
    k.ts("dve", am1.ap, abr.ap, -1.0, ALU.add, r=[abr], w=[am1])
    mul(sc[0], am1, are)
    mul(sc[1], abi, aim)
    k.tt("dve", sc[0].ap, sc[0].ap, sc[1].ap, ALU.add, r=[sc[0], sc[1]], w=[sc[0]])
    mul(fre, sc[0], inv)
    mul(sc[0], abi, are)
    mul(sc[1], am1, aim)
    k.tt("dve", sc[0].ap, sc[0].ap, sc[1].ap, ALU.subtract, r=[sc[0], sc[1]], w=[sc[0]])
    mul(fim, sc[0], inv)

    def cmul(ore, oim, ar_, ai_, br_, bi_, t1, t2, r, w):
        k.tt("dve", t1, ar_, br_, ALU.mult, r=r, w=w)
        k.tt("dve", t2, ai_, bi_, ALU.mult, r=r, w=w)
        k.tt("dve", ore, t1, t2, ALU.subtract, r=r, w=w)
        k.tt("dve", t1, ar_, bi_, ALU.mult, r=r, w=w)
        k.tt("dve", t2, ai_, br_, ALU.mult, r=r, w=w)
        k.tt("dve", oim, t1, t2, ALU.add, r=r, w=w)
    X = T(None, "prepX")
    allr = [X, are, aim, bre, bim, Rr, Ri, Lr, Li, abr, abi, mag, fre, fim, lam]
    big1, big2 = new([16, 128], "big1"), new([16, 128], "big2")

    def b16(t_):
        return t_.ap.unsqueeze(2).to_broadcast([128, 32, 16])

    def b128(t_):
        return t_.ap.unsqueeze(2).to_broadcast([128, 32, 128])
    t16a = big1.ap.rearrange("p a b -> p (a b)")[:, 0:512].rearrange("p (g h) -> p g h", h=16)
    t16b = big2.ap.rearrange("p a b -> p (a b)")[:, 0:512].rearrange("p (g h) -> p g h", h=16)
    cmul(Lr.ap[:, :, 0, :], Li.ap[:, :, 0, :], b16(fre), b16(fim), bre.ap, bim.ap, t16a, t16b, allr, [X])
    aivr, aivi, im2 = new([32], "aivr"), new([32], "aivi"), new([32], "im2")
    mul(sc[0], mag, mag)
    k.recip(im2.ap, sc[0].ap, r=[sc[0]], w=[im2])
    mul(aivr, abr, im2)
    k.stt(aivi.ap, abi.ap, -1.0, im2.ap, ALU.mult, ALU.mult, r=[abi, im2], w=[aivi])
    MLr, MLi, MRr, MRi = [new([32], "M%d" % i) for i in range(4)]
    lo, hi = slice(0, 64), slice(64, 128)
    for (dst, s_lo, s_hi) in ((MLr, aivr, abr), (MLi, aivi, abi), (MRr, abr, aivr), (MRi, abi, aivi)):
        k.cp("dve", dst.ap[lo], s_lo.ap[lo], r=[s_lo], w=[dst])
        k.cp("dve", dst.ap[hi], s_hi.ap[hi], r=[s_hi], w=[dst])
    allr += [MLr, MLi, MRr, MRi]
    for s_ in range(7):
        cmul(Lr.ap[:, :, s_ + 1, :], Li.ap[:, :, s_ + 1, :], Lr.ap[:, :, s_, :], Li.ap[:, :, s_, :], b16(MLr), b16(MLi), t16a, t16b, allr, [X])
        cmul(Rr.ap[:, :, s_ + 1, :], Ri.ap[:, :, s_ + 1, :], Rr.ap[:, :, s_, :], Ri.ap[:, :, s_, :], b16(MRr), b16(MRi), t16a, t16b, allr, [X])
    pwr = [abr] + [new([32], "pwr%d" % i) for i in range(7)]
    pwi = [abi] + [new([32], "pwi%d" % i) for i in range(7)]
    for i in range(7):
        cmul(pwr[i + 1].ap, pwi[i + 1].ap, pwr[i].ap, pwi[i].ap, abr.ap, abi.ap, sc[0].ap, sc[1].ap, allr, [X])
    FSr, FSi, FOr, FOi = [new([32], "F%d" % i) for i in range(4)]
    k.cp("dve", FSr.ap[lo], pwr[6].ap[lo], r=allr, w=[X])
    k.cp("dve", FSi.ap[lo], pwi[6].ap[lo], r=allr, w=[X])
    k.memset("dve", FSr.ap[hi], 1.0, w=[X])
    k.memset("dve", FSi.ap[hi], 0.0, w=[X])
    k.cp("dve", FOr.ap[lo], abr.ap[lo], r=allr, w=[X])
    k.cp("dve", FOi.ap[lo], abi.ap[lo], r=allr, w=[X])
    k.cp("dve", FOr.ap[hi], pwr[7].ap[hi], r=allr, w=[X])
    k.cp("dve", FOi.ap[hi], pwi[7].ap[hi], r=allr, w=[X])
    Lr3 = Lr.ap.rearrange("p g s h -> p g (s h)")
    Li3 = Li.ap.rearrange("p g s h -> p g (s h)")
    Rr3 = Rr.ap.rearrange("p g s h -> p g (s h)")
    Ri3 = Ri.ap.rearrange("p g s h -> p g (s h)")
    WSr, WSi = new([16, 128], "WSr"), new([16, 128], "WSi")

    def b128h(t_, h_):
        return t_.ap[:, 16 * h_:16 * h_ + 16].unsqueeze(2).to_broadcast([128, 16, 128])
    for h_ in range(2):
        gs = slice(16 * h_, 16 * h_ + 16)
        cmul(WSr.ap, WSi.ap, Lr3[:, gs], Li3[:, gs], b128h(FSr, h_), b128h(FSi, h_), big1.ap, big2.ap, allr, [X])
        for gl in range(16):
            g = 16 * h_ + gl
            for xi, src in enumerate((WSr, WSi)):
                pst = PS[(2 * g + xi) % 4]
                k.tr(pst.ap[:, 0:128], src.ap[:, gl, :], C.identf.ap, r=[X, C.identf], w=[pst])
                k.cp("act" if xi else "dve", WS.ap[:, g, xi, :], pst.ap[:, 0:128], r=[pst], w=[WS])
        cmul(WSr.ap, WSi.ap, Rr3[:, gs], Ri3[:, gs], b128h(FOr, h_), b128h(FOi, h_), big1.ap, big2.ap, allr, [X])
        for (dst, ps_) in ((WOF, lo), (WOB, hi)):
            k.cp("dve", dst.ap[ps_, gs, 0, :], WSr.ap[ps_], r=[X], w=[dst])
            k.ts("dve", dst.ap[ps_, gs, 1, :], WSi.ap[ps_], -1.0, ALU.mult, r=[X], w=[dst])
    Mlow, Mup, onesf = new([128], "Mlow"), new([128], "Mup"), new([128], "onesf")
    k.memset("pool", onesf.ap, 1.0, w=[onesf])
    k.asel(Mlow.ap.rearrange("p (t h) -> p t h", h=16), onesf.ap.rearrange("p (t h) -> p t h", h=16), [[16, 8], [0, 16]], ALU.is_ge, 0.0, 15, -1, r=[onesf], w=[Mlow])
    k.asel(Mup.ap.rearrange("p (t h) -> p t h", h=16), onesf.ap.rearrange("p (t h) -> p t h", h=16), [[-16, 8], [0, 16]], ALU.is_ge, 0.0, 0, 1, r=[onesf], w=[Mup])
    tacc = [new([128], "tacc%d" % i) for i in range(2)]
    for g in range(32):
        pf, pf2, pb_, pb2 = PS[4], PS[5], PS[6], PS[7]
        k.mm(pf.ap[:, 0:128], Lr3[lo, g, :], Rr3[lo, g, :], True, True, r=[X], w=[pf])
        k.mm(pf2.ap[:, 0:128], Li3[lo, g, :], Ri3[lo, g, :], True, True, r=[X], w=[pf2])
        k.mm(pb_.ap[:, 0:128], Lr3[hi, g, :], Rr3[hi, g, :], True, True, r=[X], w=[pb_])
        k.mm(pb2.ap[:, 0:128], Li3[hi, g, :], Ri3[hi, g, :], True, True, r=[X], w=[pb2])
        ta, tb = tacc[0], tacc[1]
        k.cp("act", ta.ap, pf2.ap[:, 0:128], r=[pf2], w=[ta])
        k.tt("dve", ta.ap, pf.ap[:, 0:128], ta.ap, ALU.subtract, r=[pf, ta], w=[ta])
        k.tt("dve", ta.ap, ta.ap, Mlow.ap, ALU.mult, r=[ta, Mlow], w=[ta])
        k.cp("act", tb.ap, pb2.ap[:, 0:128], r=[pb2], w=[tb])
        k.tt("dve", tb.ap, pb_.ap[:, 0:128], tb.ap, ALU.subtract, r=[pb_, tb], w=[tb])
        k.tt("dve", tb.ap, tb.ap, Mup.ap, ALU.mult, r=[tb, Mup], w=[tb])
        k.tt("dve", ta.ap, ta.ap, tb.ap, ALU.add, r=[ta, tb], w=[ta])
        k.stt(T0.ap[:, g, :], C.identf.ap, dcol.ap[:, g:g + 1], ta.ap, ALU.mult, ALU.add, r=[C.identf, dcol, ta], w=[T0])
    k.act(rho8.ap, lam.ap, AF.Exp, scale=8.0, r=[lam], w=[rho8])
    ir8 = new([32], "ir8")
    k.act(ir8.ap, lam.ap, AF.Exp, scale=-8.0, r=[lam], w=[ir8])
    k.tt("dve", Wpow.ap[:, 0, :, 0], pwr[7].ap, ir8.ap, ALU.mult, r=allr + [ir8], w=[Wpow])
    k.tt("dve", Wpow.ap[:, 0, :, 1], pwi[7].ap, ir8.ap, ALU.mult, r=allr + [ir8], w=[Wpow])
    for l in range(LV - 1):
        wr, wi = Wpow.ap[:, l, :, 0], Wpow.ap[:, l, :, 1]
        k.tt("dve", sc[0].ap, wr, wr, ALU.mult, r=[Wpow], w=[sc[0]])
        k.tt("dve", sc[1].ap, wi, wi, ALU.mult, r=[Wpow], w=[sc[1]])
        k.tt("dve", Wpow.ap[:, l + 1, :, 0], sc[0].ap, sc[1].ap, ALU.subtract, r=[sc[0], sc[1]], w=[Wpow])
        k.tt("dve", sc[2].ap, wr, wi, ALU.mult, r=[Wpow], w=[sc[2]])
        k.ts("dve", Wpow.ap[:, l + 1, :, 1], sc[2].ap, 2.0, ALU.mult, r=[sc[2]], w=[Wpow])
    k.memset("dve", Eb.ap[:, :, 0, 0:1], 1.0, w=[Eb])
    k.memset("dve", Eb.ap[:, :, 1, 0:1], 0.0, w=[Eb])
    eb1 = big1.ap.rearrange("p a b -> p (a b)")[:, 0:512].rearrange("p (g m) -> p g m", m=16)
    eb2 = big2.ap.rearrange("p a b -> p (a b)")[:, 0:512].rearrange("p (g m) -> p g m", m=16)
    for l in range(5):
        m = 1 << l
        wr = Wpow.ap[:, l, :, 0].unsqueeze(2).to_broadcast([128, 32, m])
        wi = Wpow.ap[:, l, :, 1].unsqueeze(2).to_broadcast([128, 32, m])
        ec0, es0 = Eb.ap[:, :, 0, 0:m], Eb.ap[:, :, 1, 0:m]
        ec1, es1 = Eb.ap[:, :, 0, m:2 * m], Eb.ap[:, :, 1, m:2 * m]
        t1, t2 = eb1[:, :, 0:m], eb2[:, :, 0:m]
        k.tt("dve", t1, es0, wi, ALU.mult, r=[Eb, Wpow, X], w=[X])
        k.tt("dve", t2, ec0, wr, ALU.mult, r=[Eb, Wpow, X], w=[X])
        k.tt("dve", ec1, t2, t1, ALU.subtract, r=[X, Eb], w=[Eb])
        k.tt("dve", t1, ec0, wi, ALU.mult, r=[Eb, Wpow, X], w=[X])
        k.tt("dve", t2, es0, wr, ALU.mult, r=[Eb, Wpow, X], w=[X])
        k.tt("dve", es1, t2, t1, ALU.add, r=[X, Eb], w=[Eb])
    A.pop()
    P.barrier()
    uTo = A.alloc([S], BF16, "uTo")
    gTo = A.alloc([S], BF16, "gTo")
    Yg = A.alloc([8, NCH], BF16, "Yg")
    Ugs = [A.alloc([NCH], BF16, "Ug%d" % i) for i in range(2)]
    Es = [A.alloc([2, NCH], F32, "E%d" % i) for i in range(2)]
    G = A.alloc([2, NCH], F32, "G")
    Hs = A.alloc([2, NCH], F32, "Hs")
    Hbs = [A.alloc([2, NCH + 2], BF16, "Hb%d" % i) for i in range(2)]
    tq = [A.alloc([CB], F32, "tq%d" % i) for i in range(4)]
    ge = [A.alloc([CB], F32, "ge%d" % i) for i in range(3)]
    for hb_ in Hbs:
        k.memset("pool", hb_.ap, 0.0, w=[hb_])
    tr_ = [A.alloc([CB], F32, "tr%d" % i) for i in range(4)]
    Ug3 = Ugs + [A.alloc([NCH], BF16, "Ug2")]

    def egen(g):
        E = Es[g % 2]
        Ec, Es_ = E.ap[:, 0, :], E.ap[:, 1, :]
        k.cp("dve", E.ap[:, :, 0:32], Eb.ap[:, g, :, :], r=[Eb], w=[E])
        for l in range(5, LV):
            m = 1 << l
            wr, wi = Wpow.ap[:, l, g, 0:1], Wpow.ap[:, l, g, 1:2]
            t1, t2 = tq[0].ap[:, 0:m], tq[1].ap[:, 0:m]
            k.ts("dve", t1, Es_[:, 0:m], wi, ALU.mult, r=[E, Wpow], w=[tq[0]])
            k.stt(Ec[:, m:2 * m], Ec[:, 0:m], wr, t1, ALU.mult, ALU.subtract, r=[E, Wpow, tq[0]], w=[E])
            k.ts("dve", t2, Ec[:, 0:m], wi, ALU.mult, r=[E, Wpow], w=[tq[1]])
            k.stt(Es_[:, m:2 * m], Es_[:, 0:m], wr, t2, ALU.mult, ALU.add, r=[E, Wpow, tq[1]], w=[E])

    def insel(g):
        gi, Ug = g % 8, Ug3[g % 3]
        for cb in range(NCB):
            pu = PS[cb % 2]
            for s_ in range(8):
                k.mm(pu.ap[:, 0:CB], Z.ap[:, gi, 112 - 16 * s_:240 - 16 * s_], strided(uTo.ap[:, cb * CB * 8 + s_:cb * CB * 8 + s_ + 1], 8, CB), s_ == 0, s_ == 7, r=[Z, uTo], w=[pu])
            k.cp("act", Ug.ap[:, cb * CB:(cb + 1) * CB], pu.ap[:, 0:CB], r=[pu], w=[Ug])

    def summ(g):
        Ug = Ug3[g % 3]
        for cb in range(NCB):
            for xi in range(2):
                pS = PS[2 + 2 * xi + cb]
                k.mm(pS.ap[0:64, 0:CB], WS.ap[:, g, xi, 0:64], Ug.ap[:, cb * CB:(cb + 1) * CB], True, True, r=[WS, Ug], w=[pS])
                k.mm(pS.ap[64:128, 0:CB], WS.ap[:, g, xi, 64:128], rev_ap(Ug.ap[:, NCH - 1 - cb * CB:NCH - cb * CB], CB), True, True, r=[WS, Ug], w=[pS])

    def demod(g):
        E = Es[g % 2]
        Ec, Es_ = E.ap[:, 0, :], E.ap[:, 1, :]
        for cb in range(NCB):
            cs = slice(cb * CB, (cb + 1) * CB)
            pr, pi_ = PS[2 + cb], PS[4 + cb]
            k.tt("dve", tq[0].ap, pr.ap[:, 0:CB], Ec[:, cs], ALU.mult, r=[pr, E], w=[tq[0]])
            k.tt("dve", tq[1].ap, pi_.ap[:, 0:CB], Es_[:, cs], ALU.mult, r=[pi_, E], w=[tq[1]])
            k.tt("dve", G.ap[:, 0, cs], tq[0].ap, tq[1].ap, ALU.add, r=[tq[0], tq[1]], w=[G])
            k.tt("dve", tq[2].ap, pi_.ap[:, 0:CB], Ec[:, cs], ALU.mult, r=[pi_, E], w=[tq[2]])
            k.tt("dve", tq[3].ap, pr.ap[:, 0:CB], Es_[:, cs], ALU.mult, r=[pr, E], w=[tq[3]])
            k.tt("dve", G.ap[:, 1, cs], tq[2].ap, tq[3].ap, ALU.subtract, r=[tq[2], tq[3]], w=[G])

    def scan_(g):
        dec = rho8.ap[:, g:g + 1].to_broadcast([128, NCH])
        k.scan(Hs.ap[:, 0, :], dec, G.ap[:, 0, :], 0.0, r=[G, rho8], w=[Hs])
        k.scan(Hs.ap[:, 1, :], dec, G.ap[:, 1, :], 0.0, r=[G, rho8], w=[Hs])

    def remod(g):
        E, Hb = Es[g % 2], Hbs[g % 2]
        Ec, Es_ = E.ap[:, 0, :], E.ap[:, 1, :]
        for cb in range(NCB):
            cs = slice(cb * CB, (cb + 1) * CB)
            co = slice(1 + cb * CB, 1 + (cb + 1) * CB)
            k.tt("dve", tr_[0].ap, Hs.ap[:, 0, cs], Ec[:, cs], ALU.mult, r=[Hs, E], w=[tr_[0]])
            k.tt("dve", tr_[1].ap, Hs.ap[:, 1, cs], Es_[:, cs], ALU.mult, r=[Hs, E], w=[tr_[1]])
            k.tt("dve", Hb.ap[:, 0, co], tr_[0].ap, tr_[1].ap, ALU.subtract, r=[tr_[0], tr_[1]], w=[Hb])
            k.tt("dve", tr_[2].ap, Hs.ap[:, 1, cs], Ec[:, cs], ALU.mult, r=[Hs, E], w=[tr_[2]])
            k.tt("dve", tr_[3].ap, Hs.ap[:, 0, cs], Es_[:, cs], ALU.mult, r=[Hs, E], w=[tr_[3]])
            k.tt("dve", Hb.ap[:, 1, co], tr_[2].ap, tr_[3].ap, ALU.add, r=[tr_[2], tr_[3]], w=[Hb])

    def outs(g):
        gi, Ug, Hb = g % 8, Ug3[g % 3], Hbs[g % 2]
        for cb in range(NCB):
            cs = slice(cb * CB, (cb + 1) * CB)
            py = PS[6 + cb % 2]
            k.mm(py.ap[:, 0:CB], T0.ap[:, g, :], Ug.ap[:, cs], True, False, r=[T0, Ug], w=[py])
            for xi in range(2):
                k.mm(py.ap[:, 0:CB], WOF.ap[:, g, xi, :], Hb.ap[:, xi, cb * CB:(cb + 1) * CB], False, False, r=[WOF, Hb], w=[py])
            for xi in range(2):
                st_ = NCH - 1 - cb * CB
                k.mm(py.ap[:, 0:CB], WOB.ap[:, g, xi, :], rev_ap(Hb.ap[:, xi, st_:st_ + 1], CB), False, xi == 1, r=[WOB, Hb], w=[py])
            k.cp("act", Yg.ap[:, gi, cs], py.ap[:, 0:CB], r=[py], w=[Yg])

    for o in range(4):
        k.dma("sp", uTo.ap, uT[o], w=[uTo])
        g0 = 8 * o
        insel(g0)
        insel(g0 + 1)
        summ(g0)
        egen(g0)
        for gi in range(8):
            g = g0 + gi
            demod(g)
            if gi + 1 < 8:
                summ(g + 1)
            scan_(g)
            if gi + 1 < 8:
                egen(g + 1)
            remod(g)
            if gi + 2 < 8:
                insel(g + 2)
            outs(g)
        for t in range(8):
            for cb in range(NCB):
                po = PS[(t * NCB + cb) % 2]
                for gi in range(8):
                    k.mm(po.ap[:, 0:CB], Z.ap[:, t, 112 - 16 * gi:240 - 16 * gi], Yg.ap[:, gi, cb * CB:(cb + 1) * CB], gi == 0, gi == 7, r=[Z, Yg], w=[po])
                yv = po.ap[:, 0:CB]
                k.act(ge[0].ap, yv, AF.Square, r=[po], w=[ge[0]])
                k.ts("dve", ge[0].ap, ge[0].ap, 0.044715, ALU.mult, 1.0, ALU.add, r=[ge[0]], w=[ge[0]])
                k.tt("dve", ge[1].ap, ge[0].ap, yv, ALU.mult, r=[ge[0], po], w=[ge[1]])
                k.act(ge[2].ap, ge[1].ap, AF.Sigmoid, scale=1.5957691216057308, r=[ge[1]], w=[ge[2]])
                k.tt("dve", strided(gTo.ap[:, cb * CB * 8 + t:cb * CB * 8 + t + 1], 8, CB), yv, ge[2].ap, ALU.mult, r=[po, ge[2]], w=[gTo])
        k.dma("sp", gT[o], gTo.ap, r=[gTo])
    A.pop()


def ap3(base, d1, d2):
    return bass.AP(base.tensor, base.offset, [list(base.ap[0]), list(d1), list(d2)])


def phase_B(C, x, qT, kT, Vp, gT, hs, rel_bias, Fd, w_glu, b_glu, g_att, g_ssm, w_out):
    k, A, PS, S, P = C.k, C.A, C.PS, C.S, C.P
    NSB = S // 2048
    import os
    DIL = (1, 4, 16)
    DSEL = [int(v) for v in os.environ.get('PB_DIL', '1,4,16').split(',')]
    A.push()
    EB = A.alloc([3, 8, 2, 128], BF16, "EB")
    A.push()
    rb0 = A.alloc([256], F32, "rb0")
    F0 = A.alloc([3, 8, 512], F32, "F0")
    Tt = A.alloc([24, 257], F32, "Tt")
    k.dma("sp", rb0.ap, rel_bias.rearrange("(o b) h -> o (b h)", o=1).partition_broadcast(128), w=[rb0])
    k.act(rb0.ap, rb0.ap, AF.Exp, r=[rb0], w=[rb0])
    k.memset("dve", F0.ap, 0.0, w=[F0])
    rbv = rb0.ap.rearrange("p (b h) -> p b h", h=8)
    for di, d in enumerate(DIL):
        js = np.arange(-64, 65)
        bk = t5_bucket_np(js * d)
        s0 = 0
        while s0 < len(js):
            e0 = s0
            while e0 + 1 < len(js) and bk[e0 + 1] == bk[s0]:
                e0 += 1
            ln = e0 - s0 + 1
            x0 = int(js[s0]) + 192
            bb = int(bk[s0])
            for h in range(8):
                k.ts("dve", F0.ap[:, di, h, x0:x0 + ln], F0.ap[:, di, h, x0:x0 + ln], rbv[:, bb, h:h + 1], ALU.add, r=[rb0, F0], w=[F0])
            s0 = e0 + 1
    fdt = T(None, "Fd")
    for di in range(3):
        k.dma("sp", Fd[di * 4096:(di + 1) * 4096].rearrange("(o n) -> o n", o=1), F0.ap[0:1, di].rearrange("p b c -> p (b c)"), r=[F0], w=[fdt])
    for di in range(3):
        for h in range(8):
            j = di * 8 + h
            k.dma("sp", Tt.ap[:, j, :], bass.AP(Fd.tensor, j * 512, [[1, 128], [1, 257]]), r=[fdt], w=[Tt])
    ttb = Tt.ap[:, 0, 0:1]
    pstep = list(Tt.ap.ap[0])
    for di in range(3):
        for ab, c0 in ((0, 128), (1, 256)):
            src = bass.AP(Tt.ap.tensor, Tt.ap[:, di * 8, c0:c0 + 1].offset, [pstep, [257, 8], [-1, 128]])
            k.cp("dve", EB.ap[:, di, :, ab, :], src, r=[Tt], w=[EB])
    A.pop()
    P.barrier()
    import os
    STOP = int(os.environ.get("PB_STOP", "9"))
    if STOP <= 1:
        A.pop()
        return
    ONEH = A.alloc([2, 128], BF16, "ONEH")
    k.memset("pool", ONEH.ap, 0.0, w=[ONEH])
    k.memset("pool", ONEH.ap[:, 0, 0:64], 1.0, w=[ONEH])
    k.memset("pool", ONEH.ap[:, 1, 64:128], 1.0, w=[ONEH])
    gc = A.alloc([8], F32, "gcB")
    k.dma("sp", gc.ap[:, 0:4], g_att[0].rearrange("(k p) -> p k", p=128), w=[gc], slow=True)
    k.dma("sp", gc.ap[:, 4:8], g_ssm[0].rearrange("(k p) -> p k", p=128), w=[gc], slow=True)
    bgl = C.gcol(b_glu[0], 512, "bgl")
    wout = A.alloc([8, 1024], BF16, "wout")
    wglu = A.alloc([4, 512], BF16, "wglu")
    C.load_weight_bf16(wout, w_out[0], 8, 1024, gc, "wout")
    C.load_weight_bf16(wglu, w_glu[0], 4, 512, None, "wglu")
    qs = A.alloc([2, 2048], BF16, "qs")
    ks = A.alloc([2, 4096], BF16, "ks")
    acc = A.alloc([4, 2048], F32, "acc")
    dsw = A.alloc([2, 2048], F32, "dsw")
    attT = A.alloc([4, 2048], BF16, "attT")
    pTs = [A.alloc([8, 128], BF16, "pT%d" % i) for i in range(3)]
    pTbs = [T(None, "pTb%d" % i) for i in range(3)]
    vas = [A.alloc([4, 128], BF16, "va%d" % i) for i in range(4)]
    vbs = [A.alloc([4, 128], BF16, "vb%d" % i) for i in range(4)]
    dflat = dsw.ap.rearrange("p a b -> p (a b)")
    accflat = acc.ap.rearrange("p a b -> p (a b)")

    def alias_bf(i, name):
        return T(dflat[:, i * 1024:(i + 1) * 1024].bitcast(BF16).rearrange("p (a b) -> p a b", a=4), name)
    gts = [alias_bf(i, "gt%d" % i) for i in range(2)]
    sqbs = [alias_bf(2 + i, "sqb%d" % i) for i in range(2)]
    sqbs2 = [T(accflat[:, i * 1024:(i + 1) * 1024].bitcast(BF16).rearrange("p (a b) -> p a b", a=4), "sqc%d" % i) for i in range(2)]
    sTs = [A.alloc([4, 512], BF16, "sT%d" % i) for i in range(2)]
    rs5s = [A.alloc([512], F32, "rs5%d" % i) for i in range(4)]
    sigs = [A.alloc([512], BF16, "sig%d" % i) for i in range(2)]
    mixTs = [A.alloc([8, 512], BF16, "mixT%d" % i) for i in range(2)]
    xb = [A.alloc([1024], F32, "xB%d" % i) for i in range(3)]
    pend = []
    bi = 0
    for SB in range(NSB):
        tok0 = SB * 2048
        for half in range(2):
            k.dma("sp", qs.ap, qT[2 * half:2 * half + 2, :, tok0:tok0 + 2048].rearrange("f p t -> p f t"), w=[qs])
            lo, hi = tok0 - 1024, tok0 + 3072
            vlo, vhi = max(lo, 0), min(hi, S)
            if vlo > lo:
                k.memset("pool", ks.ap[:, :, 0:vlo - lo], 0.0, w=[ks])
            if vhi < hi:
                k.memset("pool", ks.ap[:, :, vhi - lo:4096], 0.0, w=[ks])
            k.dma("sp", ks.ap[:, :, vlo - lo:vhi - lo], kT[2 * half:2 * half + 2, :, vlo:vhi].rearrange("f p t -> p f t"), w=[ks])
            blocks = []
            first = True
            for di, d in enumerate(DIL):
                if d not in DSEL:
                    continue
                for r in range(d):
                    for mb in range(16 // d):
                        blocks.append((di, d, r, mb, first))
                first = False

            def emit_scores(j):
                di, d, r, mb, isfirst = blocks[j]
                q0 = r + d * 128 * mb
                va, vb = vas[j % 4], vbs[j % 4]
                rowA = 1024 + tok0 + q0 - 64 * d
                rowB = 1024 + tok0 + q0 + 64 * d
                k.dma("sp", va.ap, Vp[rowA:rowA + 127 * d + 1:d, 4 * half:4 * half + 4, :], w=[va])
                k.dma("sp", vb.ap, Vp[rowB:rowB + 127 * d + 1:d, 4 * half:4 * half + 4, :], w=[vb])
                for hl in range(4):
                    p0, hpl = 64 * (hl % 2), hl // 2
                    qa = strided(qs.ap[p0:p0 + 64, hpl, q0:q0 + 1], d, 128)
                    for ab in range(2):
                        kc = 1024 + q0 + (-64 * d if ab == 0 else 64 * d)
                        ka = strided(ks.ap[p0:p0 + 64, hpl, kc:kc + 1], d, 128)
                        pss = PS[2 * (j % 3) + hl % 2]
                        cslot = (hl // 2) * 2 + ab
                        k.mm(pss.ap[:, cslot * 128:(cslot + 1) * 128], ka, qa, True, True, r=[ks, qs], w=[pss])

            def emit_soft(j):
                di, d, r, mb, isfirst = blocks[j]
                pT, pTb = pTs[j % 3], pTbs[j % 3]
                ps0, ps1 = PS[2 * (j % 3)], PS[2 * (j % 3) + 1]
                pTf = pT.ap.rearrange("p a b -> p (a b)")
                k.act(pTf[:, 0:512], ps0.ap, AF.Exp, r=[ps0], w=[pT])
                k.act(pTf[:, 512:1024], ps1.ap, AF.Exp, r=[ps1], w=[pTb])
                for hh in range(2):
                    ebv = EB.ap[:, di, 4 * half + hh:4 * half + 4:2, :, :]
                    pv4 = pT.ap[:, hh * 4:(hh + 1) * 4, :].rearrange("p (a b) c -> p a b c", b=2)
                    tok = pT if hh == 0 else pTb
                    k.tt("dve" if hh == 0 else "pool", pv4, pv4, ebv, ALU.mult, r=[tok, EB], w=[tok])

            def emit_pv(j):
                di, d, r, mb, isfirst = blocks[j]
                q0 = r + d * 128 * mb
                pT, pTb, va, vb = pTs[j % 3], pTbs[j % 3], vas[j % 4], vbs[j % 4]
                pnd = PS[6 + j % 2]
                for hl in range(4):
                    hh, hpl = hl % 2, hl // 2
                    for ab in range(2):
                        vt = va if ab == 0 else vb
                        k.mm(pnd.ap[:, hl * 128:(hl + 1) * 128], vt.ap[:, hl, :], pT.ap[:, hh * 4 + hpl * 2 + ab, :], ab == 0, ab == 1, r=[vt, pT, pTb], w=[pnd])
                av = ap3(acc.ap[:, 0, q0:q0 + 1], [2048, 4], [d, 128])
                pv = pnd.ap.rearrange("p (a b) -> p a b", a=4)
                if isfirst:
                    k.cp("act", av, pv, r=[pnd], w=[acc])
                else:
                    k.tt("dve", av, pv, av, ALU.add, r=[pnd, acc], w=[acc])
            nb_ = len(blocks)
            emit_scores(0)
            if nb_ > 1:
                emit_scores(1)
            emit_soft(0)
            for j in range(nb_):
                if j + 2 < nb_:
                    emit_scores(j + 2)
                if j + 1 < nb_:
                    emit_soft(j + 1)
                emit_pv(j)
            for hpl in range(2):
                k.dma("sp", dsw.ap[0:64, hpl, :], acc.ap[64:128, 2 * hpl, :], r=[acc], w=[dsw])
                k.dma("sp", dsw.ap[64:128, hpl, :], acc.ap[0:64, 2 * hpl + 1, :], r=[acc], w=[dsw])
            k.act(dsw.ap, dsw.ap, AF.Ln, r=[dsw], w=[dsw])
            k.act(dsw.ap, dsw.ap, AF.Exp, scale=-1.0, r=[dsw], w=[dsw])
            for hpl in range(2):
                k.tt("dve", attT.ap[0:64, 2 * half + hpl, :], acc.ap[0:64, 2 * hpl, :], dsw.ap[0:64, hpl, :], ALU.mult, r=[acc, dsw], w=[attT])
                k.tt("pool", attT.ap[64:128, 2 * half + hpl, :], acc.ap[64:128, 2 * hpl + 1, :], dsw.ap[64:128, hpl, :], ALU.mult, r=[acc, dsw], w=[attT])
        def stageN(bb):
            t0 = tok0 + bb * 512
            mx, sq_, sT_, gt_ = mixTs[bb % 2], sqbs[bb % 2], sTs[bb % 2], gts[bb % 2]
            rsa, rss = rs5s[2 * (bb % 2)], rs5s[2 * (bb % 2) + 1]
            av = attT.ap[:, :, bb * 512:(bb + 1) * 512]
            k.dma("sp", gt_.ap, gT[:, :, t0:t0 + 512].rearrange("f p t -> p f t"), w=[gt_])
            k.act(sq_.ap, av, AF.Square, r=[attT], w=[sq_])
            for kk in range(4):
                k.mm(PS[0].ap, C.ones_b.ap, sq_.ap[:, kk, :], kk == 0, kk == 3, r=[sq_, C.ones_b], w=[PS[0]])
            k.ts("dve", rsa.ap, PS[0].ap, 1.0 / 512, ALU.mult, EPS, ALU.add, r=[PS[0]], w=[rsa])
            for j in range(4):
                sg_, pz = sigs[j % 2], PS[2 + j % 2]
                for kk in range(4):
                    k.mm(pz.ap, wglu.ap[:, kk, j * 128:(j + 1) * 128], gt_.ap[:, kk, :], kk == 0, kk == 3, r=[wglu, gt_], w=[pz])
                k.act(sg_.ap, pz.ap, AF.Sigmoid, bias=bgl.ap[:, j:j + 1], r=[pz, bgl], w=[sg_])
                k.tt("dve", sT_.ap[:, j, :], gt_.ap[:, j, :], sg_.ap, ALU.mult, r=[gt_, sg_], w=[sT_])
            sq2 = sqbs2[bb % 2]
            k.act(sq2.ap, sT_.ap, AF.Square, r=[sT_], w=[sq2])
            for kk in range(4):
                k.mm(PS[1].ap, C.ones_b.ap, sq2.ap[:, kk, :], kk == 0, kk == 3, r=[sq2, C.ones_b], w=[PS[1]])
            k.ts("dve", rss.ap, PS[1].ap, 1.0 / 512, ALU.mult, EPS, ALU.add, r=[PS[1]], w=[rss])
            for r_ in (rsa, rss):
                k.act(r_.ap, r_.ap, AF.Ln, r=[r_], w=[r_])
            for r_ in (rsa, rss):
                k.act(r_.ap, r_.ap, AF.Exp, scale=-0.5, r=[r_], w=[r_])
            k.tt("dve", mx.ap[:, 0:4, :], av, rsa.ap.unsqueeze(1).to_broadcast([128, 4, 512]), ALU.mult, r=[attT, rsa], w=[mx])
            k.tt("pool", mx.ap[:, 4:8, :], sT_.ap, rss.ap.unsqueeze(1).to_broadcast([128, 4, 512]), ALU.mult, r=[sT_, rss], w=[mx])

        def stageW(bb):
            t0 = tok0 + bb * 512
            mx = mixTs[bb % 2]
            for t in range(4):
                ti = t0 // 128 + t
                xt = xb[ti % 3]
                k.dma("sp", xt.ap, x[ti * 128:(ti + 1) * 128, :], w=[xt])
                if pend:
                    pti, pxt = pend.pop()
                    k.dma("pool", hs[pti * 128:(pti + 1) * 128, :], pxt.ap, r=[pxt])
                for h2 in range(2):
                    po = PS[4 + (2 * t + h2) % 4]
                    cs = slice(h2 * 512, (h2 + 1) * 512)
                    for kk in range(8):
                        k.mm(po.ap, mx.ap[:, kk, t * 128:(t + 1) * 128], wout.ap[:, kk, cs], kk == 0, kk == 7, r=[mx, wout], w=[po])
                    k.tt("dve", xt.ap[:, cs], po.ap, xt.ap[:, cs], ALU.add, r=[po, xt], w=[xt])
                pend.append((ti, xt))
        if STOP > 3:
            P.barrier()
            stageN(0)
            for bb in range(4):
                if bb + 1 < 4:
                    stageN(bb + 1)
                stageW(bb)
            P.barrier()
    while pend:
        pti, pxt = pend.pop()
        k.dma("pool", hs[pti * 128:(pti + 1) * 128, :], pxt.ap, r=[pxt])
    A.pop()


def norm_T(C, src_ap, ss, tm, rs, junk, a_, pst, dstT, col0, ncols=1024, r_src=(), cp_eng="act"):
    k = C.k
    nk = ncols // 128
    k.memset("pool", ss.ap, 0.0, w=[ss])
    k.act(junk.ap[:, 0:ncols], src_ap, AF.Square, accum=ss.ap, r=list(r_src) + [ss], w=[junk, ss])
    C.rms_rstd(ss, ncols, tm, rs)
    k.ts("dve", a_.ap[:, 0:ncols], src_ap, rs.ap[:, 0:1], ALU.mult, r=list(r_src) + [rs], w=[a_])
    psv = pst.ap.bitcast(BF16).rearrange("p (a b) -> p a b", a=8)
    for kk in range(nk):
        k.tr(psv[:, kk, :], a_.ap[:, kk * 128:(kk + 1) * 128], C.identb.ap, r=[a_, C.identb], w=[pst])
    k.cp(cp_eng, dstT.ap[:, 0:nk, col0:col0 + 128], psv[:, 0:nk, :], r=[pst], w=[dstT])


def phase_C(C, hs, hs2, g_mlp, w_mlp1, w_mlp2):
    k, A, PS, S = C.k, C.A, C.PS, C.S
    A.push()
    gm = C.gcol(g_mlp[0], 1024, "gmlp")
    w1 = A.alloc([8, 4096], BF16, "w1")
    w2 = A.alloc([32, 1024], BF16, "w2")
    C.load_weight_bf16(w1, w_mlp1[0], 8, 4096, gm, "w1")
    C.load_weight_bf16(w2, w_mlp2[0], 32, 1024, None, "w2")
    hb = [A.alloc([2, 1024], F32, "hc%d" % i) for i in range(3)]
    pendc = []
    junk = A.alloc([1024], F32, "junkc")
    ab = [A.alloc([1024], BF16, "abc%d" % i) for i in range(2)]
    ss = [A.alloc([1], F32, "ssc%d" % i) for i in range(2)]
    tmp = [A.alloc([1], F32, "tmpc%d" % i) for i in range(2)]
    rstd = [A.alloc([1], F32, "rstdc%d" % i) for i in range(2)]
    fT = A.alloc([8, 256], BF16, "fT")
    hidT = A.alloc([32, 256], BF16, "hidT")
    rl = [A.alloc([256], F32, "rl%d" % i) for i in range(2)]
    fTs = [fT, A.alloc([8, 256], BF16, "fT1")]

    def cnorm(blk):
        h = hb[blk % 3]
        k.dma("sp", h.ap, hs[blk * 256:(blk + 1) * 256, :].rearrange("(t p) d -> p t d", p=128), w=[h])
        for t in range(2):
            i = (2 * blk + t) % 2
            norm_T(C, h.ap[:, t, :], ss[i], tmp[i], rstd[i], junk, ab[i], PS[i], fTs[blk % 2], t * 128, r_src=[h])

    def cup(blk):
        f_ = fTs[blk % 2]
        for j in range(32):
            pz = PS[2 + j % 2]
            for kk in range(8):
                k.mm(pz.ap[:, 0:256], w1.ap[:, kk, j * 128:(j + 1) * 128], f_.ap[:, kk, :], kk == 0, kk == 7, r=[w1, f_], w=[pz])
            r_ = rl[j % 2]
            k.act(r_.ap, pz.ap[:, 0:256], AF.Relu, r=[pz], w=[r_])
            k.tt("dve" if j % 2 == 0 else "pool", hidT.ap[:, j, :], r_.ap, r_.ap, ALU.mult, r=[r_], w=[hidT])

    def cdown(blk):
        h = hb[blk % 3]
        for t in range(2):
            for half in range(2):
                po = PS[4 + (2 * t + half) % 4]
                for j in range(32):
                    k.mm(po.ap, hidT.ap[:, j, t * 128:(t + 1) * 128], w2.ap[:, j, half * 512:(half + 1) * 512], j == 0, j == 31, r=[hidT, w2], w=[po])
                k.tt("dve", h.ap[:, t, half * 512:(half + 1) * 512], po.ap, h.ap[:, t, half * 512:(half + 1) * 512], ALU.add, r=[po, h], w=[h])
        if pendc:
            pb_, ph_ = pendc.pop()
            k.dma("pool", hs2[pb_ * 256:(pb_ + 1) * 256, :].rearrange("(t p) d -> p t d", p=128), ph_.ap, r=[ph_])
        pendc.append((blk, h))
    NBC = S // 256
    cnorm(0)
    for blk in range(NBC):
        cup(blk)
        if blk + 1 < NBC:
            cnorm(blk + 1)
        cdown(blk)
    while pendc:
        pb_, ph_ = pendc.pop()
        k.dma("pool", hs2[pb_ * 256:(pb_ + 1) * 256, :].rearrange("(t p) d -> p t d", p=128), ph_.ap, r=[ph_])
    A.pop()


def phase_D(C, hs2, pin, y, g_ple, w_gate, w_proj, g_final):
    k, A, PS, S, NT = C.k, C.A, C.PS, C.S, C.NT
    A.push()
    gp = C.gcol(g_ple[0], 1024, "gple")
    wg = A.alloc([8, 1024], BF16, "wg")
    wp = A.alloc([2, 1024], BF16, "wp")
    C.load_weight_bf16(wg, w_gate[0], 8, 1024, gp, "wg")
    C.load_weight_bf16(wp, w_proj[0], 2, 1024, None, "wp")
    gfin = A.alloc([1024], F32, "gfin")
    k.dma("sp", gfin.ap, g_final.rearrange("(o d) -> o d", o=1).partition_broadcast(128), w=[gfin])
    hb = [A.alloc([1024], F32, "hd%d" % i) for i in range(5)]
    pb = [A.alloc([256], F32, "pd%d" % i) for i in range(3)]
    junk = A.alloc([1024], F32, "junkd")
    ab = [A.alloc([1024], BF16, "abd%d" % i) for i in range(3)]
    pbb = [A.alloc([256], BF16, "pbb%d" % i) for i in range(3)]
    ss = [A.alloc([1], F32, "ssd%d" % i) for i in range(5)]
    tmp = [A.alloc([1], F32, "tmpd%d" % i) for i in range(5)]
    rstd = [A.alloc([1], F32, "rstdd%d" % i) for i in range(5)]
    eT = [A.alloc([8, 128], BF16, "eT%d" % i) for i in range(3)]
    pT = [A.alloc([2, 128], BF16, "pT%d" % i) for i in range(3)]
    sg = [A.alloc([512], F32, "sg%d" % i) for i in range(2)]
    def stage1(ti):
        h, p_, e_, pT_, pb_ = hb[ti % 5], pb[ti % 3], eT[ti % 3], pT[ti % 3], pbb[ti % 3]
        i = ti % 3
        k.dma("sp", h.ap, hs2[ti * 128:(ti + 1) * 128, :], w=[h])
        k.dma("sp", p_.ap, pin[ti * 128:(ti + 1) * 128, :], w=[p_])
        norm_T(C, h.ap, ss[i], tmp[i], rstd[i], junk, ab[i], PS[ti % 2], e_, 0, r_src=[h], cp_eng="dve")
        k.cp("pool", pb_.ap, p_.ap, r=[p_], w=[pb_])
        pst = PS[2 + ti % 2]
        psv = pst.ap.bitcast(BF16).rearrange("p (a b) -> p a b", a=8)
        for kk in range(2):
            k.tr(psv[:, kk, :], pb_.ap[:, kk * 128:(kk + 1) * 128], C.identb.ap, r=[pb_, C.identb], w=[pst])
        k.cp("dve", pT_.ap, psv[:, 0:2, :], r=[pst], w=[pT_])

    def stage2a(ti):
        e_, pT_ = eT[ti % 3], pT[ti % 3]
        for half in range(2):
            pg, pp = PS[4 + half], PS[6 + half]
            cs = slice(half * 512, (half + 1) * 512)
            for kk in range(8):
                k.mm(pg.ap, e_.ap[:, kk, :], wg.ap[:, kk, cs], kk == 0, kk == 7, r=[e_, wg], w=[pg])
            for kk in range(2):
                k.mm(pp.ap, pT_.ap[:, kk, :], wp.ap[:, kk, cs], kk == 0, kk == 1, r=[pT_, wp], w=[pp])

    def stage2b(ti):
        h = hb[ti % 5]
        for half in range(2):
            pg, pp, s_ = PS[4 + half], PS[6 + half], sg[half]
            cs = slice(half * 512, (half + 1) * 512)
            k.act(s_.ap, pg.ap, AF.Sigmoid, r=[pg], w=[s_])
            k.tt("dve", s_.ap, pp.ap, s_.ap, ALU.mult, r=[pp, s_], w=[s_])
            k.tt("pool", h.ap[:, cs], h.ap[:, cs], s_.ap, ALU.add, r=[h, s_], w=[h])
        j = 3 + ti % 2
        k.memset("pool", ss[j].ap, 0.0, w=[ss[j]])
        k.act(junk2.ap, h.ap, AF.Square, accum=ss[j].ap, r=[h, ss[j]], w=[junk2, ss[j]])
        C.rms_rstd(ss[j], 1024, tmp[j], rstd[j])
        k.stt(h.ap, h.ap, rstd[j].ap[:, 0:1], gfin.ap, ALU.mult, ALU.mult, r=[h, rstd[j], gfin], w=[h])

    def store(ti):
        k.dma("pool", y[ti * 128:(ti + 1) * 128, :], hb[ti % 5].ap, r=[hb[ti % 5]])

    junk2 = A.alloc([1024], F32, "junkd2")
    stage1(0)
    stage1(1)
    for ti in range(NT):
        stage2a(ti)
        if ti + 2 < NT:
            stage1(ti + 2)
        if ti >= 1:
            store(ti - 1)
        stage2b(ti)
    store(NT - 1)
    A.pop()


_NC_CACHE = {}


def kernel(**inputs):
    S = 8192
    xs = [inputs["x_prompt"][i] for i in range(2)] + [inputs["x_sample"][i] for i in range(4)]
    ps = [inputs["p_prompt"][0, i] for i in range(2)] + [inputs["p_sample"][0, i] for i in range(4)]
    xs += [np.zeros_like(xs[0]), np.zeros_like(xs[0])]
    ps += [np.zeros_like(ps[0]), np.zeros_like(ps[0])]
    if "nc" not in _NC_CACHE:
        _NC_CACHE["nc"] = build(S)
    nc = _NC_CACHE["nc"]
    wnames = ["rel_bias", "g_mix", "w_in", "ssm_a_re", "ssm_a_im", "ssm_log_dt", "ssm_b_re", "ssm_b_im", "ssm_c_re",
              "ssm_c_im", "ssm_d", "w_glu", "b_glu", "g_att_out", "g_ssm_out", "w_out", "g_mlp", "w_mlp1", "w_mlp2",
              "g_ple", "w_ple_gate", "w_ple_proj", "g_final"]
    in_maps = []
    for c in range(8):
        m = {"x": np.ascontiguousarray(xs[c], dtype=np.float32), "p": np.ascontiguousarray(ps[c], dtype=np.float32)}
        for n in wnames:
            m[n] = np.ascontiguousarray(inputs[n], dtype=np.float32)
        in_maps.append(m)
    res = run_bass_kernel_spmd(nc, in_maps, core_ids=list(range(8)))
    outs = [np.asarray(r["y"], dtype=np.float32) for r in res.results]
    y_prompt = np.stack(outs[0:2], axis=0)
    y_sample = np.stack(outs[2:6], axis=0)
    return (y_prompt, y_sample)
```

```python
import math
import numpy as np
from contextlib import ExitStack
import concourse.bass as bass
import concourse.mybir as mybir
from concourse.bass_utils import run_bass_kernel_spmd

F32 = mybir.dt.float32
BF16 = mybir.dt.bfloat16
I32 = mybir.dt.int32
AF = mybir.ActivationFunctionType
ALU = mybir.AluOpType

ENGS = ("pe", "act", "dve", "pool", "sp")
D = 1024
EPS = 1e-6


class Buf:
    __slots__ = ("name", "last_w", "readers")

    def __init__(self, name=""):
        self.name = name
        self.last_w = None
        self.readers = []


class Op:
    __slots__ = ("eng", "fn", "reads", "writes", "is_dma", "deps", "idx", "signal", "count", "sem", "semkey")

    def __init__(self, eng, fn, reads, writes, is_dma):
        self.eng = eng
        self.fn = fn
        self.reads = reads
        self.writes = writes
        self.is_dma = is_dma
        self.deps = set()
        self.signal = False
        self.count = 0
        self.sem = None
        self.semkey = None


class Prog:
    def __init__(self, nc, n_dma_sems=40):
        self.nc = nc
        self.ops = []
        self.n_dma_sems = n_dma_sems
        self.ALL = Buf("ALL")

    def op(self, eng, fn, reads=(), writes=()):
        self._add(Op(eng, fn, tuple(reads) + (self.ALL,), tuple(writes), False))

    def dma(self, queue, fn, reads=(), writes=()):
        self._add(Op(queue, fn, tuple(reads) + (self.ALL,), tuple(writes), True))

    def barrier(self):
        self._add(Op("dve", lambda e: e.nop() if False else e.memset(self._bar[:], 0.0), (), (self.ALL,), False))

    def _add(self, o):
        o.idx = len(self.ops)
        for b in o.reads:
            if b.last_w is not None:
                o.deps.add(b.last_w)
        for b in o.writes:
            if b.last_w is not None:
                o.deps.add(b.last_w)
            for r in b.readers:
                o.deps.add(r)
        for b in o.reads:
            b.readers.append(o.idx)
        for b in o.writes:
            b.last_w = o.idx
            b.readers = []
        o.deps.discard(o.idx)
        self.ops.append(o)

    def emit(self, stack):
        nc = self.nc
        ops = self.ops
        needed = []
        for o in ops:
            nd = []
            for d in o.deps:
                y = ops[d]
                if y.is_dma or o.is_dma or y.eng != o.eng:
                    nd.append(d)
                elif o.eng != "pe":
                    if any((b in y.writes) for b in o.reads if b is not self.ALL) or \
                       any((b in y.writes or b in y.reads) for b in o.writes if b is not self.ALL):
                        nd.append(d)
            needed.append(nd)
            for d in nd:
                ops[d].signal = True
        for o in ops:
            if o.is_dma:
                o.signal = True
        eng_sem = {e: stack.enter_context(nc.semaphore("s_" + e)) for e in ENGS}
        nqs = {"sp": self.n_dma_sems - 16, "pool": 4, "act": 6, "dve": 6}
        dma_sems = {q: [stack.enter_context(nc.semaphore("d%s%d" % (q, i))) for i in range(nqs[q])] for q in nqs}
        cnt = {e: 0 for e in ENGS}
        dcnt = {q: [0] * nqs[q] for q in nqs}
        rr = {q: 0 for q in nqs}
        for o in ops:
            if o.is_dma:
                q = o.eng
                nq = nqs[q]
                k = rr[q] % nq
                rr[q] += 1
                dcnt[q][k] += 16
                o.sem, o.count, o.semkey = dma_sems[q][k], dcnt[q][k], ("d", q, k)
            elif o.signal:
                cnt[o.eng] += 1
                o.sem, o.count, o.semkey = eng_sem[o.eng], cnt[o.eng], ("e", o.eng)
        waited = {e: {} for e in ENGS}
        block = stack.enter_context(nc.Block())
        per_eng = {e: [] for e in ENGS}
        for o in ops:
            per_eng[o.eng].append(o)
        final_waits = {}
        for o in ops:
            if o.is_dma:
                final_waits[o.semkey] = (o.sem, o.count)

        def make(ename):
            def body(eng):
                w = waited[ename]
                for o in per_eng[ename]:
                    req = {}
                    for d in needed[o.idx]:
                        y = ops[d]
                        if req.get(y.semkey, (None, 0))[1] < y.count:
                            req[y.semkey] = (y.sem, y.count)
                    for key, (sem, c) in req.items():
                        if w.get(key, 0) < c:
                            eng.wait_ge(sem, c)
                            w[key] = c
                    if o.is_dma and o.count > 16 and w.get(o.semkey, 0) < o.count - 16:
                        eng.wait_ge(o.sem, o.count - 16)
                        w[o.semkey] = o.count - 16
                    ins = o.fn(eng)
                    if o.signal:
                        ins.then_inc(o.sem, 16 if o.is_dma else 1)
                if ename == "sp":
                    for key, (sem, c) in final_waits.items():
                        if w.get(key, 0) < c:
                            eng.wait_ge(sem, c)
                            w[key] = c
            return body

        block.tensor(make("pe"))
        block.scalar(make("act"))
        block.vector(make("dve"))
        block.gpsimd(make("pool"))
        block.sync(make("sp"))


class T:
    __slots__ = ("ap", "b")

    def __init__(self, ap, name=""):
        self.ap = ap
        self.b = Buf(name)


class Arena:
    def __init__(self, big, ncols):
        self.big = big
        self.ncols = ncols
        self.off = 0
        self.marks = []

    def push(self):
        self.marks.append(self.off)

    def pop(self):
        self.off = self.marks.pop()

    def alloc(self, free_shape, dtype, name=""):
        n = int(np.prod(free_shape))
        n32 = n if dtype in (F32, I32) else (n + 1) // 2
        assert self.off + n32 <= self.ncols, ("SBUF arena overflow", name, self.off, n32, self.ncols)
        v = self.big[:, self.off:self.off + n32]
        self.off += n32
        if dtype not in (F32,):
            v = v.bitcast(dtype)
            if n % 2 and dtype == BF16:
                v = v[:, 0:n]
        if len(free_shape) == 2:
            v = v.rearrange("p (a b) -> p a b", a=free_shape[0])
        elif len(free_shape) == 3:
            v = v.rearrange("p (a b c) -> p a b c", a=free_shape[0], b=free_shape[1])
        elif len(free_shape) == 4:
            v = v.rearrange("p (a b c d) -> p a b c d", a=free_shape[0], b=free_shape[1], c=free_shape[2])
        return T(v, name)


def _bs(ts_):
    return [t.b for t in ts_]


class K:
    def __init__(self, P):
        self.P = P

    def dma(self, q, out, in_, r=(), w=(), slow=False):
        if slow:
            self.P.dma(q, lambda e: e.dma_start(out=out, in_=in_, allow_slow_non_contiguous=True), _bs(r), _bs(w))
        else:
            self.P.dma(q, lambda e: e.dma_start(out=out, in_=in_), _bs(r), _bs(w))

    def mm(self, out, lhsT, rhs, start, stop, r=(), w=()):
        self.P.op("pe", lambda e: e.matmul(out, lhsT=lhsT, rhs=rhs, start=start, stop=stop), _bs(r), _bs(w))

    def tr(self, out, in_, ident, r=(), w=()):
        self.P.op("pe", lambda e: e.transpose(out, in_, ident), _bs(r), _bs(w))

    def act(self, out, in_, func, r=(), w=(), scale=None, bias=None, accum=None):
        kw = {}
        if scale is not None:
            kw["scale"] = scale
        if bias is not None:
            kw["bias"] = bias
        if accum is not None:
            kw["accum_out"] = accum
        self.P.op("act", lambda e: e.activation(out=out, in_=in_, func=func, **kw), _bs(r), _bs(w))

    def tt(self, eng, out, a, b, op, r=(), w=()):
        self.P.op(eng, lambda e: e.tensor_tensor(out=out, in0=a, in1=b, op=op), _bs(r), _bs(w))

    def ts(self, eng, out, a, s1, op0, s2=None, op1=None, r=(), w=()):
        if op1 is None:
            self.P.op(eng, lambda e: e.tensor_scalar(out=out, in0=a, scalar1=s1, scalar2=None, op0=op0), _bs(r), _bs(w))
        else:
            self.P.op(eng, lambda e: e.tensor_scalar(out=out, in0=a, scalar1=s1, scalar2=s2, op0=op0, op1=op1), _bs(r), _bs(w))

    def stt(self, out, a, s, b, op0, op1, r=(), w=()):
        self.P.op("dve", lambda e: e.scalar_tensor_tensor(out=out, in0=a, scalar=s, in1=b, op0=op0, op1=op1), _bs(r), _bs(w))

    def cp(self, eng, out, in_, r=(), w=()):
        if eng == "act":
            self.P.op("act", lambda e: e.activation(out=out, in_=in_, func=AF.Copy), _bs(r), _bs(w))
        else:
            self.P.op(eng, lambda e: e.tensor_copy(out=out, in_=in_), _bs(r), _bs(w))

    def memset(self, eng, ap, val, w=()):
        self.P.op(eng, lambda e: e.memset(ap, val), (), _bs(w))

    def recip(self, out, in_, r=(), w=()):
        self.P.op("dve", lambda e: e.reciprocal(out=out, in_=in_), _bs(r), _bs(w))

    def scan(self, out, d0, d1, init, r=(), w=()):
        self.P.op("dve", lambda e: e.tensor_tensor_scan(out=out, data0=d0, data1=d1, initial=init, op0=ALU.mult, op1=ALU.add), _bs(r), _bs(w))

    def asel(self, out, in_, pattern, cmp, fill, base, cm, r=(), w=()):
        self.P.op("pool", lambda e: e.affine_select(out=out, in_=in_, pattern=pattern, compare_op=cmp, fill=fill, base=base, channel_multiplier=cm), _bs(r), _bs(w))


def rev_ap(ap2d_last_col, n):
    return bass.AP(ap2d_last_col.tensor, ap2d_last_col.offset, [list(ap2d_last_col.ap[0]), [-1, n]])


def strided(ap_first, step, n):
    return bass.AP(ap_first.tensor, ap_first.offset, [list(ap_first.ap[0]), [step, n]])


def t5_bucket_np(rel):
    half = 16
    n = -rel
    ret = np.where(n < 0, half, 0)
    n = np.abs(n)
    max_exact = 8
    nf = np.maximum(n, 1).astype(np.float32)
    large = max_exact + (np.log(nf / np.float32(max_exact)) / np.float32(math.log(1024 / max_exact)) * (half - max_exact)).astype(np.int32)
    large = np.minimum(large, half - 1)
    return ret + np.where(n < max_exact, n, large)


class Ctx:
    pass


def build(S, debug=None):
    nc = bass.Bass("TRN2", target_bir_lowering=False)
    NT = S // 128
    C = Ctx()
    C.nc, C.S, C.NT = nc, S, NT
    din = {}

    def inp(name, shape):
        din[name] = nc.dram_tensor(name, list(shape), F32, kind="ExternalInput").ap()
        return din[name]

    x = inp("x", [S, D])
    pin = inp("p", [S, 256])
    rel_bias = inp("rel_bias", [32, 8])
    g_mix = inp("g_mix", [1, 1024])
    w_in = inp("w_in", [1, 1024, 2048])
    a_re = inp("ssm_a_re", [1, 2, 32, 64])
    a_im = inp("ssm_a_im", [1, 2, 32, 64])
    log_dt = inp("ssm_log_dt", [1, 2, 32])
    b_re = inp("ssm_b_re", [1, 2, 32, 64, 16])
    b_im = inp("ssm_b_im", [1, 2, 32, 64, 16])
    c_re = inp("ssm_c_re", [1, 2, 32, 16, 64])
    c_im = inp("ssm_c_im", [1, 2, 32, 16, 64])
    ssm_d = inp("ssm_d", [1, 32, 16])
    w_glu = inp("w_glu", [1, 512, 512])
    b_glu = inp("b_glu", [1, 512])
    g_att = inp("g_att_out", [1, 512])
    g_ssm = inp("g_ssm_out", [1, 512])
    w_out = inp("w_out", [1, 1024, 1024])
    g_mlp = inp("g_mlp", [1, 1024])
    w_mlp1 = inp("w_mlp1", [1, 1024, 4096])
    w_mlp2 = inp("w_mlp2", [1, 4096, 1024])
    g_ple = inp("g_ple", [1, 1024])
    w_gate = inp("w_ple_gate", [1, 1024, 1024])
    w_proj = inp("w_ple_proj", [1, 256, 1024])
    g_final = inp("g_final", [1024])
    y = nc.dram_tensor("y", [S, D], F32, kind="ExternalOutput").ap()

    def scratch(name, shape, dt):
        kind = "ExternalOutput" if (debug and name in debug) else "Internal"
        return nc.dram_tensor(name, list(shape), dt, kind=kind).ap()

    qT = scratch("qT", [4, 128, S], BF16)
    kT = scratch("kT", [4, 128, S], BF16)
    uT = scratch("uT", [4, 128, S], BF16)
    gT = scratch("gT", [4, 128, S], BF16)
    Vp = scratch("Vp", [S + 2048, 8, 128], BF16)
    hs = scratch("hs", [S, D], F32)
    hs2 = scratch("hs2", [S, D], F32)
    Fd = scratch("Fd", [3 * 8 * 512 + 512], F32)

    with ExitStack() as st:
        NCOL = 49100
        big = st.enter_context(nc.sbuf_tensor("big", [128, NCOL], F32))
        psb = [st.enter_context(nc.psum_tensor("ps%d" % i, [128, 512], F32)) for i in range(8)]
        P = Prog(nc)
        k = K(P)
        A = Arena(big, NCOL)
        bar = A.alloc([1], F32, "bar")
        P._bar = bar.ap
        PS = [T(psb[i][:], "ps%d" % i) for i in range(8)]
        C.P, C.k, C.A, C.PS = P, k, A, PS
        identf = A.alloc([128], F32, "identf")
        identb = A.alloc([128], BF16, "identb")
        ones_b = A.alloc([128], BF16, "ones_b")
        k.memset("pool", identf.ap, 0.0, w=[identf])
        k.asel(identf.ap, identf.ap, [[-1, 128]], ALU.not_equal, 1.0, 0, 1, r=[identf], w=[identf])
        k.cp("dve", identb.ap, identf.ap, r=[identf], w=[identb])
        k.memset("pool", ones_b.ap, 1.0, w=[ones_b])
        C.identf, C.identb, C.ones_b = identf, identb, ones_b

        def gcol(src_1d_ap, n, name):
            t = A.alloc([n // 128], F32, name)
            k.dma("sp", t.ap, src_1d_ap.rearrange("(k p) -> p k", p=128), w=[t], slow=True)
            return t
        C.gcol = gcol

        nhalf = A.alloc([1], F32, "nhalf")
        k.memset("pool", nhalf.ap, -0.5, w=[nhalf])

        def rms_rstd(ss_t, n, tmp_t, out_t, ss_ap=None, wide=False):
            if wide:
                k.ts("dve", tmp_t.ap, ss_t.ap if ss_ap is None else ss_ap, 1.0 / n, ALU.mult, EPS, ALU.add, r=[ss_t], w=[tmp_t])
                k.act(tmp_t.ap, tmp_t.ap, AF.Ln, r=[tmp_t], w=[tmp_t])
                k.act(out_t.ap, tmp_t.ap, AF.Exp, scale=-0.5, r=[tmp_t], w=[out_t])
            else:
                k.ts("pool", tmp_t.ap, ss_t.ap if ss_ap is None else ss_ap, 1.0 / n, ALU.mult, EPS, ALU.add, r=[ss_t], w=[tmp_t])
                k.tt("pool", out_t.ap, tmp_t.ap, nhalf.ap, ALU.pow, r=[tmp_t, nhalf], w=[out_t])
        C.rms_rstd = rms_rstd

        def load_weight_bf16(dst, src2d, nk, ncols, gc, tag):
            A.push()
            tmps = [A.alloc([min(ncols, 2048)], F32, tag + "tmp%d" % i) for i in range(2)]
            i = 0
            for kk in range(nk):
                for c0 in range(0, ncols, 2048):
                    cw = min(2048, ncols - c0)
                    tm = tmps[i % 2]
                    k.dma("sp", tm.ap[:, 0:cw], src2d[kk * 128:(kk + 1) * 128, c0:c0 + cw], w=[tm])
                    if i % 2 == 0:
                        if gc is None:
                            k.cp("dve", dst.ap[:, kk, c0:c0 + cw], tm.ap[:, 0:cw], r=[tm], w=[dst])
                        else:
                            k.ts("dve", dst.ap[:, kk, c0:c0 + cw], tm.ap[:, 0:cw], gc.ap[:, kk:kk + 1], ALU.mult, r=[tm, gc], w=[dst])
                    else:
                        if gc is None:
                            k.cp("act", dst.ap[:, kk, c0:c0 + cw], tm.ap[:, 0:cw], r=[tm], w=[dst])
                        else:
                            k.act(dst.ap[:, kk, c0:c0 + cw], tm.ap[:, 0:cw], AF.Copy, scale=gc.ap[:, kk:kk + 1], r=[tm, gc], w=[dst])
                    i += 1
            A.pop()
            P.barrier()
        C.load_weight_bf16 = load_weight_bf16

        phases = debug.get("phases", "ASBCD") if debug else "ASBCD"
        if "A" in phases:
            phase_A(C, x, g_mix, w_in, qT, kT, uT, Vp)
            P.barrier()
        if "S" in phases:
            phase_S(C, uT, gT, a_re, a_im, log_dt, b_re, b_im, c_re, c_im, ssm_d)
            P.barrier()
        if "B" in phases:
            phase_B(C, x, qT, kT, Vp, gT, hs, rel_bias, Fd, w_glu, b_glu, g_att, g_ssm, w_out)
            P.barrier()
        if "X" in phases:
            for i in range(S // 512):
                k.dma("sp", hs[i * 512:(i + 1) * 512, :], x[i * 512:(i + 1) * 512, :])
            P.barrier()
        if "C" in phases:
            phase_C(C, hs, hs2, g_mlp, w_mlp1, w_mlp2)
            P.barrier()
        if "D" in phases:
            phase_D(C, hs2, pin, y, g_ple, w_gate, w_proj, g_final)
        P.emit(st)
    return nc


def phase_A(C, x, g_mix, w_in, qT, kT, uT, Vp):
    k, A, PS, S, NT = C.k, C.A, C.PS, C.S, C.NT
    A.push()
    gm = C.gcol(g_mix[0], 1024, "gm")
    win = A.alloc([8, 2048], BF16, "win")
    C.load_weight_bf16(win, w_in[0], 8, 2048, gm, "win")
    xb = [A.alloc([1024], F32, "xa%d" % i) for i in range(2)]
    junk = A.alloc([1024], F32, "junk")
    ab = [A.alloc([1024], BF16, "ab%d" % i) for i in range(2)]
    ss = [A.alloc([1], F32, "ss%d" % i) for i in range(2)]
    tmp = [A.alloc([1], F32, "tmp%d" % i) for i in range(2)]
    rstd = [A.alloc([1], F32, "rstd%d" % i) for i in range(2)]
    aT = [A.alloc([8, 512], BF16, "aT%d" % i) for i in range(2)]
    zst = [A.alloc([512], BF16, "zst%d" % i) for i in range(4)]
    vst = [A.alloc([8, 128], BF16, "vst%d" % i) for i in range(2)]
    zero = A.alloc([8, 128], BF16, "zero")
    k.memset("dve", zero.ap, 0.0, w=[zero])
    for v in vst:
        k.memset("pool", v.ap, 1.0, w=[v])
    for i in range(8):
        k.dma("sp", Vp[i * 128:(i + 1) * 128], zero.ap, r=[zero])
        k.dma("sp", Vp[1024 + S + i * 128:1024 + S + (i + 1) * 128], zero.ap, r=[zero])
    zi = [0]

    def normpre(b):
        for t in range(4):
            ti = 4 * b + t
            xt, a_, s_, tm, rs = xb[ti % 2], ab4[t], ss[ti % 2], tmp[ti % 2], rstd[ti % 2]
            k.dma("sp", xt.ap, x[ti * 128:(ti + 1) * 128, :], w=[xt])
            k.memset("pool", s_.ap, 0.0, w=[s_])
            k.act(junk.ap, xt.ap, AF.Square, accum=s_.ap, r=[xt, s_], w=[junk, s_])
            C.rms_rstd(s_, 1024, tm, rs)
            k.ts("dve", a_.ap, xt.ap, rs.ap[:, 0:1], ALU.mult, r=[xt, rs], w=[a_])

    def normtr(b):
        at = aT[b % 2]
        for t in range(4):
            ti = 4 * b + t
            a_ = ab4[t]
            pst = PS[ti % 2]
            psv = pst.ap.bitcast(BF16).rearrange("p (a b) -> p a b", a=8)
            for kk in range(8):
                k.tr(psv[:, kk, :], a_.ap[:, kk * 128:(kk + 1) * 128], C.identb.ap, r=[a_, C.identb], w=[pst])
            k.cp("dve", at.ap[:, :, t * 128:(t + 1) * 128], psv, r=[pst], w=[at])

    def projstage(b):
        at = aT[b % 2]
        for (dst, f0, sc) in ((qT, 0, 0.125), (kT, 4, None), (uT, 12, None)):
            for j in range(4):
                pz = PS[2 + zi[0] % 3]
                for kk in range(8):
                    k.mm(pz.ap, win.ap[:, kk, (f0 + j) * 128:(f0 + j + 1) * 128], at.ap[:, kk, :], kk == 0, kk == 7, r=[win, at], w=[pz])
                z = zst[zi[0] % 4]
                k.act(z.ap, pz.ap, AF.Copy, scale=(sc if sc is not None else 1.0), r=[pz], w=[z])
                k.dma("act", dst[j, :, b * 512:(b + 1) * 512], z.ap, r=[z])
                zi[0] += 1
        for t in range(4):
            ti = 4 * b + t
            pv = PS[5 + ti % 3]
            for kk in range(8):
                k.mm(pv.ap, at.ap[:, kk, t * 128:(t + 1) * 128], win.ap[:, kk, 1024:1536], kk == 0, kk == 7, r=[win, at], w=[pv])
            v = vst[ti % 2]
            pvv = pv.ap.rearrange("p (a b c) -> p a b c", a=4, b=2)
            vv = v.ap.rearrange("p (a b) c -> p a b c", b=2)
            k.cp("dve", vv[:, :, 0, 0:64], pvv[:, :, 0, :], r=[pv], w=[v])
            k.cp("act", vv[:, :, 1, 64:128], pvv[:, :, 1, :], r=[pv], w=[v])
            k.dma("act", Vp[1024 + ti * 128:1024 + (ti + 1) * 128], v.ap, r=[v])
    NB = S // 512
    ab4 = ab + [A.alloc([1024], BF16, "ab%d" % i) for i in range(2, 4)]
    normpre(0)
    normtr(0)
    if NB > 1:
        normpre(1)
    for b in range(NB):
        projstage(b)
        if b + 1 < NB:
            normtr(b + 1)
        if b + 2 < NB:
            normpre(b + 2)
    A.pop()


def phase_S(C, uT, gT, a_re, a_im, log_dt, b_re, b_im, c_re, c_im, ssm_d):
    k, A, PS, S, P = C.k, C.A, C.PS, C.S, C.P
    NCH = S // 8
    CB = min(512, NCH)
    NCB = NCH // CB
    LV = int(round(math.log2(NCH)))
    TWO_PI = 2.0 * math.pi
    A.push()
    Z = A.alloc([8, 240], BF16, "Z")
    T0 = A.alloc([32, 128], BF16, "T0")
    WS = A.alloc([32, 2, 128], BF16, "WS")
    WOF = A.alloc([32, 2, 128], BF16, "WOF")
    WOB = A.alloc([32, 2, 128], BF16, "WOB")
    k.memset("pool", WOF.ap, 0.0, w=[WOF])
    k.memset("pool", WOB.ap, 0.0, w=[WOB])
    Wpow = A.alloc([LV, 32, 2], F32, "Wpow")
    rho8 = A.alloc([32], F32, "rho8")
    Eb = A.alloc([32, 2, 32], F32, "Eb")
    k.memset("pool", Z.ap, 0.0, w=[Z])
    for gi in range(8):
        k.asel(Z.ap[:, gi, 112:128], Z.ap[:, gi, 112:128], [[-1, 16]], ALU.not_equal, 1.0, -16 * gi, 1, r=[Z], w=[Z])
    A.push()

    def new(shape, name):
        return A.alloc(shape, F32, name)
    are, aim, ldt = new([32], "are"), new([32], "aim"), new([32], "ldt")
    bre, bim = new([32, 16], "bre"), new([32, 16], "bim")
    Lr, Li = new([32, 8, 16], "Lr"), new([32, 8, 16], "Li")
    Rr, Ri = new([32, 8, 16], "Rr"), new([32, 8, 16], "Ri")
    dcol = new([32], "dcol")
    for dr in range(2):
        ps_ = slice(dr * 64, dr * 64 + 64)
        k.dma("sp", are.ap[ps_, :], a_re[0, dr].rearrange("g n -> n g"), w=[are], slow=True)
        k.dma("sp", aim.ap[ps_, :], a_im[0, dr].rearrange("g n -> n g"), w=[aim], slow=True)
        k.dma("sp", ldt.ap[ps_, :], log_dt[0, dr:dr + 1, :].partition_broadcast(64), w=[ldt])
        k.dma("sp", bre.ap[ps_], b_re[0, dr].rearrange("g n h -> n g h"), w=[bre], slow=True)
        k.dma("sp", bim.ap[ps_], b_im[0, dr].rearrange("g n h -> n g h"), w=[bim], slow=True)
    for s_ in range(8):
        k.dma("sp", dcol.ap[s_ * 16:(s_ + 1) * 16, :], ssm_d[0].rearrange("g h -> h g"), w=[dcol], slow=True)
    ct = new([128], "ct")
    for (src, dst) in ((c_re, Rr), (c_im, Ri)):
        for o in range(4):
            k.dma("sp", ct.ap[:, 0:64], src[0, 0, 8 * o:8 * o + 8].rearrange("g h n -> (g h) n"), w=[ct])
            k.dma("sp", ct.ap[:, 64:128], src[0, 1, 8 * o:8 * o + 8].rearrange("g h n -> (g h) n"), w=[ct])
            k.tr(PS[0].ap[:, 0:128], ct.ap, C.identf.ap, r=[ct, C.identf], w=[PS[0]])
            k.cp("dve", dst.ap[:, 8 * o:8 * o + 8, 0, :], PS[0].ap[:, 0:128].rearrange("p (a b) -> p a b", b=16), r=[PS[0]], w=[dst])
    sc = [new([32], "sc%d" % i) for i in range(12)]
    sci = A.alloc([32], I32, "sci")

    def mul(o, a, b, eng="dve"):
        k.tt(eng, o.ap, a.ap, b.ap, ALU.mult, r=[a, b], w=[o])

    def sin_of(out, th, shift):
        t1, tf, r_, m_ = sc[8], sc[9], sc[10], sc[11]
        k.ts("dve", r_.ap, th.ap, shift, ALU.add, r=[th], w=[r_])
        k.ts("dve", t1.ap, r_.ap, 1.0 / TWO_PI, ALU.mult, r=[r_], w=[t1])
        k.cp("dve", sci.ap, t1.ap, r=[t1], w=[sci])
        k.cp("dve", tf.ap, sci.ap, r=[sci], w=[tf])
        k.stt(r_.ap, tf.ap, -TWO_PI, r_.ap, ALU.mult, ALU.add, r=[tf, r_], w=[r_])
        k.ts("dve", m_.ap, r_.ap, math.pi, ALU.is_gt, -TWO_PI, ALU.mult, r=[r_], w=[m_])
        k.tt("dve", r_.ap, r_.ap, m_.ap, ALU.add, r=[r_, m_], w=[r_])
        k.ts("dve", m_.ap, r_.ap, -math.pi, ALU.is_lt, TWO_PI, ALU.mult, r=[r_], w=[m_])
        k.tt("dve", r_.ap, r_.ap, m_.ap, ALU.add, r=[r_, m_], w=[r_])
        k.ts("dve", r_.ap, r_.ap, 3.1415925, ALU.min, -3.1415925, ALU.max, r=[r_], w=[r_])
        k.act(out.ap, r_.ap, AF.Sin, r=[r_], w=[out])
    dt_, lam, th, mag, c1, s1, abr, abi = [new([32], "p%d" % i) for i in range(8)]
    k.act(dt_.ap, ldt.ap, AF.Exp, r=[ldt], w=[dt_])
    mul(lam, are, dt_)
    mul(th, aim, dt_)
    k.act(mag.ap, lam.ap, AF.Exp, r=[lam], w=[mag])
    sin_of(s1, th, 0.0)
    sin_of(c1, th, math.pi / 2)
    mul(abr, mag, c1)
    mul(abi, mag, s1)
    inv, fre, fim, am1 = [new([32], "q%d" % i) for i in range(4)]
    mul(sc[0], are, are)
    mul(sc[1], aim, aim)
    k.tt("dve", sc[0].ap, sc[0].ap, sc[1].ap, ALU.add, r=[sc[0], sc[1]], w=[sc[0]])
    k.recip(inv.ap, sc[0].ap, r=[sc[0]], w=[inv])
    k.ts("dve", am1.ap, abr.ap, -1.0, ALU.add, r=[abr], w=[am1])
    mul(sc[0], am1, are)
    mul(sc[1], abi, aim)
    k.tt("dve", sc[0].ap, sc[0].ap, sc[1].ap, ALU.add, r=[sc[0], sc[1]], w=[sc[0]])
    mul(fre, sc[0], inv)
    mul(sc[0], abi, are)
    mul(sc[1], am1, aim)
    k.tt("dve", sc[0].ap, sc[0].ap, sc[1].ap, ALU.subtract, r=[sc[0], sc[1]], w=[sc[0]])
    mul(fim, sc[0], inv)

    def cmul(ore, oim, ar_, ai_, br_, bi_, t1, t2, r, w):
        k.tt("dve", t1, ar_, br_, ALU.mult, r=r, w=w)
        k.tt("dve", t2, ai_, bi_, ALU.mult, r=r, w=w)
        k.tt("dve", ore, t1, t2, ALU.subtract, r=r, w=w)
        k.tt("dve", t1, ar_, bi_, ALU.mult, r=r, w=w)
        k.tt("dve", t2, ai_, br_, ALU.mult, r=r, w=w)
        k.tt("dve", oim, t1, t2, ALU.add, r=r, w=w)
    X = T(None, "prepX")
    allr = [X, are, aim, bre, bim, Rr, Ri, Lr, Li, abr, abi, mag, fre, fim, lam]
    big1, big2 = new([16, 128], "big1"), new([16, 128], "big2")

    def b16(t_):
        return t_.ap.unsqueeze(2).to_broadcast([128, 32, 16])

    def b128(t_):
        return t_.ap.unsqueeze(2).to_broadcast([128, 32, 128])
    t16a = big1.ap.rearrange("p a b -> p (a b)")[:, 0:512].rearrange("p (g h) -> p g h", h=16)
    t16b = big2.ap.rearrange("p a b -> p (a b)")[:, 0:512].rearrange("p (g h) -> p g h", h=16)
    cmul(Lr.ap[:, :, 0, :], Li.ap[:, :, 0, :], b16(fre), b16(fim), bre.ap, bim.ap, t16a, t16b, allr, [X])
    aivr, aivi, im2 = new([32], "aivr"), new([32], "aivi"), new([32], "im2")
    mul(sc[0], mag, mag)
    k.recip(im2.ap, sc[0].ap, r=[sc[0]], w=[im2])
    mul(aivr, abr, im2)
    k.stt(aivi.ap, abi.ap, -1.0, im2.ap, ALU.mult, ALU.mult, r=[abi, im2], w=[aivi])
    MLr, MLi, MRr, MRi = [new([32], "M%d" % i) for i in range(4)]
    lo, hi = slice(0, 64), slice(64, 128)
    for (dst, s_lo, s_hi) in ((MLr, aivr, abr), (MLi, aivi, abi), (MRr, abr, aivr), (MRi, abi, aivi)):
        k.cp("dve", dst.ap[lo], s_lo.ap[lo], r=[s_lo], w=[dst])
        k.cp("dve", dst.ap[hi], s_hi.ap[hi], r=[s_hi], w=[dst])
    allr += [MLr, MLi, MRr, MRi]
    for s_ in range(7):
        cmul(Lr.ap[:, :, s_ + 1, :], Li.ap[:, :, s_ + 1, :], Lr.ap[:, :, s_, :], Li.ap[:, :, s_, :], b16(MLr), b16(MLi), t16a, t16b, allr, [X])
        cmul(Rr.ap[:, :, s_ + 1, :], Ri.ap[:, :, s_ + 1, :], Rr.ap[:, :, s_, :], Ri.ap[:, :, s_, :], b16(MRr), b16(MRi), t16a, t16b, allr, [X])
    pwr = [abr] + [new([32], "pwr%d" % i) for i in range(7)]
    pwi = [abi] + [new([32], "pwi%d" % i) for i in range(7)]
    for i in range(7):
        cmul(pwr[i + 1].ap, pwi[i + 1].ap, pwr[i].ap, pwi[i].ap, abr.ap, abi.ap, sc[0].ap, sc[1].ap, allr, [X])
    FSr, FSi, FOr, FOi = [new([32], "F%d" % i) for i in range(4)]
    k.cp("dve", FSr.ap[lo], pwr[6].ap[lo], r=allr, w=[X])
    k.cp("dve", FSi.ap[lo], pwi[6].ap[lo], r=allr, w=[X])
    k.memset("dve", FSr.ap[hi], 1.0, w=[X])
    k.memset("dve", FSi.ap[hi], 0.0, w=[X])
    k.cp("dve", FOr.ap[lo], abr.ap[lo], r=allr, w=[X])
    k.cp("dve", FOi.ap[lo], abi.ap[lo], r=allr, w=[X])
    k.cp("dve", FOr.ap[hi], pwr[7].ap[hi], r=allr, w=[X])
    k.cp("dve", FOi.ap[hi], pwi[7].ap[hi], r=allr, w=[X])
    Lr3 = Lr.ap.rearrange("p g s h -> p g (s h)")
    Li3 = Li.ap.rearrange("p g s h -> p g (s h)")
    Rr3 = Rr.ap.rearrange("p g s h -> p g (s h)")
    Ri3 = Ri.ap.rearrange("p g s h -> p g (s h)")
    WSr, WSi = new([16, 128], "WSr"), new([16, 128], "WSi")

    def b128h(t_, h_):
        return t_.ap[:, 16 * h_:16 * h_ + 16].unsqueeze(2).to_broadcast([128, 16, 128])
    for h_ in range(2):
        gs = slice(16 * h_, 16 * h_ + 16)
        cmul(WSr.ap, WSi.ap, Lr3[:, gs], Li3[:, gs], b128h(FSr, h_), b128h(FSi, h_), big1.ap, big2.ap, allr, [X])
        for gl in range(16):
            g = 16 * h_ + gl
            for xi, src in enumerate((WSr, WSi)):
                pst = PS[(2 * g + xi) % 4]
                k.tr(pst.ap[:, 0:128], src.ap[:, gl, :], C.identf.ap, r=[X, C.identf], w=[pst])
                k.cp("act" if xi else "dve", WS.ap[:, g, xi, :], pst.ap[:, 0:128], r=[pst], w=[WS])
        cmul(WSr.ap, WSi.ap, Rr3[:, gs], Ri3[:, gs], b128h(FOr, h_), b128h(FOi, h_), big1.ap, big2.ap, allr, [X])
        for (dst, ps_) in ((WOF, lo), (WOB, hi)):
            k.cp("dve", dst.ap[ps_, gs, 0, :], WSr.ap[ps_], r=[X], w=[dst])
            k.ts("dve", dst.ap[ps_, gs, 1, :], WSi.ap[ps_], -1.0, ALU.mult, r=[X], w=[dst])
    Mlow, Mup, onesf = new([128], "Mlow"), new([128], "Mup"), new([128], "onesf")
    k.memset("pool", onesf.ap, 1.0, w=[onesf])
    k.asel(Mlow.ap.rearrange("p (t h) -> p t h", h=16), onesf.ap.rearrange("p (t h) -> p t h", h=16), [[16, 8], [0, 16]], ALU.is_ge, 0.0, 15, -1, r=[onesf], w=[Mlow])
    k.asel(Mup.ap.rearrange("p (t h) -> p t h", h=16), onesf.ap.rearrange("p (t h) -> p t h", h=16), [[-16, 8], [0, 16]], ALU.is_ge, 0.0, 0, 1, r=[onesf], w=[Mup])
    tacc = [new([128], "tacc%d" % i) for i in range(2)]
    for g in range(32):
        pf, pf2, pb_, pb2 = PS[4], PS[5], PS[6], PS[7]
        k.mm(pf.ap[:, 0:128], Lr3[lo, g, :], Rr3[lo, g, :], True, True, r=[X], w=[pf])
        k.mm(pf2.ap[:, 0:128], Li3[lo, g, :], Ri3[lo, g, :], True, True, r=[X], w=[pf2])
        k.mm(pb_.ap[:, 0:128], Lr3[hi, g, :], Rr3[hi, g, :], True, True, r=[X], w=[pb_])
        k.mm(pb2.ap[:, 0:128], Li3[hi, g, :], Ri3[hi, g, :], True, True, r=[X], w=[pb2])
        ta, tb = tacc[0], tacc[1]
        k.cp("act", ta.ap, pf2.ap[:, 0:128], r=[pf2], w=[ta])
        k.tt("dve", ta.ap, pf.ap[:, 0:128], ta.ap, ALU.subtract, r=[pf, ta], w=[ta])
        k.tt("dve", ta.ap, ta.ap, Mlow.ap, ALU.mult, r=[ta, Mlow], w=[ta])
        k.cp("act", tb.ap, pb2.ap[:, 0:128], r=[pb2], w=[tb])
        k.tt("dve", tb.ap, pb_.ap[:, 0:128], tb.ap, ALU.subtract, r=[pb_, tb], w=[tb])
        k.tt("dve", tb.ap, tb.ap, Mup.ap, ALU.mult, r=[tb, Mup], w=[tb])
        k.tt("dve", ta.ap, ta.ap, tb.ap, ALU.add, r=[ta, tb], w=[ta])
        k.stt(T0.ap[:, g, :], C.identf.ap, dcol.ap[:, g:g + 1], ta.ap, ALU.mult, ALU.add, r=[C.identf, dcol, ta], w=[T0])
    k.act(rho8.ap, lam.ap, AF.Exp, scale=8.0, r=[lam], w=[rho8])
    ir8 = new([32], "ir8")
    k.act(ir8.ap, lam.ap, AF.Exp, scale=-8.0, r=[lam], w=[ir8])
    k.tt("dve", Wpow.ap[:, 0, :, 0], pwr[7].ap, ir8.ap, ALU.mult, r=allr + [ir8], w=[Wpow])
    k.tt("dve", Wpow.ap[:, 0, :, 1], pwi[7].ap, ir8.ap, ALU.mult, r=allr + [ir8], w=[Wpow])
    for l in range(LV - 1):
        wr, wi = Wpow.ap[:, l, :, 0], Wpow.ap[:, l, :, 1]
        k.tt("dve", sc[0].ap, wr, wr, ALU.mult, r=[Wpow], w=[sc[0]])
        k.tt("dve", sc[1].ap, wi, wi, ALU.mult, r=[Wpow], w=[sc[1]])
        k.tt("dve", Wpow.ap[:, l + 1, :, 0], sc[0].ap, sc[1].ap, ALU.subtract, r=[sc[0], sc[1]], w=[Wpow])
        k.tt("dve", sc[2].ap, wr, wi, ALU.mult, r=[Wpow], w=[sc[2]])
        k.ts("dve", Wpow.ap[:, l + 1, :, 1], sc[2].ap, 2.0, ALU.mult, r=[sc[2]], w=[Wpow])
    k.memset("dve", Eb.ap[:, :, 0, 0:1], 1.0, w=[Eb])
    k.memset("dve", Eb.ap[:, :, 1, 0:1], 0.0, w=[Eb])
    eb1 = big1.ap.rearrange("p a b -> p (a b)")[:, 0:512].rearrange("p (g m) -> p g m", m=16)
    eb2 = big2.ap.rearrange("p a b -> p (a b)")[:, 0:512].rearrange("p (g m) -> p g m", m=16)
    for l in range(5):
        m = 1 << l
        wr = Wpow.ap[:, l, :, 0].unsqueeze(2).to_broadcast([128, 32, m])
        wi = Wpow.ap[:, l, :, 1].unsqueeze(2).to_broadcast([128, 32, m])
        ec0, es0 = Eb.ap[:, :, 0, 0:m], Eb.ap[:, :, 1, 0:m]
        ec1, es1 = Eb.ap[:, :, 0, m:2 * m], Eb.ap[:, :, 1, m:2 * m]
        t1, t2 = eb1[:, :, 0:m], eb2[:, :, 0:m]
        k.tt("dve", t1, es0, wi, ALU.mult, r=[Eb, Wpow, X], w=[X])
        k.tt("dve", t2, ec0, wr, ALU.mult, r=[Eb, Wpow, X], w=[X])
        k.tt("dve", ec1, t2, t1, ALU.subtract, r=[X, Eb], w=[Eb])
        k.tt("dve", t1, ec0, wi, ALU.mult, r=[Eb, Wpow, X], w=[X])
        k.tt("dve", t2, es0, wr, ALU.mult, r=[Eb, Wpow, X], w=[X])
        k.tt("dve", es1, t2, t1, ALU.add, r=[X, Eb], w=[Eb])
    A.pop()
    P.barrier()
    uTo = A.alloc([S], BF16, "uTo")
    gTo = A.alloc([S], BF16, "gTo")
    Yg = A.alloc([8, NCH], BF16, "Yg")
    Ugs = [A.alloc([NCH], BF16, "Ug%d" % i) for i in range(2)]
    Es = [A.alloc([2, NCH], F32, "E%d" % i) for i in range(2)]
    G = A.alloc([2, NCH], F32, "G")
    Hs = A.alloc([2, NCH], F32, "Hs")
    Hbs = [A.alloc([2, NCH + 2], BF16, "Hb%d" % i) for i in range(2)]
    tq = [A.alloc([CB], F32, "tq%d" % i) for i in range(4)]
    ge = [A.alloc([CB], F32, "ge%d" % i) for i in range(3)]
    for hb_ in Hbs:
        k.memset("pool", hb_.ap, 0.0, w=[hb_])
    tr_ = [A.alloc([CB], F32, "tr%d" % i) for i in range(4)]
    Ug3 = Ugs + [A.alloc([NCH], BF16, "Ug2")]

    def egen(g):
        E = Es[g % 2]
        Ec, Es_ = E.ap[:, 0, :], E.ap[:, 1, :]
        k.cp("dve", E.ap[:, :, 0:32], Eb.ap[:, g, :, :], r=[Eb], w=[E])
        for l in range(5, LV):
            m = 1 << l
            wr, wi = Wpow.ap[:, l, g, 0:1], Wpow.ap[:, l, g, 1:2]
            t1, t2 = tq[0].ap[:, 0:m], tq[1].ap[:, 0:m]
            k.ts("dve", t1, Es_[:, 0:m], wi, ALU.mult, r=[E, Wpow], w=[tq[0]])
            k.stt(Ec[:, m:2 * m], Ec[:, 0:m], wr, t1, ALU.mult, ALU.subtract, r=[E, Wpow, tq[0]], w=[E])
            k.ts("dve", t2, Ec[:, 0:m], wi, ALU.mult, r=[E, Wpow], w=[tq[1]])
            k.stt(Es_[:, m:2 * m], Es_[:, 0:m], wr, t2, ALU.mult, ALU.add, r=[E, Wpow, tq[1]], w=[E])

    def insel(g):
        gi, Ug = g % 8, Ug3[g % 3]
        for cb in range(NCB):
            pu = PS[cb % 2]
            for s_ in range(8):
                k.mm(pu.ap[:, 0:CB], Z.ap[:, gi, 112 - 16 * s_:240 - 16 * s_], strided(uTo.ap[:, cb * CB * 8 + s_:cb * CB * 8 + s_ + 1], 8, CB), s_ == 0, s_ == 7, r=[Z, uTo], w=[pu])
            k.cp("act", Ug.ap[:, cb * CB:(cb + 1) * CB], pu.ap[:, 0:CB], r=[pu], w=[Ug])

    def summ(g):
        Ug = Ug3[g % 3]
        for cb in range(NCB):
            for xi in range(2):
                pS = PS[2 + 2 * xi + cb]
                k.mm(pS.ap[0:64, 0:CB], WS.ap[:, g, xi, 0:64], Ug.ap[:, cb * CB:(cb + 1) * CB], True, True, r=[WS, Ug], w=[pS])
                k.mm(pS.ap[64:128, 0:CB], WS.ap[:, g, xi, 64:128], rev_ap(Ug.ap[:, NCH - 1 - cb * CB:NCH - cb * CB], CB), True, True, r=[WS, Ug], w=[pS])

    def demod(g):
        E = Es[g % 2]
        Ec, Es_ = E.ap[:, 0, :], E.ap[:, 1, :]
        for cb in range(NCB):
            cs = slice(cb * CB, (cb + 1) * CB)
            pr, pi_ = PS[2 + cb], PS[4 + cb]
            k.tt("dve", tq[0].ap, pr.ap[:, 0:CB], Ec[:, cs], ALU.mult, r=[pr, E], w=[tq[0]])
            k.tt("dve", tq[1].ap, pi_.ap[:, 0:CB], Es_[:, cs], ALU.mult, r=[pi_, E], w=[tq[1]])
            k.tt("dve", G.ap[:, 0, cs], tq[0].ap, tq[1].ap, ALU.add, r=[tq[0], tq[1]], w=[G])
            k.tt("dve", tq[2].ap, pi_.ap[:, 0:CB], Ec[:, cs], ALU.mult, r=[pi_, E], w=[tq[2]])
            k.tt("dve", tq[3].ap, pr.ap[:, 0:CB], Es_[:, cs], ALU.mult, r=[pr, E], w=[tq[3]])
            k.tt("dve", G.ap[:, 1, cs], tq[2].ap, tq[3].ap, ALU.subtract, r=[tq[2], tq[3]], w=[G])

    def scan_(g):
        dec = rho8.ap[:, g:g + 1].to_broadcast([128, NCH])
        k.scan(Hs.ap[:, 0, :], dec, G.ap[:, 0, :], 0.0, r=[G, rho8], w=[Hs])
        k.scan(Hs.ap[:, 1, :], dec, G.ap[:, 1, :], 0.0, r=[G, rho8], w=[Hs])

    def remod(g):
        E, Hb = Es[g % 2], Hbs[g % 2]
        Ec, Es_ = E.ap[:, 0, :], E.ap[:, 1, :]
        for cb in range(NCB):
            cs = slice(cb * CB, (cb + 1) * CB)
            co = slice(1 + cb * CB, 1 + (cb + 1) * CB)
            k.tt("dve", tr_[0].ap, Hs.ap[:, 0, cs], Ec[:, cs], ALU.mult, r=[Hs, E], w=[tr_[0]])
            k.tt("dve", tr_[1].ap, Hs.ap[:, 1, cs], Es_[:, cs], ALU.mult, r=[Hs, E], w=[tr_[1]])
            k.tt("dve", Hb.ap[:, 0, co], tr_[0].ap, tr_[1].ap, ALU.subtract, r=[tr_[0], tr_[1]], w=[Hb])
            k.tt("dve", tr_[2].ap, Hs.ap[:, 1, cs], Ec[:, cs], ALU.mult, r=[Hs, E], w=[tr_[2]])
            k.tt("dve", tr_[3].ap, Hs.ap[:, 0, cs], Es_[:, cs], ALU.mult, r=[Hs, E], w=[tr_[3]])
            k.tt("dve", Hb.ap[:, 1, co], tr_[2].ap, tr_[3].ap, ALU.add, r=[tr_[2], tr_[3]], w=[Hb])

    def outs(g):
        gi, Ug, Hb = g % 8, Ug3[g % 3], Hbs[g % 2]
        for cb in range(NCB):
            cs = slice(cb * CB, (cb + 1) * CB)
            py = PS[6 + cb % 2]
            k.mm(py.ap[:, 0:CB], T0.ap[:, g, :], Ug.ap[:, cs], True, False, r=[T0, Ug], w=[py])
            for xi in range(2):
                k.mm(py.ap[:, 0:CB], WOF.ap[:, g, xi, :], Hb.ap[:, xi, cb * CB:(cb + 1) * CB], False, False, r=[WOF, Hb], w=[py])
            for xi in range(2):
                st_ = NCH - 1 - cb * CB
                k.mm(py.ap[:, 0:CB], WOB.ap[:, g, xi, :], rev_ap(Hb.ap[:, xi, st_:st_ + 1], CB), False, xi == 1, r=[WOB, Hb], w=[py])
            k.cp("act", Yg.ap[:, gi, cs], py.ap[:, 0:CB], r=[py], w=[Yg])

    for o in range(4):
        k.dma("sp", uTo.ap, uT[o], w=[uTo])
        g0 = 8 * o
        insel(g0)
        insel(g0 + 1)
        summ(g0)
        egen(g0)
        for gi in range(8):
            g = g0 + gi
            demod(g)
            if gi + 1 < 8:
                summ(g + 1)
            scan_(g)
            if gi + 1 < 8:
                egen(g + 1)
            remod(g)
            if gi + 2 < 8:
                insel(g + 2)
            outs(g)
        for t in range(8):
            for cb in range(NCB):
                po = PS[(t * NCB + cb) % 2]
                for gi in range(8):
                    k.mm(po.ap[:, 0:CB], Z.ap[:, t, 112 - 16 * gi:240 - 16 * gi], Yg.ap[:, gi, cb * CB:(cb + 1) * CB], gi == 0, gi == 7, r=[Z, Yg], w=[po])
                yv = po.ap[:, 0:CB]
                k.act(ge[0].ap, yv, AF.Square, r=[po], w=[ge[0]])
                k.ts("dve", ge[0].ap, ge[0].ap, 0.044715, ALU.mult, 1.0, ALU.add, r=[ge[0]], w=[ge[0]])
                k.tt("dve", ge[1].ap, ge[0].ap, yv, ALU.mult, r=[ge[0], po], w=[ge[1]])
                k.act(ge[2].ap, ge[1].ap, AF.Sigmoid, scale=1.5957691216057308, r=[ge[1]], w=[ge[2]])
                k.tt("dve", strided(gTo.ap[:, cb * CB * 8 + t:cb * CB * 8 + t + 1], 8, CB), yv, ge[2].ap, ALU.mult, r=[po, ge[2]], w=[gTo])
        k.dma("sp", gT[o], gTo.ap, r=[gTo])
    A.pop()


def ap3(base, d1, d2):
    return bass.AP(base.tensor, base.offset, [list(base.ap[0]), list(d1), list(d2)])


def phase_B(C, x, qT, kT, Vp, gT, hs, rel_bias, Fd, w_glu, b_glu, g_att, g_ssm, w_out):
    k, A, PS, S, P = C.k, C.A, C.PS, C.S, C.P
    NSB = S // 2048
    import os
    DIL = (1, 4, 16)
    DSEL = [int(v) for v in os.environ.get('PB_DIL', '1,4,16').split(',')]
    A.push()
    EB = A.alloc([3, 8, 2, 128], BF16, "EB")
    A.push()
    rb0 = A.alloc([256], F32, "rb0")
    F0 = A.alloc([3, 8, 512], F32, "F0")
    Tt = A.alloc([24, 257], F32, "Tt")
    k.dma("sp", rb0.ap, rel_bias.rearrange("(o b) h -> o (b h)", o=1).partition_broadcast(128), w=[rb0])
    k.act(rb0.ap, rb0.ap, AF.Exp, r=[rb0], w=[rb0])
    k.memset("dve", F0.ap, 0.0, w=[F0])
    rbv = rb0.ap.rearrange("p (b h) -> p b h", h=8)
    for di, d in enumerate(DIL):
        js = np.arange(-64, 65)
        bk = t5_bucket_np(js * d)
        s0 = 0
        while s0 < len(js):
            e0 = s0
            while e0 + 1 < len(js) and bk[e0 + 1] == bk[s0]:
                e0 += 1
            ln = e0 - s0 + 1
            x0 = int(js[s0]) + 192
            bb = int(bk[s0])
            for h in range(8):
                k.ts("dve", F0.ap[:, di, h, x0:x0 + ln], F0.ap[:, di, h, x0:x0 + ln], rbv[:, bb, h:h + 1], ALU.add, r=[rb0, F0], w=[F0])
            s0 = e0 + 1
    fdt = T(None, "Fd")
    for di in range(3):
        k.dma("sp", Fd[di * 4096:(di + 1) * 4096].rearrange("(o n) -> o n", o=1), F0.ap[0:1, di].rearrange("p b c -> p (b c)"), r=[F0], w=[fdt])
    for di in range(3):
        for h in range(8):
            j = di * 8 + h
            k.dma("sp", Tt.ap[:, j, :], bass.AP(Fd.tensor, j * 512, [[1, 128], [1, 257]]), r=[fdt], w=[Tt])
    ttb = Tt.ap[:, 0, 0:1]
    pstep = list(Tt.ap.ap[0])
    for di in range(3):
        for ab, c0 in ((0, 128), (1, 256)):
            src = bass.AP(Tt.ap.tensor, Tt.ap[:, di * 8, c0:c0 + 1].offset, [pstep, [257, 8], [-1, 128]])
            k.cp("dve", EB.ap[:, di, :, ab, :], src, r=[Tt], w=[EB])
    A.pop()
    P.barrier()
    import os
    STOP = int(os.environ.get("PB_STOP", "9"))
    if STOP <= 1:
        A.pop()
        return
    ONEH = A.alloc([2, 128], BF16, "ONEH")
    k.memset("pool", ONEH.ap, 0.0, w=[ONEH])
    k.memset("pool", ONEH.ap[:, 0, 0:64], 1.0, w=[ONEH])
    k.memset("pool", ONEH.ap[:, 1, 64:128], 1.0, w=[ONEH])
    gc = A.alloc([8], F32, "gcB")
    k.dma("sp", gc.ap[:, 0:4], g_att[0].rearrange("(k p) -> p k", p=128), w=[gc], slow=True)
    k.dma("sp", gc.ap[:, 4:8], g_ssm[0].rearrange("(k p) -> p k", p=128), w=[gc], slow=True)
    bgl = C.gcol(b_glu[0], 512, "bgl")
    wout = A.alloc([8, 1024], BF16, "wout")
    wglu = A.alloc([4, 512], BF16, "wglu")
    C.load_weight_bf16(wout, w_out[0], 8, 1024, gc, "wout")
    C.load_weight_bf16(wglu, w_glu[0], 4, 512, None, "wglu")
    qs = A.alloc([2, 2048], BF16, "qs")
    ks = A.alloc([2, 4096], BF16, "ks")
    acc = A.alloc([4, 2048], F32, "acc")
    dsw = A.alloc([2, 2048], F32, "dsw")
    attT = A.alloc([4, 2048], BF16, "attT")
    pTs = [A.alloc([8, 128], BF16, "pT%d" % i) for i in range(3)]
    pTbs = [T(None, "pTb%d" % i) for i in range(3)]
    vas = [A.alloc([4, 128], BF16, "va%d" % i) for i in range(4)]
    vbs = [A.alloc([4, 128], BF16, "vb%d" % i) for i in range(4)]
    dflat = dsw.ap.rearrange("p a b -> p (a b)")
    accflat = acc.ap.rearrange("p a b -> p (a b)")

    def alias_bf(i, name):
        return T(dflat[:, i * 1024:(i + 1) * 1024].bitcast(BF16).rearrange("p (a b) -> p a b", a=4), name)
    gts = [alias_bf(i, "gt%d" % i) for i in range(2)]
    sqbs = [alias_bf(2 + i, "sqb%d" % i) for i in range(2)]
    sqbs2 = [T(accflat[:, i * 1024:(i + 1) * 1024].bitcast(BF16).rearrange("p (a b) -> p a b", a=4), "sqc%d" % i) for i in range(2)]
    sTs = [A.alloc([4, 512], BF16, "sT%d" % i) for i in range(2)]
    rs5s = [A.alloc([512], F32, "rs5%d" % i) for i in range(4)]
    sigs = [A.alloc([512], BF16, "sig%d" % i) for i in range(2)]
    mixTs = [A.alloc([8, 512], BF16, "mixT%d" % i) for i in range(2)]
    xb = [A.alloc([1024], F32, "xB%d" % i) for i in range(3)]
    pend = []
    bi = 0
    for SB in range(NSB):
        tok0 = SB * 2048
        for half in range(2):
            k.dma("sp", qs.ap, qT[2 * half:2 * half + 2, :, tok0:tok0 + 2048].rearrange("f p t -> p f t"), w=[qs])
            lo, hi = tok0 - 1024, tok0 + 3072
            vlo, vhi = max(lo, 0), min(hi, S)
            if vlo > lo:
                k.memset("pool", ks.ap[:, :, 0:vlo - lo], 0.0, w=[ks])
            if vhi < hi:
                k.memset("pool", ks.ap[:, :, vhi - lo:4096], 0.0, w=[ks])
            k.dma("sp", ks.ap[:, :, vlo - lo:vhi - lo], kT[2 * half:2 * half + 2, :, vlo:vhi].rearrange("f p t -> p f t"), w=[ks])
            blocks = []
            first = True
            for di, d in enumerate(DIL):
                if d not in DSEL:
                    continue
                for r in range(d):
                    for mb in range(16 // d):
                        blocks.append((di, d, r, mb, first))
                first = False

            def emit_scores(j):
                di, d, r, mb, isfirst = blocks[j]
                q0 = r + d * 128 * mb
                va, vb = vas[j % 4], vbs[j % 4]
                rowA = 1024 + tok0 + q0 - 64 * d
                rowB = 1024 + tok0 + q0 + 64 * d
                k.dma("sp", va.ap, Vp[rowA:rowA + 127 * d + 1:d, 4 * half:4 * half + 4, :], w=[va])
                k.dma("sp", vb.ap, Vp[rowB:rowB + 127 * d + 1:d, 4 * half:4 * half + 4, :], w=[vb])
                for hl in range(4):
                    p0, hpl = 64 * (hl % 2), hl // 2
                    qa = strided(qs.ap[p0:p0 + 64, hpl, q0:q0 + 1], d, 128)
                    for ab in range(2):
                        kc = 1024 + q0 + (-64 * d if ab == 0 else 64 * d)
                        ka = strided(ks.ap[p0:p0 + 64, hpl, kc:kc + 1], d, 128)
                        pss = PS[2 * (j % 3) + hl % 2]
                        cslot = (hl // 2) * 2 + ab
                        k.mm(pss.ap[:, cslot * 128:(cslot + 1) * 128], ka, qa, True, True, r=[ks, qs], w=[pss])

            def emit_soft(j):
                di, d, r, mb, isfirst = blocks[j]
                pT, pTb = pTs[j % 3], pTbs[j % 3]
                ps0, ps1 = PS[2 * (j % 3)], PS[2 * (j % 3) + 1]
                pTf = pT.ap.rearrange("p a b -> p (a b)")
                k.act(pTf[:, 0:512], ps0.ap, AF.Exp, r=[ps0], w=[pT])
                k.act(pTf[:, 512:1024], ps1.ap, AF.Exp, r=[ps1], w=[pTb])
                for hh in range(2):
                    ebv = EB.ap[:, di, 4 * half + hh:4 * half + 4:2, :, :]
                    pv4 = pT.ap[:, hh * 4:(hh + 1) * 4, :].rearrange("p (a b) c -> p a b c", b=2)
                    tok = pT if hh == 0 else pTb
                    k.tt("dve" if hh == 0 else "pool", pv4, pv4, ebv, ALU.mult, r=[tok, EB], w=[tok])

            def emit_pv(j):
                di, d, r, mb, isfirst = blocks[j]
                q0 = r + d * 128 * mb
                pT, pTb, va, vb = pTs[j % 3], pTbs[j % 3], vas[j % 4], vbs[j % 4]
                pnd = PS[6 + j % 2]
                for hl in range(4):
                    hh, hpl = hl % 2, hl // 2
                    for ab in range(2):
                        vt = va if ab == 0 else vb
                        k.mm(pnd.ap[:, hl * 128:(hl + 1) * 128], vt.ap[:, hl, :], pT.ap[:, hh * 4 + hpl * 2 + ab, :], ab == 0, ab == 1, r=[vt, pT, pTb], w=[pnd])
                av = ap3(acc.ap[:, 0, q0:q0 + 1], [2048, 4], [d, 128])
                pv = pnd.ap.rearrange("p (a b) -> p a b", a=4)
                if isfirst:
                    k.cp("act", av, pv, r=[pnd], w=[acc])
                else:
                    k.tt("dve", av, pv, av, ALU.add, r=[pnd, acc], w=[acc])
            nb_ = len(blocks)
            emit_scores(0)
            if nb_ > 1:
                emit_scores(1)
            emit_soft(0)
            for j in range(nb_):
                if j + 2 < nb_:
                    emit_scores(j + 2)
                if j + 1 < nb_:
                    emit_soft(j + 1)
                emit_pv(j)
            for hpl in range(2):
                k.dma("sp", dsw.ap[0:64, hpl, :], acc.ap[64:128, 2 * hpl, :], r=[acc], w=[dsw])
                k.dma("sp", dsw.ap[64:128, hpl, :], acc.ap[0:64, 2 * hpl + 1, :], r=[acc], w=[dsw])
            k.act(dsw.ap, dsw.ap, AF.Ln, r=[dsw], w=[dsw])
            k.act(dsw.ap, dsw.ap, AF.Exp, scale=-1.0, r=[dsw], w=[dsw])
            for hpl in range(2):
                k.tt("dve", attT.ap[0:64, 2 * half + hpl, :], acc.ap[0:64, 2 * hpl, :], dsw.ap[0:64, hpl, :], ALU.mult, r=[acc, dsw], w=[attT])
                k.tt("pool", attT.ap[64:128, 2 * half + hpl, :], acc.ap[64:128, 2 * hpl + 1, :], dsw.ap[64:128, hpl, :], ALU.mult, r=[acc, dsw], w=[attT])
        def stageN(bb):
            t0 = tok0 + bb * 512
            mx, sq_, sT_, gt_ = mixTs[bb % 2], sqbs[bb % 2], sTs[bb % 2], gts[bb % 2]
            rsa, rss = rs5s[2 * (bb % 2)], rs5s[2 * (bb % 2) + 1]
            av = attT.ap[:, :, bb * 512:(bb + 1) * 512]
            k.dma("sp", gt_.ap, gT[:, :, t0:t0 + 512].rearrange("f p t -> p f t"), w=[gt_])
            k.act(sq_.ap, av, AF.Square, r=[attT], w=[sq_])
            for kk in range(4):
                k.mm(PS[0].ap, C.ones_b.ap, sq_.ap[:, kk, :], kk == 0, kk == 3, r=[sq_, C.ones_b], w=[PS[0]])
            k.ts("dve", rsa.ap, PS[0].ap, 1.0 / 512, ALU.mult, EPS, ALU.add, r=[PS[0]], w=[rsa])
            for j in range(4):
                sg_, pz = sigs[j % 2], PS[2 + j % 2]
                for kk in range(4):
                    k.mm(pz.ap, wglu.ap[:, kk, j * 128:(j + 1) * 128], gt_.ap[:, kk, :], kk == 0, kk == 3, r=[wglu, gt_], w=[pz])
                k.act(sg_.ap, pz.ap, AF.Sigmoid, bias=bgl.ap[:, j:j + 1], r=[pz, bgl], w=[sg_])
                k.tt("dve", sT_.ap[:, j, :], gt_.ap[:, j, :], sg_.ap, ALU.mult, r=[gt_, sg_], w=[sT_])
            sq2 = sqbs2[bb % 2]
            k.act(sq2.ap, sT_.ap, AF.Square, r=[sT_], w=[sq2])
            for kk in range(4):
                k.mm(PS[1].ap, C.ones_b.ap, sq2.ap[:, kk, :], kk == 0, kk == 3, r=[sq2, C.ones_b], w=[PS[1]])
            k.ts("dve", rss.ap, PS[1].ap, 1.0 / 512, ALU.mult, EPS, ALU.add, r=[PS[1]], w=[rss])
            for r_ in (rsa, rss):
                k.act(r_.ap, r_.ap, AF.Ln, r=[r_], w=[r_])
            for r_ in (rsa, rss):
                k.act(r_.ap, r_.ap, AF.Exp, scale=-0.5, r=[r_], w=[r_])
            k.tt("dve", mx.ap[:, 0:4, :], av, rsa.ap.unsqueeze(1).to_broadcast([128, 4, 512]), ALU.mult, r=[attT, rsa], w=[mx])
            k.tt("pool", mx.ap[:, 4:8, :], sT_.ap, rss.ap.unsqueeze(1).to_broadcast([128, 4, 512]), ALU.mult, r=[sT_, rss], w=[mx])

        def stageW(bb):
            t0 = tok0 + bb * 512
            mx = mixTs[bb % 2]
            for t in range(4):
                ti = t0 // 128 + t
                xt = xb[ti % 3]
                k.dma("sp", xt.ap, x[ti * 128:(ti + 1) * 128, :], w=[xt])
                if pend:
                    pti, pxt = pend.pop()
                    k.dma("pool", hs[pti * 128:(pti + 1) * 128, :], pxt.ap, r=[pxt])
                for h2 in range(2):
                    po = PS[4 + (2 * t + h2) % 4]
                    cs = slice(h2 * 512, (h2 + 1) * 512)
                    for kk in range(8):
                        k.mm(po.ap, mx.ap[:, kk, t * 128:(t + 1) * 128], wout.ap[:, kk, cs], kk == 0, kk == 7, r=[mx, wout], w=[po])
                    k.tt("dve", xt.ap[:, cs], po.ap, xt.ap[:, cs], ALU.add, r=[po, xt], w=[xt])
                pend.append((ti, xt))
        if STOP > 3:
            P.barrier()
            stageN(0)
            for bb in range(4):
                if bb + 1 < 4:
                    stageN(bb + 1)
                stageW(bb)
            P.barrier()
    while pend:
        pti, pxt = pend.pop()
        k.dma("pool", hs[pti * 128:(pti + 1) * 128, :], pxt.ap, r=[pxt])
    A.pop()


def norm_T(C, src_ap, ss, tm, rs, junk, a_, pst, dstT, col0, ncols=1024, r_src=(), cp_eng="act"):
    k = C.k
    nk = ncols // 128
    k.memset("pool", ss.ap, 0.0, w=[ss])
    k.act(junk.ap[:, 0:ncols], src_ap, AF.Square, accum=ss.ap, r=list(r_src) + [ss], w=[junk, ss])
    C.rms_rstd(ss, ncols, tm, rs)
    k.ts("dve", a_.ap[:, 0:ncols], src_ap, rs.ap[:, 0:1], ALU.mult, r=list(r_src) + [rs], w=[a_])
    psv = pst.ap.bitcast(BF16).rearrange("p (a b) -> p a b", a=8)
    for kk in range(nk):
        k.tr(psv[:, kk, :], a_.ap[:, kk * 128:(kk + 1) * 128], C.identb.ap, r=[a_, C.identb], w=[pst])
    k.cp(cp_eng, dstT.ap[:, 0:nk, col0:col0 + 128], psv[:, 0:nk, :], r=[pst], w=[dstT])


def phase_C(C, hs, hs2, g_mlp, w_mlp1, w_mlp2):
    k, A, PS, S = C.k, C.A, C.PS, C.S
    A.push()
    gm = C.gcol(g_mlp[0], 1024, "gmlp")
    w1 = A.alloc([8, 4096], BF16, "w1")
    w2 = A.alloc([32, 1024], BF16, "w2")
    C.load_weight_bf16(w1, w_mlp1[0], 8, 4096, gm, "w1")
    C.load_weight_bf16(w2, w_mlp2[0], 32, 1024, None, "w2")
    hb = [A.alloc([2, 1024], F32, "hc%d" % i) for i in range(3)]
    pendc = []
    junk = A.alloc([1024], F32, "junkc")
    ab = [A.alloc([1024], BF16, "abc%d" % i) for i in range(2)]
    ss = [A.alloc([1], F32, "ssc%d" % i) for i in range(2)]
    tmp = [A.alloc([1], F32, "tmpc%d" % i) for i in range(2)]
    rstd = [A.alloc([1], F32, "rstdc%d" % i) for i in range(2)]
    fT = A.alloc([8, 256], BF16, "fT")
    hidT = A.alloc([32, 256], BF16, "hidT")
    rl = [A.alloc([256], F32, "rl%d" % i) for i in range(2)]
    fTs = [fT, A.alloc([8, 256], BF16, "fT1")]

    def cpre(blk):
        h = hb[blk % 3]
        k.dma("sp", h.ap, hs[blk * 256:(blk + 1) * 256, :].rearrange("(t p) d -> p t d", p=128), w=[h])
        for t in range(2):
            k.memset("pool", ss[t].ap, 0.0, w=[ss[t]])
            k.act(junk.ap, h.ap[:, t, :], AF.Square, accum=ss[t].ap, r=[h, ss[t]], w=[junk, ss[t]])
            C.rms_rstd(ss[t], 1024, tmp[t], rstd[t])
            k.ts("dve", ab[t].ap, h.ap[:, t, :], rstd[t].ap[:, 0:1], ALU.mult, r=[h, rstd[t]], w=[ab[t]])

    def ctr(blk):
        f_ = fTs[blk % 2]
        for t in range(2):
            pst = PS[t]
            psv = pst.ap.bitcast(BF16).rearrange("p (a b) -> p a b", a=8)
            for kk in range(8):
                k.tr(psv[:, kk, :], ab[t].ap[:, kk * 128:(kk + 1) * 128], C.identb.ap, r=[ab[t], C.identb], w=[pst])
            k.cp("act", f_.ap[:, :, t * 128:(t + 1) * 128], psv, r=[pst], w=[f_])

    def cup(blk):
        f_ = fTs[blk % 2]
        for j in range(32):
            pz = PS[2 + j % 2]
            for kk in range(8):
                k.mm(pz.ap[:, 0:256], w1.ap[:, kk, j * 128:(j + 1) * 128], f_.ap[:, kk, :], kk == 0, kk == 7, r=[w1, f_], w=[pz])
            r_ = rl[j % 2]
            k.act(r_.ap, pz.ap[:, 0:256], AF.Relu, r=[pz], w=[r_])
            k.tt("dve" if j % 2 == 0 else "pool", hidT.ap[:, j, :], r_.ap, r_.ap, ALU.mult, r=[r_], w=[hidT])

    def cdown(blk):
        h = hb[blk % 3]
        for t in range(2):
            for half in range(2):
                po = PS[4 + (2 * t + half) % 4]
                for j in range(32):
                    k.mm(po.ap, hidT.ap[:, j, t * 128:(t + 1) * 128], w2.ap[:, j, half * 512:(half + 1) * 512], j == 0, j == 31, r=[hidT, w2], w=[po])
                k.tt("dve", h.ap[:, t, half * 512:(half + 1) * 512], po.ap, h.ap[:, t, half * 512:(half + 1) * 512], ALU.add, r=[po, h], w=[h])
        if pendc:
            pb_, ph_ = pendc.pop()
            k.dma("pool", hs2[pb_ * 256:(pb_ + 1) * 256, :].rearrange("(t p) d -> p t d", p=128), ph_.ap, r=[ph_])
        pendc.append((blk, h))
    NBC = S // 256
    cpre(0)
    ctr(0)
    if NBC > 1:
        cpre(1)
    for blk in range(NBC):
        cup(blk)
        if blk + 1 < NBC:
            ctr(blk + 1)
        cdown(blk)
        if blk + 2 < NBC:
            cpre(blk + 2)
    while pendc:
        pb_, ph_ = pendc.pop()
        k.dma("pool", hs2[pb_ * 256:(pb_ + 1) * 256, :].rearrange("(t p) d -> p t d", p=128), ph_.ap, r=[ph_])
    A.pop()


def phase_D(C, hs2, pin, y, g_ple, w_gate, w_proj, g_final):
    k, A, PS, S, NT = C.k, C.A, C.PS, C.S, C.NT
    A.push()
    gp = C.gcol(g_ple[0], 1024, "gple")
    wg = A.alloc([8, 1024], BF16, "wg")
    wp = A.alloc([2, 1024], BF16, "wp")
    C.load_weight_bf16(wg, w_gate[0], 8, 1024, gp, "wg")
    C.load_weight_bf16(wp, w_proj[0], 2, 1024, None, "wp")
    gfin = A.alloc([1024], F32, "gfin")
    k.dma("sp", gfin.ap, g_final.rearrange("(o d) -> o d", o=1).partition_broadcast(128), w=[gfin])
    hb = [A.alloc([1024], F32, "hd%d" % i) for i in range(5)]
    pb = [A.alloc([256], F32, "pd%d" % i) for i in range(3)]
    junk = A.alloc([1024], F32, "junkd")
    ab = [A.alloc([1024], BF16, "abd%d" % i) for i in range(3)]
    pbb = [A.alloc([256], BF16, "pbb%d" % i) for i in range(3)]
    ss = [A.alloc([1], F32, "ssd%d" % i) for i in range(5)]
    tmp = [A.alloc([1], F32, "tmpd%d" % i) for i in range(5)]
    rstd = [A.alloc([1], F32, "rstdd%d" % i) for i in range(5)]
    eT = [A.alloc([8, 128], BF16, "eT%d" % i) for i in range(3)]
    pT = [A.alloc([2, 128], BF16, "pT%d" % i) for i in range(3)]
    sg = [A.alloc([512], F32, "sg%d" % i) for i in range(2)]
    def stage1(ti):
        h, p_, e_, pT_, pb_ = hb[ti % 5], pb[ti % 3], eT[ti % 3], pT[ti % 3], pbb[ti % 3]
        i = ti % 3
        k.dma("sp", h.ap, hs2[ti * 128:(ti + 1) * 128, :], w=[h])
        k.dma("sp", p_.ap, pin[ti * 128:(ti + 1) * 128, :], w=[p_])
        norm_T(C, h.ap, ss[i], tmp[i], rstd[i], junk, ab[i], PS[ti % 2], e_, 0, r_src=[h], cp_eng="dve")
        k.cp("pool", pb_.ap, p_.ap, r=[p_], w=[pb_])
        pst = PS[2 + ti % 2]
        psv = pst.ap.bitcast(BF16).rearrange("p (a b) -> p a b", a=8)
        for kk in range(2):
            k.tr(psv[:, kk, :], pb_.ap[:, kk * 128:(kk + 1) * 128], C.identb.ap, r=[pb_, C.identb], w=[pst])
        k.cp("dve", pT_.ap, psv[:, 0:2, :], r=[pst], w=[pT_])

    def stage2a(ti):
        e_, pT_ = eT[ti % 3], pT[ti % 3]
        for half in range(2):
            pg, pp = PS[4 + half], PS[6 + half]
            cs = slice(half * 512, (half + 1) * 512)
            for kk in range(8):
                k.mm(pg.ap, e_.ap[:, kk, :], wg.ap[:, kk, cs], kk == 0, kk == 7, r=[e_, wg], w=[pg])
            for kk in range(2):
                k.mm(pp.ap, pT_.ap[:, kk, :], wp.ap[:, kk, cs], kk == 0, kk == 1, r=[pT_, wp], w=[pp])

    def stage2b(ti):
        h = hb[ti % 5]
        for half in range(2):
            pg, pp, s_ = PS[4 + half], PS[6 + half], sg[half]
            cs = slice(half * 512, (half + 1) * 512)
            k.act(s_.ap, pg.ap, AF.Sigmoid, r=[pg], w=[s_])
            k.tt("dve", s_.ap, pp.ap, s_.ap, ALU.mult, r=[pp, s_], w=[s_])
            k.tt("pool", h.ap[:, cs], h.ap[:, cs], s_.ap, ALU.add, r=[h, s_], w=[h])
        j = 3 + ti % 2
        k.memset("pool", ss[j].ap, 0.0, w=[ss[j]])
        k.act(junk2.ap, h.ap, AF.Square, accum=ss[j].ap, r=[h, ss[j]], w=[junk2, ss[j]])
        C.rms_rstd(ss[j], 1024, tmp[j], rstd[j])
        k.stt(h.ap, h.ap, rstd[j].ap[:, 0:1], gfin.ap, ALU.mult, ALU.mult, r=[h, rstd[j], gfin], w=[h])

    def store(ti):
        k.dma("pool", y[ti * 128:(ti + 1) * 128, :], hb[ti % 5].ap, r=[hb[ti % 5]])

    junk2 = A.alloc([1024], F32, "junkd2")
    stage1(0)
    stage1(1)
    for ti in range(NT):
        stage2a(ti)
        if ti + 2 < NT:
            stage1(ti + 2)
        if ti >= 1:
            store(ti - 1)
        stage2b(ti)
    store(NT - 1)
    A.pop()


_NC_CACHE = {}


def kernel(**inputs):
    S = 8192
    xs = [inputs["x_prompt"][i] for i in range(2)] + [inputs["x_sample"][i] for i in range(4)]
    ps = [inputs["p_prompt"][0, i] for i in range(2)] + [inputs["p_sample"][0, i] for i in range(4)]
    xs += [np.zeros_like(xs[0]), np.zeros_like(xs[0])]
    ps += [np.zeros_like(ps[0]), np.zeros_like(ps[0])]
    if "nc" not in _NC_CACHE:
        _NC_CACHE["nc"] = build(S)
    nc = _NC_CACHE["nc"]
    wnames = ["rel_bias", "g_mix", "w_in", "ssm_a_re", "ssm_a_im", "ssm_log_dt", "ssm_b_re", "ssm_b_im", "ssm_c_re",
              "ssm_c_im", "ssm_d", "w_glu", "b_glu", "g_att_out", "g_ssm_out", "w_out", "g_mlp", "w_mlp1", "w_mlp2",
              "g_ple", "w_ple_gate", "w_ple_proj", "g_final"]
    in_maps = []
    for c in range(8):
        m = {"x": np.ascontiguousarray(xs[c], dtype=np.float32), "p": np.ascontiguousarray(ps[c], dtype=np.float32)}
        for n in wnames:
            m[n] = np.ascontiguousarray(inputs[n], dtype=np.float32)
        in_maps.append(m)
    res = run_bass_kernel_spmd(nc, in_maps, core_ids=list(range(8)))
    outs = [np.asarray(r["y"], dtype=np.float32) for r in res.results]
    y_prompt = np.stack(outs[0:2], axis=0)
    y_sample = np.stack(outs[2:6], axis=0)
    return (y_prompt, y_sample)
```

```python
import math
import numpy as np
from contextlib import ExitStack
import concourse.bass as bass
import concourse.mybir as mybir
from concourse.bass_utils import run_bass_kernel_spmd

F32 = mybir.dt.float32
BF16 = mybir.dt.bfloat16
I32 = mybir.dt.int32
AF = mybir.ActivationFunctionType
ALU = mybir.AluOpType

ENGS = ("pe", "act", "dve", "pool", "sp")
D = 1024
EPS = 1e-6


class Buf:
    __slots__ = ("name", "last_w", "readers")

    def __init__(self, name=""):
        self.name = name
        self.last_w = None
        self.readers = []


class Op:
    __slots__ = ("eng", "fn", "reads", "writes", "is_dma", "deps", "idx", "signal", "count", "sem", "semkey")

    def __init__(self, eng, fn, reads, writes, is_dma):
        self.eng = eng
        self.fn = fn
        self.reads = reads
        self.writes = writes
        self.is_dma = is_dma
        self.deps = set()
        self.signal = False
        self.count = 0
        self.sem = None
        self.semkey = None


class Prog:
    def __init__(self, nc, n_dma_sems=40):
        self.nc = nc
        self.ops = []
        self.n_dma_sems = n_dma_sems
        self.ALL = Buf("ALL")

    def op(self, eng, fn, reads=(), writes=()):
        self._add(Op(eng, fn, tuple(reads) + (self.ALL,), tuple(writes), False))

    def dma(self, queue, fn, reads=(), writes=()):
        self._add(Op(queue, fn, tuple(reads) + (self.ALL,), tuple(writes), True))

    def barrier(self):
        self._add(Op("dve", lambda e: e.nop() if False else e.memset(self._bar[:], 0.0), (), (self.ALL,), False))

    def _add(self, o):
        o.idx = len(self.ops)
        for b in o.reads:
            if b.last_w is not None:
                o.deps.add(b.last_w)
        for b in o.writes:
            if b.last_w is not None:
                o.deps.add(b.last_w)
            for r in b.readers:
                o.deps.add(r)
        for b in o.reads:
            b.readers.append(o.idx)
        for b in o.writes:
            b.last_w = o.idx
            b.readers = []
        o.deps.discard(o.idx)
        self.ops.append(o)

    def emit(self, stack):
        nc = self.nc
        ops = self.ops
        needed = []
        for o in ops:
            nd = []
            for d in o.deps:
                y = ops[d]
                if y.is_dma or o.is_dma or y.eng != o.eng:
                    nd.append(d)
                elif o.eng != "pe":
                    if any((b in y.writes) for b in o.reads if b is not self.ALL) or \
                       any((b in y.writes or b in y.reads) for b in o.writes if b is not self.ALL):
                        nd.append(d)
            needed.append(nd)
            for d in nd:
                ops[d].signal = True
        for o in ops:
            if o.is_dma:
                o.signal = True
        eng_sem = {e: stack.enter_context(nc.semaphore("s_" + e)) for e in ENGS}
        nqs = {"sp": self.n_dma_sems - 16, "pool": 4, "act": 6, "dve": 6}
        dma_sems = {q: [stack.enter_context(nc.semaphore("d%s%d" % (q, i))) for i in range(nqs[q])] for q in nqs}
        cnt = {e: 0 for e in ENGS}
        dcnt = {q: [0] * nqs[q] for q in nqs}
        rr = {q: 0 for q in nqs}
        for o in ops:
            if o.is_dma:
                q = o.eng
                nq = nqs[q]
                k = rr[q] % nq
                rr[q] += 1
                dcnt[q][k] += 16
                o.sem, o.count, o.semkey = dma_sems[q][k], dcnt[q][k], ("d", q, k)
            elif o.signal:
                cnt[o.eng] += 1
                o.sem, o.count, o.semkey = eng_sem[o.eng], cnt[o.eng], ("e", o.eng)
        waited = {e: {} for e in ENGS}
        block = stack.enter_context(nc.Block())
        per_eng = {e: [] for e in ENGS}
        for o in ops:
            per_eng[o.eng].append(o)
        final_waits = {}
        for o in ops:
            if o.is_dma:
                final_waits[o.semkey] = (o.sem, o.count)

        def make(ename):
            def body(eng):
                w = waited[ename]
                for o in per_eng[ename]:
                    req = {}
                    for d in needed[o.idx]:
                        y = ops[d]
                        if req.get(y.semkey, (None, 0))[1] < y.count:
                            req[y.semkey] = (y.sem, y.count)
                    for key, (sem, c) in req.items():
                        if w.get(key, 0) < c:
                            eng.wait_ge(sem, c)
                            w[key] = c
                    if o.is_dma and o.count > 16 and w.get(o.semkey, 0) < o.count - 16:
                        eng.wait_ge(o.sem, o.count - 16)
                        w[o.semkey] = o.count - 16
                    ins = o.fn(eng)
                    if o.signal:
                        ins.then_inc(o.sem, 16 if o.is_dma else 1)
                if ename == "sp":
                    for key, (sem, c) in final_waits.items():
                        if w.get(key, 0) < c:
                            eng.wait_ge(sem, c)
                            w[key] = c
            return body

        block.tensor(make("pe"))
        block.scalar(make("act"))
        block.vector(make("dve"))
        block.gpsimd(make("pool"))
        block.sync(make("sp"))


class T:
    __slots__ = ("ap", "b")

    def __init__(self, ap, name=""):
        self.ap = ap
        self.b = Buf(name)


class Arena:
    def __init__(self, big, ncols):
        self.big = big
        self.ncols = ncols
        self.off = 0
        self.marks = []

    def push(self):
        self.marks.append(self.off)

    def pop(self):
        self.off = self.marks.pop()

    def alloc(self, free_shape, dtype, name=""):
        n = int(np.prod(free_shape))
        n32 = n if dtype in (F32, I32) else (n + 1) // 2
        assert self.off + n32 <= self.ncols, ("SBUF arena overflow", name, self.off, n32, self.ncols)
        v = self.big[:, self.off:self.off + n32]
        self.off += n32
        if dtype not in (F32,):
            v = v.bitcast(dtype)
            if n % 2 and dtype == BF16:
                v = v[:, 0:n]
        if len(free_shape) == 2:
            v = v.rearrange("p (a b) -> p a b", a=free_shape[0])
        elif len(free_shape) == 3:
            v = v.rearrange("p (a b c) -> p a b c", a=free_shape[0], b=free_shape[1])
        elif len(free_shape) == 4:
            v = v.rearrange("p (a b c d) -> p a b c d", a=free_shape[0], b=free_shape[1], c=free_shape[2])
        return T(v, name)


def _bs(ts_):
    return [t.b for t in ts_]


class K:
    def __init__(self, P):
        self.P = P

    def dma(self, q, out, in_, r=(), w=(), slow=False):
        if slow:
            self.P.dma(q, lambda e: e.dma_start(out=out, in_=in_, allow_slow_non_contiguous=True), _bs(r), _bs(w))
        else:
            self.P.dma(q, lambda e: e.dma_start(out=out, in_=in_), _bs(r), _bs(w))

    def mm(self, out, lhsT, rhs, start, stop, r=(), w=()):
        self.P.op("pe", lambda e: e.matmul(out, lhsT=lhsT, rhs=rhs, start=start, stop=stop), _bs(r), _bs(w))

    def tr(self, out, in_, ident, r=(), w=()):
        self.P.op("pe", lambda e: e.transpose(out, in_, ident), _bs(r), _bs(w))

    def act(self, out, in_, func, r=(), w=(), scale=None, bias=None, accum=None):
        kw = {}
        if scale is not None:
            kw["scale"] = scale
        if bias is not None:
            kw["bias"] = bias
        if accum is not None:
            kw["accum_out"] = accum
        self.P.op("act", lambda e: e.activation(out=out, in_=in_, func=func, **kw), _bs(r), _bs(w))

    def tt(self, eng, out, a, b, op, r=(), w=()):
        self.P.op(eng, lambda e: e.tensor_tensor(out=out, in0=a, in1=b, op=op), _bs(r), _bs(w))

    def ts(self, eng, out, a, s1, op0, s2=None, op1=None, r=(), w=()):
        if op1 is None:
            self.P.op(eng, lambda e: e.tensor_scalar(out=out, in0=a, scalar1=s1, scalar2=None, op0=op0), _bs(r), _bs(w))
        else:
            self.P.op(eng, lambda e: e.tensor_scalar(out=out, in0=a, scalar1=s1, scalar2=s2, op0=op0, op1=op1), _bs(r), _bs(w))

    def stt(self, out, a, s, b, op0, op1, r=(), w=()):
        self.P.op("dve", lambda e: e.scalar_tensor_tensor(out=out, in0=a, scalar=s, in1=b, op0=op0, op1=op1), _bs(r), _bs(w))

    def cp(self, eng, out, in_, r=(), w=()):
        if eng == "act":
            self.P.op("act", lambda e: e.activation(out=out, in_=in_, func=AF.Copy), _bs(r), _bs(w))
        else:
            self.P.op(eng, lambda e: e.tensor_copy(out=out, in_=in_), _bs(r), _bs(w))

    def memset(self, eng, ap, val, w=()):
        self.P.op(eng, lambda e: e.memset(ap, val), (), _bs(w))

    def recip(self, out, in_, r=(), w=()):
        self.P.op("dve", lambda e: e.reciprocal(out=out, in_=in_), _bs(r), _bs(w))

    def scan(self, out, d0, d1, init, r=(), w=()):
        self.P.op("dve", lambda e: e.tensor_tensor_scan(out=out, data0=d0, data1=d1, initial=init, op0=ALU.mult, op1=ALU.add), _bs(r), _bs(w))

    def asel(self, out, in_, pattern, cmp, fill, base, cm, r=(), w=()):
        self.P.op("pool", lambda e: e.affine_select(out=out, in_=in_, pattern=pattern, compare_op=cmp, fill=fill, base=base, channel_multiplier=cm), _bs(r), _bs(w))


def rev_ap(ap2d_last_col, n):
    return bass.AP(ap2d_last_col.tensor, ap2d_last_col.offset, [list(ap2d_last_col.ap[0]), [-1, n]])


def strided(ap_first, step, n):
    return bass.AP(ap_first.tensor, ap_first.offset, [list(ap_first.ap[0]), [step, n]])


def t5_bucket_np(rel):
    half = 16
    n = -rel
    ret = np.where(n < 0, half, 0)
    n = np.abs(n)
    max_exact = 8
    nf = np.maximum(n, 1).astype(np.float32)
    large = max_exact + (np.log(nf / np.float32(max_exact)) / np.float32(math.log(1024 / max_exact)) * (half - max_exact)).astype(np.int32)
    large = np.minimum(large, half - 1)
    return ret + np.where(n < max_exact, n, large)


class Ctx:
    pass


def build(S, debug=None):
    nc = bass.Bass("TRN2", target_bir_lowering=False)
    NT = S // 128
    C = Ctx()
    C.nc, C.S, C.NT = nc, S, NT
    din = {}

    def inp(name, shape):
        din[name] = nc.dram_tensor(name, list(shape), F32, kind="ExternalInput").ap()
        return din[name]

    x = inp("x", [S, D])
    pin = inp("p", [S, 256])
    rel_bias = inp("rel_bias", [32, 8])
    g_mix = inp("g_mix", [1, 1024])
    w_in = inp("w_in", [1, 1024, 2048])
    a_re = inp("ssm_a_re", [1, 2, 32, 64])
    a_im = inp("ssm_a_im", [1, 2, 32, 64])
    log_dt = inp("ssm_log_dt", [1, 2, 32])
    b_re = inp("ssm_b_re", [1, 2, 32, 64, 16])
    b_im = inp("ssm_b_im", [1, 2, 32, 64, 16])
    c_re = inp("ssm_c_re", [1, 2, 32, 16, 64])
    c_im = inp("ssm_c_im", [1, 2, 32, 16, 64])
    ssm_d = inp("ssm_d", [1, 32, 16])
    w_glu = inp("w_glu", [1, 512, 512])
    b_glu = inp("b_glu", [1, 512])
    g_att = inp("g_att_out", [1, 512])
    g_ssm = inp("g_ssm_out", [1, 512])
    w_out = inp("w_out", [1, 1024, 1024])
    g_mlp = inp("g_mlp", [1, 1024])
    w_mlp1 = inp("w_mlp1", [1, 1024, 4096])
    w_mlp2 = inp("w_mlp2", [1, 4096, 1024])
    g_ple = inp("g_ple", [1, 1024])
    w_gate = inp("w_ple_gate", [1, 1024, 1024])
    w_proj = inp("w_ple_proj", [1, 256, 1024])
    g_final = inp("g_final", [1024])
    y = nc.dram_tensor("y", [S, D], F32, kind="ExternalOutput").ap()

    def scratch(name, shape, dt):
        kind = "ExternalOutput" if (debug and name in debug) else "Internal"
        return nc.dram_tensor(name, list(shape), dt, kind=kind).ap()

    qT = scratch("qT", [4, 128, S], BF16)
    kT = scratch("kT", [4, 128, S], BF16)
    uT = scratch("uT", [4, 128, S], BF16)
    gT = scratch("gT", [4, 128, S], BF16)
    Vp = scratch("Vp", [S + 2048, 8, 128], BF16)
    hs = scratch("hs", [S, D], F32)
    hs2 = scratch("hs2", [S, D], F32)
    Fd = scratch("Fd", [3 * 8 * 512 + 512], F32)

    with ExitStack() as st:
        NCOL = 49100
        big = st.enter_context(nc.sbuf_tensor("big", [128, NCOL], F32))
        psb = [st.enter_context(nc.psum_tensor("ps%d" % i, [128, 512], F32)) for i in range(8)]
        P = Prog(nc)
        k = K(P)
        A = Arena(big, NCOL)
        bar = A.alloc([1], F32, "bar")
        P._bar = bar.ap
        PS = [T(psb[i][:], "ps%d" % i) for i in range(8)]
        C.P, C.k, C.A, C.PS = P, k, A, PS
        identf = A.alloc([128], F32, "identf")
        identb = A.alloc([128], BF16, "identb")
        ones_b = A.alloc([128], BF16, "ones_b")
        k.memset("pool", identf.ap, 0.0, w=[identf])
        k.asel(identf.ap, identf.ap, [[-1, 128]], ALU.not_equal, 1.0, 0, 1, r=[identf], w=[identf])
        k.cp("dve", identb.ap, identf.ap, r=[identf], w=[identb])
        k.memset("pool", ones_b.ap, 1.0, w=[ones_b])
        C.identf, C.identb, C.ones_b = identf, identb, ones_b

        def gcol(src_1d_ap, n, name):
            t = A.alloc([n // 128], F32, name)
            k.dma("sp", t.ap, src_1d_ap.rearrange("(k p) -> p k", p=128), w=[t], slow=True)
            return t
        C.gcol = gcol

        nhalf = A.alloc([1], F32, "nhalf")
        k.memset("pool", nhalf.ap, -0.5, w=[nhalf])

        def rms_rstd(ss_t, n, tmp_t, out_t, ss_ap=None, wide=False):
            if wide:
                k.ts("dve", tmp_t.ap, ss_t.ap if ss_ap is None else ss_ap, 1.0 / n, ALU.mult, EPS, ALU.add, r=[ss_t], w=[tmp_t])
                k.act(tmp_t.ap, tmp_t.ap, AF.Ln, r=[tmp_t], w=[tmp_t])
                k.act(out_t.ap, tmp_t.ap, AF.Exp, scale=-0.5, r=[tmp_t], w=[out_t])
            else:
                k.ts("pool", tmp_t.ap, ss_t.ap if ss_ap is None else ss_ap, 1.0 / n, ALU.mult, EPS, ALU.add, r=[ss_t], w=[tmp_t])
                k.tt("pool", out_t.ap, tmp_t.ap, nhalf.ap, ALU.pow, r=[tmp_t, nhalf], w=[out_t])
        C.rms_rstd = rms_rstd

        def load_weight_bf16(dst, src2d, nk, ncols, gc, tag):
            A.push()
            tmps = [A.alloc([min(ncols, 2048)], F32, tag + "tmp%d" % i) for i in range(2)]
            i = 0
            for kk in range(nk):
                for c0 in range(0, ncols, 2048):
                    cw = min(2048, ncols - c0)
                    tm = tmps[i % 2]
                    k.dma("sp", tm.ap[:, 0:cw], src2d[kk * 128:(kk + 1) * 128, c0:c0 + cw], w=[tm])
                    if i % 2 == 0:
                        if gc is None:
                            k.cp("dve", dst.ap[:, kk, c0:c0 + cw], tm.ap[:, 0:cw], r=[tm], w=[dst])
                        else:
                            k.ts("dve", dst.ap[:, kk, c0:c0 + cw], tm.ap[:, 0:cw], gc.ap[:, kk:kk + 1], ALU.mult, r=[tm, gc], w=[dst])
                    else:
                        if gc is None:
                            k.cp("act", dst.ap[:, kk, c0:c0 + cw], tm.ap[:, 0:cw], r=[tm], w=[dst])
                        else:
                            k.act(dst.ap[:, kk, c0:c0 + cw], tm.ap[:, 0:cw], AF.Copy, scale=gc.ap[:, kk:kk + 1], r=[tm, gc], w=[dst])
                    i += 1
            A.pop()
            P.barrier()
        C.load_weight_bf16 = load_weight_bf16

        phases = debug.get("phases", "ASBCD") if debug else "ASBCD"
        if "A" in phases:
            phase_A(C, x, g_mix, w_in, qT, kT, uT, Vp)
            P.barrier()
        if "S" in phases:
            phase_S(C, uT, gT, a_re, a_im, log_dt, b_re, b_im, c_re, c_im, ssm_d)
            P.barrier()
        if "B" in phases:
            phase_B(C, x, qT, kT, Vp, gT, hs, rel_bias, Fd, w_glu, b_glu, g_att, g_ssm, w_out)
            P.barrier()
        if "X" in phases:
            for i in range(S // 512):
                k.dma("sp", hs[i * 512:(i + 1) * 512, :], x[i * 512:(i + 1) * 512, :])
            P.barrier()
        if "C" in phases:
            phase_C(C, hs, hs2, g_mlp, w_mlp1, w_mlp2)
            P.barrier()
        if "D" in phases:
            phase_D(C, hs2, pin, y, g_ple, w_gate, w_proj, g_final)
        P.emit(st)
    return nc


def phase_A(C, x, g_mix, w_in, qT, kT, uT, Vp):
    k, A, PS, S, NT = C.k, C.A, C.PS, C.S, C.NT
    A.push()
    gm = C.gcol(g_mix[0], 1024, "gm")
    win = A.alloc([8, 2048], BF16, "win")
    C.load_weight_bf16(win, w_in[0], 8, 2048, gm, "win")
    xb = [A.alloc([1024], F32, "xa%d" % i) for i in range(2)]
    junk = A.alloc([1024], F32, "junk")
    ab = [A.alloc([1024], BF16, "ab%d" % i) for i in range(2)]
    ss = [A.alloc([1], F32, "ss%d" % i) for i in range(2)]
    tmp = [A.alloc([1], F32, "tmp%d" % i) for i in range(2)]
    rstd = [A.alloc([1], F32, "rstd%d" % i) for i in range(2)]
    aT = [A.alloc([8, 512], BF16, "aT%d" % i) for i in range(2)]
    zst = [A.alloc([512], BF16, "zst%d" % i) for i in range(4)]
    vst = [A.alloc([8, 128], BF16, "vst%d" % i) for i in range(2)]
    zero = A.alloc([8, 128], BF16, "zero")
    k.memset("dve", zero.ap, 0.0, w=[zero])
    for v in vst:
        k.memset("pool", v.ap, 1.0, w=[v])
    for i in range(8):
        k.dma("sp", Vp[i * 128:(i + 1) * 128], zero.ap, r=[zero])
        k.dma("sp", Vp[1024 + S + i * 128:1024 + S + (i + 1) * 128], zero.ap, r=[zero])
    zi = [0]

    def normpre(b):
        for t in range(4):
            ti = 4 * b + t
            xt, a_, s_, tm, rs = xb[ti % 2], ab4[t], ss[ti % 2], tmp[ti % 2], rstd[ti % 2]
            k.dma("sp", xt.ap, x[ti * 128:(ti + 1) * 128, :], w=[xt])
            k.memset("pool", s_.ap, 0.0, w=[s_])
            k.act(junk.ap, xt.ap, AF.Square, accum=s_.ap, r=[xt, s_], w=[junk, s_])
            C.rms_rstd(s_, 1024, tm, rs)
            k.ts("dve", a_.ap, xt.ap, rs.ap[:, 0:1], ALU.mult, r=[xt, rs], w=[a_])

    def normtr(b):
        at = aT[b % 2]
        for t in range(4):
            ti = 4 * b + t
            a_ = ab4[t]
            pst = PS[ti % 2]
            psv = pst.ap.bitcast(BF16).rearrange("p (a b) -> p a b", a=8)
            for kk in range(8):
                k.tr(psv[:, kk, :], a_.ap[:, kk * 128:(kk + 1) * 128], C.identb.ap, r=[a_, C.identb], w=[pst])
            k.cp("dve", at.ap[:, :, t * 128:(t + 1) * 128], psv, r=[pst], w=[at])

    def projstage(b):
        at = aT[b % 2]
        for (dst, f0, sc) in ((qT, 0, 0.125), (kT, 4, None), (uT, 12, None)):
            for j in range(4):
                pz = PS[2 + zi[0] % 3]
                for kk in range(8):
                    k.mm(pz.ap, win.ap[:, kk, (f0 + j) * 128:(f0 + j + 1) * 128], at.ap[:, kk, :], kk == 0, kk == 7, r=[win, at], w=[pz])
                z = zst[zi[0] % 4]
                k.act(z.ap, pz.ap, AF.Copy, scale=(sc if sc is not None else 1.0), r=[pz], w=[z])
                k.dma("act", dst[j, :, b * 512:(b + 1) * 512], z.ap, r=[z])
                zi[0] += 1
        for t in range(4):
            ti = 4 * b + t
            pv = PS[5 + ti % 3]
            for kk in range(8):
                k.mm(pv.ap, at.ap[:, kk, t * 128:(t + 1) * 128], win.ap[:, kk, 1024:1536], kk == 0, kk == 7, r=[win, at], w=[pv])
            v = vst[ti % 2]
            pvv = pv.ap.rearrange("p (a b c) -> p a b c", a=4, b=2)
            vv = v.ap.rearrange("p (a b) c -> p a b c", b=2)
            k.cp("dve", vv[:, :, 0, 0:64], pvv[:, :, 0, :], r=[pv], w=[v])
            k.cp("act", vv[:, :, 1, 64:128], pvv[:, :, 1, :], r=[pv], w=[v])
            k.dma("act", Vp[1024 + ti * 128:1024 + (ti + 1) * 128], v.ap, r=[v])
    NB = S // 512
    ab4 = ab + [A.alloc([1024], BF16, "ab%d" % i) for i in range(2, 4)]
    normpre(0)
    normtr(0)
    if NB > 1:
        normpre(1)
    for b in range(NB):
        projstage(b)
        if b + 1 < NB:
            normtr(b + 1)
        if b + 2 < NB:
            normpre(b + 2)
    A.pop()


def phase_S(C, uT, gT, a_re, a_im, log_dt, b_re, b_im, c_re, c_im, ssm_d):
    k, A, PS, S, P = C.k, C.A, C.PS, C.S, C.P
    NCH = S // 8
    CB = min(512, NCH)
    NCB = NCH // CB
    LV = int(round(math.log2(NCH)))
    TWO_PI = 2.0 * math.pi
    A.push()
    Z = A.alloc([8, 240], BF16, "Z")
    T0 = A.alloc([32, 128], BF16, "T0")
    WS = A.alloc([32, 2, 128], BF16, "WS")
    WOF = A.alloc([32, 2, 128], BF16, "WOF")
    WOB = A.alloc([32, 2, 128], BF16, "WOB")
    k.memset("pool", WOF.ap, 0.0, w=[WOF])
    k.memset("pool", WOB.ap, 0.0, w=[WOB])
    Wpow = A.alloc([LV, 32, 2], F32, "Wpow")
    rho8 = A.alloc([32], F32, "rho8")
    Eb = A.alloc([32, 2, 32], F32, "Eb")
    k.memset("pool", Z.ap, 0.0, w=[Z])
    for gi in range(8):
        k.asel(Z.ap[:, gi, 112:128], Z.ap[:, gi, 112:128], [[-1, 16]], ALU.not_equal, 1.0, -16 * gi, 1, r=[Z], w=[Z])
    A.push()

    def new(shape, name):
        return A.alloc(shape, F32, name)
    are, aim, ldt = new([32], "are"), new([32], "aim"), new([32], "ldt")
    bre, bim = new([32, 16], "bre"), new([32, 16], "bim")
    Lr, Li = new([32, 8, 16], "Lr"), new([32, 8, 16], "Li")
    Rr, Ri = new([32, 8, 16], "Rr"), new([32, 8, 16], "Ri")
    dcol = new([32], "dcol")
    for dr in range(2):
        ps_ = slice(dr * 64, dr * 64 + 64)
        k.dma("sp", are.ap[ps_, :], a_re[0, dr].rearrange("g n -> n g"), w=[are], slow=True)
        k.dma("sp", aim.ap[ps_, :], a_im[0, dr].rearrange("g n -> n g"), w=[aim], slow=True)
        k.dma("sp", ldt.ap[ps_, :], log_dt[0, dr:dr + 1, :].partition_broadcast(64), w=[ldt])
        k.dma("sp", bre.ap[ps_], b_re[0, dr].rearrange("g n h -> n g h"), w=[bre], slow=True)
        k.dma("sp", bim.ap[ps_], b_im[0, dr].rearrange("g n h -> n g h"), w=[bim], slow=True)
    for s_ in range(8):
        k.dma("sp", dcol.ap[s_ * 16:(s_ + 1) * 16, :], ssm_d[0].rearrange("g h -> h g"), w=[dcol], slow=True)
    ct = new([128], "ct")
    for (src, dst) in ((c_re, Rr), (c_im, Ri)):
        for o in range(4):
            k.dma("sp", ct.ap[:, 0:64], src[0, 0, 8 * o:8 * o + 8].rearrange("g h n -> (g h) n"), w=[ct])
            k.dma("sp", ct.ap[:, 64:128], src[0, 1, 8 * o:8 * o + 8].rearrange("g h n -> (g h) n"), w=[ct])
            k.tr(PS[0].ap[:, 0:128], ct.ap, C.identf.ap, r=[ct, C.identf], w=[PS[0]])
            k.cp("dve", dst.ap[:, 8 * o:8 * o + 8, 0, :], PS[0].ap[:, 0:128].rearrange("p (a b) -> p a b", b=16), r=[PS[0]], w=[dst])
    sc = [new([32], "sc%d" % i) for i in range(12)]
    sci = A.alloc([32], I32, "sci")

    def mul(o, a, b, eng="dve"):
        k.tt(eng, o.ap, a.ap, b.ap, ALU.mult, r=[a, b], w=[o])

    def sin_of(out, th, shift):
        t1, tf, r_, m_ = sc[8], sc[9], sc[10], sc[11]
        k.ts("dve", r_.ap, th.ap, shift, ALU.add, r=[th], w=[r_])
        k.ts("dve", t1.ap, r_.ap, 1.0 / TWO_PI, ALU.mult, r=[r_], w=[t1])
        k.cp("dve", sci.ap, t1.ap, r=[t1], w=[sci])
        k.cp("dve", tf.ap, sci.ap, r=[sci], w=[tf])
        k.stt(r_.ap, tf.ap, -TWO_PI, r_.ap, ALU.mult, ALU.add, r=[tf, r_], w=[r_])
        k.ts("dve", m_.ap, r_.ap, math.pi, ALU.is_gt, -TWO_PI, ALU.mult, r=[r_], w=[m_])
        k.tt("dve", r_.ap, r_.ap, m_.ap, ALU.add, r=[r_, m_], w=[r_])
        k.ts("dve", m_.ap, r_.ap, -math.pi, ALU.is_lt, TWO_PI, ALU.mult, r=[r_], w=[m_])
        k.tt("dve", r_.ap, r_.ap, m_.ap, ALU.add, r=[r_, m_], w=[r_])
        k.ts("dve", r_.ap, r_.ap, 3.1415925, ALU.min, -3.1415925, ALU.max, r=[r_], w=[r_])
        k.act(out.ap, r_.ap, AF.Sin, r=[r_], w=[out])
    dt_, lam, th, mag, c1, s1, abr, abi = [new([32], "p%d" % i) for i in range(8)]
    k.act(dt_.ap, ldt.ap, AF.Exp, r=[ldt], w=[dt_])
    mul(lam, are, dt_)
    mul(th, aim, dt_)
    k.act(mag.ap, lam.ap, AF.Exp, r=[lam], w=[mag])
    sin_of(s1, th, 0.0)
    sin_of(c1, th, math.pi / 2)
    mul(abr, mag, c1)
    mul(abi, mag, s1)
    inv, fre, fim, am1 = [new([32], "q%d" % i) for i in range(4)]
    mul(sc[0], are, are)
    mul(sc[1], aim, aim)
    k.tt("dve", sc[0].ap, sc[0].ap, sc[1].ap, ALU.add, r=[sc[0], sc[1]], w=[sc[0]])
    k.recip(inv.ap, sc[0].ap, r=[sc[0]], w=[inv])
    k.ts("dve", am1.ap, abr.ap, -1.0, ALU.add, r=[abr], w=[am1])
    mul(sc[0], am1, are)
    mul(sc[1], abi, aim)
    k.tt("dve", sc[0].ap, sc[0].ap, sc[1].ap, ALU.add, r=[sc[0], sc[1]], w=[sc[0]])
    mul(fre, sc[0], inv)
    mul(sc[0], abi, are)
    mul(sc[1], am1, aim)
    k.tt("dve", sc[0].ap, sc[0].ap, sc[1].ap, ALU.subtract, r=[sc[0], sc[1]], w=[sc[0]])
    mul(fim, sc[0], inv)

    def cmul(ore, oim, ar_, ai_, br_, bi_, t1, t2, r, w):
        k.tt("dve", t1, ar_, br_, ALU.mult, r=r, w=w)
        k.tt("dve", t2, ai_, bi_, ALU.mult, r=r, w=w)
        k.tt("dve", ore, t1, t2, ALU.subtract, r=r, w=w)
        k.tt("dve", t1, ar_, bi_, ALU.mult, r=r, w=w)
        k.tt("dve", t2, ai_, br_, ALU.mult, r=r, w=w)
        k.tt("dve", oim, t1, t2, ALU.add, r=r, w=w)
    X = T(None, "prepX")
    allr = [X, are, aim, bre, bim, Rr, Ri, Lr, Li, abr, abi, mag, fre, fim, lam]
    big1, big2 = new([16, 128], "big1"), new([16, 128], "big2")

    def b16(t_):
        return t_.ap.unsqueeze(2).to_broadcast([128, 32, 16])

    def b128(t_):
        return t_.ap.unsqueeze(2).to_broadcast([128, 32, 128])
    t16a = big1.ap.rearrange("p a b -> p (a b)")[:, 0:512].rearrange("p (g h) -> p g h", h=16)
    t16b = big2.ap.rearrange("p a b -> p (a b)")[:, 0:512].rearrange("p (g h) -> p g h", h=16)
    cmul(Lr.ap[:, :, 0, :], Li.ap[:, :, 0, :], b16(fre), b16(fim), bre.ap, bim.ap, t16a, t16b, allr, [X])
    aivr, aivi, im2 = new([32], "aivr"), new([32], "aivi"), new([32], "im2")
    mul(sc[0], mag, mag)
    k.recip(im2.ap, sc[0].ap, r=[sc[0]], w=[im2])
    mul(aivr, abr, im2)
    k.stt(aivi.ap, abi.ap, -1.0, im2.ap, ALU.mult, ALU.mult, r=[abi, im2], w=[aivi])
    MLr, MLi, MRr, MRi = [new([32], "M%d" % i) for i in range(4)]
    lo, hi = slice(0, 64), slice(64, 128)
    for (dst, s_lo, s_hi) in ((MLr, aivr, abr), (MLi, aivi, abi), (MRr, abr, aivr), (MRi, abi, aivi)):
        k.cp("dve", dst.ap[lo], s_lo.ap[lo], r=[s_lo], w=[dst])
        k.cp("dve", dst.ap[hi], s_hi.ap[hi], r=[s_hi], w=[dst])
    allr += [MLr, MLi, MRr, MRi]
    for s_ in range(7):
        cmul(Lr.ap[:, :, s_ + 1, :], Li.ap[:, :, s_ + 1, :], Lr.ap[:, :, s_, :], Li.ap[:, :, s_, :], b16(MLr), b16(MLi), t16a, t16b, allr, [X])
        cmul(Rr.ap[:, :, s_ + 1, :], Ri.ap[:, :, s_ + 1, :], Rr.ap[:, :, s_, :], Ri.ap[:, :, s_, :], b16(MRr), b16(MRi), t16a, t16b, allr, [X])
    pwr = [abr] + [new([32], "pwr%d" % i) for i in range(7)]
    pwi = [abi] + [new([32], "pwi%d" % i) for i in range(7)]
    for i in range(7):
        cmul(pwr[i + 1].ap, pwi[i + 1].ap, pwr[i].ap, pwi[i].ap, abr.ap, abi.ap, sc[0].ap, sc[1].ap, allr, [X])
    FSr, FSi, FOr, FOi = [new([32], "F%d" % i) for i in range(4)]
    k.cp("dve", FSr.ap[lo], pwr[6].ap[lo], r=allr, w=[X])
    k.cp("dve", FSi.ap[lo], pwi[6].ap[lo], r=allr, w=[X])
    k.memset("dve", FSr.ap[hi], 1.0, w=[X])
    k.memset("dve", FSi.ap[hi], 0.0, w=[X])
    k.cp("dve", FOr.ap[lo], abr.ap[lo], r=allr, w=[X])
    k.cp("dve", FOi.ap[lo], abi.ap[lo], r=allr, w=[X])
    k.cp("dve", FOr.ap[hi], pwr[7].ap[hi], r=allr, w=[X])
    k.cp("dve", FOi.ap[hi], pwi[7].ap[hi], r=allr, w=[X])
    Lr3 = Lr.ap.rearrange("p g s h -> p g (s h)")
    Li3 = Li.ap.rearrange("p g s h -> p g (s h)")
    Rr3 = Rr.ap.rearrange("p g s h -> p g (s h)")
    Ri3 = Ri.ap.rearrange("p g s h -> p g (s h)")
    WSr, WSi = new([16, 128], "WSr"), new([16, 128], "WSi")

    def b128h(t_, h_):
        return t_.ap[:, 16 * h_:16 * h_ + 16].unsqueeze(2).to_broadcast([128, 16, 128])
    for h_ in range(2):
        gs = slice(16 * h_, 16 * h_ + 16)
        cmul(WSr.ap, WSi.ap, Lr3[:, gs], Li3[:, gs], b128h(FSr, h_), b128h(FSi, h_), big1.ap, big2.ap, allr, [X])
        for gl in range(16):
            g = 16 * h_ + gl
            for xi, src in enumerate((WSr, WSi)):
                pst = PS[(2 * g + xi) % 4]
                k.tr(pst.ap[:, 0:128], src.ap[:, gl, :], C.identf.ap, r=[X, C.identf], w=[pst])
                k.cp("act" if xi else "dve", WS.ap[:, g, xi, :], pst.ap[:, 0:128], r=[pst], w=[WS])
        cmul(WSr.ap, WSi.ap, Rr3[:, gs], Ri3[:, gs], b128h(FOr, h_), b128h(FOi, h_), big1.ap, big2.ap, allr, [X])
        for (dst, ps_) in ((WOF, lo), (WOB, hi)):
            k.cp("dve", dst.ap[ps_, gs, 0, :], WSr.ap[ps_], r=[X], w=[dst])
            k.ts("dve", dst.ap[ps_, gs, 1, :], WSi.ap[ps_], -1.0, ALU.mult, r=[X], w=[dst])
    Mlow, Mup, onesf = new([128], "Mlow"), new([128], "Mup"), new([128], "onesf")
    k.memset("pool", onesf.ap, 1.0, w=[onesf])
    k.asel(Mlow.ap.rearrange("p (t h) -> p t h", h=16), onesf.ap.rearrange("p (t h) -> p t h", h=16), [[16, 8], [0, 16]], ALU.is_ge, 0.0, 15, -1, r=[onesf], w=[Mlow])
    k.asel(Mup.ap.rearrange("p (t h) -> p t h", h=16), onesf.ap.rearrange("p (t h) -> p t h", h=16), [[-16, 8], [0, 16]], ALU.is_ge, 0.0, 0, 1, r=[onesf], w=[Mup])
    tacc = [new([128], "tacc%d" % i) for i in range(2)]
    for g in range(32):
        pf, pf2, pb_, pb2 = PS[4], PS[5], PS[6], PS[7]
        k.mm(pf.ap[:, 0:128], Lr3[lo, g, :], Rr3[lo, g, :], True, True, r=[X], w=[pf])
        k.mm(pf2.ap[:, 0:128], Li3[lo, g, :], Ri3[lo, g, :], True, True, r=[X], w=[pf2])
        k.mm(pb_.ap[:, 0:128], Lr3[hi, g, :], Rr3[hi, g, :], True, True, r=[X], w=[pb_])
        k.mm(pb2.ap[:, 0:128], Li3[hi, g, :], Ri3[hi, g, :], True, True, r=[X], w=[pb2])
        ta, tb = tacc[0], tacc[1]
        k.cp("act", ta.ap, pf2.ap[:, 0:128], r=[pf2], w=[ta])
        k.tt("dve", ta.ap, pf.ap[:, 0:128], ta.ap, ALU.subtract, r=[pf, ta], w=[ta])
        k.tt("dve", ta.ap, ta.ap, Mlow.ap, ALU.mult, r=[ta, Mlow], w=[ta])
        k.cp("act", tb.ap, pb2.ap[:, 0:128], r=[pb2], w=[tb])
        k.tt("dve", tb.ap, pb_.ap[:, 0:128], tb.ap, ALU.subtract, r=[pb_, tb], w=[tb])
        k.tt("dve", tb.ap, tb.ap, Mup.ap, ALU.mult, r=[tb, Mup], w=[tb])
        k.tt("dve", ta.ap, ta.ap, tb.ap, ALU.add, r=[ta, tb], w=[ta])
        k.stt(T0.ap[:, g, :], C.identf.ap, dcol.ap[:, g:g + 1], ta.ap, ALU.mult, ALU.add, r=[C.identf, dcol, ta], w=[T0])
    k.act(rho8.ap, lam.ap, AF.Exp, scale=8.0, r=[lam], w=[rho8])
    ir8 = new([32], "ir8")
    k.act(ir8.ap, lam.ap, AF.Exp, scale=-8.0, r=[lam], w=[ir8])
    k.tt("dve", Wpow.ap[:, 0, :, 0], pwr[7].ap, ir8.ap, ALU.mult, r=allr + [ir8], w=[Wpow])
    k.tt("dve", Wpow.ap[:, 0, :, 1], pwi[7].ap, ir8.ap, ALU.mult, r=allr + [ir8], w=[Wpow])
    for l in range(LV - 1):
        wr, wi = Wpow.ap[:, l, :, 0], Wpow.ap[:, l, :, 1]
        k.tt("dve", sc[0].ap, wr, wr, ALU.mult, r=[Wpow], w=[sc[0]])
        k.tt("dve", sc[1].ap, wi, wi, ALU.mult, r=[Wpow], w=[sc[1]])
        k.tt("dve", Wpow.ap[:, l + 1, :, 0], sc[0].ap, sc[1].ap, ALU.subtract, r=[sc[0], sc[1]], w=[Wpow])
        k.tt("dve", sc[2].ap, wr, wi, ALU.mult, r=[Wpow], w=[sc[2]])
        k.ts("dve", Wpow.ap[:, l + 1, :, 1], sc[2].ap, 2.0, ALU.mult, r=[sc[2]], w=[Wpow])
    k.memset("dve", Eb.ap[:, :, 0, 0:1], 1.0, w=[Eb])
    k.memset("dve", Eb.ap[:, :, 1, 0:1], 0.0, w=[Eb])
    eb1 = big1.ap.rearrange("p a b -> p (a b)")[:, 0:512].rearrange("p (g m) -> p g m", m=16)
    eb2 = big2.ap.rearrange("p a b -> p (a b)")[:, 0:512].rearrange("p (g m) -> p g m", m=16)
    for l in range(5):
        m = 1 << l
        wr = Wpow.ap[:, l, :, 0].unsqueeze(2).to_broadcast([128, 32, m])
        wi = Wpow.ap[:, l, :, 1].unsqueeze(2).to_broadcast([128, 32, m])
        ec0, es0 = Eb.ap[:, :, 0, 0:m], Eb.ap[:, :, 1, 0:m]
        ec1, es1 = Eb.ap[:, :, 0, m:2 * m], Eb.ap[:, :, 1, m:2 * m]
        t1, t2 = eb1[:, :, 0:m], eb2[:, :, 0:m]
        k.tt("dve", t1, es0, wi, ALU.mult, r=[Eb, Wpow, X], w=[X])
        k.tt("dve", t2, ec0, wr, ALU.mult, r=[Eb, Wpow, X], w=[X])
        k.tt("dve", ec1, t2, t1, ALU.subtract, r=[X, Eb], w=[Eb])
        k.tt("dve", t1, ec0, wi, ALU.mult, r=[Eb, Wpow, X], w=[X])
        k.tt("dve", t2, es0, wr, ALU.mult, r=[Eb, Wpow, X], w=[X])
        k.tt("dve", es1, t2, t1, ALU.add, r=[X, Eb], w=[Eb])
    A.pop()
    P.barrier()
    uTo = A.alloc([S], BF16, "uTo")
    gTo = A.alloc([S], BF16, "gTo")
    Yg = A.alloc([8, NCH], BF16, "Yg")
    Ugs = [A.alloc([NCH], BF16, "Ug%d" % i) for i in range(2)]
    Es = [A.alloc([2, NCH], F32, "E%d" % i) for i in range(2)]
    G = A.alloc([2, NCH], F32, "G")
    Hs = A.alloc([2, NCH], F32, "Hs")
    Hbs = [A.alloc([2, NCH + 2], BF16, "Hb%d" % i) for i in range(2)]
    tq = [A.alloc([CB], F32, "tq%d" % i) for i in range(4)]
    ge = [A.alloc([CB], F32, "ge%d" % i) for i in range(3)]
    for hb_ in Hbs:
        k.memset("pool", hb_.ap, 0.0, w=[hb_])
    tr_ = [A.alloc([CB], F32, "tr%d" % i) for i in range(4)]
    Ug3 = Ugs + [A.alloc([NCH], BF16, "Ug2")]

    def egen(g):
        E = Es[g % 2]
        Ec, Es_ = E.ap[:, 0, :], E.ap[:, 1, :]
        k.cp("dve", E.ap[:, :, 0:32], Eb.ap[:, g, :, :], r=[Eb], w=[E])
        for l in range(5, LV):
            m = 1 << l
            wr, wi = Wpow.ap[:, l, g, 0:1], Wpow.ap[:, l, g, 1:2]
            t1, t2 = tq[0].ap[:, 0:m], tq[1].ap[:, 0:m]
            k.ts("dve", t1, Es_[:, 0:m], wi, ALU.mult, r=[E, Wpow], w=[tq[0]])
            k.stt(Ec[:, m:2 * m], Ec[:, 0:m], wr, t1, ALU.mult, ALU.subtract, r=[E, Wpow, tq[0]], w=[E])
            k.ts("dve", t2, Ec[:, 0:m], wi, ALU.mult, r=[E, Wpow], w=[tq[1]])
            k.stt(Es_[:, m:2 * m], Es_[:, 0:m], wr, t2, ALU.mult, ALU.add, r=[E, Wpow, tq[1]], w=[E])

    def insel(g):
        gi, Ug = g % 8, Ug3[g % 3]
        for cb in range(NCB):
            pu = PS[cb % 2]
            for s_ in range(8):
                k.mm(pu.ap[:, 0:CB], Z.ap[:, gi, 112 - 16 * s_:240 - 16 * s_], strided(uTo.ap[:, cb * CB * 8 + s_:cb * CB * 8 + s_ + 1], 8, CB), s_ == 0, s_ == 7, r=[Z, uTo], w=[pu])
            k.cp("act", Ug.ap[:, cb * CB:(cb + 1) * CB], pu.ap[:, 0:CB], r=[pu], w=[Ug])

    def summ(g):
        Ug = Ug3[g % 3]
        for cb in range(NCB):
            for xi in range(2):
                pS = PS[2 + 2 * xi + cb]
                k.mm(pS.ap[0:64, 0:CB], WS.ap[:, g, xi, 0:64], Ug.ap[:, cb * CB:(cb + 1) * CB], True, True, r=[WS, Ug], w=[pS])
                k.mm(pS.ap[64:128, 0:CB], WS.ap[:, g, xi, 64:128], rev_ap(Ug.ap[:, NCH - 1 - cb * CB:NCH - cb * CB], CB), True, True, r=[WS, Ug], w=[pS])

    def demod(g):
        E = Es[g % 2]
        Ec, Es_ = E.ap[:, 0, :], E.ap[:, 1, :]
        for cb in range(NCB):
            cs = slice(cb * CB, (cb + 1) * CB)
            pr, pi_ = PS[2 + cb], PS[4 + cb]
            k.tt("dve", tq[0].ap, pr.ap[:, 0:CB], Ec[:, cs], ALU.mult, r=[pr, E], w=[tq[0]])
            k.tt("dve", tq[1].ap, pi_.ap[:, 0:CB], Es_[:, cs], ALU.mult, r=[pi_, E], w=[tq[1]])
            k.tt("dve", G.ap[:, 0, cs], tq[0].ap, tq[1].ap, ALU.add, r=[tq[0], tq[1]], w=[G])
            k.tt("dve", tq[2].ap, pi_.ap[:, 0:CB], Ec[:, cs], ALU.mult, r=[pi_, E], w=[tq[2]])
            k.tt("dve", tq[3].ap, pr.ap[:, 0:CB], Es_[:, cs], ALU.mult, r=[pr, E], w=[tq[3]])
            k.tt("dve", G.ap[:, 1, cs], tq[2].ap, tq[3].ap, ALU.subtract, r=[tq[2], tq[3]], w=[G])

    def scan_(g):
        dec = rho8.ap[:, g:g + 1].to_broadcast([128, NCH])
        k.scan(Hs.ap[:, 0, :], dec, G.ap[:, 0, :], 0.0, r=[G, rho8], w=[Hs])
        k.scan(Hs.ap[:, 1, :], dec, G.ap[:, 1, :], 0.0, r=[G, rho8], w=[Hs])

    def remod(g):
        E, Hb = Es[g % 2], Hbs[g % 2]
        Ec, Es_ = E.ap[:, 0, :], E.ap[:, 1, :]
        for cb in range(NCB):
            cs = slice(cb * CB, (cb + 1) * CB)
            co = slice(1 + cb * CB, 1 + (cb + 1) * CB)
            k.tt("dve", tr_[0].ap, Hs.ap[:, 0, cs], Ec[:, cs], ALU.mult, r=[Hs, E], w=[tr_[0]])
            k.tt("dve", tr_[1].ap, Hs.ap[:, 1, cs], Es_[:, cs], ALU.mult, r=[Hs, E], w=[tr_[1]])
            k.tt("dve", Hb.ap[:, 0, co], tr_[0].ap, tr_[1].ap, ALU.subtract, r=[tr_[0], tr_[1]], w=[Hb])
            k.tt("dve", tr_[2].ap, Hs.ap[:, 1, cs], Ec[:, cs], ALU.mult, r=[Hs, E], w=[tr_[2]])
            k.tt("dve", tr_[3].ap, Hs.ap[:, 0, cs], Es_[:, cs], ALU.mult, r=[Hs, E], w=[tr_[3]])
            k.tt("dve", Hb.ap[:, 1, co], tr_[2].ap, tr_[3].ap, ALU.add, r=[tr_[2], tr_[3]], w=[Hb])

    def outs(g):
        gi, Ug, Hb = g % 8, Ug3[g % 3], Hbs[g % 2]
        for cb in range(NCB):
            cs = slice(cb * CB, (cb + 1) * CB)
            py = PS[6 + cb % 2]
            k.mm(py.ap[:, 0:CB], T0.ap[:, g, :], Ug.ap[:, cs], True, False, r=[T0, Ug], w=[py])
            for xi in range(2):
                k.mm(py.ap[:, 0:CB], WOF.ap[:, g, xi, :], Hb.ap[:, xi, cb * CB:(cb + 1) * CB], False, False, r=[WOF, Hb], w=[py])
            for xi in range(2):
                st_ = NCH - 1 - cb * CB
                k.mm(py.ap[:, 0:CB], WOB.ap[:, g, xi, :], rev_ap(Hb.ap[:, xi, st_:st_ + 1], CB), False, xi == 1, r=[WOB, Hb], w=[py])
            k.cp("act", Yg.ap[:, gi, cs], py.ap[:, 0:CB], r=[py], w=[Yg])

    for o in range(4):
        k.dma("sp", uTo.ap, uT[o], w=[uTo])
        g0 = 8 * o
        insel(g0)
        insel(g0 + 1)
        summ(g0)
        egen(g0)
        for gi in range(8):
            g = g0 + gi
            demod(g)
            if gi + 1 < 8:
                summ(g + 1)
            scan_(g)
            if gi + 1 < 8:
                egen(g + 1)
            remod(g)
            if gi + 2 < 8:
                insel(g + 2)
            outs(g)
        for t in range(8):
            for cb in range(NCB):
                po = PS[(t * NCB + cb) % 2]
                for gi in range(8):
                    k.mm(po.ap[:, 0:CB], Z.ap[:, t, 112 - 16 * gi:240 - 16 * gi], Yg.ap[:, gi, cb * CB:(cb + 1) * CB], gi == 0, gi == 7, r=[Z, Yg], w=[po])
                yv = po.ap[:, 0:CB]
                k.act(ge[0].ap, yv, AF.Square, r=[po], w=[ge[0]])
                k.ts("dve", ge[0].ap, ge[0].ap, 0.044715, ALU.mult, 1.0, ALU.add, r=[ge[0]], w=[ge[0]])
                k.tt("dve", ge[1].ap, ge[0].ap, yv, ALU.mult, r=[ge[0], po], w=[ge[1]])
                k.act(ge[2].ap, ge[1].ap, AF.Sigmoid, scale=1.5957691216057308, r=[ge[1]], w=[ge[2]])
                k.tt("dve", strided(gTo.ap[:, cb * CB * 8 + t:cb * CB * 8 + t + 1], 8, CB), yv, ge[2].ap, ALU.mult, r=[po, ge[2]], w=[gTo])
        k.dma("sp", gT[o], gTo.ap, r=[gTo])
    A.pop()


def ap3(base, d1, d2):
    return bass.AP(base.tensor, base.offset, [list(base.ap[0]), list(d1), list(d2)])


def phase_B(C, x, qT, kT, Vp, gT, hs, rel_bias, Fd, w_glu, b_glu, g_att, g_ssm, w_out):
    k, A, PS, S, P = C.k, C.A, C.PS, C.S, C.P
    NSB = S // 2048
    import os
    DIL = (1, 4, 16)
    DSEL = [int(v) for v in os.environ.get('PB_DIL', '1,4,16').split(',')]
    A.push()
    EB = A.alloc([3, 8, 2, 128], BF16, "EB")
    A.push()
    rb0 = A.alloc([256], F32, "rb0")
    F0 = A.alloc([3, 8, 512], F32, "F0")
    Tt = A.alloc([24, 257], F32, "Tt")
    k.dma("sp", rb0.ap, rel_bias.rearrange("(o b) h -> o (b h)", o=1).partition_broadcast(128), w=[rb0])
    k.act(rb0.ap, rb0.ap, AF.Exp, r=[rb0], w=[rb0])
    k.memset("dve", F0.ap, 0.0, w=[F0])
    rbv = rb0.ap.rearrange("p (b h) -> p b h", h=8)
    for di, d in enumerate(DIL):
        js = np.arange(-64, 65)
        bk = t5_bucket_np(js * d)
        s0 = 0
        while s0 < len(js):
            e0 = s0
            while e0 + 1 < len(js) and bk[e0 + 1] == bk[s0]:
                e0 += 1
            ln = e0 - s0 + 1
            x0 = int(js[s0]) + 192
            bb = int(bk[s0])
            for h in range(8):
                k.ts("dve", F0.ap[:, di, h, x0:x0 + ln], F0.ap[:, di, h, x0:x0 + ln], rbv[:, bb, h:h + 1], ALU.add, r=[rb0, F0], w=[F0])
            s0 = e0 + 1
    fdt = T(None, "Fd")
    for di in range(3):
        k.dma("sp", Fd[di * 4096:(di + 1) * 4096].rearrange("(o n) -> o n", o=1), F0.ap[0:1, di].rearrange("p b c -> p (b c)"), r=[F0], w=[fdt])
    for di in range(3):
        for h in range(8):
            j = di * 8 + h
            k.dma("sp", Tt.ap[:, j, :], bass.AP(Fd.tensor, j * 512, [[1, 128], [1, 257]]), r=[fdt], w=[Tt])
    ttb = Tt.ap[:, 0, 0:1]
    pstep = list(Tt.ap.ap[0])
    for di in range(3):
        for ab, c0 in ((0, 128), (1, 256)):
            src = bass.AP(Tt.ap.tensor, Tt.ap[:, di * 8, c0:c0 + 1].offset, [pstep, [257, 8], [-1, 128]])
            k.cp("dve", EB.ap[:, di, :, ab, :], src, r=[Tt], w=[EB])
    A.pop()
    P.barrier()
    import os
    STOP = int(os.environ.get("PB_STOP", "9"))
    if STOP <= 1:
        A.pop()
        return
    ONEH = A.alloc([2, 128], BF16, "ONEH")
    k.memset("pool", ONEH.ap, 0.0, w=[ONEH])
    k.memset("pool", ONEH.ap[:, 0, 0:64], 1.0, w=[ONEH])
    k.memset("pool", ONEH.ap[:, 1, 64:128], 1.0, w=[ONEH])
    gc = A.alloc([8], F32, "gcB")
    k.dma("sp", gc.ap[:, 0:4], g_att[0].rearrange("(k p) -> p k", p=128), w=[gc], slow=True)
    k.dma("sp", gc.ap[:, 4:8], g_ssm[0].rearrange("(k p) -> p k", p=128), w=[gc], slow=True)
    bgl = C.gcol(b_glu[0], 512, "bgl")
    wout = A.alloc([8, 1024], BF16, "wout")
    wglu = A.alloc([4, 512], BF16, "wglu")
    C.load_weight_bf16(wout, w_out[0], 8, 1024, gc, "wout")
    C.load_weight_bf16(wglu, w_glu[0], 4, 512, None, "wglu")
    qs = A.alloc([2, 2048], BF16, "qs")
    ks = A.alloc([2, 4096], BF16, "ks")
    acc = A.alloc([4, 2048], F32, "acc")
    dsw = A.alloc([2, 2048], F32, "dsw")
    attT = A.alloc([4, 2048], BF16, "attT")
    pTs = [A.alloc([8, 128], BF16, "pT%d" % i) for i in range(3)]
    pTbs = [T(None, "pTb%d" % i) for i in range(3)]
    vas = [A.alloc([4, 128], BF16, "va%d" % i) for i in range(4)]
    vbs = [A.alloc([4, 128], BF16, "vb%d" % i) for i in range(4)]
    dflat = dsw.ap.rearrange("p a b -> p (a b)")
    accflat = acc.ap.rearrange("p a b -> p (a b)")

    def alias_bf(i, name):
        return T(dflat[:, i * 1024:(i + 1) * 1024].bitcast(BF16).rearrange("p (a b) -> p a b", a=4), name)
    gts = [alias_bf(i, "gt%d" % i) for i in range(2)]
    sqbs = [alias_bf(2 + i, "sqb%d" % i) for i in range(2)]
    sqbs2 = [T(accflat[:, i * 1024:(i + 1) * 1024].bitcast(BF16).rearrange("p (a b) -> p a b", a=4), "sqc%d" % i) for i in range(2)]
    sTs = [A.alloc([4, 512], BF16, "sT%d" % i) for i in range(2)]
    rs5s = [A.alloc([512], F32, "rs5%d" % i) for i in range(4)]
    sigs = [A.alloc([512], BF16, "sig%d" % i) for i in range(2)]
    mixTs = [A.alloc([8, 512], BF16, "mixT%d" % i) for i in range(2)]
    xb = [A.alloc([1024], F32, "xB%d" % i) for i in range(3)]
    pend = []
    bi = 0
    for SB in range(NSB):
        tok0 = SB * 2048
        for half in range(2):
            k.dma("sp", qs.ap, qT[2 * half:2 * half + 2, :, tok0:tok0 + 2048].rearrange("f p t -> p f t"), w=[qs])
            lo, hi = tok0 - 1024, tok0 + 3072
            vlo, vhi = max(lo, 0), min(hi, S)
            if vlo > lo:
                k.memset("pool", ks.ap[:, :, 0:vlo - lo], 0.0, w=[ks])
            if vhi < hi:
                k.memset("pool", ks.ap[:, :, vhi - lo:4096], 0.0, w=[ks])
            k.dma("sp", ks.ap[:, :, vlo - lo:vhi - lo], kT[2 * half:2 * half + 2, :, vlo:vhi].rearrange("f p t -> p f t"), w=[ks])
            blocks = []
            first = True
            for di, d in enumerate(DIL):
                if d not in DSEL:
                    continue
                for r in range(d):
                    for mb in range(16 // d):
                        blocks.append((di, d, r, mb, first))
                first = False

            def emit_scores(j):
                di, d, r, mb, isfirst = blocks[j]
                q0 = r + d * 128 * mb
                va, vb = vas[j % 4], vbs[j % 4]
                rowA = 1024 + tok0 + q0 - 64 * d
                rowB = 1024 + tok0 + q0 + 64 * d
                k.dma("sp", va.ap, Vp[rowA:rowA + 127 * d + 1:d, 4 * half:4 * half + 4, :], w=[va])
                k.dma("sp", vb.ap, Vp[rowB:rowB + 127 * d + 1:d, 4 * half:4 * half + 4, :], w=[vb])
                for hl in range(4):
                    p0, hpl = 64 * (hl % 2), hl // 2
                    qa = strided(qs.ap[p0:p0 + 64, hpl, q0:q0 + 1], d, 128)
                    for ab in range(2):
                        kc = 1024 + q0 + (-64 * d if ab == 0 else 64 * d)
                        ka = strided(ks.ap[p0:p0 + 64, hpl, kc:kc + 1], d, 128)
                        pss = PS[2 * (j % 3) + hl % 2]
                        cslot = (hl // 2) * 2 + ab
                        k.mm(pss.ap[:, cslot * 128:(cslot + 1) * 128], ka, qa, True, True, r=[ks, qs], w=[pss])

            def emit_soft(j):
                di, d, r, mb, isfirst = blocks[j]
                pT, pTb = pTs[j % 3], pTbs[j % 3]
                ps0, ps1 = PS[2 * (j % 3)], PS[2 * (j % 3) + 1]
                pTf = pT.ap.rearrange("p a b -> p (a b)")
                k.act(pTf[:, 0:512], ps0.ap, AF.Exp, r=[ps0], w=[pT])
                k.act(pTf[:, 512:1024], ps1.ap, AF.Exp, r=[ps1], w=[pTb])
                for hh in range(2):
                    ebv = EB.ap[:, di, 4 * half + hh:4 * half + 4:2, :, :]
                    pv4 = pT.ap[:, hh * 4:(hh + 1) * 4, :].rearrange("p (a b) c -> p a b c", b=2)
                    tok = pT if hh == 0 else pTb
                    k.tt("dve" if hh == 0 else "pool", pv4, pv4, ebv, ALU.mult, r=[tok, EB], w=[tok])

            def emit_pv(j):
                di, d, r, mb, isfirst = blocks[j]
                q0 = r + d * 128 * mb
                pT, pTb, va, vb = pTs[j % 3], pTbs[j % 3], vas[j % 4], vbs[j % 4]
                pnd = PS[6 + j % 2]
                for hl in range(4):
                    hh, hpl = hl % 2, hl // 2
                    for ab in range(2):
                        vt = va if ab == 0 else vb
                        k.mm(pnd.ap[:, hl * 128:(hl + 1) * 128], vt.ap[:, hl, :], pT.ap[:, hh * 4 + hpl * 2 + ab, :], ab == 0, ab == 1, r=[vt, pT, pTb], w=[pnd])
                av = ap3(acc.ap[:, 0, q0:q0 + 1], [2048, 4], [d, 128])
                pv = pnd.ap.rearrange("p (a b) -> p a b", a=4)
                if isfirst:
                    k.cp("act", av, pv, r=[pnd], w=[acc])
                else:
                    k.tt("dve", av, pv, av, ALU.add, r=[pnd, acc], w=[acc])
            nb_ = len(blocks)
            emit_scores(0)
            if nb_ > 1:
                emit_scores(1)
            emit_soft(0)
            for j in range(nb_):
                if j + 2 < nb_:
                    emit_scores(j + 2)
                if j + 1 < nb_:
                    emit_soft(j + 1)
                emit_pv(j)
            for hpl in range(2):
                k.dma("sp", dsw.ap[0:64, hpl, :], acc.ap[64:128, 2 * hpl, :], r=[acc], w=[dsw])
                k.dma("sp", dsw.ap[64:128, hpl, :], acc.ap[0:64, 2 * hpl + 1, :], r=[acc], w=[dsw])
            k.act(dsw.ap, dsw.ap, AF.Ln, r=[dsw], w=[dsw])
            k.act(dsw.ap, dsw.ap, AF.Exp, scale=-1.0, r=[dsw], w=[dsw])
            for hpl in range(2):
                k.tt("dve", attT.ap[0:64, 2 * half + hpl, :], acc.ap[0:64, 2 * hpl, :], dsw.ap[0:64, hpl, :], ALU.mult, r=[acc, dsw], w=[attT])
                k.tt("pool", attT.ap[64:128, 2 * half + hpl, :], acc.ap[64:128, 2 * hpl + 1, :], dsw.ap[64:128, hpl, :], ALU.mult, r=[acc, dsw], w=[attT])
        def stageN(bb):
            t0 = tok0 + bb * 512
            mx, sq_, sT_, gt_ = mixTs[bb % 2], sqbs[bb % 2], sTs[bb % 2], gts[bb % 2]
            rsa, rss = rs5s[2 * (bb % 2)], rs5s[2 * (bb % 2) + 1]
            av = attT.ap[:, :, bb * 512:(bb + 1) * 512]
            k.dma("sp", gt_.ap, gT[:, :, t0:t0 + 512].rearrange("f p t -> p f t"), w=[gt_])
            k.act(sq_.ap, av, AF.Square, r=[attT], w=[sq_])
            for kk in range(4):
                k.mm(PS[0].ap, C.ones_b.ap, sq_.ap[:, kk, :], kk == 0, kk == 3, r=[sq_, C.ones_b], w=[PS[0]])
            k.ts("dve", rsa.ap, PS[0].ap, 1.0 / 512, ALU.mult, EPS, ALU.add, r=[PS[0]], w=[rsa])
            for j in range(4):
                sg_, pz = sigs[j % 2], PS[2 + j % 2]
                for kk in range(4):
                    k.mm(pz.ap, wglu.ap[:, kk, j * 128:(j + 1) * 128], gt_.ap[:, kk, :], kk == 0, kk == 3, r=[wglu, gt_], w=[pz])
                k.act(sg_.ap, pz.ap, AF.Sigmoid, bias=bgl.ap[:, j:j + 1], r=[pz, bgl], w=[sg_])
                k.tt("dve", sT_.ap[:, j, :], gt_.ap[:, j, :], sg_.ap, ALU.mult, r=[gt_, sg_], w=[sT_])
            sq2 = sqbs2[bb % 2]
            k.act(sq2.ap, sT_.ap, AF.Square, r=[sT_], w=[sq2])
            for kk in range(4):
                k.mm(PS[1].ap, C.ones_b.ap, sq2.ap[:, kk, :], kk == 0, kk == 3, r=[sq2, C.ones_b], w=[PS[1]])
            k.ts("dve", rss.ap, PS[1].ap, 1.0 / 512, ALU.mult, EPS, ALU.add, r=[PS[1]], w=[rss])
            for r_ in (rsa, rss):
                k.act(r_.ap, r_.ap, AF.Ln, r=[r_], w=[r_])
            for r_ in (rsa, rss):
                k.act(r_.ap, r_.ap, AF.Exp, scale=-0.5, r=[r_], w=[r_])
            k.tt("dve", mx.ap[:, 0:4, :], av, rsa.ap.unsqueeze(1).to_broadcast([128, 4, 512]), ALU.mult, r=[attT, rsa], w=[mx])
            k.tt("pool", mx.ap[:, 4:8, :], sT_.ap, rss.ap.unsqueeze(1).to_broadcast([128, 4, 512]), ALU.mult, r=[sT_, rss], w=[mx])

        def stageW(bb):
            t0 = tok0 + bb * 512
            mx = mixTs[bb % 2]
            for t in range(4):
                ti = t0 // 128 + t
                xt = xb[ti % 3]
                k.dma("sp", xt.ap, x[ti * 128:(ti + 1) * 128, :], w=[xt])
                if pend:
                    pti, pxt = pend.pop()
                    k.dma("pool", hs[pti * 128:(pti + 1) * 128, :], pxt.ap, r=[pxt])
                for h2 in range(2):
                    po = PS[4 + (2 * t + h2) % 4]
                    cs = slice(h2 * 512, (h2 + 1) * 512)
                    for kk in range(8):
                        k.mm(po.ap, mx.ap[:, kk, t * 128:(t + 1) * 128], wout.ap[:, kk, cs], kk == 0, kk == 7, r=[mx, wout], w=[po])
                    k.tt("dve", xt.ap[:, cs], po.ap, xt.ap[:, cs], ALU.add, r=[po, xt], w=[xt])
                pend.append((ti, xt))
        if STOP > 3:
            P.barrier()
            stageN(0)
            for bb in range(4):
                if bb + 1 < 4:
                    stageN(bb + 1)
                stageW(bb)
            P.barrier()
    while pend:
        pti, pxt = pend.pop()
        k.dma("pool", hs[pti * 128:(pti + 1) * 128, :], pxt.ap, r=[pxt])
    A.pop()


def norm_T(C, src_ap, ss, tm, rs, junk, a_, pst, dstT, col0, ncols=1024, r_src=(), cp_eng="act"):
    k = C.k
    nk = ncols // 128
    k.memset("pool", ss.ap, 0.0, w=[ss])
    k.act(junk.ap[:, 0:ncols], src_ap, AF.Square, accum=ss.ap, r=list(r_src) + [ss], w=[junk, ss])
    C.rms_rstd(ss, ncols, tm, rs)
    k.ts("dve", a_.ap[:, 0:ncols], src_ap, rs.ap[:, 0:1], ALU.mult, r=list(r_src) + [rs], w=[a_])
    psv = pst.ap.bitcast(BF16).rearrange("p (a b) -> p a b", a=8)
    for kk in range(nk):
        k.tr(psv[:, kk, :], a_.ap[:, kk * 128:(kk + 1) * 128], C.identb.ap, r=[a_, C.identb], w=[pst])
    k.cp(cp_eng, dstT.ap[:, 0:nk, col0:col0 + 128], psv[:, 0:nk, :], r=[pst], w=[dstT])


def phase_C(C, hs, hs2, g_mlp, w_mlp1, w_mlp2):
    k, A, PS, S = C.k, C.A, C.PS, C.S
    A.push()
    gm = C.gcol(g_mlp[0], 1024, "gmlp")
    w1 = A.alloc([8, 4096], BF16, "w1")
    w2 = A.alloc([32, 1024], BF16, "w2")
    C.load_weight_bf16(w1, w_mlp1[0], 8, 4096, gm, "w1")
    C.load_weight_bf16(w2, w_mlp2[0], 32, 1024, None, "w2")
    hb = [A.alloc([2, 1024], F32, "hc%d" % i) for i in range(3)]
    pendc = []
    junk = A.alloc([1024], F32, "junkc")
    ab = [A.alloc([1024], BF16, "abc%d" % i) for i in range(2)]
    ss = [A.alloc([1], F32, "ssc%d" % i) for i in range(2)]
    tmp = [A.alloc([1], F32, "tmpc%d" % i) for i in range(2)]
    rstd = [A.alloc([1], F32, "rstdc%d" % i) for i in range(2)]
    fT = A.alloc([8, 256], BF16, "fT")
    hidT = A.alloc([32, 256], BF16, "hidT")
    rl = [A.alloc([256], F32, "rl%d" % i) for i in range(2)]
    fTs = [fT, A.alloc([8, 256], BF16, "fT1")]

    def cpre(blk):
        h = hb[blk % 3]
        k.dma("sp", h.ap, hs[blk * 256:(blk + 1) * 256, :].rearrange("(t p) d -> p t d", p=128), w=[h])
        for t in range(2):
            k.memset("pool", ss[t].ap, 0.0, w=[ss[t]])
            k.act(junk.ap, h.ap[:, t, :], AF.Square, accum=ss[t].ap, r=[h, ss[t]], w=[junk, ss[t]])
            C.rms_rstd(ss[t], 1024, tmp[t], rstd[t])
            k.ts("dve", ab[t].ap, h.ap[:, t, :], rstd[t].ap[:, 0:1], ALU.mult, r=[h, rstd[t]], w=[ab[t]])

    def ctr(blk):
        f_ = fTs[blk % 2]
        for t in range(2):
            pst = PS[t]
            psv = pst.ap.bitcast(BF16).rearrange("p (a b) -> p a b", a=8)
            for kk in range(8):
                k.tr(psv[:, kk, :], ab[t].ap[:, kk * 128:(kk + 1) * 128], C.identb.ap, r=[ab[t], C.identb], w=[pst])
            k.cp("act", f_.ap[:, :, t * 128:(t + 1) * 128], psv, r=[pst], w=[f_])

    def cup(blk):
        f_ = fTs[blk % 2]
        for j in range(32):
            pz = PS[2 + j % 2]
            for kk in range(8):
                k.mm(pz.ap[:, 0:256], w1.ap[:, kk, j * 128:(j + 1) * 128], f_.ap[:, kk, :], kk == 0, kk == 7, r=[w1, f_], w=[pz])
            r_ = rl[j % 2]
            k.act(r_.ap, pz.ap[:, 0:256], AF.Relu, r=[pz], w=[r_])
            k.tt("dve" if j % 2 == 0 else "pool", hidT.ap[:, j, :], r_.ap, r_.ap, ALU.mult, r=[r_], w=[hidT])

    def cdown(blk):
        h = hb[blk % 3]
        for t in range(2):
            for half in range(2):
                po = PS[4 + (2 * t + half) % 4]
                for j in range(32):
                    k.mm(po.ap, hidT.ap[:, j, t * 128:(t + 1) * 128], w2.ap[:, j, half * 512:(half + 1) * 512], j == 0, j == 31, r=[hidT, w2], w=[po])
                k.tt("dve", h.ap[:, t, half * 512:(half + 1) * 512], po.ap, h.ap[:, t, half * 512:(half + 1) * 512], ALU.add, r=[po, h], w=[h])
        if pendc:
            pb_, ph_ = pendc.pop()
            k.dma("pool", hs2[pb_ * 256:(pb_ + 1) * 256, :].rearrange("(t p) d -> p t d", p=128), ph_.ap, r=[ph_])
        pendc.append((blk, h))
    NBC = S // 256
    cpre(0)
    ctr(0)
    if NBC > 1:
        cpre(1)
    for blk in range(NBC):
        cup(blk)
        if blk + 1 < NBC:
            ctr(blk + 1)
        cdown(blk)
        if blk + 2 < NBC:
            cpre(blk + 2)
    while pendc:
        pb_, ph_ = pendc.pop()
        k.dma("pool", hs2[pb_ * 256:(pb_ + 1) * 256, :].rearrange("(t p) d -> p t d", p=128), ph_.ap, r=[ph_])
    A.pop()


def phase_D(C, hs2, pin, y, g_ple, w_gate, w_proj, g_final):
    k, A, PS, S, NT = C.k, C.A, C.PS, C.S, C.NT
    A.push()
    gp = C.gcol(g_ple[0], 1024, "gple")
    wg = A.alloc([8, 1024], BF16, "wg")
    wp = A.alloc([2, 1024], BF16, "wp")
    C.load_weight_bf16(wg, w_gate[0], 8, 1024, gp, "wg")
    C.load_weight_bf16(wp, w_proj[0], 2, 1024, None, "wp")
    gfin = A.alloc([1024], F32, "gfin")
    k.dma("sp", gfin.ap, g_final.rearrange("(o d) -> o d", o=1).partition_broadcast(128), w=[gfin])
    hb = [A.alloc([1024], F32, "hd%d" % i) for i in range(6)]
    pb = [A.alloc([256], F32, "pd%d" % i) for i in range(4)]
    junk = A.alloc([1024], F32, "junkd")
    ab = [A.alloc([1024], BF16, "abd%d" % i) for i in range(4)]
    pbb = [A.alloc([256], BF16, "pbb%d" % i) for i in range(4)]
    ss = [A.alloc([1], F32, "ssd%d" % i) for i in range(6)]
    tmp = [A.alloc([1], F32, "tmpd%d" % i) for i in range(6)]
    rstd = [A.alloc([1], F32, "rstdd%d" % i) for i in range(6)]
    eT = [A.alloc([8, 128], BF16, "eT%d" % i) for i in range(3)]
    pT = [A.alloc([2, 128], BF16, "pT%d" % i) for i in range(3)]
    sg = [A.alloc([512], F32, "sg%d" % i) for i in range(2)]
    def s1pre(ti):
        h, p_, pb_, a_ = hb[ti % 6], pb[ti % 4], pbb[ti % 4], ab[ti % 4]
        i = ti % 4
        k.dma("sp", h.ap, hs2[ti * 128:(ti + 1) * 128, :], w=[h])
        k.dma("sp", p_.ap, pin[ti * 128:(ti + 1) * 128, :], w=[p_])
        k.memset("pool", ss[i].ap, 0.0, w=[ss[i]])
        k.act(junk.ap, h.ap, AF.Square, accum=ss[i].ap, r=[h, ss[i]], w=[junk, ss[i]])
        C.rms_rstd(ss[i], 1024, tmp[i], rstd[i])
        k.ts("dve", a_.ap, h.ap, rstd[i].ap[:, 0:1], ALU.mult, r=[h, rstd[i]], w=[a_])
        k.cp("pool", pb_.ap, p_.ap, r=[p_], w=[pb_])

    def s1tr(ti):
        e_, pT_, pb_, a_ = eT[ti % 3], pT[ti % 3], pbb[ti % 4], ab[ti % 4]
        pst = PS[ti % 2]
        psv = pst.ap.bitcast(BF16).rearrange("p (a b) -> p a b", a=8)
        for kk in range(8):
            k.tr(psv[:, kk, :], a_.ap[:, kk * 128:(kk + 1) * 128], C.identb.ap, r=[a_, C.identb], w=[pst])
        k.cp("dve", e_.ap, psv, r=[pst], w=[e_])
        pst2 = PS[2 + ti % 2]
        psv2 = pst2.ap.bitcast(BF16).rearrange("p (a b) -> p a b", a=8)
        for kk in range(2):
            k.tr(psv2[:, kk, :], pb_.ap[:, kk * 128:(kk + 1) * 128], C.identb.ap, r=[pb_, C.identb], w=[pst2])
        k.cp("dve", pT_.ap, psv2[:, 0:2, :], r=[pst2], w=[pT_])

    def stage2a(ti):
        e_, pT_ = eT[ti % 3], pT[ti % 3]
        for half in range(2):
            pg, pp = PS[4 + half], PS[6 + half]
            cs = slice(half * 512, (half + 1) * 512)
            for kk in range(8):
                k.mm(pg.ap, e_.ap[:, kk, :], wg.ap[:, kk, cs], kk == 0, kk == 7, r=[e_, wg], w=[pg])
            for kk in range(2):
                k.mm(pp.ap, pT_.ap[:, kk, :], wp.ap[:, kk, cs], kk == 0, kk == 1, r=[pT_, wp], w=[pp])

    def stage2b(ti):
        h = hb[ti % 6]
        for half in range(2):
            pg, pp, s_ = PS[4 + half], PS[6 + half], sg[half]
            cs = slice(half * 512, (half + 1) * 512)
            k.act(s_.ap, pg.ap, AF.Sigmoid, r=[pg], w=[s_])
            k.tt("dve", s_.ap, pp.ap, s_.ap, ALU.mult, r=[pp, s_], w=[s_])
            k.tt("pool", h.ap[:, cs], h.ap[:, cs], s_.ap, ALU.add, r=[h, s_], w=[h])
        j = 4 + ti % 2
        k.memset("pool", ss[j].ap, 0.0, w=[ss[j]])
        k.act(junk2.ap, h.ap, AF.Square, accum=ss[j].ap, r=[h, ss[j]], w=[junk2, ss[j]])
        C.rms_rstd(ss[j], 1024, tmp[j], rstd[j])
        k.stt(h.ap, h.ap, rstd[j].ap[:, 0:1], gfin.ap, ALU.mult, ALU.mult, r=[h, rstd[j], gfin], w=[h])

    def store(ti):
        k.dma("pool", y[ti * 128:(ti + 1) * 128, :], hb[ti % 6].ap, r=[hb[ti % 6]])

    junk2 = A.alloc([1024], F32, "junkd2")
    s1pre(0)
    s1tr(0)
    s1pre(1)
    s1tr(1)
    s1pre(2)
    for ti in range(NT):
        stage2a(ti)
        if ti + 2 < NT:
            s1tr(ti + 2)
        if ti + 3 < NT:
            s1pre(ti + 3)
        if ti >= 1:
            store(ti - 1)
        stage2b(ti)
    store(NT - 1)
    A.pop()


_NC_CACHE = {}


def kernel(**inputs):
    S = 8192
    xs = [inputs["x_prompt"][i] for i in range(2)] + [inputs["x_sample"][i] for i in range(4)]
    ps = [inputs["p_prompt"][0, i] for i in range(2)] + [inputs["p_sample"][0, i] for i in range(4)]
    xs += [np.zeros_like(xs[0]), np.zeros_like(xs[0])]
    ps += [np.zeros_like(ps[0]), np.zeros_like(ps[0])]
    if "nc" not in _NC_CACHE:
        _NC_CACHE["nc"] = build(S)
    nc = _NC_CACHE["nc"]
    wnames = ["rel_bias", "g_mix", "w_in", "ssm_a_re", "ssm_a_im", "ssm_log_dt", "ssm_b_re", "ssm_b_im", "ssm_c_re",
              "ssm_c_im", "ssm_d", "w_glu", "b_glu", "g_att_out", "g_ssm_out", "w_out", "g_mlp", "w_mlp1", "w_mlp2",
              "g_ple", "w_ple_gate", "w_ple_proj", "g_final"]
    in_maps = []
    for c in range(8):
        m = {"x": np.ascontiguousarray(xs[c], dtype=np.float32), "p": np.ascontiguousarray(ps[c], dtype=np.float32)}
        for n in wnames:
            m[n] = np.ascontiguousarray(inputs[n], dtype=np.float32)
        in_maps.append(m)
    res = run_bass_kernel_spmd(nc, in_maps, core_ids=list(range(8)))
    outs = [np.asarray(r["y"], dtype=np.float32) for r in res.results]
    y_prompt = np.stack(outs[0:2], axis=0)
    y_sample = np.stack(outs[2:6], axis=0)
    return (y_prompt, y_sample)
```

```python
import math
import numpy as np
from contextlib import ExitStack
import concourse.bass as bass
import concourse.mybir as mybir
from concourse.bass_utils import run_bass_kernel_spmd

F32 = mybir.dt.float32
BF16 = mybir.dt.bfloat16
I32 = mybir.dt.int32
AF = mybir.ActivationFunctionType
ALU = mybir.AluOpType

ENGS = ("pe", "act", "dve", "pool", "sp")
D = 1024
EPS = 1e-6


class Buf:
    __slots__ = ("name", "last_w", "readers")

    def __init__(self, name=""):
        self.name = name
        self.last_w = None
        self.readers = []


class Op:
    __slots__ = ("eng", "fn", "reads", "writes", "is_dma", "deps", "idx", "signal", "count", "sem", "semkey")

    def __init__(self, eng, fn, reads, writes, is_dma):
        self.eng = eng
        self.fn = fn
        self.reads = reads
        self.writes = writes
        self.is_dma = is_dma
        self.deps = set()
        self.signal = False
        self.count = 0
        self.sem = None
        self.semkey = None


class Prog:
    def __init__(self, nc, n_dma_sems=40):
        self.nc = nc
        self.ops = []
        self.n_dma_sems = n_dma_sems
        self.ALL = Buf("ALL")

    def op(self, eng, fn, reads=(), writes=()):
        self._add(Op(eng, fn, tuple(reads) + (self.ALL,), tuple(writes), False))

    def dma(self, queue, fn, reads=(), writes=()):
        self._add(Op(queue, fn, tuple(reads) + (self.ALL,), tuple(writes), True))

    def barrier(self):
        self._add(Op("dve", lambda e: e.nop() if False else e.memset(self._bar[:], 0.0), (), (self.ALL,), False))

    def _add(self, o):
        o.idx = len(self.ops)
        for b in o.reads:
            if b.last_w is not None:
                o.deps.add(b.last_w)
        for b in o.writes:
            if b.last_w is not None:
                o.deps.add(b.last_w)
            for r in b.readers:
                o.deps.add(r)
        for b in o.reads:
            b.readers.append(o.idx)
        for b in o.writes:
            b.last_w = o.idx
            b.readers = []
        o.deps.discard(o.idx)
        self.ops.append(o)

    def emit(self, stack):
        nc = self.nc
        ops = self.ops
        needed = []
        for o in ops:
            nd = []
            for d in o.deps:
                y = ops[d]
                if y.is_dma or o.is_dma or y.eng != o.eng:
                    nd.append(d)
                elif o.eng != "pe":
                    if any((b in y.writes) for b in o.reads if b is not self.ALL) or \
                       any((b in y.writes or b in y.reads) for b in o.writes if b is not self.ALL):
                        nd.append(d)
            needed.append(nd)
            for d in nd:
                ops[d].signal = True
        for o in ops:
            if o.is_dma:
                o.signal = True
        eng_sem = {e: stack.enter_context(nc.semaphore("s_" + e)) for e in ENGS}
        nqs = {"sp": self.n_dma_sems - 16, "pool": 4, "act": 6, "dve": 6}
        dma_sems = {q: [stack.enter_context(nc.semaphore("d%s%d" % (q, i))) for i in range(nqs[q])] for q in nqs}
        cnt = {e: 0 for e in ENGS}
        dcnt = {q: [0] * nqs[q] for q in nqs}
        rr = {q: 0 for q in nqs}
        for o in ops:
            if o.is_dma:
                q = o.eng
                nq = nqs[q]
                k = rr[q] % nq
                rr[q] += 1
                dcnt[q][k] += 16
                o.sem, o.count, o.semkey = dma_sems[q][k], dcnt[q][k], ("d", q, k)
            elif o.signal:
                cnt[o.eng] += 1
                o.sem, o.count, o.semkey = eng_sem[o.eng], cnt[o.eng], ("e", o.eng)
        waited = {e: {} for e in ENGS}
        block = stack.enter_context(nc.Block())
        per_eng = {e: [] for e in ENGS}
        for o in ops:
            per_eng[o.eng].append(o)
        final_waits = {}
        for o in ops:
            if o.is_dma:
                final_waits[o.semkey] = (o.sem, o.count)

        def make(ename):
            def body(eng):
                w = waited[ename]
                for o in per_eng[ename]:
                    req = {}
                    for d in needed[o.idx]:
                        y = ops[d]
                        if req.get(y.semkey, (None, 0))[1] < y.count:
                            req[y.semkey] = (y.sem, y.count)
                    for key, (sem, c) in req.items():
                        if w.get(key, 0) < c:
                            eng.wait_ge(sem, c)
                            w[key] = c
                    if o.is_dma and o.count > 16 and w.get(o.semkey, 0) < o.count - 16:
                        eng.wait_ge(o.sem, o.count - 16)
                        w[o.semkey] = o.count - 16
                    ins = o.fn(eng)
                    if o.signal:
                        ins.then_inc(o.sem, 16 if o.is_dma else 1)
                if ename == "sp":
                    for key, (sem, c) in final_waits.items():
                        if w.get(key, 0) < c:
                            eng.wait_ge(sem, c)
                            w[key] = c
            return body

        block.tensor(make("pe"))
        block.scalar(make("act"))
        block.vector(make("dve"))
        block.gpsimd(make("pool"))
        block.sync(make("sp"))


class T:
    __slots__ = ("ap", "b")

    def __init__(self, ap, name=""):
        self.ap = ap
        self.b = Buf(name)


class Arena:
    def __init__(self, big, ncols):
        self.big = big
        self.ncols = ncols
        self.off = 0
        self.marks = []

    def push(self):
        self.marks.append(self.off)

    def pop(self):
        self.off = self.marks.pop()

    def alloc(self, free_shape, dtype, name=""):
        n = int(np.prod(free_shape))
        n32 = n if dtype in (F32, I32) else (n + 1) // 2
        assert self.off + n32 <= self.ncols, ("SBUF arena overflow", name, self.off, n32, self.ncols)
        v = self.big[:, self.off:self.off + n32]
        self.off += n32
        if dtype not in (F32,):
            v = v.bitcast(dtype)
            if n % 2 and dtype == BF16:
                v = v[:, 0:n]
        if len(free_shape) == 2:
            v = v.rearrange("p (a b) -> p a b", a=free_shape[0])
        elif len(free_shape) == 3:
            v = v.rearrange("p (a b c) -> p a b c", a=free_shape[0], b=free_shape[1])
        elif len(free_shape) == 4:
            v = v.rearrange("p (a b c d) -> p a b c d", a=free_shape[0], b=free_shape[1], c=free_shape[2])
        return T(v, name)


def _bs(ts_):
    return [t.b for t in ts_]


class K:
    def __init__(self, P):
        self.P = P

    def dma(self, q, out, in_, r=(), w=(), slow=False):
        if slow:
            self.P.dma(q, lambda e: e.dma_start(out=out, in_=in_, allow_slow_non_contiguous=True), _bs(r), _bs(w))
        else:
            self.P.dma(q, lambda e: e.dma_start(out=out, in_=in_), _bs(r), _bs(w))

    def mm(self, out, lhsT, rhs, start, stop, r=(), w=()):
        self.P.op("pe", lambda e: e.matmul(out, lhsT=lhsT, rhs=rhs, start=start, stop=stop), _bs(r), _bs(w))

    def tr(self, out, in_, ident, r=(), w=()):
        self.P.op("pe", lambda e: e.transpose(out, in_, ident), _bs(r), _bs(w))

    def act(self, out, in_, func, r=(), w=(), scale=None, bias=None, accum=None):
        kw = {}
        if scale is not None:
            kw["scale"] = scale
        if bias is not None:
            kw["bias"] = bias
        if accum is not None:
            kw["accum_out"] = accum
        self.P.op("act", lambda e: e.activation(out=out, in_=in_, func=func, **kw), _bs(r), _bs(w))

    def tt(self, eng, out, a, b, op, r=(), w=()):
        self.P.op(eng, lambda e: e.tensor_tensor(out=out, in0=a, in1=b, op=op), _bs(r), _bs(w))

    def ts(self, eng, out, a, s1, op0, s2=None, op1=None, r=(), w=()):
        if op1 is None:
            self.P.op(eng, lambda e: e.tensor_scalar(out=out, in0=a, scalar1=s1, scalar2=None, op0=op0), _bs(r), _bs(w))
        else:
            self.P.op(eng, lambda e: e.tensor_scalar(out=out, in0=a, scalar1=s1, scalar2=s2, op0=op0, op1=op1), _bs(r), _bs(w))

    def stt(self, out, a, s, b, op0, op1, r=(), w=()):
        self.P.op("dve", lambda e: e.scalar_tensor_tensor(out=out, in0=a, scalar=s, in1=b, op0=op0, op1=op1), _bs(r), _bs(w))

    def cp(self, eng, out, in_, r=(), w=()):
        if eng == "act":
            self.P.op("act", lambda e: e.activation(out=out, in_=in_, func=AF.Copy), _bs(r), _bs(w))
        else:
            self.P.op(eng, lambda e: e.tensor_copy(out=out, in_=in_), _bs(r), _bs(w))

    def memset(self, eng, ap, val, w=()):
        self.P.op(eng, lambda e: e.memset(ap, val), (), _bs(w))

    def recip(self, out, in_, r=(), w=()):
        self.P.op("dve", lambda e: e.reciprocal(out=out, in_=in_), _bs(r), _bs(w))

    def scan(self, out, d0, d1, init, r=(), w=()):
        self.P.op("dve", lambda e: e.tensor_tensor_scan(out=out, data0=d0, data1=d1, initial=init, op0=ALU.mult, op1=ALU.add), _bs(r), _bs(w))

    def asel(self, out, in_, pattern, cmp, fill, base, cm, r=(), w=()):
        self.P.op("pool", lambda e: e.affine_select(out=out, in_=in_, pattern=pattern, compare_op=cmp, fill=fill, base=base, channel_multiplier=cm), _bs(r), _bs(w))


def rev_ap(ap2d_last_col, n):
    return bass.AP(ap2d_last_col.tensor, ap2d_last_col.offset, [list(ap2d_last_col.ap[0]), [-1, n]])


def strided(ap_first, step, n):
    return bass.AP(ap_first.tensor, ap_first.offset, [list(ap_first.ap[0]), [step, n]])


def t5_bucket_np(rel):
    half = 16
    n = -rel
    ret = np.where(n < 0, half, 0)
    n = np.abs(n)
    max_exact = 8
    nf = np.maximum(n, 1).astype(np.float32)
    large = max_exact + (np.log(nf / np.float32(max_exact)) / np.float32(math.log(1024 / max_exact)) * (half - max_exact)).astype(np.int32)
    large = np.minimum(large, half - 1)
    return ret + np.where(n < max_exact, n, large)


class Ctx:
    pass


def build(S, debug=None):
    nc = bass.Bass("TRN2", target_bir_lowering=False)
    NT = S // 128
    C = Ctx()
    C.nc, C.S, C.NT = nc, S, NT
    din = {}

    def inp(name, shape):
        din[name] = nc.dram_tensor(name, list(shape), F32, kind="ExternalInput").ap()
        return din[name]

    x = inp("x", [S, D])
    pin = inp("p", [S, 256])
    rel_bias = inp("rel_bias", [32, 8])
    g_mix = inp("g_mix", [1, 1024])
    w_in = inp("w_in", [1, 1024, 2048])
    a_re = inp("ssm_a_re", [1, 2, 32, 64])
    a_im = inp("ssm_a_im", [1, 2, 32, 64])
    log_dt = inp("ssm_log_dt", [1, 2, 32])
    b_re = inp("ssm_b_re", [1, 2, 32, 64, 16])
    b_im = inp("ssm_b_im", [1, 2, 32, 64, 16])
    c_re = inp("ssm_c_re", [1, 2, 32, 16, 64])
    c_im = inp("ssm_c_im", [1, 2, 32, 16, 64])
    ssm_d = inp("ssm_d", [1, 32, 16])
    w_glu = inp("w_glu", [1, 512, 512])
    b_glu = inp("b_glu", [1, 512])
    g_att = inp("g_att_out", [1, 512])
    g_ssm = inp("g_ssm_out", [1, 512])
    w_out = inp("w_out", [1, 1024, 1024])
    g_mlp = inp("g_mlp", [1, 1024])
    w_mlp1 = inp("w_mlp1", [1, 1024, 4096])
    w_mlp2 = inp("w_mlp2", [1, 4096, 1024])
    g_ple = inp("g_ple", [1, 1024])
    w_gate = inp("w_ple_gate", [1, 1024, 1024])
    w_proj = inp("w_ple_proj", [1, 256, 1024])
    g_final = inp("g_final", [1024])
    y = nc.dram_tensor("y", [S, D], F32, kind="ExternalOutput").ap()

    def scratch(name, shape, dt):
        kind = "ExternalOutput" if (debug and name in debug) else "Internal"
        return nc.dram_tensor(name, list(shape), dt, kind=kind).ap()

    qT = scratch("qT", [4, 128, S], BF16)
    kT = scratch("kT", [4, 128, S], BF16)
    uT = scratch("uT", [4, 128, S], BF16)
    gT = scratch("gT", [4, 128, S], BF16)
    Vp = scratch("Vp", [S + 2048, 8, 128], BF16)
    hs = scratch("hs", [S, D], F32)
    hs2 = scratch("hs2", [S, D], F32)
    Fd = scratch("Fd", [3 * 8 * 512 + 512], F32)

    with ExitStack() as st:
        NCOL = 49100
        big = st.enter_context(nc.sbuf_tensor("big", [128, NCOL], F32))
        psb = [st.enter_context(nc.psum_tensor("ps%d" % i, [128, 512], F32)) for i in range(8)]
        P = Prog(nc)
        k = K(P)
        A = Arena(big, NCOL)
        bar = A.alloc([1], F32, "bar")
        P._bar = bar.ap
        PS = [T(psb[i][:], "ps%d" % i) for i in range(8)]
        C.P, C.k, C.A, C.PS = P, k, A, PS
        identf = A.alloc([128], F32, "identf")
        identb = A.alloc([128], BF16, "identb")
        ones_b = A.alloc([128], BF16, "ones_b")
        k.memset("pool", identf.ap, 0.0, w=[identf])
        k.asel(identf.ap, identf.ap, [[-1, 128]], ALU.not_equal, 1.0, 0, 1, r=[identf], w=[identf])
        k.cp("dve", identb.ap, identf.ap, r=[identf], w=[identb])
        k.memset("pool", ones_b.ap, 1.0, w=[ones_b])
        C.identf, C.identb, C.ones_b = identf, identb, ones_b

        def gcol(src_1d_ap, n, name):
            t = A.alloc([n // 128], F32, name)
            k.dma("sp", t.ap, src_1d_ap.rearrange("(k p) -> p k", p=128), w=[t], slow=True)
            return t
        C.gcol = gcol

        nhalf = A.alloc([1], F32, "nhalf")
        k.memset("pool", nhalf.ap, -0.5, w=[nhalf])

        def rms_rstd(ss_t, n, tmp_t, out_t, ss_ap=None, wide=False):
            if wide:
                k.ts("dve", tmp_t.ap, ss_t.ap if ss_ap is None else ss_ap, 1.0 / n, ALU.mult, EPS, ALU.add, r=[ss_t], w=[tmp_t])
                k.act(tmp_t.ap, tmp_t.ap, AF.Ln, r=[tmp_t], w=[tmp_t])
                k.act(out_t.ap, tmp_t.ap, AF.Exp, scale=-0.5, r=[tmp_t], w=[out_t])
            else:
                k.ts("pool", tmp_t.ap, ss_t.ap if ss_ap is None else ss_ap, 1.0 / n, ALU.mult, EPS, ALU.add, r=[ss_t], w=[tmp_t])
                k.tt("pool", out_t.ap, tmp_t.ap, nhalf.ap, ALU.pow, r=[tmp_t, nhalf], w=[out_t])
        C.rms_rstd = rms_rstd

        def load_weight_bf16(dst, src2d, nk, ncols, gc, tag):
            A.push()
            tmps = [A.alloc([min(ncols, 2048)], F32, tag + "tmp%d" % i) for i in range(2)]
            i = 0
            for kk in range(nk):
                for c0 in range(0, ncols, 2048):
                    cw = min(2048, ncols - c0)
                    tm = tmps[i % 2]
                    k.dma("sp", tm.ap[:, 0:cw], src2d[kk * 128:(kk + 1) * 128, c0:c0 + cw], w=[tm])
                    if i % 2 == 0:
                        if gc is None:
                            k.cp("dve", dst.ap[:, kk, c0:c0 + cw], tm.ap[:, 0:cw], r=[tm], w=[dst])
                        else:
                            k.ts("dve", dst.ap[:, kk, c0:c0 + cw], tm.ap[:, 0:cw], gc.ap[:, kk:kk + 1], ALU.mult, r=[tm, gc], w=[dst])
                    else:
                        if gc is None:
                            k.cp("act", dst.ap[:, kk, c0:c0 + cw], tm.ap[:, 0:cw], r=[tm], w=[dst])
                        else:
                            k.act(dst.ap[:, kk, c0:c0 + cw], tm.ap[:, 0:cw], AF.Copy, scale=gc.ap[:, kk:kk + 1], r=[tm, gc], w=[dst])
                    i += 1
            A.pop()
            P.barrier()
        C.load_weight_bf16 = load_weight_bf16

        phases = debug.get("phases", "ASBCD") if debug else "ASBCD"
        if "A" in phases:
            phase_A(C, x, g_mix, w_in, qT, kT, uT, Vp)
            P.barrier()
        if "S" in phases:
            phase_S(C, uT, gT, a_re, a_im, log_dt, b_re, b_im, c_re, c_im, ssm_d)
            P.barrier()
        if "B" in phases:
            phase_B(C, x, qT, kT, Vp, gT, hs, rel_bias, Fd, w_glu, b_glu, g_att, g_ssm, w_out)
            P.barrier()
        if "X" in phases:
            for i in range(S // 512):
                k.dma("sp", hs[i * 512:(i + 1) * 512, :], x[i * 512:(i + 1) * 512, :])
            P.barrier()
        if "C" in phases:
            phase_C(C, hs, hs2, g_mlp, w_mlp1, w_mlp2)
            P.barrier()
        if "D" in phases:
            phase_D(C, hs2, pin, y, g_ple, w_gate, w_proj, g_final)
        P.emit(st)
    return nc


def phase_A(C, x, g_mix, w_in, qT, kT, uT, Vp):
    k, A, PS, S, NT = C.k, C.A, C.PS, C.S, C.NT
    A.push()
    gm = C.gcol(g_mix[0], 1024, "gm")
    win = A.alloc([8, 2048], BF16, "win")
    C.load_weight_bf16(win, w_in[0], 8, 2048, gm, "win")
    xb = [A.alloc([1024], F32, "xa%d" % i) for i in range(2)]
    junk = A.alloc([1024], F32, "junk")
    ab = [A.alloc([1024], BF16, "ab%d" % i) for i in range(2)]
    ss = [A.alloc([1], F32, "ss%d" % i) for i in range(2)]
    tmp = [A.alloc([1], F32, "tmp%d" % i) for i in range(2)]
    rstd = [A.alloc([1], F32, "rstd%d" % i) for i in range(2)]
    aT = [A.alloc([8, 512], BF16, "aT%d" % i) for i in range(2)]
    zst = [A.alloc([512], BF16, "zst%d" % i) for i in range(4)]
    vst = [A.alloc([8, 128], BF16, "vst%d" % i) for i in range(2)]
    zero = A.alloc([8, 128], BF16, "zero")
    k.memset("dve", zero.ap, 0.0, w=[zero])
    for v in vst:
        k.memset("pool", v.ap, 1.0, w=[v])
    for i in range(8):
        k.dma("sp", Vp[i * 128:(i + 1) * 128], zero.ap, r=[zero])
        k.dma("sp", Vp[1024 + S + i * 128:1024 + S + (i + 1) * 128], zero.ap, r=[zero])
    zi = [0]

    def normpre(b):
        for t in range(4):
            ti = 4 * b + t
            xt, a_, s_, tm, rs = xb[ti % 2], ab4[t], ss[ti % 2], tmp[ti % 2], rstd[ti % 2]
            k.dma("sp", xt.ap, x[ti * 128:(ti + 1) * 128, :], w=[xt])
            k.memset("pool", s_.ap, 0.0, w=[s_])
            k.act(junk.ap, xt.ap, AF.Square, accum=s_.ap, r=[xt, s_], w=[junk, s_])
            C.rms_rstd(s_, 1024, tm, rs)
            k.ts("dve", a_.ap, xt.ap, rs.ap[:, 0:1], ALU.mult, r=[xt, rs], w=[a_])

    def normtr(b):
        at = aT[b % 2]
        for t in range(4):
            ti = 4 * b + t
            a_ = ab4[t]
            pst = PS[ti % 2]
            psv = pst.ap.bitcast(BF16).rearrange("p (a b) -> p a b", a=8)
            for kk in range(8):
                k.tr(psv[:, kk, :], a_.ap[:, kk * 128:(kk + 1) * 128], C.identb.ap, r=[a_, C.identb], w=[pst])
            k.cp("dve", at.ap[:, :, t * 128:(t + 1) * 128], psv, r=[pst], w=[at])

    def projstage(b):
        at = aT[b % 2]
        for (dst, f0, sc) in ((qT, 0, 0.125), (kT, 4, None), (uT, 12, None)):
            for j in range(4):
                pz = PS[2 + zi[0] % 3]
                for kk in range(8):
                    k.mm(pz.ap, win.ap[:, kk, (f0 + j) * 128:(f0 + j + 1) * 128], at.ap[:, kk, :], kk == 0, kk == 7, r=[win, at], w=[pz])
                z = zst[zi[0] % 4]
                k.act(z.ap, pz.ap, AF.Copy, scale=(sc if sc is not None else 1.0), r=[pz], w=[z])
                k.dma("act", dst[j, :, b * 512:(b + 1) * 512], z.ap, r=[z])
                zi[0] += 1
        for t in range(4):
            ti = 4 * b + t
            pv = PS[5 + ti % 3]
            for kk in range(8):
                k.mm(pv.ap, at.ap[:, kk, t * 128:(t + 1) * 128], win.ap[:, kk, 1024:1536], kk == 0, kk == 7, r=[win, at], w=[pv])
            v = vst[ti % 2]
            pvv = pv.ap.rearrange("p (a b c) -> p a b c", a=4, b=2)
            vv = v.ap.rearrange("p (a b) c -> p a b c", b=2)
            k.cp("dve", vv[:, :, 0, 0:64], pvv[:, :, 0, :], r=[pv], w=[v])
            k.cp("act", vv[:, :, 1, 64:128], pvv[:, :, 1, :], r=[pv], w=[v])
            k.dma("act", Vp[1024 + ti * 128:1024 + (ti + 1) * 128], v.ap, r=[v])
    NB = S // 512
    ab4 = ab + [A.alloc([1024], BF16, "ab%d" % i) for i in range(2, 4)]
    normpre(0)
    normtr(0)
    if NB > 1:
        normpre(1)
    for b in range(NB):
        projstage(b)
        if b + 1 < NB:
            normtr(b + 1)
        if b + 2 < NB:
            normpre(b + 2)
    A.pop()


def phase_S(C, uT, gT, a_re, a_im, log_dt, b_re, b_im, c_re, c_im, ssm_d):
    k, A, PS, S, P = C.k, C.A, C.PS, C.S, C.P
    NCH = S // 8
    CB = min(512, NCH)
    NCB = NCH // CB
    LV = int(round(math.log2(NCH)))
    TWO_PI = 2.0 * math.pi
    A.push()
    Z = A.alloc([8, 240], BF16, "Z")
    T0 = A.alloc([32, 128], BF16, "T0")
    WS = A.alloc([32, 2, 128], BF16, "WS")
    WOF = A.alloc([32, 2, 128], BF16, "WOF")
    WOB = A.alloc([32, 2, 128], BF16, "WOB")
    k.memset("pool", WOF.ap, 0.0, w=[WOF])
    k.memset("pool", WOB.ap, 0.0, w=[WOB])
    Wpow = A.alloc([LV, 32, 2], F32, "Wpow")
    rho8 = A.alloc([32], F32, "rho8")
    Eb = A.alloc([32, 2, 32], F32, "Eb")
    k.memset("pool", Z.ap, 0.0, w=[Z])
    for gi in range(8):
        k.asel(Z.ap[:, gi, 112:128], Z.ap[:, gi, 112:128], [[-1, 16]], ALU.not_equal, 1.0, -16 * gi, 1, r=[Z], w=[Z])
    A.push()

    def new(shape, name):
        return A.alloc(shape, F32, name)
    are, aim, ldt = new([32], "are"), new([32], "aim"), new([32], "ldt")
    bre, bim = new([32, 16], "bre"), new([32, 16], "bim")
    Lr, Li = new([32, 8, 16], "Lr"), new([32, 8, 16], "Li")
    Rr, Ri = new([32, 8, 16], "Rr"), new([32, 8, 16], "Ri")
    dcol = new([32], "dcol")
    for dr in range(2):
        ps_ = slice(dr * 64, dr * 64 + 64)
        k.dma("sp", are.ap[ps_, :], a_re[0, dr].rearrange("g n -> n g"), w=[are], slow=True)
        k.dma("sp", aim.ap[ps_, :], a_im[0, dr].rearrange("g n -> n g"), w=[aim], slow=True)
        k.dma("sp", ldt.ap[ps_, :], log_dt[0, dr:dr + 1, :].partition_broadcast(64), w=[ldt])
        k.dma("sp", bre.ap[ps_], b_re[0, dr].rearrange("g n h -> n g h"), w=[bre], slow=True)
        k.dma("sp", bim.ap[ps_], b_im[0, dr].rearrange("g n h -> n g h"), w=[bim], slow=True)
    for s_ in range(8):
        k.dma("sp", dcol.ap[s_ * 16:(s_ + 1) * 16, :], ssm_d[0].rearrange("g h -> h g"), w=[dcol], slow=True)
    ct = new([128], "ct")
    for (src, dst) in ((c_re, Rr), (c_im, Ri)):
        for o in range(4):
            k.dma("sp", ct.ap[:, 0:64], src[0, 0, 8 * o:8 * o + 8].rearrange("g h n -> (g h) n"), w=[ct])
            k.dma("sp", ct.ap[:, 64:128], src[0, 1, 8 * o:8 * o + 8].rearrange("g h n -> (g h) n"), w=[ct])
            k.tr(PS[0].ap[:, 0:128], ct.ap, C.identf.ap, r=[ct, C.identf], w=[PS[0]])
            k.cp("dve", dst.ap[:, 8 * o:8 * o + 8, 0, :], PS[0].ap[:, 0:128].rearrange("p (a b) -> p a b", b=16), r=[PS[0]], w=[dst])
    sc = [new([32], "sc%d" % i) for i in range(12)]
    sci = A.alloc([32], I32, "sci")

    def mul(o, a, b, eng="dve"):
        k.tt(eng, o.ap, a.ap, b.ap, ALU.mult, r=[a, b], w=[o])

    def sin_of(out, th, shift):
        t1, tf, r_, m_ = sc[8], sc[9], sc[10], sc[11]
        k.ts("dve", r_.ap, th.ap, shift, ALU.add, r=[th], w=[r_])
        k.ts("dve", t1.ap, r_.ap, 1.0 / TWO_PI, ALU.mult, r=[r_], w=[t1])
        k.cp("dve", sci.ap, t1.ap, r=[t1], w=[sci])
        k.cp("dve", tf.ap, sci.ap, r=[sci], w=[tf])
        k.stt(r_.ap, tf.ap, -TWO_PI, r_.ap, ALU.mult, ALU.add, r=[tf, r_], w=[r_])
        k.ts("dve", m_.ap, r_.ap, math.pi, ALU.is_gt, -TWO_PI, ALU.mult, r=[r_], w=[m_])
        k.tt("dve", r_.ap, r_.ap, m_.ap, ALU.add, r=[r_, m_], w=[r_])
        k.ts("dve", m_.ap, r_.ap, -math.pi, ALU.is_lt, TWO_PI, ALU.mult, r=[r_], w=[m_])
        k.tt("dve", r_.ap, r_.ap, m_.ap, ALU.add, r=[r_, m_], w=[r_])
        k.ts("dve", r_.ap, r_.ap, 3.1415925, ALU.min, -3.1415925, ALU.max, r=[r_], w=[r_])
        k.act(out.ap, r_.ap, AF.Sin, r=[r_], w=[out])
    dt_, lam, th, mag, c1, s1, abr, abi = [new([32], "p%d" % i) for i in range(8)]
    k.act(dt_.ap, ldt.ap, AF.Exp, r=[ldt], w=[dt_])
    mul(lam, are, dt_)
    mul(th, aim, dt_)
    k.act(mag.ap, lam.ap, AF.Exp, r=[lam], w=[mag])
    sin_of(s1, th, 0.0)
    sin_of(c1, th, math.pi / 2)
    mul(abr, mag, c1)
    mul(abi, mag, s1)
    inv, fre, fim, am1 = [new([32], "q%d" % i) for i in range(4)]
    mul(sc[0], are, are)
    mul(sc[1], aim, aim)
    k.tt("dve", sc[0].ap, sc[0].ap, sc[1].ap, ALU.add, r=[sc[0], sc[1]], w=[sc[0]])
    k.recip(inv.ap, sc[0].ap, r=[sc[0]], w=[inv])
    k.ts("dve", am1.ap, abr.ap, -1.0, ALU.add, r=[abr], w=[am1])
    mul(sc[0], am1, are)
    mul(sc[1], abi, aim)
    k.tt("dve", sc[0].ap, sc[0].ap, sc[1].ap, ALU.add, r=[sc[0], sc[1]], w=[sc[0]])
    mul(fre, sc[0], inv)
    mul(sc[0], abi, are)
    mul(sc[1], am1, aim)
    k.tt("dve", sc[0].ap, sc[0].ap, sc[1].ap, ALU.subtract, r=[sc[0], sc[1]], w=[sc[0]])
    mul(fim, sc[0], inv)

    def cmul(ore, oim, ar_, ai_, br_, bi_, t1, t2, r, w):
        k.tt("dve", t1, ar_, br_, ALU.mult, r=r, w=w)
        k.tt("dve", t2, ai_, bi_, ALU.mult, r=r, w=w)
        k.tt("dve", ore, t1, t2, ALU.subtract, r=r, w=w)
        k.tt("dve", t1, ar_, bi_, ALU.mult, r=r, w=w)
        k.tt("dve", t2, ai_, br_, ALU.mult, r=r, w=w)
        k.tt("dve", oim, t1, t2, ALU.add, r=r, w=w)
    X = T(None, "prepX")
    allr = [X, are, aim, bre, bim, Rr, Ri, Lr, Li, abr, abi, mag, fre, fim, lam]
    big1, big2 = new([16, 128], "big1"), new([16, 128], "big2")

    def b16(t_):
        return t_.ap.unsqueeze(2).to_broadcast([128, 32, 16])

    def b128(t_):
        return t_.ap.unsqueeze(2).to_broadcast([128, 32, 128])
    t16a = big1.ap.rearrange("p a b -> p (a b)")[:, 0:512].rearrange("p (g h) -> p g h", h=16)
    t16b = big2.ap.rearrange("p a b -> p (a b)")[:, 0:512].rearrange("p (g h) -> p g h", h=16)
    cmul(Lr.ap[:, :, 0, :], Li.ap[:, :, 0, :], b16(fre), b16(fim), bre.ap, bim.ap, t16a, t16b, allr, [X])
    aivr, aivi, im2 = new([32], "aivr"), new([32], "aivi"), new([32], "im2")
    mul(sc[0], mag, mag)
    k.recip(im2.ap, sc[0].ap, r=[sc[0]], w=[im2])
    mul(aivr, abr, im2)
    k.stt(aivi.ap, abi.ap, -1.0, im2.ap, ALU.mult, ALU.mult, r=[abi, im2], w=[aivi])
    MLr, MLi, MRr, MRi = [new([32], "M%d" % i) for i in range(4)]
    lo, hi = slice(0, 64), slice(64, 128)
    for (dst, s_lo, s_hi) in ((MLr, aivr, abr), (MLi, aivi, abi), (MRr, abr, aivr), (MRi, abi, aivi)):
        k.cp("dve", dst.ap[lo], s_lo.ap[lo], r=[s_lo], w=[dst])
        k.cp("dve", dst.ap[hi], s_hi.ap[hi], r=[s_hi], w=[dst])
    allr += [MLr, MLi, MRr, MRi]
    for s_ in range(7):
        cmul(Lr.ap[:, :, s_ + 1, :], Li.ap[:, :, s_ + 1, :], Lr.ap[:, :, s_, :], Li.ap[:, :, s_, :], b16(MLr), b16(MLi), t16a, t16b, allr, [X])
        cmul(Rr.ap[:, :, s_ + 1, :], Ri.ap[:, :, s_ + 1, :], Rr.ap[:, :, s_, :], Ri.ap[:, :, s_, :], b16(MRr), b16(MRi), t16a, t16b, allr, [X])
    pwr = [abr] + [new([32], "pwr%d" % i) for i in range(7)]
    pwi = [abi] + [new([32], "pwi%d" % i) for i in range(7)]
    for i in range(7):
        cmul(pwr[i + 1].ap, pwi[i + 1].ap, pwr[i].ap, pwi[i].ap, abr.ap, abi.ap, sc[0].ap, sc[1].ap, allr, [X])
    FSr, FSi, FOr, FOi = [new([32], "F%d" % i) for i in range(4)]
    k.cp("dve", FSr.ap[lo], pwr[6].ap[lo], r=allr, w=[X])
    k.cp("dve", FSi.ap[lo], pwi[6].ap[lo], r=allr, w=[X])
    k.memset("dve", FSr.ap[hi], 1.0, w=[X])
    k.memset("dve", FSi.ap[hi], 0.0, w=[X])
    k.cp("dve", FOr.ap[lo], abr.ap[lo], r=allr, w=[X])
    k.cp("dve", FOi.ap[lo], abi.ap[lo], r=allr, w=[X])
    k.cp("dve", FOr.ap[hi], pwr[7].ap[hi], r=allr, w=[X])
    k.cp("dve", FOi.ap[hi], pwi[7].ap[hi], r=allr, w=[X])
    Lr3 = Lr.ap.rearrange("p g s h -> p g (s h)")
    Li3 = Li.ap.rearrange("p g s h -> p g (s h)")
    Rr3 = Rr.ap.rearrange("p g s h -> p g (s h)")
    Ri3 = Ri.ap.rearrange("p g s h -> p g (s h)")
    WSr, WSi = new([16, 128], "WSr"), new([16, 128], "WSi")

    def b128h(t_, h_):
        return t_.ap[:, 16 * h_:16 * h_ + 16].unsqueeze(2).to_broadcast([128, 16, 128])
    for h_ in range(2):
        gs = slice(16 * h_, 16 * h_ + 16)
        cmul(WSr.ap, WSi.ap, Lr3[:, gs], Li3[:, gs], b128h(FSr, h_), b128h(FSi, h_), big1.ap, big2.ap, allr, [X])
        for gl in range(16):
            g = 16 * h_ + gl
            for xi, src in enumerate((WSr, WSi)):
                pst = PS[(2 * g + xi) % 4]
                k.tr(pst.ap[:, 0:128], src.ap[:, gl, :], C.identf.ap, r=[X, C.identf], w=[pst])
                k.cp("act" if xi else "dve", WS.ap[:, g, xi, :], pst.ap[:, 0:128], r=[pst], w=[WS])
        cmul(WSr.ap, WSi.ap, Rr3[:, gs], Ri3[:, gs], b128h(FOr, h_), b128h(FOi, h_), big1.ap, big2.ap, allr, [X])
        for (dst, ps_) in ((WOF, lo), (WOB, hi)):
            k.cp("dve", dst.ap[ps_, gs, 0, :], WSr.ap[ps_], r=[X], w=[dst])
            k.ts("dve", dst.ap[ps_, gs, 1, :], WSi.ap[ps_], -1.0, ALU.mult, r=[X], w=[dst])
    Mlow, Mup, onesf = new([128], "Mlow"), new([128], "Mup"), new([128], "onesf")
    k.memset("pool", onesf.ap, 1.0, w=[onesf])
    k.asel(Mlow.ap.rearrange("p (t h) -> p t h", h=16), onesf.ap.rearrange("p (t h) -> p t h", h=16), [[16, 8], [0, 16]], ALU.is_ge, 0.0, 15, -1, r=[onesf], w=[Mlow])
    k.asel(Mup.ap.rearrange("p (t h) -> p t h", h=16), onesf.ap.rearrange("p (t h) -> p t h", h=16), [[-16, 8], [0, 16]], ALU.is_ge, 0.0, 0, 1, r=[onesf], w=[Mup])
    tacc = [new([128], "tacc%d" % i) for i in range(4)]
    for g in range(32):
        pf, pf2, pb_, pb2 = [PS[4 * (g % 2) + i] for i in range(4)]
        k.mm(pf.ap[:, 0:128], Lr3[lo, g, :], Rr3[lo, g, :], True, True, r=[X], w=[pf])
        k.mm(pf2.ap[:, 0:128], Li3[lo, g, :], Ri3[lo, g, :], True, True, r=[X], w=[pf2])
        k.mm(pb_.ap[:, 0:128], Lr3[hi, g, :], Rr3[hi, g, :], True, True, r=[X], w=[pb_])
        k.mm(pb2.ap[:, 0:128], Li3[hi, g, :], Ri3[hi, g, :], True, True, r=[X], w=[pb2])
        ta, tb = tacc[2 * (g % 2)], tacc[2 * (g % 2) + 1]
        k.cp("act", ta.ap, pf2.ap[:, 0:128], r=[pf2], w=[ta])
        k.tt("dve", ta.ap, pf.ap[:, 0:128], ta.ap, ALU.subtract, r=[pf, ta], w=[ta])
        k.tt("dve", ta.ap, ta.ap, Mlow.ap, ALU.mult, r=[ta, Mlow], w=[ta])
        k.cp("act", tb.ap, pb2.ap[:, 0:128], r=[pb2], w=[tb])
        k.tt("dve", tb.ap, pb_.ap[:, 0:128], tb.ap, ALU.subtract, r=[pb_, tb], w=[tb])
        k.tt("dve", tb.ap, tb.ap, Mup.ap, ALU.mult, r=[tb, Mup], w=[tb])
        k.tt("dve", ta.ap, ta.ap, tb.ap, ALU.add, r=[ta, tb], w=[ta])
        k.stt(T0.ap[:, g, :], C.identf.ap, dcol.ap[:, g:g + 1], ta.ap, ALU.mult, ALU.add, r=[C.identf, dcol, ta], w=[T0])
    k.act(rho8.ap, lam.ap, AF.Exp, scale=8.0, r=[lam], w=[rho8])
    ir8 = new([32], "ir8")
    k.act(ir8.ap, lam.ap, AF.Exp, scale=-8.0, r=[lam], w=[ir8])
    k.tt("dve", Wpow.ap[:, 0, :, 0], pwr[7].ap, ir8.ap, ALU.mult, r=allr + [ir8], w=[Wpow])
    k.tt("dve", Wpow.ap[:, 0, :, 1], pwi[7].ap, ir8.ap, ALU.mult, r=allr + [ir8], w=[Wpow])
    for l in range(LV - 1):
        wr, wi = Wpow.ap[:, l, :, 0], Wpow.ap[:, l, :, 1]
        k.tt("dve", sc[0].ap, wr, wr, ALU.mult, r=[Wpow], w=[sc[0]])
        k.tt("dve", sc[1].ap, wi, wi, ALU.mult, r=[Wpow], w=[sc[1]])
        k.tt("dve", Wpow.ap[:, l + 1, :, 0], sc[0].ap, sc[1].ap, ALU.subtract, r=[sc[0], sc[1]], w=[Wpow])
        k.tt("dve", sc[2].ap, wr, wi, ALU.mult, r=[Wpow], w=[sc[2]])
        k.ts("dve", Wpow.ap[:, l + 1, :, 1], sc[2].ap, 2.0, ALU.mult, r=[sc[2]], w=[Wpow])
    k.memset("dve", Eb.ap[:, :, 0, 0:1], 1.0, w=[Eb])
    k.memset("dve", Eb.ap[:, :, 1, 0:1], 0.0, w=[Eb])
    eb1 = big1.ap.rearrange("p a b -> p (a b)")[:, 0:512].rearrange("p (g m) -> p g m", m=16)
    eb2 = big2.ap.rearrange("p a b -> p (a b)")[:, 0:512].rearrange("p (g m) -> p g m", m=16)
    for l in range(5):
        m = 1 << l
        wr = Wpow.ap[:, l, :, 0].unsqueeze(2).to_broadcast([128, 32, m])
        wi = Wpow.ap[:, l, :, 1].unsqueeze(2).to_broadcast([128, 32, m])
        ec0, es0 = Eb.ap[:, :, 0, 0:m], Eb.ap[:, :, 1, 0:m]
        ec1, es1 = Eb.ap[:, :, 0, m:2 * m], Eb.ap[:, :, 1, m:2 * m]
        t1, t2 = eb1[:, :, 0:m], eb2[:, :, 0:m]
        k.tt("dve", t1, es0, wi, ALU.mult, r=[Eb, Wpow, X], w=[X])
        k.tt("dve", t2, ec0, wr, ALU.mult, r=[Eb, Wpow, X], w=[X])
        k.tt("dve", ec1, t2, t1, ALU.subtract, r=[X, Eb], w=[Eb])
        k.tt("dve", t1, ec0, wi, ALU.mult, r=[Eb, Wpow, X], w=[X])
        k.tt("dve", t2, es0, wr, ALU.mult, r=[Eb, Wpow, X], w=[X])
        k.tt("dve", es1, t2, t1, ALU.add, r=[X, Eb], w=[Eb])
    A.pop()
    P.barrier()
    uTo = A.alloc([S], BF16, "uTo")
    gTo = A.alloc([S], BF16, "gTo")
    Yg = A.alloc([8, NCH], BF16, "Yg")
    Ugs = [A.alloc([NCH], BF16, "Ug%d" % i) for i in range(2)]
    Es = [A.alloc([2, NCH], F32, "E%d" % i) for i in range(2)]
    G = A.alloc([2, NCH], F32, "G")
    Hs = A.alloc([2, NCH], F32, "Hs")
    Hbs = [A.alloc([2, NCH + 2], BF16, "Hb%d" % i) for i in range(2)]
    tq = [A.alloc([CB], F32, "tq%d" % i) for i in range(4)]
    ge = [A.alloc([CB], F32, "ge%d" % i) for i in range(3)]
    for hb_ in Hbs:
        k.memset("pool", hb_.ap, 0.0, w=[hb_])
    tr_ = [A.alloc([CB], F32, "tr%d" % i) for i in range(4)]
    Ug3 = Ugs + [A.alloc([NCH], BF16, "Ug2")]

    def egen(g):
        E = Es[g % 2]
        Ec, Es_ = E.ap[:, 0, :], E.ap[:, 1, :]
        k.cp("dve", E.ap[:, :, 0:32], Eb.ap[:, g, :, :], r=[Eb], w=[E])
        for l in range(5, LV):
            m = 1 << l
            wr, wi = Wpow.ap[:, l, g, 0:1], Wpow.ap[:, l, g, 1:2]
            t1, t2 = tq[0].ap[:, 0:m], tq[1].ap[:, 0:m]
            k.ts("dve", t1, Es_[:, 0:m], wi, ALU.mult, r=[E, Wpow], w=[tq[0]])
            k.stt(Ec[:, m:2 * m], Ec[:, 0:m], wr, t1, ALU.mult, ALU.subtract, r=[E, Wpow, tq[0]], w=[E])
            k.ts("dve", t2, Ec[:, 0:m], wi, ALU.mult, r=[E, Wpow], w=[tq[1]])
            k.stt(Es_[:, m:2 * m], Es_[:, 0:m], wr, t2, ALU.mult, ALU.add, r=[E, Wpow, tq[1]], w=[E])

    def insel(g):
        gi, Ug = g % 8, Ug3[g % 3]
        for cb in range(NCB):
            pu = PS[cb % 2]
            for s_ in range(8):
                k.mm(pu.ap[:, 0:CB], Z.ap[:, gi, 112 - 16 * s_:240 - 16 * s_], strided(uTo.ap[:, cb * CB * 8 + s_:cb * CB * 8 + s_ + 1], 8, CB), s_ == 0, s_ == 7, r=[Z, uTo], w=[pu])
            k.cp("act", Ug.ap[:, cb * CB:(cb + 1) * CB], pu.ap[:, 0:CB], r=[pu], w=[Ug])

    def summ(g):
        Ug = Ug3[g % 3]
        for cb in range(NCB):
            for xi in range(2):
                pS = PS[2 + 2 * xi + cb]
                k.mm(pS.ap[0:64, 0:CB], WS.ap[:, g, xi, 0:64], Ug.ap[:, cb * CB:(cb + 1) * CB], True, True, r=[WS, Ug], w=[pS])
                k.mm(pS.ap[64:128, 0:CB], WS.ap[:, g, xi, 64:128], rev_ap(Ug.ap[:, NCH - 1 - cb * CB:NCH - cb * CB], CB), True, True, r=[WS, Ug], w=[pS])

    def demod(g):
        E = Es[g % 2]
        Ec, Es_ = E.ap[:, 0, :], E.ap[:, 1, :]
        for cb in range(NCB):
            cs = slice(cb * CB, (cb + 1) * CB)
            pr, pi_ = PS[2 + cb], PS[4 + cb]
            k.tt("dve", tq[0].ap, pr.ap[:, 0:CB], Ec[:, cs], ALU.mult, r=[pr, E], w=[tq[0]])
            k.tt("dve", tq[1].ap, pi_.ap[:, 0:CB], Es_[:, cs], ALU.mult, r=[pi_, E], w=[tq[1]])
            k.tt("dve", G.ap[:, 0, cs], tq[0].ap, tq[1].ap, ALU.add, r=[tq[0], tq[1]], w=[G])
            k.tt("dve", tq[2].ap, pi_.ap[:, 0:CB], Ec[:, cs], ALU.mult, r=[pi_, E], w=[tq[2]])
            k.tt("dve", tq[3].ap, pr.ap[:, 0:CB], Es_[:, cs], ALU.mult, r=[pr, E], w=[tq[3]])
            k.tt("dve", G.ap[:, 1, cs], tq[2].ap, tq[3].ap, ALU.subtract, r=[tq[2], tq[3]], w=[G])

    def scan_(g):
        dec = rho8.ap[:, g:g + 1].to_broadcast([128, NCH])
        k.scan(Hs.ap[:, 0, :], dec, G.ap[:, 0, :], 0.0, r=[G, rho8], w=[Hs])
        k.scan(Hs.ap[:, 1, :], dec, G.ap[:, 1, :], 0.0, r=[G, rho8], w=[Hs])

    def remod(g):
        E, Hb = Es[g % 2], Hbs[g % 2]
        Ec, Es_ = E.ap[:, 0, :], E.ap[:, 1, :]
        for cb in range(NCB):
            cs = slice(cb * CB, (cb + 1) * CB)
            co = slice(1 + cb * CB, 1 + (cb + 1) * CB)
            k.tt("dve", tr_[0].ap, Hs.ap[:, 0, cs], Ec[:, cs], ALU.mult, r=[Hs, E], w=[tr_[0]])
            k.tt("dve", tr_[1].ap, Hs.ap[:, 1, cs], Es_[:, cs], ALU.mult, r=[Hs, E], w=[tr_[1]])
            k.tt("dve", Hb.ap[:, 0, co], tr_[0].ap, tr_[1].ap, ALU.subtract, r=[tr_[0], tr_[1]], w=[Hb])
            k.tt("dve", tr_[2].ap, Hs.ap[:, 1, cs], Ec[:, cs], ALU.mult, r=[Hs, E], w=[tr_[2]])
            k.tt("dve", tr_[3].ap, Hs.ap[:, 0, cs], Es_[:, cs], ALU.mult, r=[Hs, E], w=[tr_[3]])
            k.tt("dve", Hb.ap[:, 1, co], tr_[2].ap, tr_[3].ap, ALU.add, r=[tr_[2], tr_[3]], w=[Hb])

    def outs(g):
        gi, Ug, Hb = g % 8, Ug3[g % 3], Hbs[g % 2]
        for cb in range(NCB):
            cs = slice(cb * CB, (cb + 1) * CB)
            py = PS[6 + cb % 2]
            k.mm(py.ap[:, 0:CB], T0.ap[:, g, :], Ug.ap[:, cs], True, False, r=[T0, Ug], w=[py])
            for xi in range(2):
                k.mm(py.ap[:, 0:CB], WOF.ap[:, g, xi, :], Hb.ap[:, xi, cb * CB:(cb + 1) * CB], False, False, r=[WOF, Hb], w=[py])
            for xi in range(2):
                st_ = NCH - 1 - cb * CB
                k.mm(py.ap[:, 0:CB], WOB.ap[:, g, xi, :], rev_ap(Hb.ap[:, xi, st_:st_ + 1], CB), False, xi == 1, r=[WOB, Hb], w=[py])
            k.cp("act", Yg.ap[:, gi, cs], py.ap[:, 0:CB], r=[py], w=[Yg])

    for o in range(4):
        k.dma("sp", uTo.ap, uT[o], w=[uTo])
        g0 = 8 * o
        insel(g0)
        insel(g0 + 1)
        summ(g0)
        egen(g0)
        for gi in range(8):
            g = g0 + gi
            demod(g)
            if gi + 1 < 8:
                summ(g + 1)
            scan_(g)
            if gi + 1 < 8:
                egen(g + 1)
            remod(g)
            if gi + 2 < 8:
                insel(g + 2)
            outs(g)
        for t in range(8):
            for cb in range(NCB):
                po = PS[(t * NCB + cb) % 2]
                for gi in range(8):
                    k.mm(po.ap[:, 0:CB], Z.ap[:, t, 112 - 16 * gi:240 - 16 * gi], Yg.ap[:, gi, cb * CB:(cb + 1) * CB], gi == 0, gi == 7, r=[Z, Yg], w=[po])
                yv = po.ap[:, 0:CB]
                k.act(ge[0].ap, yv, AF.Square, r=[po], w=[ge[0]])
                k.ts("dve", ge[0].ap, ge[0].ap, 0.044715, ALU.mult, 1.0, ALU.add, r=[ge[0]], w=[ge[0]])
                k.tt("dve", ge[1].ap, ge[0].ap, yv, ALU.mult, r=[ge[0], po], w=[ge[1]])
                k.act(ge[2].ap, ge[1].ap, AF.Sigmoid, scale=1.5957691216057308, r=[ge[1]], w=[ge[2]])
                k.tt("dve", strided(gTo.ap[:, cb * CB * 8 + t:cb * CB * 8 + t + 1], 8, CB), yv, ge[2].ap, ALU.mult, r=[po, ge[2]], w=[gTo])
        k.dma("sp", gT[o], gTo.ap, r=[gTo])
    A.pop()


def ap3(base, d1, d2):
    return bass.AP(base.tensor, base.offset, [list(base.ap[0]), list(d1), list(d2)])


def phase_B(C, x, qT, kT, Vp, gT, hs, rel_bias, Fd, w_glu, b_glu, g_att, g_ssm, w_out):
    k, A, PS, S, P = C.k, C.A, C.PS, C.S, C.P
    NSB = S // 2048
    import os
    DIL = (1, 4, 16)
    DSEL = [int(v) for v in os.environ.get('PB_DIL', '1,4,16').split(',')]
    A.push()
    EB = A.alloc([3, 8, 2, 128], BF16, "EB")
    A.push()
    rb0 = A.alloc([256], F32, "rb0")
    F0 = A.alloc([3, 8, 512], F32, "F0")
    Tt = A.alloc([24, 257], F32, "Tt")
    k.dma("sp", rb0.ap, rel_bias.rearrange("(o b) h -> o (b h)", o=1).partition_broadcast(128), w=[rb0])
    k.act(rb0.ap, rb0.ap, AF.Exp, r=[rb0], w=[rb0])
    k.memset("dve", F0.ap, 0.0, w=[F0])
    rbv = rb0.ap.rearrange("p (b h) -> p b h", h=8)
    for di, d in enumerate(DIL):
        js = np.arange(-64, 65)
        bk = t5_bucket_np(js * d)
        s0 = 0
        while s0 < len(js):
            e0 = s0
            while e0 + 1 < len(js) and bk[e0 + 1] == bk[s0]:
                e0 += 1
            ln = e0 - s0 + 1
            x0 = int(js[s0]) + 192
            bb = int(bk[s0])
            for h in range(8):
                k.ts("dve", F0.ap[:, di, h, x0:x0 + ln], F0.ap[:, di, h, x0:x0 + ln], rbv[:, bb, h:h + 1], ALU.add, r=[rb0, F0], w=[F0])
            s0 = e0 + 1
    fdt = T(None, "Fd")
    for di in range(3):
        k.dma("sp", Fd[di * 4096:(di + 1) * 4096].rearrange("(o n) -> o n", o=1), F0.ap[0:1, di].rearrange("p b c -> p (b c)"), r=[F0], w=[fdt])
    for di in range(3):
        for h in range(8):
            j = di * 8 + h
            k.dma("sp", Tt.ap[:, j, :], bass.AP(Fd.tensor, j * 512, [[1, 128], [1, 257]]), r=[fdt], w=[Tt])
    ttb = Tt.ap[:, 0, 0:1]
    pstep = list(Tt.ap.ap[0])
    for di in range(3):
        for ab, c0 in ((0, 128), (1, 256)):
            src = bass.AP(Tt.ap.tensor, Tt.ap[:, di * 8, c0:c0 + 1].offset, [pstep, [257, 8], [-1, 128]])
            k.cp("dve", EB.ap[:, di, :, ab, :], src, r=[Tt], w=[EB])
    A.pop()
    P.barrier()
    import os
    STOP = int(os.environ.get("PB_STOP", "9"))
    if STOP <= 1:
        A.pop()
        return
    ONEH = A.alloc([2, 128], BF16, "ONEH")
    k.memset("pool", ONEH.ap, 0.0, w=[ONEH])
    k.memset("pool", ONEH.ap[:, 0, 0:64], 1.0, w=[ONEH])
    k.memset("pool", ONEH.ap[:, 1, 64:128], 1.0, w=[ONEH])
    gc = A.alloc([8], F32, "gcB")
    k.dma("sp", gc.ap[:, 0:4], g_att[0].rearrange("(k p) -> p k", p=128), w=[gc], slow=True)
    k.dma("sp", gc.ap[:, 4:8], g_ssm[0].rearrange("(k p) -> p k", p=128), w=[gc], slow=True)
    bgl = C.gcol(b_glu[0], 512, "bgl")
    wout = A.alloc([8, 1024], BF16, "wout")
    wglu = A.alloc([4, 512], BF16, "wglu")
    C.load_weight_bf16(wout, w_out[0], 8, 1024, gc, "wout")
    C.load_weight_bf16(wglu, w_glu[0], 4, 512, None, "wglu")
    qs = A.alloc([2, 2048], BF16, "qs")
    ks = A.alloc([2, 4096], BF16, "ks")
    acc = A.alloc([4, 2048], F32, "acc")
    dsw = A.alloc([2, 2048], F32, "dsw")
    attT = A.alloc([4, 2048], BF16, "attT")
    pTs = [A.alloc([8, 128], BF16, "pT%d" % i) for i in range(3)]
    pTbs = [T(None, "pTb%d" % i) for i in range(3)]
    vas = [A.alloc([4, 128], BF16, "va%d" % i) for i in range(4)]
    vbs = [A.alloc([4, 128], BF16, "vb%d" % i) for i in range(4)]
    dflat = dsw.ap.rearrange("p a b -> p (a b)")
    accflat = acc.ap.rearrange("p a b -> p (a b)")

    def alias_bf(i, name):
        return T(dflat[:, i * 1024:(i + 1) * 1024].bitcast(BF16).rearrange("p (a b) -> p a b", a=4), name)
    gts = [alias_bf(i, "gt%d" % i) for i in range(2)]
    sqbs = [alias_bf(2 + i, "sqb%d" % i) for i in range(2)]
    sqbs2 = [T(accflat[:, i * 1024:(i + 1) * 1024].bitcast(BF16).rearrange("p (a b) -> p a b", a=4), "sqc%d" % i) for i in range(2)]
    sTs = [A.alloc([4, 512], BF16, "sT%d" % i) for i in range(2)]
    rs5s = [A.alloc([512], F32, "rs5%d" % i) for i in range(4)]
    sigs = [A.alloc([512], BF16, "sig%d" % i) for i in range(2)]
    mixTs = [A.alloc([8, 512], BF16, "mixT%d" % i) for i in range(2)]
    xb = [A.alloc([1024], F32, "xB%d" % i) for i in range(3)]
    pend = []
    bi = 0
    for SB in range(NSB):
        tok0 = SB * 2048
        for half in range(2):
            k.dma("sp", qs.ap, qT[2 * half:2 * half + 2, :, tok0:tok0 + 2048].rearrange("f p t -> p f t"), w=[qs])
            lo, hi = tok0 - 1024, tok0 + 3072
            vlo, vhi = max(lo, 0), min(hi, S)
            if vlo > lo:
                k.memset("pool", ks.ap[:, :, 0:vlo - lo], 0.0, w=[ks])
            if vhi < hi:
                k.memset("pool", ks.ap[:, :, vhi - lo:4096], 0.0, w=[ks])
            k.dma("sp", ks.ap[:, :, vlo - lo:vhi - lo], kT[2 * half:2 * half + 2, :, vlo:vhi].rearrange("f p t -> p f t"), w=[ks])
            blocks = []
            first = True
            for di, d in enumerate(DIL):
                if d not in DSEL:
                    continue
                for r in range(d):
                    for mb in range(16 // d):
                        blocks.append((di, d, r, mb, first))
                first = False

            def emit_scores(j):
                di, d, r, mb, isfirst = blocks[j]
                q0 = r + d * 128 * mb
                va, vb = vas[j % 4], vbs[j % 4]
                rowA = 1024 + tok0 + q0 - 64 * d
                rowB = 1024 + tok0 + q0 + 64 * d
                k.dma("sp", va.ap, Vp[rowA:rowA + 127 * d + 1:d, 4 * half:4 * half + 4, :], w=[va])
                k.dma("sp", vb.ap, Vp[rowB:rowB + 127 * d + 1:d, 4 * half:4 * half + 4, :], w=[vb])
                for hl in range(4):
                    p0, hpl = 64 * (hl % 2), hl // 2
                    qa = strided(qs.ap[p0:p0 + 64, hpl, q0:q0 + 1], d, 128)
                    for ab in range(2):
                        kc = 1024 + q0 + (-64 * d if ab == 0 else 64 * d)
                        ka = strided(ks.ap[p0:p0 + 64, hpl, kc:kc + 1], d, 128)
                        pss = PS[2 * (j % 3) + hl % 2]
                        cslot = (hl // 2) * 2 + ab
                        k.mm(pss.ap[:, cslot * 128:(cslot + 1) * 128], ka, qa, True, True, r=[ks, qs], w=[pss])

            def emit_soft(j):
                di, d, r, mb, isfirst = blocks[j]
                pT, pTb = pTs[j % 3], pTbs[j % 3]
                ps0, ps1 = PS[2 * (j % 3)], PS[2 * (j % 3) + 1]
                pTf = pT.ap.rearrange("p a b -> p (a b)")
                k.act(pTf[:, 0:512], ps0.ap, AF.Exp, r=[ps0], w=[pT])
                k.act(pTf[:, 512:1024], ps1.ap, AF.Exp, r=[ps1], w=[pTb])
                for hh in range(2):
                    ebv = EB.ap[:, di, 4 * half + hh:4 * half + 4:2, :, :]
                    pv4 = pT.ap[:, hh * 4:(hh + 1) * 4, :].rearrange("p (a b) c -> p a b c", b=2)
                    tok = pT if hh == 0 else pTb
                    k.tt("dve" if hh == 0 else "pool", pv4, pv4, ebv, ALU.mult, r=[tok, EB], w=[tok])

            def emit_pv(j):
                di, d, r, mb, isfirst = blocks[j]
                q0 = r + d * 128 * mb
                pT, pTb, va, vb = pTs[j % 3], pTbs[j % 3], vas[j % 4], vbs[j % 4]
                pnd = PS[6 + j % 2]
                for hl in range(4):
                    hh, hpl = hl % 2, hl // 2
                    for ab in range(2):
                        vt = va if ab == 0 else vb
                        k.mm(pnd.ap[:, hl * 128:(hl + 1) * 128], vt.ap[:, hl, :], pT.ap[:, hh * 4 + hpl * 2 + ab, :], ab == 0, ab == 1, r=[vt, pT, pTb], w=[pnd])
                av = ap3(acc.ap[:, 0, q0:q0 + 1], [2048, 4], [d, 128])
                pv = pnd.ap.rearrange("p (a b) -> p a b", a=4)
                if isfirst:
                    k.cp("act", av, pv, r=[pnd], w=[acc])
                else:
                    k.tt("dve", av, pv, av, ALU.add, r=[pnd, acc], w=[acc])
            nb_ = len(blocks)
            emit_scores(0)
            if nb_ > 1:
                emit_scores(1)
            emit_soft(0)
            for j in range(nb_):
                if j + 2 < nb_:
                    emit_scores(j + 2)
                if j + 1 < nb_:
                    emit_soft(j + 1)
                emit_pv(j)
            for hpl in range(2):
                k.dma("sp", dsw.ap[0:64, hpl, :], acc.ap[64:128, 2 * hpl, :], r=[acc], w=[dsw])
                k.dma("sp", dsw.ap[64:128, hpl, :], acc.ap[0:64, 2 * hpl + 1, :], r=[acc], w=[dsw])
            k.act(dsw.ap, dsw.ap, AF.Ln, r=[dsw], w=[dsw])
            k.act(dsw.ap, dsw.ap, AF.Exp, scale=-1.0, r=[dsw], w=[dsw])
            for hpl in range(2):
                k.tt("dve", attT.ap[0:64, 2 * half + hpl, :], acc.ap[0:64, 2 * hpl, :], dsw.ap[0:64, hpl, :], ALU.mult, r=[acc, dsw], w=[attT])
                k.tt("pool", attT.ap[64:128, 2 * half + hpl, :], acc.ap[64:128, 2 * hpl + 1, :], dsw.ap[64:128, hpl, :], ALU.mult, r=[acc, dsw], w=[attT])
        def stageN(bb):
            t0 = tok0 + bb * 512
            mx, sq_, sT_, gt_ = mixTs[bb % 2], sqbs[bb % 2], sTs[bb % 2], gts[bb % 2]
            rsa, rss = rs5s[2 * (bb % 2)], rs5s[2 * (bb % 2) + 1]
            av = attT.ap[:, :, bb * 512:(bb + 1) * 512]
            k.dma("sp", gt_.ap, gT[:, :, t0:t0 + 512].rearrange("f p t -> p f t"), w=[gt_])
            k.act(sq_.ap, av, AF.Square, r=[attT], w=[sq_])
            for kk in range(4):
                k.mm(PS[0].ap, C.ones_b.ap, sq_.ap[:, kk, :], kk == 0, kk == 3, r=[sq_, C.ones_b], w=[PS[0]])
            k.ts("dve", rsa.ap, PS[0].ap, 1.0 / 512, ALU.mult, EPS, ALU.add, r=[PS[0]], w=[rsa])
            for j in range(4):
                sg_, pz = sigs[j % 2], PS[2 + j % 2]
                for kk in range(4):
                    k.mm(pz.ap, wglu.ap[:, kk, j * 128:(j + 1) * 128], gt_.ap[:, kk, :], kk == 0, kk == 3, r=[wglu, gt_], w=[pz])
                k.act(sg_.ap, pz.ap, AF.Sigmoid, bias=bgl.ap[:, j:j + 1], r=[pz, bgl], w=[sg_])
                k.tt("dve", sT_.ap[:, j, :], gt_.ap[:, j, :], sg_.ap, ALU.mult, r=[gt_, sg_], w=[sT_])
            sq2 = sqbs2[bb % 2]
            k.act(sq2.ap, sT_.ap, AF.Square, r=[sT_], w=[sq2])
            for kk in range(4):
                k.mm(PS[1].ap, C.ones_b.ap, sq2.ap[:, kk, :], kk == 0, kk == 3, r=[sq2, C.ones_b], w=[PS[1]])
            k.ts("dve", rss.ap, PS[1].ap, 1.0 / 512, ALU.mult, EPS, ALU.add, r=[PS[1]], w=[rss])
            for r_ in (rsa, rss):
                k.act(r_.ap, r_.ap, AF.Ln, r=[r_], w=[r_])
            for r_ in (rsa, rss):
                k.act(r_.ap, r_.ap, AF.Exp, scale=-0.5, r=[r_], w=[r_])
            k.tt("dve", mx.ap[:, 0:4, :], av, rsa.ap.unsqueeze(1).to_broadcast([128, 4, 512]), ALU.mult, r=[attT, rsa], w=[mx])
            k.tt("pool", mx.ap[:, 4:8, :], sT_.ap, rss.ap.unsqueeze(1).to_broadcast([128, 4, 512]), ALU.mult, r=[sT_, rss], w=[mx])

        def stageW(bb):
            t0 = tok0 + bb * 512
            mx = mixTs[bb % 2]
            for t in range(4):
                ti = t0 // 128 + t
                xt = xb[ti % 3]
                k.dma("sp", xt.ap, x[ti * 128:(ti + 1) * 128, :], w=[xt])
                if pend:
                    pti, pxt = pend.pop()
                    k.dma("pool", hs[pti * 128:(pti + 1) * 128, :], pxt.ap, r=[pxt])
                for h2 in range(2):
                    po = PS[4 + (2 * t + h2) % 4]
                    cs = slice(h2 * 512, (h2 + 1) * 512)
                    for kk in range(8):
                        k.mm(po.ap, mx.ap[:, kk, t * 128:(t + 1) * 128], wout.ap[:, kk, cs], kk == 0, kk == 7, r=[mx, wout], w=[po])
                    k.tt("dve", xt.ap[:, cs], po.ap, xt.ap[:, cs], ALU.add, r=[po, xt], w=[xt])
                pend.append((ti, xt))
        if STOP > 3:
            P.barrier()
            stageN(0)
            for bb in range(4):
                if bb + 1 < 4:
                    stageN(bb + 1)
                stageW(bb)
            P.barrier()
    while pend:
        pti, pxt = pend.pop()
        k.dma("pool", hs[pti * 128:(pti + 1) * 128, :], pxt.ap, r=[pxt])
    A.pop()


def norm_T(C, src_ap, ss, tm, rs, junk, a_, pst, dstT, col0, ncols=1024, r_src=(), cp_eng="act"):
    k = C.k
    nk = ncols // 128
    k.memset("pool", ss.ap, 0.0, w=[ss])
    k.act(junk.ap[:, 0:ncols], src_ap, AF.Square, accum=ss.ap, r=list(r_src) + [ss], w=[junk, ss])
    C.rms_rstd(ss, ncols, tm, rs)
    k.ts("dve", a_.ap[:, 0:ncols], src_ap, rs.ap[:, 0:1], ALU.mult, r=list(r_src) + [rs], w=[a_])
    psv = pst.ap.bitcast(BF16).rearrange("p (a b) -> p a b", a=8)
    for kk in range(nk):
        k.tr(psv[:, kk, :], a_.ap[:, kk * 128:(kk + 1) * 128], C.identb.ap, r=[a_, C.identb], w=[pst])
    k.cp(cp_eng, dstT.ap[:, 0:nk, col0:col0 + 128], psv[:, 0:nk, :], r=[pst], w=[dstT])


def phase_C(C, hs, hs2, g_mlp, w_mlp1, w_mlp2):
    k, A, PS, S = C.k, C.A, C.PS, C.S
    A.push()
    gm = C.gcol(g_mlp[0], 1024, "gmlp")
    w1 = A.alloc([8, 4096], BF16, "w1")
    w2 = A.alloc([32, 1024], BF16, "w2")
    C.load_weight_bf16(w1, w_mlp1[0], 8, 4096, gm, "w1")
    C.load_weight_bf16(w2, w_mlp2[0], 32, 1024, None, "w2")
    hb = [A.alloc([2, 1024], F32, "hc%d" % i) for i in range(3)]
    pendc = []
    junk = A.alloc([1024], F32, "junkc")
    ab = [A.alloc([1024], BF16, "abc%d" % i) for i in range(2)]
    ss = [A.alloc([1], F32, "ssc%d" % i) for i in range(2)]
    tmp = [A.alloc([1], F32, "tmpc%d" % i) for i in range(2)]
    rstd = [A.alloc([1], F32, "rstdc%d" % i) for i in range(2)]
    fT = A.alloc([8, 256], BF16, "fT")
    hidT = A.alloc([32, 256], BF16, "hidT")
    rl = [A.alloc([256], F32, "rl%d" % i) for i in range(2)]
    fTs = [fT, A.alloc([8, 256], BF16, "fT1")]

    def cpre(blk):
        h = hb[blk % 3]
        k.dma("sp", h.ap, hs[blk * 256:(blk + 1) * 256, :].rearrange("(t p) d -> p t d", p=128), w=[h])
        for t in range(2):
            k.memset("pool", ss[t].ap, 0.0, w=[ss[t]])
            k.act(junk.ap, h.ap[:, t, :], AF.Square, accum=ss[t].ap, r=[h, ss[t]], w=[junk, ss[t]])
            C.rms_rstd(ss[t], 1024, tmp[t], rstd[t])
            k.ts("dve", ab[t].ap, h.ap[:, t, :], rstd[t].ap[:, 0:1], ALU.mult, r=[h, rstd[t]], w=[ab[t]])

    def ctr(blk):
        f_ = fTs[blk % 2]
        for t in range(2):
            pst = PS[t]
            psv = pst.ap.bitcast(BF16).rearrange("p (a b) -> p a b", a=8)
            for kk in range(8):
                k.tr(psv[:, kk, :], ab[t].ap[:, kk * 128:(kk + 1) * 128], C.identb.ap, r=[ab[t], C.identb], w=[pst])
            k.cp("act", f_.ap[:, :, t * 128:(t + 1) * 128], psv, r=[pst], w=[f_])

    def cup(blk):
        f_ = fTs[blk % 2]
        for j in range(32):
            pz = PS[2 + j % 2]
            for kk in range(8):
                k.mm(pz.ap[:, 0:256], w1.ap[:, kk, j * 128:(j + 1) * 128], f_.ap[:, kk, :], kk == 0, kk == 7, r=[w1, f_], w=[pz])
            r_ = rl[j % 2]
            k.act(r_.ap, pz.ap[:, 0:256], AF.Relu, r=[pz], w=[r_])
            k.tt("dve" if j % 2 == 0 else "pool", hidT.ap[:, j, :], r_.ap, r_.ap, ALU.mult, r=[r_], w=[hidT])

    def cdown(blk):
        h = hb[blk % 3]
        for t in range(2):
            for half in range(2):
                po = PS[4 + (2 * t + half) % 4]
                for j in range(32):
                    k.mm(po.ap, hidT.ap[:, j, t * 128:(t + 1) * 128], w2.ap[:, j, half * 512:(half + 1) * 512], j == 0, j == 31, r=[hidT, w2], w=[po])
                k.tt("dve", h.ap[:, t, half * 512:(half + 1) * 512], po.ap, h.ap[:, t, half * 512:(half + 1) * 512], ALU.add, r=[po, h], w=[h])
        if pendc:
            pb_, ph_ = pendc.pop()
            k.dma("pool", hs2[pb_ * 256:(pb_ + 1) * 256, :].rearrange("(t p) d -> p t d", p=128), ph_.ap, r=[ph_])
        pendc.append((blk, h))
    NBC = S // 256
    cpre(0)
    ctr(0)
    if NBC > 1:
        cpre(1)
    for blk in range(NBC):
        cup(blk)
        if blk + 1 < NBC:
            ctr(blk + 1)
        cdown(blk)
        if blk + 2 < NBC:
            cpre(blk + 2)
    while pendc:
        pb_, ph_ = pendc.pop()
        k.dma("pool", hs2[pb_ * 256:(pb_ + 1) * 256, :].rearrange("(t p) d -> p t d", p=128), ph_.ap, r=[ph_])
    A.pop()


def phase_D(C, hs2, pin, y, g_ple, w_gate, w_proj, g_final):
    k, A, PS, S, NT = C.k, C.A, C.PS, C.S, C.NT
    A.push()
    gp = C.gcol(g_ple[0], 1024, "gple")
    wg = A.alloc([8, 1024], BF16, "wg")
    wp = A.alloc([2, 1024], BF16, "wp")
    C.load_weight_bf16(wg, w_gate[0], 8, 1024, gp, "wg")
    C.load_weight_bf16(wp, w_proj[0], 2, 1024, None, "wp")
    gfin = A.alloc([1024], F32, "gfin")
    k.dma("sp", gfin.ap, g_final.rearrange("(o d) -> o d", o=1).partition_broadcast(128), w=[gfin])
    hb = [A.alloc([1024], F32, "hd%d" % i) for i in range(6)]
    pb = [A.alloc([256], F32, "pd%d" % i) for i in range(4)]
    junk = A.alloc([1024], F32, "junkd")
    ab = [A.alloc([1024], BF16, "abd%d" % i) for i in range(4)]
    pbb = [A.alloc([256], BF16, "pbb%d" % i) for i in range(4)]
    ss = [A.alloc([1], F32, "ssd%d" % i) for i in range(6)]
    tmp = [A.alloc([1], F32, "tmpd%d" % i) for i in range(6)]
    rstd = [A.alloc([1], F32, "rstdd%d" % i) for i in range(6)]
    eT = [A.alloc([8, 128], BF16, "eT%d" % i) for i in range(3)]
    pT = [A.alloc([2, 128], BF16, "pT%d" % i) for i in range(3)]
    sg = [A.alloc([512], F32, "sg%d" % i) for i in range(2)]
    def s1pre(ti):
        h, p_, pb_, a_ = hb[ti % 6], pb[ti % 4], pbb[ti % 4], ab[ti % 4]
        i = ti % 4
        k.dma("sp", h.ap, hs2[ti * 128:(ti + 1) * 128, :], w=[h])
        k.dma("sp", p_.ap, pin[ti * 128:(ti + 1) * 128, :], w=[p_])
        k.memset("pool", ss[i].ap, 0.0, w=[ss[i]])
        k.act(junk.ap, h.ap, AF.Square, accum=ss[i].ap, r=[h, ss[i]], w=[junk, ss[i]])
        C.rms_rstd(ss[i], 1024, tmp[i], rstd[i])
        k.ts("dve", a_.ap, h.ap, rstd[i].ap[:, 0:1], ALU.mult, r=[h, rstd[i]], w=[a_])
        k.cp("pool", pb_.ap, p_.ap, r=[p_], w=[pb_])

    def s1tr(ti):
        e_, pT_, pb_, a_ = eT[ti % 3], pT[ti % 3], pbb[ti % 4], ab[ti % 4]
        pst = PS[ti % 2]
        psv = pst.ap.bitcast(BF16).rearrange("p (a b) -> p a b", a=8)
        for kk in range(8):
            k.tr(psv[:, kk, :], a_.ap[:, kk * 128:(kk + 1) * 128], C.identb.ap, r=[a_, C.identb], w=[pst])
        k.cp("dve", e_.ap, psv, r=[pst], w=[e_])
        pst2 = PS[2 + ti % 2]
        psv2 = pst2.ap.bitcast(BF16).rearrange("p (a b) -> p a b", a=8)
        for kk in range(2):
            k.tr(psv2[:, kk, :], pb_.ap[:, kk * 128:(kk + 1) * 128], C.identb.ap, r=[pb_, C.identb], w=[pst2])
        k.cp("dve", pT_.ap, psv2[:, 0:2, :], r=[pst2], w=[pT_])

    def stage2a(ti):
        e_, pT_ = eT[ti % 3], pT[ti % 3]
        for half in range(2):
            pg, pp = PS[4 + half], PS[6 + half]
            cs = slice(half * 512, (half + 1) * 512)
            for kk in range(8):
                k.mm(pg.ap, e_.ap[:, kk, :], wg.ap[:, kk, cs], kk == 0, kk == 7, r=[e_, wg], w=[pg])
            for kk in range(2):
                k.mm(pp.ap, pT_.ap[:, kk, :], wp.ap[:, kk, cs], kk == 0, kk == 1, r=[pT_, wp], w=[pp])

    def stage2b(ti):
        h = hb[ti % 6]
        for half in range(2):
            pg, pp, s_ = PS[4 + half], PS[6 + half], sg[half]
            cs = slice(half * 512, (half + 1) * 512)
            k.act(s_.ap, pg.ap, AF.Sigmoid, r=[pg], w=[s_])
            k.tt("dve", s_.ap, pp.ap, s_.ap, ALU.mult, r=[pp, s_], w=[s_])
            k.tt("pool", h.ap[:, cs], h.ap[:, cs], s_.ap, ALU.add, r=[h, s_], w=[h])
        j = 4 + ti % 2
        k.memset("pool", ss[j].ap, 0.0, w=[ss[j]])
        k.act(junk2.ap, h.ap, AF.Square, accum=ss[j].ap, r=[h, ss[j]], w=[junk2, ss[j]])
        C.rms_rstd(ss[j], 1024, tmp[j], rstd[j])
        k.stt(h.ap, h.ap, rstd[j].ap[:, 0:1], gfin.ap, ALU.mult, ALU.mult, r=[h, rstd[j], gfin], w=[h])

    def store(ti):
        k.dma("pool", y[ti * 128:(ti + 1) * 128, :], hb[ti % 6].ap, r=[hb[ti % 6]])

    junk2 = A.alloc([1024], F32, "junkd2")
    s1pre(0)
    s1tr(0)
    s1pre(1)
    s1tr(1)
    s1pre(2)
    for ti in range(NT):
        stage2a(ti)
        if ti + 2 < NT:
            s1tr(ti + 2)
        if ti + 3 < NT:
            s1pre(ti + 3)
        if ti >= 1:
            store(ti - 1)
        stage2b(ti)
    store(NT - 1)
    A.pop()


_NC_CACHE = {}


def kernel(**inputs):
    S = 8192
    xs = [inputs["x_prompt"][i] for i in range(2)] + [inputs["x_sample"][i] for i in range(4)]
    ps = [inputs["p_prompt"][0, i] for i in range(2)] + [inputs["p_sample"][0, i] for i in range(4)]
    xs += [np.zeros_like(xs[0]), np.zeros_like(xs[0])]
    ps += [np.zeros_like(ps[0]), np.zeros_like(ps[0])]
    if "nc" not in _NC_CACHE:
        _NC_CACHE["nc"] = build(S)
    nc = _NC_CACHE["nc"]
    wnames = ["rel_bias", "g_mix", "w_in", "ssm_a_re", "ssm_a_im", "ssm_log_dt", "ssm_b_re", "ssm_b_im", "ssm_c_re",
              "ssm_c_im", "ssm_d", "w_glu", "b_glu", "g_att_out", "g_ssm_out", "w_out", "g_mlp", "w_mlp1", "w_mlp2",
              "g_ple", "w_ple_gate", "w_ple_proj", "g_final"]
    in_maps = []
    for c in range(8):
        m = {"x": np.ascontiguousarray(xs[c], dtype=np.float32), "p": np.ascontiguousarray(ps[c], dtype=np.float32)}
        for n in wnames:
            m[n] = np.ascontiguousarray(inputs[n], dtype=np.float32)
        in_maps.append(m)
    res = run_bass_kernel_spmd(nc, in_maps, core_ids=list(range(8)))
    outs = [np.asarray(r["y"], dtype=np.float32) for r in res.results]
    y_prompt = np.stack(outs[0:2], axis=0)
    y_sample = np.stack(outs[2:6], axis=0)
    return (y_prompt, y_sample)
```
